# Optimizing a Trainium2 kernel written in Bass

```python
import math
import jax
import jax.numpy as jnp
from jax import lax
import numpy as np

D_MODEL = 2048
BATCH = 32
SEQ = 256
DEPTH = 4
DEC_BATCH = 8
DEC_SEQ = 4096
PAST_LEN = 256

GRID_W = 64
HEAD_DIM = 128
QBLOCK = 128
ROPE_BASE = 10000.0
NEG_INF = -1e30
N_MIXERS = 3
A_HEADS = D_MODEL // HEAD_DIM
A_KV_HEADS = A_HEADS // 4
A_GROUP = A_HEADS // A_KV_HEADS
WINDOW = 128
A_SPLIT = (A_HEADS * HEAD_DIM, A_HEADS * HEAD_DIM + A_KV_HEADS * HEAD_DIM, A_HEADS * HEAD_DIM + 2 * A_KV_HEADS * HEAD_DIM)
A_IN = 2 * A_HEADS * HEAD_DIM + 2 * A_KV_HEADS * HEAD_DIM
B_HEADS = D_MODEL // (2 * HEAD_DIM)
B_IN = 4 * B_HEADS * 2 * HEAD_DIM
C_HEADS = D_MODEL // HEAD_DIM
NA_KH = 8
NA_KW = 16
C_IN = 4 * C_HEADS * HEAD_DIM
N_A = len(range(0, DEPTH, N_MIXERS))
N_B = len(range(1, DEPTH, N_MIXERS))
N_C = len(range(2, DEPTH, N_MIXERS))
SCALE = HEAD_DIM ** -0.5

kernel_name = 'hybrid_diffusion_prefix_trunk'


def rms_norm(x, g, eps=1e-6):
    xf = x.astype(jnp.float32)
    y = xf * lax.rsqrt(jnp.mean(xf * xf, axis=-1, keepdims=True) + eps)
    return (y * g.astype(jnp.float32)).astype(x.dtype)


def ada_modulation(cvec, w, b):
    m = (jax.nn.silu(cvec) @ w + b)[:, None, :]
    shift, scale, gate = jnp.split(m, 3, axis=-1)
    return shift, scale, gate


def axial_rope_tables(n_tokens):
    d4 = HEAD_DIM // 4
    t = jnp.arange(n_tokens)
    pos = jnp.stack([t // GRID_W, t % GRID_W], axis=-1).astype(jnp.float32)
    inv_freq = ROPE_BASE ** (-jnp.arange(d4, dtype=jnp.float32) / d4)
    ang = pos[:, :, None] * inv_freq
    return jnp.cos(ang), jnp.sin(ang)


def apply_rope(x, cos, sin):
    n = x.shape[1]
    d4 = x.shape[-1] // 4
    bshape = (n,) + (1,) * (x.ndim - 3) + (2, d4)
    c = cos.reshape(bshape)
    s = sin.reshape(bshape)
    xf = x.astype(jnp.float32).reshape(x.shape[:-1] + (2, 2, d4))
    x1 = xf[..., 0, :]
    x2 = xf[..., 1, :]
    out = jnp.stack([x1 * c - x2 * s, x2 * c + x1 * s], axis=-2)
    return out.reshape(x.shape).astype(x.dtype)


def dense_attention(q, k, v, sink=None):
    b, n, h, d = q.shape
    kv = k.shape[2]
    grp = h // kv
    qb = q.reshape(b, n // QBLOCK, QBLOCK, kv, grp, d).transpose(1, 0, 2, 3, 4, 5)

    def block(qi):
        s = jnp.einsum('bqkgd,blkd->bkgql', qi, k, preferred_element_type=jnp.float32) * SCALE
        if sink is not None:
            sk = jnp.broadcast_to(sink.astype(jnp.float32).reshape(kv, grp, 1, 1), s.shape[:-1] + (1,))
            p = jax.nn.softmax(jnp.concatenate([s, sk], axis=-1), axis=-1)[..., :-1]
        else:
            p = jax.nn.softmax(s, axis=-1)
        return jnp.einsum('bkgql,blkd->bqkgd', p.astype(v.dtype), v)

    o = lax.map(block, qb)
    return o.transpose(1, 0, 2, 3, 4, 5).reshape(b, n, h * d)


def a_project(h, w_in, qg, kg):
    b, n, _ = h.shape
    q, k, v, g = jnp.split(h @ w_in, A_SPLIT, axis=-1)
    q = rms_norm(q.reshape(b, n, A_HEADS, HEAD_DIM), qg)
    k = rms_norm(k.reshape(b, n, A_KV_HEADS, HEAD_DIM), kg)
    v = v.reshape(b, n, A_KV_HEADS, HEAD_DIM)
    return q, k, v, g


def window_attn_ctx(h, w_in, qg, kg, sink):
    q, k, v, g = a_project(h, w_in, qg, kg)
    o = dense_attention(q, k, v, sink)
    return o * jax.nn.silu(g), k, v


def window_attn_lat(h, k_ctx, v_ctx, w_in, qg, kg, sink, cos, sin):
    b, n, _ = h.shape
    nb = n // QBLOCK
    n_ctx = k_ctx.shape[1]
    span = 3 * QBLOCK
    q, k, v, g = a_project(h, w_in, qg, kg)
    q = apply_rope(q, cos, sin)
    k = apply_rope(k, cos, sin)
    pad = ((0, 0), (QBLOCK, QBLOCK), (0, 0), (0, 0))
    kp = jnp.pad(k, pad).reshape(b, nb + 2, QBLOCK, A_KV_HEADS, HEAD_DIM)
    vp = jnp.pad(v, pad).reshape(b, nb + 2, QBLOCK, A_KV_HEADS, HEAD_DIM)
    kband = jnp.concatenate([kp[:, :-2], kp[:, 1:-1], kp[:, 2:]], axis=2).transpose(1, 0, 2, 3, 4)
    vband = jnp.concatenate([vp[:, :-2], vp[:, 1:-1], vp[:, 2:]], axis=2).transpose(1, 0, 2, 3, 4)
    qb = q.reshape(b, nb, QBLOCK, A_KV_HEADS, A_GROUP, HEAD_DIM).transpose(1, 0, 2, 3, 4, 5)
    rel = jnp.arange(span)[None, :] - QBLOCK - jnp.arange(QBLOCK)[:, None]
    in_window = jnp.abs(rel) <= WINDOW
    sk = jnp.broadcast_to(sink.astype(jnp.float32).reshape(A_KV_HEADS, A_GROUP, 1, 1),
                          (b, A_KV_HEADS, A_GROUP, QBLOCK, 1))

    def block(args):
        qi, ki, vi, blk = args
        kpos = (blk - 1) * QBLOCK + jnp.arange(span)
        valid = in_window & ((kpos >= 0) & (kpos < n))[None, :]
        s_loc = jnp.einsum('bqkgd,bjkd->bkgqj', qi, ki, preferred_element_type=jnp.float32) * SCALE
        s_loc = jnp.where(valid, s_loc, NEG_INF)
        s_ctx = jnp.einsum('bqkgd,blkd->bkgql', qi, k_ctx, preferred_element_type=jnp.float32) * SCALE
        p = jax.nn.softmax(jnp.concatenate([s_loc, s_ctx, sk], axis=-1), axis=-1)
        p_loc = p[..., :span].astype(vi.dtype)
        p_ctx = p[..., span:span + n_ctx].astype(v_ctx.dtype)
        return (jnp.einsum('bkgqj,bjkd->bqkgd', p_loc, vi)
                + jnp.einsum('bkgql,blkd->bqkgd', p_ctx, v_ctx))

    o = lax.map(block, (qb, kband, vband, jnp.arange(nb)))
    o = o.transpose(1, 0, 2, 3, 4, 5).reshape(b, n, A_HEADS * HEAD_DIM)
    return o * jax.nn.silu(g)


def b_project(h, w_in, qg, kg):
    b, n, _ = h.shape
    q, k, v, g = jnp.split(h @ w_in, 4, axis=-1)
    q = rms_norm(q.reshape(b, n, B_HEADS, 2, HEAD_DIM), qg)
    k = rms_norm(k.reshape(b, n, B_HEADS, 2, HEAD_DIM), kg)
    v = v.reshape(b, n, B_HEADS, 2 * HEAD_DIM)
    return q, k, v, g


def diff_lambda(lam, lambda_init):
    lam = lam.astype(jnp.float32)
    return jnp.exp(jnp.sum(lam[0] * lam[1])) - jnp.exp(jnp.sum(lam[2] * lam[3])) + lambda_init


def diff_attention(q, k, v, lam_full, subln, lambda_init, g):
    b, n = q.shape[:2]
    qb = q.reshape(b, n // QBLOCK, QBLOCK, B_HEADS, 2, HEAD_DIM).transpose(1, 0, 2, 3, 4, 5)

    def block(qi):
        s = jnp.einsum('bqhmd,bkhmd->bhmqk', qi, k, preferred_element_type=jnp.float32) * SCALE
        p = jax.nn.softmax(s, axis=-1)
        a = p[:, :, 0] - lam_full * p[:, :, 1]
        return jnp.einsum('bhqk,bkhe->bqhe', a.astype(v.dtype), v)

    o = lax.map(block, qb).transpose(1, 0, 2, 3, 4).reshape(b, n, B_HEADS, 2 * HEAD_DIM)
    o = rms_norm(o, subln) * (1.0 - lambda_init)
    return o.reshape(b, n, B_HEADS * 2 * HEAD_DIM) * jax.nn.silu(g)


def diff_attn_ctx(h, w_in, qg, kg, lam, subln, lambda_init):
    q, k, v, g = b_project(h, w_in, qg, kg)
    o = diff_attention(q, k, v, diff_lambda(lam, lambda_init), subln, lambda_init, g)
    return o, k, v


def diff_attn_lat(h, k_ctx, v_ctx, w_in, qg, kg, lam, subln, lambda_init, cos, sin):
    q, k, v, g = b_project(h, w_in, qg, kg)
    q = apply_rope(q, cos, sin)
    k = apply_rope(k, cos, sin)
    k_all = jnp.concatenate([k, k_ctx.astype(k.dtype)], axis=1)
    v_all = jnp.concatenate([v, v_ctx.astype(v.dtype)], axis=1)
    return diff_attention(q, k_all, v_all, diff_lambda(lam, lambda_init), subln, lambda_init, g)


def c_project(h, w_in, qg, kg):
    b, n, _ = h.shape
    q, k, v, g = jnp.split(h @ w_in, 4, axis=-1)
    q = rms_norm(q.reshape(b, n, C_HEADS, HEAD_DIM), qg)
    k = rms_norm(k.reshape(b, n, C_HEADS, HEAD_DIM), kg)
    v = v.reshape(b, n, C_HEADS, HEAD_DIM)
    return q, k, v, g


def na_ctx(h, w_in, qg, kg):
    q, k, v, g = c_project(h, w_in, qg, kg)
    return dense_attention(q, k, v) * jax.nn.silu(g), k, v


def na_lat(h, k_ctx, v_ctx, w_in, qg, kg, rpb):
    b, n, _ = h.shape
    rows = n // GRID_W
    kh = min(NA_KH, rows)
    q, k, v, g = c_project(h, w_in, qg, kg)
    q_rows = q.reshape(b, rows, GRID_W, C_HEADS, HEAD_DIM).transpose(1, 0, 2, 3, 4)
    k_grid = k.reshape(b, rows, GRID_W, C_HEADS, HEAD_DIM)
    v_grid = v.reshape(b, rows, GRID_W, C_HEADS, HEAD_DIM)
    cols = jnp.arange(GRID_W)
    col_start = jnp.clip(cols - NA_KW // 2, 0, GRID_W - NA_KW)
    col_idx = col_start[:, None] + jnp.arange(NA_KW)
    col_bias_idx = col_idx - cols[:, None] + NA_KW - 1
    n_loc = kh * NA_KW

    def row_block(args):
        qr, r = args
        rs = jnp.clip(r - kh // 2, 0, rows - kh)
        kb = lax.dynamic_slice_in_dim(k_grid, rs, kh, axis=1)
        vb = lax.dynamic_slice_in_dim(v_grid, rs, kh, axis=1)
        kw = kb[:, :, col_idx]
        vw = vb[:, :, col_idx]
        row_bias_idx = rs + jnp.arange(kh) - r + NA_KH - 1
        bias = rpb[:, row_bias_idx[None, :, None], col_bias_idx[:, None, :]]
        s_loc = jnp.einsum('bchd,brcwhd->bhcrw', qr, kw, preferred_element_type=jnp.float32) * SCALE
        s_loc = (s_loc + bias.astype(jnp.float32)).reshape(b, C_HEADS, GRID_W, n_loc)
        s_ctx = jnp.einsum('bchd,blhd->bhcl', qr, k_ctx, preferred_element_type=jnp.float32) * SCALE
        p = jax.nn.softmax(jnp.concatenate([s_loc, s_ctx], axis=-1), axis=-1)
        p_loc = p[..., :n_loc].reshape(b, C_HEADS, GRID_W, kh, NA_KW).astype(vw.dtype)
        p_ctx = p[..., n_loc:].astype(v_ctx.dtype)
        return (jnp.einsum('bhcrw,brcwhd->bchd', p_loc, vw)
                + jnp.einsum('bhcl,blhd->bchd', p_ctx, v_ctx))

    o = lax.map(row_block, (q_rows, jnp.arange(rows)))
    o = o.transpose(1, 0, 2, 3, 4).reshape(b, n, C_HEADS * HEAD_DIM)
    return o * jax.nn.silu(g)


def setup_inputs(seed: int = 0) -> dict:
    key = jax.random.key(seed)
    ks = jax.random.split(key, 23)
    d = D_MODEL

    def nrm(i, shape, std):
        return std * jax.random.normal(ks[i], shape, jnp.float32)

    w_std = d ** -0.5
    return {
        'x_prompt': nrm(0, (BATCH, SEQ, d), 1.0),
        'x_sample': nrm(1, (DEC_BATCH, DEC_SEQ, d), 1.0),
        'cache_a_k': nrm(2, (DEC_BATCH, N_A, PAST_LEN, A_KV_HEADS, HEAD_DIM), 1.0),
        'cache_a_v': nrm(3, (DEC_BATCH, N_A, PAST_LEN, A_KV_HEADS, HEAD_DIM), 1.0),
        'cache_b_k': nrm(4, (DEC_BATCH, N_B, PAST_LEN, B_HEADS, 2, HEAD_DIM), 1.0),
        'cache_b_v': nrm(5, (DEC_BATCH, N_B, PAST_LEN, B_HEADS, 2 * HEAD_DIM), 1.0),
        'cache_c_k': nrm(6, (DEC_BATCH, N_C, PAST_LEN, C_HEADS, HEAD_DIM), 1.0),
        'cache_c_v': nrm(7, (DEC_BATCH, N_C, PAST_LEN, C_HEADS, HEAD_DIM), 1.0),
        'c': nrm(8, (DEC_BATCH, d), 1.0),
        'c_ctx': nrm(9, (d,), 1.0),
        'ln_g': 1.0 + nrm(10, (DEPTH, d), 0.02),
        'ada_w': nrm(11, (DEPTH, d, 3 * d), 0.5 * w_std),
        'ada_b': nrm(12, (DEPTH, 3 * d), 0.02),
        'w_out': nrm(13, (DEPTH, d, d), w_std),
        'qn_g': 1.0 + nrm(14, (DEPTH, HEAD_DIM), 0.02),
        'kn_g': 1.0 + nrm(15, (DEPTH, HEAD_DIM), 0.02),
        'w_in_a': nrm(16, (N_A, d, A_IN), w_std),
        'sink_a': nrm(17, (N_A, A_HEADS), 0.5),
        'w_in_b': nrm(18, (N_B, d, B_IN), w_std),
        'lam_b': nrm(19, (N_B, 4, HEAD_DIM), 0.1),
        'subln_b': 1.0 + nrm(20, (N_B, 2 * HEAD_DIM), 0.02),
        'w_in_c': nrm(21, (N_C, d, C_IN), w_std),
        'rpb_c': nrm(22, (N_C, C_HEADS, 2 * NA_KH - 1, 2 * NA_KW - 1), 0.5),
    }


def reference(x_prompt, x_sample, cache_a_k, cache_a_v, cache_b_k, cache_b_v, cache_c_k, cache_c_v,
              c, c_ctx, ln_g, ada_w, ada_b, w_out, qn_g, kn_g, w_in_a, sink_a, w_in_b, lam_b,
              subln_b, w_in_c, rpb_c):
    cos, sin = axial_rope_tables(x_sample.shape[1])
    xp = x_prompt
    xs = x_sample
    a_k, a_v, b_k, b_v, c_k, c_v = [], [], [], [], [], []
    for l in range(DEPTH):
        kind = l % N_MIXERS
        j = l // N_MIXERS
        sh_p, sc_p, gt_p = ada_modulation(c_ctx[None, :], ada_w[l], ada_b[l])
        sh_s, sc_s, gt_s = ada_modulation(c, ada_w[l], ada_b[l])
        hp = rms_norm(xp, ln_g[l]) * (1.0 + sc_p) + sh_p
        hs = rms_norm(xs, ln_g[l]) * (1.0 + sc_s) + sh_s
        if kind == 0:
            op, kp, vp = window_attn_ctx(hp, w_in_a[j], qn_g[l], kn_g[l], sink_a[j])
            os_ = window_attn_lat(hs, cache_a_k[:, j], cache_a_v[:, j], w_in_a[j], qn_g[l], kn_g[l],
                                  sink_a[j], cos, sin)
            a_k.append(kp)
            a_v.append(vp)
        elif kind == 1:
            lambda_init = 0.8 - 0.6 * math.exp(-0.3 * l)
            op, kp, vp = diff_attn_ctx(hp, w_in_b[j], qn_g[l], kn_g[l], lam_b[j], subln_b[j], lambda_init)
            os_ = diff_attn_lat(hs, cache_b_k[:, j], cache_b_v[:, j], w_in_b[j], qn_g[l], kn_g[l],
                                lam_b[j], subln_b[j], lambda_init, cos, sin)
            b_k.append(kp)
            b_v.append(vp)
        else:
            op, kp, vp = na_ctx(hp, w_in_c[j], qn_g[l], kn_g[l])
            os_ = na_lat(hs, cache_c_k[:, j], cache_c_v[:, j], w_in_c[j], qn_g[l], kn_g[l], rpb_c[j])
            c_k.append(kp)
            c_v.append(vp)
        xp = xp + gt_p * (op @ w_out[l])
        xs = xs + gt_s * (os_ @ w_out[l])
    return (xp, xs, jnp.stack(a_k, axis=1), jnp.stack(a_v, axis=1), jnp.stack(b_k, axis=1),
            jnp.stack(b_v, axis=1), jnp.stack(c_k, axis=1), jnp.stack(c_v, axis=1))
```

```python
import math
import numpy as np
import concourse.bass as bass
import concourse.mybir as mybir
from concourse.bass_utils import run_bass_kernel_spmd

F32 = mybir.dt.float32
BF16 = mybir.dt.bfloat16
I32 = mybir.dt.int32
AF = mybir.ActivationFunctionType
ALU = mybir.AluOpType
AX = mybir.AxisListType

D = 2048
NTOK = 5120
NS = 4096
TT = 1024
HD = 128
SCALE = HD ** -0.5
EPS = 1e-6
DEPTH = 4
KINDS = [0, 1, 2, 0]
FIN = [5120, 8192, 8192, 5120]
LAMBDA_INIT = [0.8 - 0.6 * math.exp(-0.3 * l) for l in range(DEPTH)]


class Buf:
    __slots__ = ("name", "w", "r", "excl")

    def __init__(self, name):
        self.name = name
        self.w = None
        self.r = {}
        self.excl = False


class DBuf:
    __slots__ = ("name", "writers", "readers", "prev_readers")

    def __init__(self, name):
        self.name = name
        self.writers = {}
        self.readers = {}
        self.prev_readers = {}


class Op:
    __slots__ = ("eng", "fn", "deps", "needs_inc", "dma")

    def __init__(self, eng, fn, deps, dma=None):
        self.eng = eng
        self.fn = fn
        self.deps = deps
        self.needs_inc = False
        self.dma = dma


ENGS = ("pe", "act", "dve", "pool", "sp")


def _add(dd, tok):
    key = (tok[0], tok[1])
    if dd.get(key, -1) < tok[2]:
        dd[key] = tok[2]


class Prog:
    def __init__(self):
        self.ops = {e: [] for e in ENGS}
        self.names = {}
        self.dpool = []
        self.kind_idxs = {'hw': [], 'sw': []}
        self.used = {'hw': 0, 'sw': 0}

    def _deps_for(self, reads, writes):
        deps = {}
        for b in reads:
            if isinstance(b, DBuf):
                for k, v in b.writers.items():
                    _add(deps, (k[0], k[1], v))
            else:
                if b.w is not None:
                    _add(deps, b.w)
                if b.excl:
                    for k, v in b.r.items():
                        _add(deps, (k[0], k[1], v))
        for b in writes:
            if isinstance(b, DBuf):
                if b.readers:
                    b.prev_readers = b.readers
                    b.readers = {}
                    b.writers = {}
                for k, v in b.prev_readers.items():
                    _add(deps, (k[0], k[1], v))
            else:
                if b.w is not None:
                    _add(deps, b.w)
                for k, v in b.r.items():
                    _add(deps, (k[0], k[1], v))
        return deps

    def _mark(self, tok, reads, writes):
        for b in reads:
            if isinstance(b, DBuf):
                _add(b.readers, tok)
            else:
                _add(b.r, tok)
        for b in writes:
            if isinstance(b, DBuf):
                _add(b.writers, tok)
            else:
                b.w = tok
                b.r = {}

    def op(self, eng, fn, reads=(), writes=()):
        deps = self._deps_for(reads, writes)
        if eng == "pe":
            deps.pop(("e", "pe"), None)
        lst = self.ops[eng]
        tok = ("e", eng, len(lst))
        lst.append(Op(eng, fn, deps))
        self._mark(tok, reads, writes)
        return tok

    def dma(self, eng, out, in_, reads, writes, owner, **kw):
        deps = self._deps_for(reads, writes)
        kind = "sw" if eng == "pool" else "hw"
        idx = self.names.get((owner.name, kind))
        if idx is None:
            k = self.used[kind]
            if k < len(self.kind_idxs[kind]):
                idx = self.kind_idxs[kind][k]
            else:
                idx = len(self.dpool)
                self.dpool.append(0)
                self.kind_idxs[kind].append(idx)
            self.used[kind] += 1
            self.names[(owner.name, kind)] = idx
        self.dpool[idx] += 16
        ent = (idx, self.dpool[idx])
        tok = ("d", ent[0], ent[1])

        def fn(e, out=out, in_=in_, kw=kw):
            return e.dma_start(out=out, in_=in_, **kw)

        self.ops[eng].append(Op(eng, fn, deps, dma=ent[0]))
        self._mark(tok, reads, writes)
        return tok

    def barrier(self):
        deps = {}
        for e in ENGS:
            for i in range(len(self.ops[e]) - 1, -1, -1):
                o = self.ops[e][i]
                if o.dma is None and o.fn is not None:
                    deps[("e", e)] = i
                    break
        for idx, cnt in enumerate(self.dpool):
            if cnt:
                deps[("d", idx)] = cnt
        for e in ENGS:
            self.ops[e].append(Op(e, None, dict(deps)))
        self.names = {}
        self.used = {'hw': 0, 'sw': 0}

    def emit(self, nc, stack):
        for e in ENGS:
            for o in self.ops[e]:
                for k, v in o.deps.items():
                    if k[0] == "e":
                        self.ops[k[1]][v].needs_inc = True
        inc_count = {}
        for e in ENGS:
            c = 0
            arr = []
            for o in self.ops[e]:
                if o.needs_inc:
                    c += 1
                arr.append(c)
            inc_count[e] = arr
        esem = {e: stack.enter_context(nc.semaphore("es_" + e)) for e in ENGS if e != "sp"}
        dsem = [stack.enter_context(nc.semaphore("ds%d" % i)) for i in range(len(self.dpool))]
        self.n_sems = len(esem) + len(dsem)
        block = stack.enter_context(nc.Block())
        self.stats = {e: [0, 0] for e in ENGS}

        def run(e, eng):
            known = {}
            st = self.stats[e]
            for o in self.ops[e]:
                for k, v in o.deps.items():
                    if k[0] == "e":
                        val = inc_count[k[1]][v]
                        sem = esem[k[1]]
                    else:
                        val = v
                        sem = dsem[k[1]]
                    if known.get(k, 0) >= val:
                        continue
                    known[k] = val
                    eng.wait_ge(sem, val)
                    st[1] += 1
                if o.fn is None:
                    continue
                ins = o.fn(eng)
                st[0] += 1
                if o.dma is not None:
                    ins.then_inc(dsem[o.dma], 16)
                elif o.needs_inc:
                    ins.then_inc(esem[e], 1)

        @block.tensor
        def _(t):
            run("pe", t)

        @block.scalar
        def _(a):
            run("act", a)

        @block.vector
        def _(v):
            run("dve", v)

        @block.gpsimd
        def _(g):
            run("pool", g)

        @block.sync
        def _(s):
            run("sp", s)


class Arena:
    def __init__(self, nc, name, nbytes):
        self.t = nc.alloc_sbuf_tensor(name, [128, nbytes // 4], F32)
        self.cap = nbytes
        self.off = 0

    def reset(self):
        self.off = 0

    def alloc(self, free_shape, dtype):
        es = 2 if dtype == BF16 else 4
        n = 1
        for s in free_shape:
            n *= s
        nb = (n * es + 31) // 32 * 32
        assert self.off + nb <= self.cap, ("arena overflow", self.off, nb, self.cap)
        v = self.t[:, self.off // 4:(self.off + nb) // 4]
        self.off += nb
        if dtype != F32:
            v = v.bitcast(dtype)
        v = v[:, 0:n]
        if len(free_shape) == 2:
            v = v.rearrange("p (a b) -> p a b", b=free_shape[1])
        elif len(free_shape) == 3:
            v = v.rearrange("p (a b c) -> p a b c", b=free_shape[1], c=free_shape[2])
        elif len(free_shape) == 4:
            v = v.rearrange("p (a b c d) -> p a b c d", b=free_shape[1], c=free_shape[2], d=free_shape[3])
        return v


class T:
    __slots__ = ("ap", "b")

    def __init__(self, ap, name):
        self.ap = ap
        self.b = Buf(name)


def dbc(row, nparts):
    n = row.shape[-1]
    return bass.AP(tensor=row.tensor, offset=row.offset, ap=[[0, nparts], [1, n]])


def bc_last(ap2d, n):
    return ap2d.unsqueeze(2).broadcast_to([ap2d.shape[0], ap2d.shape[1], n])


def bc_mid(ap2d, n):
    return ap2d.unsqueeze(1).broadcast_to([ap2d.shape[0], n, ap2d.shape[1]])


def build(n_layers=DEPTH, stop_phase=None, dbg=0):
    nc = bass.Bass("TRN2", target_bir_lowering=False)
    P = Prog()

    def din(name, shape):
        return nc.dram_tensor(name, list(shape), F32, kind="ExternalInput").ap()

    def dout(name, shape):
        return nc.dram_tensor(name, list(shape), F32, kind="ExternalOutput").ap()

    x_in = din("x", [NTOK, D])
    cpair = din("cpair", [2, D])
    ln_g = din("ln_g", [DEPTH, D])
    ada_w = din("ada_w", [n_layers, D, 3 * D])
    ada_b = din("ada_b", [DEPTH, 3 * D])
    w_out = din("w_out", [n_layers, D, D])
    qn_g = din("qn_g", [DEPTH, HD])
    kn_g = din("kn_g", [DEPTH, HD])
    w_in_a = din("w_in_a", [2 if n_layers > 3 else 1, D, 5120])
    w_in_b = din("w_in_b", [1, D, 8192] if n_layers > 1 else [1, 8, 8192])
    w_in_c = din("w_in_c", [1, D, 8192] if n_layers > 2 else [1, 8, 8192])
    sink_a = din("sink_a", [2, 16])
    lam_b = din("lam_b", [1, 4 * HD])
    subln_b = din("subln_b", [1, 256])
    rpbt = din("rpbt", [16, 64, 15, 64])
    cak = din("cak", [2, 256, 4 * HD])
    cav = din("cav", [2, 256, 4 * HD])
    cbk = din("cbk", [1, 256, 16 * HD])
    cbv = din("cbv", [1, 256, 8 * 256])
    cck = din("cck", [1, 256, 16 * HD])
    ccv = din("ccv", [1, 256, 16 * HD])
    y_out = dout("y", [NTOK, D])
    nak = dout("nak", [4, 2, 256, 4 * HD])
    nav = dout("nav", [4, 2, 256, 4 * HD])
    nbk = dout("nbk", [4, 1, 256, 16 * HD])
    nbv = dout("nbv", [4, 1, 256, 8 * 256])
    nck = dout("nck", [4, 1, 256, 16 * HD])
    ncv = dout("ncv", [4, 1, 256, 16 * HD])
    XA = nc.dram_tensor("XA", [NTOK, D], F32).ap()
    XB = nc.dram_tensor("XB", [NTOK, D], F32).ap()
    MOD = nc.dram_tensor("MOD", [DEPTH, 2, 3 * D], F32).ap()
    QT = nc.dram_tensor("QT", [16, 128, NTOK], BF16).ap()
    KT = nc.dram_tensor("KT", [16, 128, NTOK], BF16).ap()
    VS = nc.dram_tensor("VS", [NTOK, D], BF16).ap()
    GT = nc.dram_tensor("GT", [16, 128, NTOK], BF16).ap()
    OT = nc.dram_tensor("OT", [16, 128, NTOK], BF16).ap()
    WIB = [nc.dram_tensor("WIB%d" % l, [D, FIN[l]], BF16).ap() for l in range(DEPTH)]
    WOB = [nc.dram_tensor("WOB%d" % l, [D, D], BF16).ap() for l in range(DEPTH)]
    w_in_src = [w_in_a[0], w_in_b[0], w_in_c[0], w_in_a[1 if n_layers > 3 else 0]]

    d_x = [DBuf("x_in"), DBuf("XA"), DBuf("XB"), DBuf("XA"), DBuf("y")]
    d_x[3] = d_x[1]
    x_aps = [x_in, XA, XB, XA, y_out]
    d_mod = DBuf("MOD")
    d_qt, d_kt, d_vs, d_gt, d_ot = DBuf("QT"), DBuf("KT"), DBuf("VS"), DBuf("GT"), DBuf("OT")
    d_wib = [DBuf("WIB%d" % l) for l in range(DEPTH)]
    d_wob = [DBuf("WOB%d" % l) for l in range(DEPTH)]
    d_outs = DBuf("outs")
    d_ro = DBuf("ro")

    ar = Arena(nc, "arena", 180 * 1024)
    car = Arena(nc, "consts", 20 * 1024)
    banks = [nc.alloc_psum_tensor("bank%d" % i, [128, 512], F32) for i in range(8)]
    pb = [T(banks[i][:], "bank%d" % i) for i in range(8)]
    for t_ in pb:
        t_.b.excl = True

    def mk(arena, free_shape, dtype, name):
        t = T(arena.alloc(free_shape, dtype), name)
        return t

    ident = mk(car, [128], BF16, "ident")
    ones = mk(car, [128], BF16, "ones")
    cosT = mk(car, [32, 2, 32], F32, "cosT")
    sinT = mk(car, [32, 2, 32], F32, "sinT")
    gq = mk(car, [HD], F32, "gq")
    gk = mk(car, [HD], F32, "gk")

    def build_consts():
        P.op("pool", lambda g: g.memset(ident.ap, 0.0), writes=[ident.b])
        P.op("pool", lambda g: g.affine_select(out=ident.ap, in_=ident.ap, compare_op=ALU.not_equal, fill=1.0,
                                               base=0, pattern=[[-1, 128]], channel_multiplier=1),
             reads=[ident.b], writes=[ident.b])
        P.op("pool", lambda g: g.memset(ones.ap, 1.0), writes=[ones.b])
        ar.reset()
        posr = mk(ar, [32], I32, "posr")
        posc = mk(ar, [1], I32, "posc")
        fi = mk(ar, [32], I32, "fi")
        posrf = mk(ar, [32], F32, "posrf")
        poscf = mk(ar, [1], F32, "poscf")
        ff = mk(ar, [32], F32, "ff")
        invf = mk(ar, [32], F32, "invf")
        ang = mk(ar, [32, 2, 32], F32, "ang")
        tmp = mk(ar, [32, 2, 32], F32, "angt")
        tmpi = mk(ar, [32, 2, 32], I32, "angi")
        for h0, base in ((0, 0), (64, 1)):
            P.op("pool", lambda g, h0=h0, base=base: g.iota(posr.ap[h0:h0 + 64, :], pattern=[[2, 32]], base=base,
                                                            channel_multiplier=0), writes=[posr.b])
            P.op("pool", lambda g, h0=h0: g.iota(posc.ap[h0:h0 + 64, :], pattern=[[0, 1]], base=0,
                                                 channel_multiplier=1), writes=[posc.b])
        P.op("pool", lambda g: g.iota(fi.ap, pattern=[[1, 32]], base=0, channel_multiplier=0), writes=[fi.b])
        P.op("dve", lambda v: v.tensor_copy(out=posrf.ap, in_=posr.ap), reads=[posr.b], writes=[posrf.b])
        P.op("dve", lambda v: v.tensor_copy(out=poscf.ap, in_=posc.ap), reads=[posc.b], writes=[poscf.b])
        P.op("dve", lambda v: v.tensor_copy(out=ff.ap, in_=fi.ap), reads=[fi.b], writes=[ff.b])
        P.op("act", lambda a: a.activation(out=invf.ap, in_=ff.ap, func=AF.Exp, scale=-math.log(10000.0) / 32.0),
             reads=[ff.b], writes=[invf.b])
        P.op("dve", lambda v: v.tensor_tensor(out=ang.ap[:, :, 0, :], in0=bc_last(posrf.ap, 32), in1=bc_mid(invf.ap, 32),
                                              op=ALU.mult), reads=[posrf.b, invf.b], writes=[ang.b])
        P.op("dve", lambda v: v.tensor_scalar(out=ang.ap[:, :, 1, :], in0=bc_mid(invf.ap, 32), scalar1=poscf.ap[:, 0:1],
                                              scalar2=None, op0=ALU.mult), reads=[poscf.b, invf.b, ang.b], writes=[ang.b])
        TWO_PI = 2.0 * math.pi
        for dst, shift in ((sinT, 0.0), (cosT, math.pi / 2)):
            P.op("dve", lambda v, shift=shift: v.tensor_scalar(out=tmp.ap, in0=ang.ap, scalar1=shift, scalar2=1.0 / TWO_PI,
                                                               op0=ALU.add, op1=ALU.mult), reads=[ang.b], writes=[tmp.b])
            P.op("dve", lambda v: v.tensor_copy(out=tmpi.ap, in_=tmp.ap), reads=[tmp.b], writes=[tmpi.b])
            P.op("dve", lambda v: v.tensor_copy(out=tmp.ap, in_=tmpi.ap), reads=[tmpi.b], writes=[tmp.b])
            P.op("dve", lambda v: v.scalar_tensor_tensor(out=tmp.ap, in0=tmp.ap, scalar=-TWO_PI, in1=ang.ap,
                                                         op0=ALU.mult, op1=ALU.add), reads=[tmp.b, ang.b], writes=[tmp.b])
            P.op("dve", lambda v, shift=shift: v.tensor_scalar(out=tmp.ap, in0=tmp.ap, scalar1=shift, scalar2=3.1415925,
                                                               op0=ALU.add, op1=ALU.min), reads=[tmp.b], writes=[tmp.b])
            P.op("dve", lambda v: v.tensor_scalar(out=tmp.ap, in0=tmp.ap, scalar1=-3.1415925, scalar2=None,
                                                  op0=ALU.max), reads=[tmp.b], writes=[tmp.b])
            P.op("act", lambda a, dst=dst: a.activation(out=dst.ap, in_=tmp.ap, func=AF.Sin), reads=[tmp.b], writes=[dst.b])

    castb = Buf("castsem")

    def cast_weights(l):
        src = w_in_src[l]
        F = FIN[l]
        for r0 in range(0, D, 256):
            P.dma("pool", WIB[l][r0:r0 + 256, :].rearrange("r (a b) -> r a b", b=1024),
                  src[r0:r0 + 256, :].rearrange("r (a b) -> r a b", b=1024),
                  reads=[d_ro], writes=[d_wib[l]], owner=castb)
        for r0 in range(0, D, 512):
            P.dma("pool", WOB[l][r0:r0 + 512, :].rearrange("r (a b) -> r a b", b=1024),
                  w_out[l, r0:r0 + 512, :].rearrange("r (a b) -> r a b", b=1024),
                  reads=[d_ro], writes=[d_wob[l]], owner=castb)

    def phase0():
        ar.reset()
        cT = mk(ar, [16, 2], F32, "cT")
        sT = mk(ar, [16, 2], F32, "sT")
        sg = mk(ar, [16, 2], F32, "sg")
        wt = [mk(ar, [16, 512], F32, "adaw%d" % i) for i in range(2)]
        adab = T(ar.alloc([3 * D], F32)[0:2, :], "adab")
        msb = T(ar.alloc([3 * D], F32)[0:2, :], "msb")
        for r in range(2):
            P.dma("sp", cT.ap[:, :, r], cpair[r].rearrange("(c p) -> p c", p=128), reads=[d_ro], writes=[cT.b],
                  owner=cT.b, allow_slow_non_contiguous=True)
        P.op("act", lambda a: a.activation(out=sg.ap, in_=cT.ap, func=AF.Exp, scale=-1.0), reads=[cT.b], writes=[sg.b])
        P.op("dve", lambda v: v.tensor_scalar(out=sg.ap, in0=sg.ap, scalar1=1.0, scalar2=None, op0=ALU.add),
             reads=[sg.b], writes=[sg.b])
        P.op("dve", lambda v: v.reciprocal(out=sg.ap, in_=sg.ap), reads=[sg.b], writes=[sg.b])
        P.op("dve", lambda v: v.tensor_tensor(out=sT.ap, in0=cT.ap, in1=sg.ap, op=ALU.mult), reads=[cT.b, sg.b],
             writes=[sT.b])
        i = 0
        for l in range(n_layers):
            P.dma("sp", adab.ap, dbc(ada_b[l:l + 1, :], 2), reads=[d_ro], writes=[adab.b], owner=adab.b)
            for fb in range(12):
                w = wt[i % 2]
                P.dma("sp", w.ap, ada_w[l, :, fb * 512:(fb + 1) * 512].rearrange("(c p) f -> p c f", p=128),
                      reads=[d_ro], writes=[w.b], owner=w.b)
                ps = pb[i % 2]

                def mm(t, w=w, ps=ps):
                    for c in range(16):
                        ins = t.matmul(ps.ap[0:2, :], lhsT=sT.ap[:, c, :], rhs=w.ap[:, c, :], start=(c == 0), stop=(c == 15))
                    return ins
                P.op("pe", mm, reads=[sT.b, w.b], writes=[ps.b])
                P.op("dve", lambda v, ps=ps, fb=fb: v.tensor_tensor(out=msb.ap[:, fb * 512:(fb + 1) * 512], in0=ps.ap[0:2, :],
                                                                    in1=adab.ap[:, fb * 512:(fb + 1) * 512], op=ALU.add),
                     reads=[ps.b, adab.b], writes=[msb.b])
                i += 1
            P.dma("sp", MOD[l], msb.ap, reads=[msb.b], writes=[d_mod], owner=msb.b)

    def layer_cfg(l):
        kind = KINDS[l]
        j = l // 3
        if kind == 0:
            return dict(kind=0, j=j, nq=16, nk=4, vcols=512, F=5120, knew=nak, vnew=nav, kcols=512)
        if kind == 1:
            return dict(kind=1, j=j, nq=16, nk=16, vcols=2048, F=8192, knew=nbk, vnew=nbv, kcols=2048)
        return dict(kind=2, j=j, nq=16, nk=16, vcols=2048, F=8192, knew=nck, vnew=ncv, kcols=2048)

    def phase1(l):
        cfg = layer_cfg(l)
        nq, nk, vcols, F = cfg["nq"], cfg["nk"], cfg["vcols"], cfg["F"]
        xin, dxin = x_aps[l], d_x[l]
        ar.reset()
        mod1 = mk(ar, [D], F32, "mod1")
        sh = mk(ar, [D], F32, "sh")
        xt = [mk(ar, [D], F32, "xt%d" % i) for i in range(2)]
        tmpf = mk(ar, [D], F32, "tmpf")
        lng = tmpf
        hb = [mk(ar, [D], BF16, "hb%d" % i) for i in range(2)]
        hT = mk(ar, [16, TT], BF16, "hT")
        wt = [mk(ar, [16, 512], BF16, "wt%d" % i) for i in range(3)]
        ssx = [mk(ar, [1], F32, "ssx%d" % i) for i in range(2)]
        rsx = [mk(ar, [1], F32, "rsx%d" % i) for i in range(2)]
        junk = mk(ar, [D], BF16, "junk")
        ss4 = [mk(ar, [4], F32, "ss4%d" % i) for i in range(2)]
        rs4 = [mk(ar, [4], F32, "rs4%d" % i) for i in range(2)]
        yq = [mk(ar, [4, HD], F32, "yq%d" % i) for i in range(2)]
        rt = [mk(ar, [4, 2, 32], F32, "rt%d" % i) for i in range(4)]
        ob = [mk(ar, [4, HD], BF16, "ob%d" % i) for i in range(2)]
        qTs = [mk(ar, [4, TT], BF16, "qTs%d" % i) for i in range(2)]
        vst = [mk(ar, [512], BF16, "vst%d" % i) for i in range(2)]
        vsf = [mk(ar, [512], F32, "vsf%d" % i) for i in range(2)]
        gst = [mk(ar, [512], BF16, "gst%d" % i) for i in range(2)]
        hps = pb[0:2]
        mps = pb[2:5]
        tps = pb[5:7]

        P.dma("sp", gq.ap, dbc(qn_g[l:l + 1, :], 128), reads=[d_ro], writes=[gq.b], owner=gq.b)
        P.dma("sp", gk.ap, dbc(kn_g[l:l + 1, :], 128), reads=[d_ro], writes=[gk.b], owner=gk.b)

        cnt = dict(x=0, w=0, m=0, q=0, v=0, g=0, t=0, qs=0)

        def load_mod(r):
            P.dma("sp", lng.ap, dbc(ln_g[l:l + 1, :], 128), reads=[d_ro], writes=[lng.b], owner=lng.b)
            P.dma("sp", sh.ap, dbc(MOD[l, r:r + 1, 0:D], 128), reads=[d_mod], writes=[sh.b], owner=sh.b)
            P.dma("sp", mod1.ap, dbc(MOD[l, r:r + 1, D:2 * D], 128), reads=[d_mod], writes=[mod1.b],
                  owner=mod1.b)
            P.op("dve", lambda v: v.scalar_tensor_tensor(out=mod1.ap, in0=mod1.ap, scalar=1.0, in1=lng.ap, op0=ALU.add,
                                                         op1=ALU.mult), reads=[mod1.b, lng.b], writes=[mod1.b])

        def load_w(fb):
            w = wt[cnt["w"] % 3]
            cnt["w"] += 1
            P.dma("sp", w.ap, WIB[l][:, fb * 512:(fb + 1) * 512].rearrange("(c p) f -> p c f", p=128),
                  reads=[d_wib[l]], writes=[w.b], owner=w.b)
            return w

        def norm_block(tt, tb):
            t0 = tt * TT + tb * 128
            x = xt[cnt["x"] % 2]
            h = hb[cnt["x"] % 2]
            s1 = ssx[cnt["x"] % 2]
            r1 = rsx[cnt["x"] % 2]
            cnt["x"] += 1
            P.dma("sp", x.ap, xin[t0:t0 + 128, :], reads=[dxin], writes=[x.b], owner=x.b)
            P.op("act", lambda a: a.activation(out=junk.ap, in_=x.ap, func=AF.Square, accum_out=s1.ap[:, 0:1]),
                 reads=[x.b], writes=[s1.b])
            P.op("act", lambda a: a.activation(out=s1.ap, in_=s1.ap, func=AF.Sqrt, scale=1.0 / D, bias=EPS),
                 reads=[s1.b], writes=[s1.b])
            P.op("dve", lambda v: v.reciprocal(out=r1.ap, in_=s1.ap), reads=[s1.b], writes=[r1.b])
            P.op("dve", lambda v: v.scalar_tensor_tensor(out=tmpf.ap, in0=x.ap, scalar=r1.ap[:, 0:1], in1=mod1.ap,
                                                         op0=ALU.mult, op1=ALU.mult), reads=[x.b, r1.b, mod1.b],
                 writes=[tmpf.b])
            P.op("pool", lambda g: g.tensor_tensor(out=h.ap, in0=tmpf.ap, in1=sh.ap, op=ALU.add), reads=[tmpf.b, sh.b],
                 writes=[h.b])
            hv = [hps[i].ap.bitcast(BF16).rearrange("p (c t) -> p c t", t=128) for i in range(2)]

            def tr(t):
                for c in range(16):
                    ins = t.transpose(hv[c // 8][:, c % 8, :], h.ap[:, c * 128:(c + 1) * 128], ident.ap)
                return ins
            P.op("pe", tr, reads=[h.b, ident.b], writes=[hps[0].b, hps[1].b])
            P.op("act", lambda a: a.activation(out=hT.ap[:, 0:8, tb * 128:(tb + 1) * 128], in_=hv[0], func=AF.Copy),
                 reads=[hps[0].b], writes=[hT.b])
            P.op("dve", lambda v: v.tensor_copy(out=hT.ap[:, 8:16, tb * 128:(tb + 1) * 128], in_=hv[1]),
                 reads=[hps[1].b, hT.b], writes=[hT.b])

        def mm_tok(w, tb):
            ps = mps[cnt["m"] % 3]
            cnt["m"] += 1

            def mm(t):
                for c in range(16):
                    ins = t.matmul(ps.ap, lhsT=hT.ap[:, c, tb * 128:(tb + 1) * 128], rhs=w.ap[:, c, :],
                                   start=(c == 0), stop=(c == 15))
                return ins
            P.op("pe", mm, reads=[hT.b, w.b], writes=[ps.b])
            return ps

        def qk_post(ps, tt, tb, is_k, u0, stage, gain):
            i2 = cnt["q"] % 2
            cnt["q"] += 1
            s4, r4, y, o = ss4[i2], rs4[i2], yq[i2], ob[i2]
            psv = ps.ap.rearrange("p (u d) -> p u d", d=HD)
            for u in range(4):
                P.op("act", lambda a, u=u: a.activation(out=junk.ap[:, 0:HD], in_=psv[:, u, :], func=AF.Square,
                                                        accum_out=s4.ap[:, u:u + 1]), reads=[ps.b], writes=[s4.b])
            P.op("act", lambda a: a.activation(out=s4.ap, in_=s4.ap, func=AF.Sqrt, scale=1.0 / HD, bias=EPS),
                 reads=[s4.b], writes=[s4.b])
            P.op("dve", lambda v: v.reciprocal(out=r4.ap, in_=s4.ap), reads=[s4.b], writes=[r4.b])
            P.op("dve", lambda v: v.tensor_tensor(out=y.ap, in0=psv, in1=bc_mid(gain.ap, 4), op=ALU.mult),
                 reads=[ps.b, gain.b], writes=[y.b])
            P.op("pool", lambda g: g.tensor_tensor(out=y.ap, in0=y.ap, in1=bc_last(r4.ap, HD), op=ALU.mult),
                 reads=[y.b, r4.b], writes=[y.b])
            is_prompt = (tt == 4)
            if is_prompt or cfg["kind"] == 2:
                if is_k and is_prompt:
                    seq = tb // 2
                    r0 = (tb % 2) * 128
                    P.dma("sp", cfg["knew"][seq, cfg["j"], r0:r0 + 128, u0 * HD:(u0 + 4) * HD],
                          y.ap.rearrange("p u d -> p (u d)"), reads=[y.b], writes=[d_outs], owner=y.b)
                P.op("dve", lambda v: v.tensor_copy(out=o.ap, in_=y.ap), reads=[y.b], writes=[o.b])
            else:
                blk = tt * 8 + tb
                yv = y.ap.rearrange("p u (a h f) -> p u a h f", a=2, h=2)
                ov = o.ap.rearrange("p u (a h f) -> p u a h f", a=2, h=2)
                cs = cosT.ap[:, blk, :, :].unsqueeze(1).broadcast_to([128, 4, 2, 32])
                sn = sinT.ap[:, blk, :, :].unsqueeze(1).broadcast_to([128, 4, 2, 32])
                x1 = yv[:, :, :, 0, :]
                x2 = yv[:, :, :, 1, :]
                t1, t2, t3, t4 = [rt[k] for k in range(4)]
                P.op("dve", lambda v: v.tensor_tensor(out=t1.ap, in0=x1, in1=cs, op=ALU.mult), reads=[y.b, cosT.b], writes=[t1.b])
                P.op("dve", lambda v: v.tensor_tensor(out=t2.ap, in0=x2, in1=sn, op=ALU.mult), reads=[y.b, sinT.b], writes=[t2.b])
                P.op("dve", lambda v: v.tensor_tensor(out=ov[:, :, :, 0, :], in0=t1.ap, in1=t2.ap, op=ALU.subtract),
                     reads=[t1.b, t2.b], writes=[o.b])
                P.op("pool", lambda g: g.tensor_tensor(out=t3.ap, in0=x2, in1=cs, op=ALU.mult), reads=[y.b, cosT.b], writes=[t3.b])
                P.op("pool", lambda g: g.tensor_tensor(out=t4.ap, in0=x1, in1=sn, op=ALU.mult), reads=[y.b, sinT.b], writes=[t4.b])
                P.op("pool", lambda g: g.tensor_tensor(out=ov[:, :, :, 1, :], in0=t3.ap, in1=t4.ap, op=ALU.add),
                     reads=[t3.b, t4.b, o.b], writes=[o.b])
            def part_b():
                tp = tps[cnt["t"] % 2]
                cnt["t"] += 1
                tpv = tp.ap.bitcast(BF16)[:, 0:512].rearrange("p (u t) -> p u t", t=128)

                def tr(t):
                    for u in range(4):
                        ins = t.transpose(tpv[:, u, :], o.ap[:, u, :], ident.ap)
                    return ins
                P.op("pe", tr, reads=[o.b, ident.b], writes=[tp.b])
                P.op("act", lambda a: a.activation(out=stage.ap[:, :, tb * 128:(tb + 1) * 128], in_=tpv, func=AF.Copy),
                     reads=[tp.b, stage.b], writes=[stage.b])
            return part_b

        def v_post(ps, tt, tb, c0):
            v = vst[cnt["v"] % 2]
            t0 = tt * TT + tb * 128
            P.op("act", lambda a: a.activation(out=v.ap, in_=ps.ap, func=AF.Copy), reads=[ps.b], writes=[v.b])
            P.dma("pool", VS[t0:t0 + 128, c0:c0 + 512], v.ap, reads=[v.b], writes=[d_vs], owner=v.b)
            if tt == 4 and dbg != 7:
                vf = vsf[cnt["v"] % 2]
                seq = tb // 2
                r0 = (tb % 2) * 128
                P.op("dve", lambda vv: vv.tensor_copy(out=vf.ap, in_=ps.ap), reads=[ps.b, v.b], writes=[vf.b])
                P.dma("sp", cfg["vnew"][seq, cfg["j"], r0:r0 + 128, c0:c0 + 512], vf.ap, reads=[vf.b], writes=[d_outs],
                      owner=vf.b)
            cnt["v"] += 1

        def g_block(w, tt, fb_g):
            for fc in range(4):
                unit = fb_g * 4 + fc
                for th in range(2):
                    ps = mps[cnt["m"] % 3]
                    cnt["m"] += 1

                    def mm(t, fc=fc, th=th, ps=ps):
                        for c in range(16):
                            ins = t.matmul(ps.ap, lhsT=w.ap[:, c, fc * 128:(fc + 1) * 128],
                                           rhs=hT.ap[:, c, th * 512:(th + 1) * 512], start=(c == 0), stop=(c == 15))
                        return ins
                    P.op("pe", mm, reads=[hT.b, w.b], writes=[ps.b])
                    g = gst[cnt["g"] % 2]
                    cnt["g"] += 1
                    P.op("act", lambda a, ps=ps, g=g: a.activation(out=g.ap, in_=ps.ap, func=AF.Silu), reads=[ps.b],
                         writes=[g.b])
                    t0 = tt * TT + th * 512
                    P.dma("pool", GT[unit, :, t0:t0 + 512], g.ap, reads=[g.b], writes=[d_gt], owner=g.b)

        nfb = F // 512
        nqb = nq // 4
        nkb = nk // 4
        nvb = vcols // 512
        for tt in range(5):
            if dbg in (1, 2, 3, 4) and tt > 0:
                break
            if dbg == 5 and tt > 1:
                break
            if dbg in (6, 7) and tt in (1, 2, 3):
                continue
            if tt == 0:
                load_mod(0)
            if tt == 4:
                load_mod(1)
            wq = [load_w(0), load_w(1)]
            pq = []
            for tb in range(8):
                norm_block(tt, tb)
            for fb in range(nfb):
                if dbg == 1 or (dbg == 2 and fb >= nqb + nkb) or (dbg == 3 and fb >= nqb + nkb + nvb):
                    break
                w = wq.pop(0)
                if fb + 2 < nfb:
                    wq.append(load_w(fb + 2))
                if fb < nqb + nkb:
                    is_k = fb >= nqb
                    u0 = (fb - nqb) * 4 if is_k else fb * 4
                    stage = qTs[cnt["qs"] % 2]
                    cnt["qs"] += 1
                    for tb in range(8):
                        ps = mm_tok(w, tb)
                        pq.append(qk_post(ps, tt, tb, is_k, u0, stage, gk if is_k else gq))
                        if len(pq) > 1:
                            pq.pop(0)()
                    dst = (KT if is_k else QT)[u0:u0 + 4, :, tt * TT:(tt + 1) * TT].rearrange("u p t -> p u t")

                    def st(dst=dst, stage=stage, is_k=is_k):
                        P.dma("pool", dst, stage.ap, reads=[stage.b], writes=[d_kt if is_k else d_qt], owner=stage.b)
                    last_b = pq[-1]
                    pq[-1] = (lambda last_b=last_b, st=st: (last_b(), st()))
                elif fb < nqb + nkb + nvb:
                    c0 = (fb - nqb - nkb) * 512
                    for tb in range(8):
                        ps = mm_tok(w, tb)
                        v_post(ps, tt, tb, c0)
                else:
                    g_block(w, tt, fb - nqb - nkb - nvb)
                if fb == nqb + nkb and pq:
                    while pq:
                        pq.pop(0)()

    class Item:
        __slots__ = ("s_mms", "s_reads", "n", "mask", "pv", "pv_reads", "pv_writes", "first", "last", "after", "post", "E")

    def run_attention(items, sbanks, etiles, eshape, depth=2):
        assert len(sbanks) >= depth + 1 and len(etiles) >= depth + 2
        cnt = 0
        q = []
        for it in list(items) + [None] * depth:
            if it is not None:
                psS = sbanks[cnt % len(sbanks)]
                E = etiles[cnt % len(etiles)]
                cnt += 1

                def smm(t, it=it, psS=psS):
                    for (off, n, lhsT, rhs) in it.s_mms:
                        ins = t.matmul(psS.ap[:, off:off + n], lhsT=lhsT, rhs=rhs, start=True, stop=True)
                    return ins
                P.op("pe", smm, reads=it.s_reads, writes=[psS.b])
                P.op("act", lambda a, psS=psS, E=E, n=it.n: a.activation(out=E.ap[:, 0:n], in_=psS.ap[:, 0:n], func=AF.Exp,
                                                                         scale=SCALE), reads=[psS.b], writes=[E.b])
                if it.mask is not None:
                    mk_ap, mk_b = it.mask
                    ev = E.ap[:, 0:it.n]
                    if len(mk_ap.shape) == 3:
                        ev = ev.rearrange("p (a b) -> p a b", b=mk_ap.shape[2])
                    P.op("pool", lambda g, ev=ev, mk_ap=mk_ap: g.tensor_tensor(out=ev, in0=ev, in1=mk_ap, op=ALU.mult),
                         reads=[E.b, mk_b], writes=[E.b])
                it.E = E
                q.append(it)
            if q and (len(q) > depth or it is None):
                p_ = q.pop(0)

                def pvmm(t, it=p_):
                    for (out_ap, lhsT, off, n) in it.pv:
                        ins = t.matmul(out_ap, lhsT=lhsT, rhs=it.E.ap[:, off:off + n], start=it.first, stop=it.last)
                    return ins
                P.op("pe", pvmm, reads=[p_.E.b, ones.b] + p_.pv_reads, writes=p_.pv_writes)
                if p_.after is not None:
                    p_.after()
                if getattr(p_, "post", None) is not None:
                    p_.post()

    def load_ctx(cache_k, cache_v, j, kcol0, nku, vcol0, vw, ckf, ckb, ckT, cvf, cvb, tp):
        P.dma("sp", ckf.ap, cache_k[j, :, kcol0:kcol0 + nku * HD].rearrange("(c p) f -> p c f", p=128),
              reads=[d_ro], writes=[ckf.b], owner=ckf.b)
        P.dma("sp", cvf.ap, cache_v[j, :, vcol0:vcol0 + vw].rearrange("(c p) f -> p c f", p=128),
              reads=[d_ro], writes=[cvf.b], owner=cvf.b)
        P.op("dve", lambda v: v.tensor_copy(out=ckb.ap, in_=ckf.ap), reads=[ckf.b], writes=[ckb.b])
        P.op("pool", lambda g: g.tensor_copy(out=cvb.ap, in_=cvf.ap), reads=[cvf.b], writes=[cvb.b])
        tpv = tp.ap.bitcast(BF16)[:, 0:nku * 256].rearrange("p (u t) -> p u t", t=256)

        def tr(t):
            for u in range(nku):
                for c in range(2):
                    ins = t.transpose(tpv[:, u, c * 128:(c + 1) * 128], ckb.ap[:, c, u * HD:(u + 1) * HD], ident.ap)
            return ins
        P.op("pe", tr, reads=[ckb.b, ident.b], writes=[tp.b])
        P.op("act", lambda a: a.activation(out=ckT.ap, in_=tpv, func=AF.Copy), reads=[tp.b], writes=[ckT.b])

    def phase2_A(l):
        j = l // 3
        ar.reset()
        kT = [mk(ar, [NTOK], BF16, "kT%d" % i) for i in range(2)]
        vv = [mk(ar, [40, HD], BF16, "vv%d" % i) for i in range(2)]
        ckf = mk(ar, [2, HD], F32, "ckf")
        ckb = mk(ar, [2, HD], BF16, "ckb")
        ckT = [mk(ar, [1, 256], BF16, "ckT%d" % i) for i in range(2)]
        cvf = mk(ar, [2, HD], F32, "cvf")
        cvb = [mk(ar, [2, HD], BF16, "cvb%d" % i) for i in range(2)]
        qT = [mk(ar, [4, 512], BF16, "qT%d" % i) for i in range(2)]
        gT = [mk(ar, [4, 512], BF16, "gT%d" % i) for i in range(2)]
        oT = [mk(ar, [4, 512], BF16, "oT%d" % i) for i in range(2)]
        mprev = mk(ar, [4, 128], BF16, "mprev")
        mnext = mk(ar, [4, 128], BF16, "mnext")
        sexp = mk(ar, [16], F32, "sexp")
        et = [mk(ar, [512], BF16, "et%d" % i) for i in range(4)]
        den = [mk(ar, [4, 128], F32, "den%d" % i) for i in range(2)]
        of = [mk(ar, [4, 128], F32, "of%d" % i) for i in range(2)]
        sb_, ob_, lb_, tp = pb[0:3], pb[3:5], pb[5:7], pb[7]
        P.op("pool", lambda g: g.memset(mprev.ap, 1.0), writes=[mprev.b])
        P.op("pool", lambda g: g.affine_select(out=mprev.ap, in_=mprev.ap, compare_op=ALU.is_ge, fill=0.0, base=0,
                                               pattern=[[0, 4], [-1, 128]], channel_multiplier=1), reads=[mprev.b], writes=[mprev.b])
        P.op("pool", lambda g: g.memset(mnext.ap, 1.0), writes=[mnext.b])
        P.op("pool", lambda g: g.affine_select(out=mnext.ap, in_=mnext.ap, compare_op=ALU.is_ge, fill=0.0, base=0,
                                               pattern=[[0, 4], [1, 128]], channel_multiplier=-1), reads=[mnext.b], writes=[mnext.b])
        P.dma("sp", sexp.ap, dbc(sink_a[j:j + 1, :], 128), reads=[d_ro], writes=[sexp.b], owner=sexp.b)
        P.op("act", lambda a: a.activation(out=sexp.ap, in_=sexp.ap, func=AF.Exp), reads=[sexp.b], writes=[sexp.b])
        cnt = dict(q=0, f=0)
        for kvh in range(4):
            k_, v_, ckT_, cvb_ = kT[kvh % 2], vv[kvh % 2], ckT[kvh % 2], cvb[kvh % 2]
            P.dma("sp", k_.ap, KT[kvh], reads=[d_kt], writes=[k_.b], owner=k_.b)
            P.dma("sp", v_.ap, VS[:, kvh * HD:(kvh + 1) * HD].rearrange("(c p) d -> p c d", p=128), reads=[d_vs],
                  writes=[v_.b], owner=v_.b)
            load_ctx(cak, cav, j, kvh * HD, 1, kvh * HD, HD, ckf, ckb, ckT_, cvf, cvb_, tp)
            items = []
            loaders = []
            slab_first = []
            for q512 in range(10):
                q_, g_, o_ = qT[cnt["q"] % 2], gT[cnt["q"] % 2], oT[cnt["q"] % 2]
                cnt["q"] += 1
                t0 = q512 * 512

                def ld(q_=q_, g_=g_, t0=t0, kvh=kvh):
                    P.dma("sp", q_.ap, QT[kvh * 4:(kvh + 1) * 4, :, t0:t0 + 512].rearrange("u p t -> p u t"), reads=[d_qt],
                          writes=[q_.b], owner=q_.b)
                    P.dma("sp", g_.ap, GT[kvh * 4:(kvh + 1) * 4, :, t0:t0 + 512].rearrange("u p t -> p u t"), reads=[d_gt],
                          writes=[g_.b], owner=g_.b)
                loaders.append(ld)
                slab_first.append(len(items))
                for qb in range(4):
                    B = q512 * 4 + qb
                    if B < 32:
                        ch = []
                        if B > 0:
                            ch.append(("l", B - 1, mprev))
                        ch.append(("l", B, None))
                        if B < 31:
                            ch.append(("l", B + 1, mnext))
                        ch += [("c", 0, None), ("c", 1, None)]
                        import os as _os
                        if _os.environ.get("A_NOCTX"):
                            ch = ch[:-2]
                        if _os.environ.get("A_NOMASK"):
                            ch = [(a, b, None) for (a, b, c_) in ch]
                    else:
                        sq = (B - 32) // 2
                        ch = [("l", 32 + 2 * sq, None), ("l", 32 + 2 * sq + 1, None)]
                    fi = cnt["f"] % 2
                    cnt["f"] += 1
                    psO, psL = ob_[fi], lb_[fi]
                    rhs = q_.ap[:, :, qb * 128:(qb + 1) * 128]
                    for ci, (typ, c, msk) in enumerate(ch):
                        it = Item()
                        if typ == "l":
                            lk, kb = k_.ap[:, c * 128:(c + 1) * 128], k_.b
                            lv, vb = v_.ap[:, c, :], v_.b
                        else:
                            lk, kb = ckT_.ap[:, 0, c * 128:(c + 1) * 128], ckT_.b
                            lv, vb = cvb_.ap[:, c, :], cvb_.b
                        it.s_mms = [(0, 512, lk, rhs)]
                        it.s_reads = [kb, q_.b]
                        it.n = 512
                        it.mask = (msk.ap, msk.b) if msk is not None else None
                        it.pv = [(psO.ap, lv, 0, 512), (psL.ap, ones.ap, 0, 512)]
                        it.pv_reads = [vb]
                        it.pv_writes = [psO.b, psL.b]
                        it.first = (ci == 0)
                        it.last = (ci == len(ch) - 1)
                        it.after = None
                        it.post = None
                        if ci == len(ch) - 1:
                            def fin(psO=psO, psL=psL, fi=fi, kvh=kvh, g_=g_, o_=o_, qb=qb):
                                d_, f_ = den[fi], of[fi]
                                lv3 = psL.ap.rearrange("p (a b) -> p a b", b=128)
                                ov3 = psO.ap.rearrange("p (a b) -> p a b", b=128)
                                P.op("dve", lambda v: v.tensor_tensor(out=d_.ap, in0=lv3, in1=bc_last(sexp.ap[:, kvh * 4:(kvh + 1) * 4], 128),
                                                                      op=ALU.add), reads=[psL.b, sexp.b], writes=[d_.b])
                                P.op("dve", lambda v: v.reciprocal(out=d_.ap, in_=d_.ap), reads=[d_.b], writes=[d_.b])
                                P.op("dve", lambda v: v.tensor_tensor(out=f_.ap, in0=ov3, in1=d_.ap, op=ALU.mult),
                                     reads=[psO.b, d_.b], writes=[f_.b])
                                P.op("pool", lambda g: g.tensor_tensor(out=o_.ap[:, :, qb * 128:(qb + 1) * 128], in0=f_.ap,
                                                                       in1=g_.ap[:, :, qb * 128:(qb + 1) * 128], op=ALU.mult),
                                     reads=[f_.b, g_.b, o_.b], writes=[o_.b])
                                if qb == 3:
                                    pass
                            it.after = fin
                        items.append(it)
                    if qb == 3:
                        last = items[-1]
                        prev_after = last.after

                        def fin2(prev_after=prev_after, o_=o_, kvh=kvh, t0=t0):
                            prev_after()
                            P.dma("pool", OT[kvh * 4:(kvh + 1) * 4, :, t0:t0 + 512].rearrange("u p t -> p u t"), o_.ap,
                                  reads=[o_.b], writes=[d_ot], owner=o_.b)
                        last.after = fin2
            loaders[0]()
            for si in range(len(loaders) - 1):
                items[slab_first[si]].post = loaders[si + 1]
            run_attention(items, sb_, et, None)

    def phase2_B(l):
        j = 0
        lam_init = LAMBDA_INIT[l]
        ar.reset()
        kT = [mk(ar, [2, NTOK], BF16, "kTb%d" % i) for i in range(2)]
        vv = [mk(ar, [40, 256], BF16, "vvb%d" % i) for i in range(2)]
        ckf = mk(ar, [2, 256], F32, "ckfb")
        ckb = mk(ar, [2, 256], BF16, "ckbb")
        ckT = [mk(ar, [2, 256], BF16, "ckTb%d" % i) for i in range(2)]
        cvf = mk(ar, [2, 256], F32, "cvfb")
        cvb = [mk(ar, [2, 256], BF16, "cvbb%d" % i) for i in range(2)]
        qT = [mk(ar, [2, 1024], BF16, "qTb%d" % i) for i in range(2)]
        gT = [mk(ar, [2, 1024], BF16, "gTb%d" % i) for i in range(2)]
        oT = [mk(ar, [2, 1024], BF16, "oTb%d" % i) for i in range(2)]
        et = [mk(ar, [512], BF16, "et%d" % i) for i in range(4)]
        lamt = mk(ar, [512], F32, "lamt")
        ltmp = mk(ar, [2, 128], F32, "ltmp")
        ls = mk(ar, [2], F32, "ls")
        nlam = mk(ar, [1], F32, "nlam")
        sube = mk(ar, [2], F32, "sube")
        R = [mk(ar, [2, 256], F32, "Rb%d" % i) for i in range(2)]
        t1 = [mk(ar, [2, 256], F32, "t1b%d" % i) for i in range(2)]
        t2 = [mk(ar, [2, 256], F32, "t2b%d" % i) for i in range(2)]
        ob32 = [mk(ar, [2, 256], F32, "ob32%d" % i) for i in range(2)]
        sqb = [mk(ar, [2, 256], BF16, "sqb%d" % i) for i in range(2)]
        sd = [mk(ar, [256], F32, "sdb%d" % i) for i in range(2)]
        sb_, o0_, o1_, lb_, xb_, tp = pb[0:3], pb[3], pb[4], pb[5], pb[6], pb[7]
        P.dma("sp", lamt.ap, dbc(lam_b[0:1, :], 128), reads=[d_ro], writes=[lamt.b], owner=lamt.b)
        lv = lamt.ap.rearrange("p (a b) -> p a b", b=128)
        P.op("dve", lambda v: v.tensor_tensor(out=ltmp.ap[:, 0, :], in0=lv[:, 0, :], in1=lv[:, 1, :], op=ALU.mult),
             reads=[lamt.b], writes=[ltmp.b])
        P.op("dve", lambda v: v.tensor_tensor(out=ltmp.ap[:, 1, :], in0=lv[:, 2, :], in1=lv[:, 3, :], op=ALU.mult),
             reads=[lamt.b, ltmp.b], writes=[ltmp.b])
        P.op("dve", lambda v: v.tensor_reduce(out=ls.ap, in_=ltmp.ap, axis=AX.X, op=ALU.add), reads=[ltmp.b], writes=[ls.b])
        P.op("act", lambda a: a.activation(out=ls.ap, in_=ls.ap, func=AF.Exp), reads=[ls.b], writes=[ls.b])
        P.op("dve", lambda v: v.tensor_tensor(out=nlam.ap, in0=ls.ap[:, 1:2], in1=ls.ap[:, 0:1], op=ALU.subtract),
             reads=[ls.b], writes=[nlam.b])
        P.op("dve", lambda v: v.tensor_scalar(out=nlam.ap, in0=nlam.ap, scalar1=-lam_init, scalar2=None, op0=ALU.add),
             reads=[nlam.b], writes=[nlam.b])
        P.dma("sp", sube.ap, subln_b[0].rearrange("(c p) -> p c", p=128), reads=[d_ro], writes=[sube.b], owner=sube.b,
              allow_slow_non_contiguous=True)
        P.op("dve", lambda v: v.tensor_scalar(out=sube.ap, in0=sube.ap, scalar1=1.0 - lam_init, scalar2=None, op0=ALU.mult),
             reads=[sube.b], writes=[sube.b])
        cnt = dict(q=0, f=0)
        for h in range(8):
            k_, v_, ckT_, cvb_ = kT[h % 2], vv[h % 2], ckT[h % 2], cvb[h % 2]
            P.dma("sp", k_.ap, KT[2 * h:2 * h + 2].rearrange("u p t -> p u t"), reads=[d_kt], writes=[k_.b], owner=k_.b)
            P.dma("sp", v_.ap, VS[:, h * 256:(h + 1) * 256].rearrange("(c p) d -> p c d", p=128), reads=[d_vs],
                  writes=[v_.b], owner=v_.b)
            load_ctx(cbk, cbv, 0, 2 * h * HD, 2, h * 256, 256, ckf, ckb, ckT_, cvf, cvb_, tp)
            items = []
            loaders = []
            slab_first = []
            for q1k in range(5):
                q_, g_, o_ = qT[cnt["q"] % 2], gT[cnt["q"] % 2], oT[cnt["q"] % 2]
                cnt["q"] += 1
                t0 = q1k * 1024

                def ld(q_=q_, g_=g_, t0=t0, h=h):
                    for (dst, src, dd) in ((q_, QT, d_qt), (g_, GT, d_gt)):
                        P.dma("sp", dst.ap, src[2 * h:2 * h + 2, :, t0:t0 + 1024].rearrange("u p t -> p u t"), reads=[dd],
                              writes=[dst.b], owner=dst.b)
                loaders.append(ld)
                slab_first.append(len(items))
                for qi in range(4):
                    B = q1k * 4 + qi
                    if B < 16:
                        ch = [("l", c) for c in range(32)] + [("c", 0), ("c", 1)]
                    else:
                        ch = [("l", 32 + 2 * (B - 16)), ("l", 32 + 2 * (B - 16) + 1)]
                    fi = cnt["f"] % 2
                    cnt["f"] += 1
                    qs = slice(qi * 256, (qi + 1) * 256)
                    for ci, (typ, c) in enumerate(ch):
                        it = Item()
                        it.s_mms = []
                        for m in range(2):
                            if typ == "l":
                                lk = k_.ap[:, m, c * 128:(c + 1) * 128]
                            else:
                                lk = ckT_.ap[:, m, c * 128:(c + 1) * 128]
                            it.s_mms.append((m * 256, 256, lk, q_.ap[:, m, qs]))
                        if typ == "l":
                            kb, vb = k_.b, v_.b
                            lvs = [v_.ap[:, c, e * 128:(e + 1) * 128] for e in range(2)]
                        else:
                            kb, vb = ckT_.b, cvb_.b
                            lvs = [cvb_.ap[:, c, e * 128:(e + 1) * 128] for e in range(2)]
                        it.s_reads = [kb, q_.b]
                        it.n = 512
                        it.mask = None
                        it.pv = [(o0_.ap, lvs[0], 0, 512), (o1_.ap, lvs[1], 0, 512), (lb_.ap, ones.ap, 0, 512)]
                        it.pv_reads = [vb]
                        it.pv_writes = [o0_.b, o1_.b, lb_.b]
                        it.first = (ci == 0)
                        it.last = (ci == len(ch) - 1)
                        it.after = None
                        it.post = None
                        if ci == len(ch) - 1:
                            def fin(fi=fi, h=h, g_=g_, o_=o_, qs=qs, qi=qi, t0=t0):
                                R_, t1_, t2_, o32, sq_, sd_ = R[fi], t1[fi], t2[fi], ob32[fi], sqb[fi], sd[fi]
                                l3 = lb_.ap.rearrange("p (a b) -> p a b", b=256)
                                P.op("dve", lambda v: v.reciprocal(out=R_.ap, in_=l3), reads=[lb_.b], writes=[R_.b])
                                for e, ob in enumerate((o0_, o1_)):
                                    o3 = ob.ap.rearrange("p (a b) -> p a b", b=256)
                                    P.op("dve", lambda v, o3=o3, e=e: v.tensor_tensor(out=t1_.ap[:, e, :], in0=o3[:, 0, :], in1=R_.ap[:, 0, :],
                                                                                      op=ALU.mult), reads=[ob.b, R_.b], writes=[t1_.b])
                                    P.op("dve", lambda v, o3=o3, e=e: v.tensor_tensor(out=t2_.ap[:, e, :], in0=o3[:, 1, :], in1=R_.ap[:, 1, :],
                                                                                      op=ALU.mult), reads=[ob.b, R_.b], writes=[t2_.b])
                                P.op("dve", lambda g: g.scalar_tensor_tensor(out=o32.ap, in0=t2_.ap, scalar=nlam.ap[:, 0:1], in1=t1_.ap,
                                                                             op0=ALU.mult, op1=ALU.add), reads=[t1_.b, t2_.b, nlam.b],
                                     writes=[o32.b])
                                P.op("pool", lambda g: g.tensor_tensor(out=sq_.ap, in0=o32.ap, in1=o32.ap, op=ALU.mult), reads=[o32.b],
                                     writes=[sq_.b])

                                def ssmm(t):
                                    for e in range(2):
                                        ins = t.matmul(xb_.ap[:, 0:256], lhsT=ones.ap, rhs=sq_.ap[:, e, :], start=(e == 0), stop=(e == 1))
                                    return ins
                                P.op("pe", ssmm, reads=[sq_.b, ones.b], writes=[xb_.b])
                                P.op("act", lambda a: a.activation(out=sd_.ap, in_=xb_.ap[:, 0:256], func=AF.Sqrt, scale=1.0 / 256, bias=EPS),
                                     reads=[xb_.b], writes=[sd_.b])
                                P.op("dve", lambda v: v.reciprocal(out=sd_.ap, in_=sd_.ap), reads=[sd_.b], writes=[sd_.b])
                                P.op("dve", lambda v: v.tensor_tensor(out=o32.ap, in0=o32.ap, in1=bc_mid(sd_.ap, 2), op=ALU.mult),
                                     reads=[o32.b, sd_.b], writes=[o32.b])
                                for e in range(2):
                                    P.op("dve", lambda g, e=e: g.scalar_tensor_tensor(out=o_.ap[:, e, qs], in0=o32.ap[:, e, :],
                                                                                       scalar=sube.ap[:, e:e + 1], in1=g_.ap[:, e, qs],
                                                                                       op0=ALU.mult, op1=ALU.mult),
                                         reads=[o32.b, sube.b, g_.b, o_.b], writes=[o_.b])
                                if qi == 3:
                                    P.dma("pool", OT[2 * h:2 * h + 2, :, t0:t0 + 1024].rearrange("u p t -> p u t"), o_.ap,
                                          reads=[o_.b], writes=[d_ot], owner=o_.b)
                            it.after = fin
                        items.append(it)
            loaders[0]()
            for si in range(len(loaders) - 1):
                items[slab_first[si]].post = loaders[si + 1]
            run_attention(items, sb_, et, None)

    def phase2_C(l):
        ar.reset()
        kT = [mk(ar, [NTOK], BF16, "kT%d" % i) for i in range(2)]
        vv = [mk(ar, [40, HD], BF16, "vv%d" % i) for i in range(2)]
        ckf = mk(ar, [2, HD], F32, "ckf")
        ckb = mk(ar, [2, HD], BF16, "ckb")
        ckT = [mk(ar, [1, 256], BF16, "ckT%d" % i) for i in range(2)]
        cvf = mk(ar, [2, HD], F32, "cvf")
        cvb = [mk(ar, [2, HD], BF16, "cvb%d" % i) for i in range(2)]
        qT = [mk(ar, [1024], BF16, "qTc%d" % i) for i in range(2)]
        gT = [mk(ar, [1024], BF16, "gTc%d" % i) for i in range(2)]
        oT = [mk(ar, [1024], BF16, "oTc%d" % i) for i in range(2)]
        et = [mk(ar, [128], BF16, "etc%d" % i) for i in range(5)]
        cbr = mk(ar, [15, 64], F32, "cbr")
        cbm = mk(ar, [15, 64], BF16, "cbm")
        colm = mk(ar, [64], F32, "colm")
        EB = [mk(ar, [25, 128], BF16, "EB%d" % i) for i in range(2)]
        rr = [mk(ar, [128], F32, "rrc%d" % i) for i in range(2)]
        of = [mk(ar, [128], F32, "ofc%d" % i) for i in range(2)]
        sb_, ob_, lb_, tp = pb[0:3], pb[3:5], pb[5:7], pb[7]
        P.op("pool", lambda g: g.memset(colm.ap, 1.0), writes=[colm.b])
        for h0 in (0, 64):
            pr = slice(h0, h0 + 64)
            P.op("pool", lambda g, pr=pr: g.affine_select(out=colm.ap[pr, 0:8], in_=colm.ap[pr, 0:8], compare_op=ALU.is_ge, fill=0.0,
                                                          base=15, pattern=[[0, 8]], channel_multiplier=-1), reads=[colm.b], writes=[colm.b])
            P.op("pool", lambda g, pr=pr: g.affine_select(out=colm.ap[pr, 8:57], in_=colm.ap[pr, 8:57], compare_op=ALU.is_ge, fill=0.0,
                                                          base=0, pattern=[[-1, 49]], channel_multiplier=1), reads=[colm.b], writes=[colm.b])
            P.op("pool", lambda g, pr=pr: g.affine_select(out=colm.ap[pr, 8:57], in_=colm.ap[pr, 8:57], compare_op=ALU.is_ge, fill=0.0,
                                                          base=15, pattern=[[1, 49]], channel_multiplier=-1), reads=[colm.b], writes=[colm.b])
            P.op("pool", lambda g, pr=pr: g.affine_select(out=colm.ap[pr, 57:64], in_=colm.ap[pr, 57:64], compare_op=ALU.is_ge, fill=0.0,
                                                          base=-48, pattern=[[0, 7]], channel_multiplier=1), reads=[colm.b], writes=[colm.b])

        def rs_(r):
            return min(max(r - 4, 0), 56)

        def qblock_chunks(jb):
            lo = rs_(2 * jb) // 2
            hi = (rs_(2 * jb + 1) + 7) // 2
            return list(range(lo, hi + 1))

        classes = {0: 0, 1: 1, 30: 3, 31: 4}

        def cls_of(jb):
            return classes.get(jb, 2)

        rep = {0: 0, 1: 1, 2: 10, 3: 30, 4: 31}

        def build_EB(EB_):
            P.op("pool", lambda g: g.memset(EB_.ap, 0.0), writes=[EB_.b])
            for ci in range(5):
                jb = rep[ci]
                for c in qblock_chunks(jb):
                    dlt = c - jb
                    slot = ci * 5 + (dlt + 3 if ci == 4 else (dlt if ci == 0 else dlt + 2 if ci in (2, 3) else dlt + 1))
                    for qr in range(2):
                        r = 2 * jb + qr
                        for kr in range(2):
                            ka = 2 * c + kr
                            if rs_(r) <= ka <= rs_(r) + 7:
                                i = ka - r + 7
                                P.op("pool", lambda g, kr=kr, qr=qr, slot=slot, i=i: g.tensor_copy(
                                    out=EB_.ap[kr * 64:(kr + 1) * 64, slot, qr * 64:(qr + 1) * 64],
                                    in_=cbm.ap[kr * 64:(kr + 1) * 64, i, :]), reads=[cbm.b, EB_.b], writes=[EB_.b])

        def slot_of(jb, c):
            ci = cls_of(jb)
            dlt = c - jb
            return ci * 5 + (dlt + 3 if ci == 4 else (dlt if ci == 0 else dlt + 2 if ci in (2, 3) else dlt + 1))

        cnt = dict(q=0, f=0)
        for h in range(16):
            k_, v_, ckT_, cvb_, EB_ = kT[h % 2], vv[h % 2], ckT[h % 2], cvb[h % 2], EB[h % 2]
            P.dma("sp", k_.ap, KT[h], reads=[d_kt], writes=[k_.b], owner=k_.b)
            P.dma("sp", v_.ap, VS[:, h * HD:(h + 1) * HD].rearrange("(c p) d -> p c d", p=128), reads=[d_vs],
                  writes=[v_.b], owner=v_.b)
            load_ctx(cck, ccv, 0, h * HD, 1, h * HD, HD, ckf, ckb, ckT_, cvf, cvb_, tp)
            for h0 in (0, 64):
                P.dma("sp", cbr.ap[h0:h0 + 64], rpbt[h], reads=[d_ro], writes=[cbr.b], owner=cbr.b)
            P.op("act", lambda a: a.activation(out=cbr.ap, in_=cbr.ap, func=AF.Exp), reads=[cbr.b], writes=[cbr.b])
            P.op("pool", lambda g: g.tensor_tensor(out=cbm.ap, in0=cbr.ap, in1=bc_mid(colm.ap, 15), op=ALU.mult),
                 reads=[cbr.b, colm.b], writes=[cbm.b])
            build_EB(EB_)
            items = []
            loaders = []
            slab_first = []
            for q1k in range(5):
                q_, g_, o_ = qT[cnt["q"] % 2], gT[cnt["q"] % 2], oT[cnt["q"] % 2]
                cnt["q"] += 1
                t0 = q1k * 1024

                def ld(q_=q_, g_=g_, t0=t0, h=h):
                    P.dma("sp", q_.ap, QT[h, :, t0:t0 + 1024], reads=[d_qt], writes=[q_.b], owner=q_.b)
                    P.dma("sp", g_.ap, GT[h, :, t0:t0 + 1024], reads=[d_gt], writes=[g_.b], owner=g_.b)
                loaders.append(ld)
                slab_first.append(len(items))
                for qi in range(8):
                    B = q1k * 8 + qi
                    if B < 32:
                        ch = [("l", c, slot_of(B, c)) for c in qblock_chunks(B)] + [("c", 0, None), ("c", 1, None)]
                    else:
                        sq = (B - 32) // 2
                        ch = [("l", 32 + 2 * sq, None), ("l", 32 + 2 * sq + 1, None)]
                    fi = cnt["f"] % 2
                    cnt["f"] += 1
                    psO, psL = ob_[fi], lb_[fi]
                    qs = slice(qi * 128, (qi + 1) * 128)
                    for ci, (typ, c, slot) in enumerate(ch):
                        it = Item()
                        if typ == "l":
                            lk, kb = k_.ap[:, c * 128:(c + 1) * 128], k_.b
                            lv_, vb = v_.ap[:, c, :], v_.b
                        else:
                            lk, kb = ckT_.ap[:, 0, c * 128:(c + 1) * 128], ckT_.b
                            lv_, vb = cvb_.ap[:, c, :], cvb_.b
                        it.s_mms = [(0, 128, lk, q_.ap[:, qs])]
                        it.s_reads = [kb, q_.b]
                        it.n = 128
                        it.mask = (EB_.ap[:, slot, :], EB_.b) if slot is not None else None
                        it.pv = [(psO.ap[:, 0:128], lv_, 0, 128), (psL.ap[:, 0:128], ones.ap, 0, 128)]
                        it.pv_reads = [vb]
                        it.pv_writes = [psO.b, psL.b]
                        it.first = (ci == 0)
                        it.last = (ci == len(ch) - 1)
                        it.after = None
                        it.post = None
                        if ci == len(ch) - 1:
                            def fin(psO=psO, psL=psL, fi=fi, h=h, g_=g_, o_=o_, qs=qs, qi=qi, t0=t0):
                                r_, f_ = rr[fi], of[fi]
                                P.op("dve", lambda v: v.reciprocal(out=r_.ap, in_=psL.ap[:, 0:128]), reads=[psL.b], writes=[r_.b])
                                P.op("dve", lambda v: v.tensor_tensor(out=f_.ap, in0=psO.ap[:, 0:128], in1=r_.ap, op=ALU.mult),
                                     reads=[psO.b, r_.b], writes=[f_.b])
                                P.op("pool", lambda g: g.tensor_tensor(out=o_.ap[:, qs], in0=f_.ap, in1=g_.ap[:, qs], op=ALU.mult),
                                     reads=[f_.b, g_.b, o_.b], writes=[o_.b])
                                if qi == 7:
                                    P.dma("pool", OT[h, :, t0:t0 + 1024], o_.ap, reads=[o_.b], writes=[d_ot], owner=o_.b)
                            it.after = fin
                        items.append(it)
            loaders[0]()
            for si in range(len(loaders) - 1):
                items[slab_first[si]].post = loaders[si + 1]
            run_attention(items, sb_, et, None)

    def phase3(l):
        xin, dxin = x_aps[l], d_x[l]
        xout, dxout = x_aps[l + 1], d_x[l + 1]
        ar.reset()
        wo = mk(ar, [16, D], BF16, "wo")
        gt = mk(ar, [D], F32, "gt")
        xt = [mk(ar, [D], F32, "xt%d" % i) for i in range(2)]
        xo = [mk(ar, [D], F32, "xo%d" % i) for i in range(2)]
        ot = [mk(ar, [16, 512], BF16, "ot%d" % i) for i in range(2)]
        P.dma("sp", wo.ap, WOB[l].rearrange("(c p) f -> p c f", p=128), reads=[d_wob[l]], writes=[wo.b], owner=wo.b)
        for tb in range(40):
            t0 = tb * 128
            if tb == 0 or tb == 32:
                r = 0 if tb == 0 else 1
                P.dma("sp", gt.ap, dbc(MOD[l, r:r + 1, 2 * D:3 * D], 128), reads=[d_mod], writes=[gt.b], owner=gt.b)
            o_ = ot[(tb // 4) % 2]
            if tb % 4 == 0:
                P.dma("sp", o_.ap, OT[:, :, t0:t0 + 512].rearrange("u p t -> p u t"), reads=[d_ot], writes=[o_.b], owner=o_.b)
            x = xt[tb % 2]
            y = xo[tb % 2]
            P.dma("sp", x.ap, xin[t0:t0 + 128, :], reads=[dxin], writes=[x.b], owner=x.b)
            tl = tb % 4
            for fb in range(4):
                ps = pb[(tb * 4 + fb) % 4]

                def mm(t, ps=ps, o_=o_, tl=tl, fb=fb):
                    for c in range(16):
                        ins = t.matmul(ps.ap, lhsT=o_.ap[:, c, tl * 128:(tl + 1) * 128], rhs=wo.ap[:, c, fb * 512:(fb + 1) * 512],
                                       start=(c == 0), stop=(c == 15))
                    return ins
                P.op("pe", mm, reads=[o_.b, wo.b], writes=[ps.b])
                fs = slice(fb * 512, (fb + 1) * 512)
                P.op("dve", lambda v, ps=ps, y=y, fs=fs: v.tensor_tensor(out=y.ap[:, fs], in0=ps.ap, in1=gt.ap[:, fs], op=ALU.mult),
                     reads=[ps.b, gt.b, y.b], writes=[y.b])
            P.op("pool", lambda g, x=x, y=y: g.tensor_tensor(out=y.ap, in0=y.ap, in1=x.ap, op=ALU.add), reads=[x.b, y.b], writes=[y.b])
            P.dma("pool", xout[t0:t0 + 128, :], y.ap, reads=[y.b], writes=[dxout], owner=y.b)

    build_consts()
    P.barrier()
    for l in range(n_layers):
        cast_weights(l)
    phase0()
    P.barrier()
    for l in range(n_layers):
        if stop_phase == (l, 0):
            break
        phase1(l)
        P.barrier()
        if stop_phase == (l, 1):
            break
        [phase2_A, phase2_B, phase2_C][KINDS[l]](l)
        P.barrier()
        if stop_phase == (l, 2):
            break
        phase3(l)
        P.barrier()
    P.barrier()

    from contextlib import ExitStack
    with ExitStack() as stack:
        P.emit(nc, stack)
    nc._n_sems = P.n_sems
    return nc


def make_in_maps(inp, n_layers=DEPTH):
    f = lambda a: np.ascontiguousarray(a, dtype=np.float32)
    xs, xp = inp["x_sample"], inp["x_prompt"]
    rpb = np.asarray(inp["rpb_c"])[0]
    kc = np.arange(64)[:, None]
    qc = np.arange(64)[None, :]
    idx = np.clip(kc - qc + 15, 0, 30)
    rpbt = f(rpb[:, :, idx].transpose(0, 2, 1, 3))
    shared = dict(
        ln_g=f(inp["ln_g"]), ada_w=f(inp["ada_w"][:n_layers]), ada_b=f(inp["ada_b"]), w_out=f(inp["w_out"][:n_layers]),
        qn_g=f(inp["qn_g"]), kn_g=f(inp["kn_g"]), w_in_a=f(inp["w_in_a"][:2 if n_layers > 3 else 1]),
        w_in_b=f(inp["w_in_b"] if n_layers > 1 else np.asarray(inp["w_in_b"])[:, :8]),
        w_in_c=f(inp["w_in_c"] if n_layers > 2 else np.asarray(inp["w_in_c"])[:, :8]), sink_a=f(inp["sink_a"]), lam_b=f(np.asarray(inp["lam_b"]).reshape(1, 512)),
        subln_b=f(inp["subln_b"]), rpbt=rpbt,
    )
    maps = []
    for i in range(8):
        m = dict(shared)
        m["x"] = f(np.concatenate([np.asarray(xs[i]), np.asarray(xp[4 * i:4 * i + 4]).reshape(1024, D)], axis=0))
        m["cpair"] = f(np.stack([np.asarray(inp["c"])[i], np.asarray(inp["c_ctx"])], axis=0))
        m["cak"] = f(np.asarray(inp["cache_a_k"])[i].reshape(2, 256, 512))
        m["cav"] = f(np.asarray(inp["cache_a_v"])[i].reshape(2, 256, 512))
        m["cbk"] = f(np.asarray(inp["cache_b_k"])[i].reshape(1, 256, 2048))
        m["cbv"] = f(np.asarray(inp["cache_b_v"])[i].reshape(1, 256, 2048))
        m["cck"] = f(np.asarray(inp["cache_c_k"])[i].reshape(1, 256, 2048))
        m["ccv"] = f(np.asarray(inp["cache_c_v"])[i].reshape(1, 256, 2048))
        maps.append(m)
    return maps


def assemble(results):
    y = np.stack([r["y"] for r in results], axis=0)
    y_sample = np.ascontiguousarray(y[:, :NS, :])
    y_prompt = np.ascontiguousarray(y[:, NS:, :].reshape(32, 256, D))
    cat = lambda k: np.concatenate([r[k] for r in results], axis=0)
    return (y_prompt, y_sample,
            cat("nak").reshape(32, 2, 256, 4, 128), cat("nav").reshape(32, 2, 256, 4, 128),
            cat("nbk").reshape(32, 1, 256, 8, 2, 128), cat("nbv").reshape(32, 1, 256, 8, 256),
            cat("nck").reshape(32, 1, 256, 16, 128), cat("ncv").reshape(32, 1, 256, 16, 128))


def kernel(**inputs):
    nc = build()
    in_maps = make_in_maps(inputs)
    res = run_bass_kernel_spmd(nc, in_maps, core_ids=list(range(8)))
    return assemble(res.results)
```

```python
import math
import numpy as np
import concourse.bass as bass
import concourse.mybir as mybir
from concourse.bass_utils import run_bass_kernel_spmd

F32 = mybir.dt.float32
BF16 = mybir.dt.bfloat16
I32 = mybir.dt.int32
AF = mybir.ActivationFunctionType
ALU = mybir.AluOpType
AX = mybir.AxisListType

D = 2048
NTOK = 5120
NS = 4096
TT = 1024
HD = 128
SCALE = HD ** -0.5
EPS = 1e-6
DEPTH = 4
KINDS = [0, 1, 2, 0]
FIN = [5120, 8192, 8192, 5120]
LAMBDA_INIT = [0.8 - 0.6 * math.exp(-0.3 * l) for l in range(DEPTH)]


class Buf:
    __slots__ = ("name", "w", "r", "excl")

    def __init__(self, name):
        self.name = name
        self.w = None
        self.r = {}
        self.excl = False


class DBuf:
    __slots__ = ("name", "writers", "readers", "prev_readers")

    def __init__(self, name):
        self.name = name
        self.writers = {}
        self.readers = {}
        self.prev_readers = {}


class Op:
    __slots__ = ("eng", "fn", "deps", "needs_inc", "dma")

    def __init__(self, eng, fn, deps, dma=None):
        self.eng = eng
        self.fn = fn
        self.deps = deps
        self.needs_inc = False
        self.dma = dma


ENGS = ("pe", "act", "dve", "pool", "sp")


def _add(dd, tok):
    key = (tok[0], tok[1])
    if dd.get(key, -1) < tok[2]:
        dd[key] = tok[2]


class Prog:
    def __init__(self):
        self.ops = {e: [] for e in ENGS}
        self.names = {}
        self.dpool = []
        self.kind_idxs = {'hw': [], 'sw': []}
        self.used = {'hw': 0, 'sw': 0}

    def _deps_for(self, reads, writes):
        deps = {}
        for b in reads:
            if isinstance(b, DBuf):
                for k, v in b.writers.items():
                    _add(deps, (k[0], k[1], v))
            else:
                if b.w is not None:
                    _add(deps, b.w)
                if b.excl:
                    for k, v in b.r.items():
                        _add(deps, (k[0], k[1], v))
        for b in writes:
            if isinstance(b, DBuf):
                if b.readers:
                    b.prev_readers = b.readers
                    b.readers = {}
                    b.writers = {}
                for k, v in b.prev_readers.items():
                    _add(deps, (k[0], k[1], v))
            else:
                if b.w is not None:
                    _add(deps, b.w)
                for k, v in b.r.items():
                    _add(deps, (k[0], k[1], v))
        return deps

    def _mark(self, tok, reads, writes):
        for b in reads:
            if isinstance(b, DBuf):
                _add(b.readers, tok)
            else:
                _add(b.r, tok)
        for b in writes:
            if isinstance(b, DBuf):
                _add(b.writers, tok)
            else:
                b.w = tok
                b.r = {}

    def op(self, eng, fn, reads=(), writes=()):
        deps = self._deps_for(reads, writes)
        if eng == "pe":
            deps.pop(("e", "pe"), None)
        lst = self.ops[eng]
        tok = ("e", eng, len(lst))
        lst.append(Op(eng, fn, deps))
        self._mark(tok, reads, writes)
        return tok

    def dma(self, eng, out, in_, reads, writes, owner, **kw):
        deps = self._deps_for(reads, writes)
        kind = "sw" if eng == "pool" else "hw"
        idx = self.names.get((owner.name, kind))
        if idx is None:
            k = self.used[kind]
            if k < len(self.kind_idxs[kind]):
                idx = self.kind_idxs[kind][k]
            else:
                idx = len(self.dpool)
                self.dpool.append(0)
                self.kind_idxs[kind].append(idx)
            self.used[kind] += 1
            self.names[(owner.name, kind)] = idx
        self.dpool[idx] += 16
        ent = (idx, self.dpool[idx])
        tok = ("d", ent[0], ent[1])

        def fn(e, out=out, in_=in_, kw=kw):
            return e.dma_start(out=out, in_=in_, **kw)

        self.ops[eng].append(Op(eng, fn, deps, dma=ent[0]))
        self._mark(tok, reads, writes)
        return tok

    def barrier(self):
        deps = {}
        for e in ENGS:
            for i in range(len(self.ops[e]) - 1, -1, -1):
                o = self.ops[e][i]
                if o.dma is None and o.fn is not None:
                    deps[("e", e)] = i
                    break
        for idx, cnt in enumerate(self.dpool):
            if cnt:
                deps[("d", idx)] = cnt
        for e in ENGS:
            self.ops[e].append(Op(e, None, dict(deps)))
        self.names = {}
        self.used = {'hw': 0, 'sw': 0}

    def emit(self, nc, stack):
        for e in ENGS:
            for o in self.ops[e]:
                for k, v in o.deps.items():
                    if k[0] == "e":
                        self.ops[k[1]][v].needs_inc = True
        inc_count = {}
        for e in ENGS:
            c = 0
            arr = []
            for o in self.ops[e]:
                if o.needs_inc:
                    c += 1
                arr.append(c)
            inc_count[e] = arr
        esem = {e: stack.enter_context(nc.semaphore("es_" + e)) for e in ENGS if e != "sp"}
        dsem = [stack.enter_context(nc.semaphore("ds%d" % i)) for i in range(len(self.dpool))]
        self.n_sems = len(esem) + len(dsem)
        block = stack.enter_context(nc.Block())
        self.stats = {e: [0, 0] for e in ENGS}

        def run(e, eng):
            known = {}
            st = self.stats[e]
            for o in self.ops[e]:
                for k, v in o.deps.items():
                    if k[0] == "e":
                        val = inc_count[k[1]][v]
                        sem = esem[k[1]]
                    else:
                        val = v
                        sem = dsem[k[1]]
                    if known.get(k, 0) >= val:
                        continue
                    known[k] = val
                    eng.wait_ge(sem, val)
                    st[1] += 1
                if o.fn is None:
                    continue
                ins = o.fn(eng)
                st[0] += 1
                if o.dma is not None:
                    ins.then_inc(dsem[o.dma], 16)
                elif o.needs_inc:
                    ins.then_inc(esem[e], 1)

        @block.tensor
        def _(t):
            run("pe", t)

        @block.scalar
        def _(a):
            run("act", a)

        @block.vector
        def _(v):
            run("dve", v)

        @block.gpsimd
        def _(g):
            run("pool", g)

        @block.sync
        def _(s):
            run("sp", s)


class Arena:
    def __init__(self, nc, name, nbytes):
        self.t = nc.alloc_sbuf_tensor(name, [128, nbytes // 4], F32)
        self.cap = nbytes
        self.off = 0

    def reset(self):
        self.off = 0

    def alloc(self, free_shape, dtype):
        es = 2 if dtype == BF16 else 4
        n = 1
        for s in free_shape:
            n *= s
        nb = (n * es + 31) // 32 * 32
        assert self.off + nb <= self.cap, ("arena overflow", self.off, nb, self.cap)
        v = self.t[:, self.off // 4:(self.off + nb) // 4]
        self.off += nb
        if dtype != F32:
            v = v.bitcast(dtype)
        v = v[:, 0:n]
        if len(free_shape) == 2:
            v = v.rearrange("p (a b) -> p a b", b=free_shape[1])
        elif len(free_shape) == 3:
            v = v.rearrange("p (a b c) -> p a b c", b=free_shape[1], c=free_shape[2])
        elif len(free_shape) == 4:
            v = v.rearrange("p (a b c d) -> p a b c d", b=free_shape[1], c=free_shape[2], d=free_shape[3])
        return v


class T:
    __slots__ = ("ap", "b")

    def __init__(self, ap, name):
        self.ap = ap
        self.b = Buf(name)


def dbc(row, nparts):
    n = row.shape[-1]
    return bass.AP(tensor=row.tensor, offset=row.offset, ap=[[0, nparts], [1, n]])


def bc_last(ap2d, n):
    return ap2d.unsqueeze(2).broadcast_to([ap2d.shape[0], ap2d.shape[1], n])


def bc_mid(ap2d, n):
    return ap2d.unsqueeze(1).broadcast_to([ap2d.shape[0], n, ap2d.shape[1]])


def build(n_layers=DEPTH, stop_phase=None, dbg=0):
    nc = bass.Bass("TRN2", target_bir_lowering=False)
    P = Prog()

    def din(name, shape):
        return nc.dram_tensor(name, list(shape), F32, kind="ExternalInput").ap()

    def dout(name, shape):
        return nc.dram_tensor(name, list(shape), F32, kind="ExternalOutput").ap()

    x_in = din("x", [NTOK, D])
    cpair = din("cpair", [2, D])
    ln_g = din("ln_g", [DEPTH, D])
    ada_w = din("ada_w", [n_layers, D, 3 * D])
    ada_b = din("ada_b", [DEPTH, 3 * D])
    w_out = din("w_out", [n_layers, D, D])
    qn_g = din("qn_g", [DEPTH, HD])
    kn_g = din("kn_g", [DEPTH, HD])
    w_in_a = din("w_in_a", [2 if n_layers > 3 else 1, D, 5120])
    w_in_b = din("w_in_b", [1, D, 8192] if n_layers > 1 else [1, 8, 8192])
    w_in_c = din("w_in_c", [1, D, 8192] if n_layers > 2 else [1, 8, 8192])
    sink_a = din("sink_a", [2, 16])
    lam_b = din("lam_b", [1, 4 * HD])
    subln_b = din("subln_b", [1, 256])
    rpbt = din("rpbt", [16, 64, 15, 64])
    cak = din("cak", [2, 256, 4 * HD])
    cav = din("cav", [2, 256, 4 * HD])
    cbk = din("cbk", [1, 256, 16 * HD])
    cbv = din("cbv", [1, 256, 8 * 256])
    cck = din("cck", [1, 256, 16 * HD])
    ccv = din("ccv", [1, 256, 16 * HD])
    y_out = dout("y", [NTOK, D])
    nak = dout("nak", [4, 2, 256, 4 * HD])
    nav = dout("nav", [4, 2, 256, 4 * HD])
    nbk = dout("nbk", [4, 1, 256, 16 * HD])
    nbv = dout("nbv", [4, 1, 256, 8 * 256])
    nck = dout("nck", [4, 1, 256, 16 * HD])
    ncv = dout("ncv", [4, 1, 256, 16 * HD])
    XA = nc.dram_tensor("XA", [NTOK, D], F32).ap()
    XB = nc.dram_tensor("XB", [NTOK, D], F32).ap()
    MOD = nc.dram_tensor("MOD", [DEPTH, 2, 3 * D], F32).ap()
    QT = nc.dram_tensor("QT", [16, 128, NTOK], BF16).ap()
    KT = nc.dram_tensor("KT", [16, 128, NTOK], BF16).ap()
    VS = nc.dram_tensor("VS", [NTOK, D], BF16).ap()
    GT = nc.dram_tensor("GT", [16, 128, NTOK], BF16).ap()
    OT = nc.dram_tensor("OT", [16, 128, NTOK], BF16).ap()
    WIB = [nc.dram_tensor("WIB%d" % l, [D, FIN[l]], BF16).ap() for l in range(DEPTH)]
    WOB = [nc.dram_tensor("WOB%d" % l, [D, D], BF16).ap() for l in range(DEPTH)]
    w_in_src = [w_in_a[0], w_in_b[0], w_in_c[0], w_in_a[1 if n_layers > 3 else 0]]

    d_x = [DBuf("x_in"), DBuf("XA"), DBuf("XB"), DBuf("XA"), DBuf("y")]
    d_x[3] = d_x[1]
    x_aps = [x_in, XA, XB, XA, y_out]
    d_mod = DBuf("MOD")
    d_qt, d_kt, d_vs, d_gt, d_ot = DBuf("QT"), DBuf("KT"), DBuf("VS"), DBuf("GT"), DBuf("OT")
    d_wib = [DBuf("WIB%d" % l) for l in range(DEPTH)]
    d_wob = [DBuf("WOB%d" % l) for l in range(DEPTH)]
    d_outs = DBuf("outs")
    d_ro = DBuf("ro")

    ar = Arena(nc, "arena", 180 * 1024)
    car = Arena(nc, "consts", 20 * 1024)
    banks = [nc.alloc_psum_tensor("bank%d" % i, [128, 512], F32) for i in range(8)]
    pb = [T(banks[i][:], "bank%d" % i) for i in range(8)]
    for t_ in pb:
        t_.b.excl = True

    def mk(arena, free_shape, dtype, name):
        t = T(arena.alloc(free_shape, dtype), name)
        return t

    ident = mk(car, [128], BF16, "ident")
    ones = mk(car, [128], BF16, "ones")
    cosT = mk(car, [32, 2, 32], F32, "cosT")
    sinT = mk(car, [32, 2, 32], F32, "sinT")
    gq = mk(car, [HD], F32, "gq")
    gk = mk(car, [HD], F32, "gk")

    def build_consts():
        P.op("pool", lambda g: g.memset(ident.ap, 0.0), writes=[ident.b])
        P.op("pool", lambda g: g.affine_select(out=ident.ap, in_=ident.ap, compare_op=ALU.not_equal, fill=1.0,
                                               base=0, pattern=[[-1, 128]], channel_multiplier=1),
             reads=[ident.b], writes=[ident.b])
        P.op("pool", lambda g: g.memset(ones.ap, 1.0), writes=[ones.b])
        ar.reset()
        posr = mk(ar, [32], I32, "posr")
        posc = mk(ar, [1], I32, "posc")
        fi = mk(ar, [32], I32, "fi")
        posrf = mk(ar, [32], F32, "posrf")
        poscf = mk(ar, [1], F32, "poscf")
        ff = mk(ar, [32], F32, "ff")
        invf = mk(ar, [32], F32, "invf")
        ang = mk(ar, [32, 2, 32], F32, "ang")
        tmp = mk(ar, [32, 2, 32], F32, "angt")
        tmpi = mk(ar, [32, 2, 32], I32, "angi")
        for h0, base in ((0, 0), (64, 1)):
            P.op("pool", lambda g, h0=h0, base=base: g.iota(posr.ap[h0:h0 + 64, :], pattern=[[2, 32]], base=base,
                                                            channel_multiplier=0), writes=[posr.b])
            P.op("pool", lambda g, h0=h0: g.iota(posc.ap[h0:h0 + 64, :], pattern=[[0, 1]], base=0,
                                                 channel_multiplier=1), writes=[posc.b])
        P.op("pool", lambda g: g.iota(fi.ap, pattern=[[1, 32]], base=0, channel_multiplier=0), writes=[fi.b])
        P.op("dve", lambda v: v.tensor_copy(out=posrf.ap, in_=posr.ap), reads=[posr.b], writes=[posrf.b])
        P.op("dve", lambda v: v.tensor_copy(out=poscf.ap, in_=posc.ap), reads=[posc.b], writes=[poscf.b])
        P.op("dve", lambda v: v.tensor_copy(out=ff.ap, in_=fi.ap), reads=[fi.b], writes=[ff.b])
        P.op("act", lambda a: a.activation(out=invf.ap, in_=ff.ap, func=AF.Exp, scale=-math.log(10000.0) / 32.0),
             reads=[ff.b], writes=[invf.b])
        P.op("dve", lambda v: v.tensor_tensor(out=ang.ap[:, :, 0, :], in0=bc_last(posrf.ap, 32), in1=bc_mid(invf.ap, 32),
                                              op=ALU.mult), reads=[posrf.b, invf.b], writes=[ang.b])
        P.op("dve", lambda v: v.tensor_scalar(out=ang.ap[:, :, 1, :], in0=bc_mid(invf.ap, 32), scalar1=poscf.ap[:, 0:1],
                                              scalar2=None, op0=ALU.mult), reads=[poscf.b, invf.b, ang.b], writes=[ang.b])
        TWO_PI = 2.0 * math.pi
        for dst, shift in ((sinT, 0.0), (cosT, math.pi / 2)):
            P.op("dve", lambda v, shift=shift: v.tensor_scalar(out=tmp.ap, in0=ang.ap, scalar1=shift, scalar2=1.0 / TWO_PI,
                                                               op0=ALU.add, op1=ALU.mult), reads=[ang.b], writes=[tmp.b])
            P.op("dve", lambda v: v.tensor_copy(out=tmpi.ap, in_=tmp.ap), reads=[tmp.b], writes=[tmpi.b])
            P.op("dve", lambda v: v.tensor_copy(out=tmp.ap, in_=tmpi.ap), reads=[tmpi.b], writes=[tmp.b])
            P.op("dve", lambda v: v.scalar_tensor_tensor(out=tmp.ap, in0=tmp.ap, scalar=-TWO_PI, in1=ang.ap,
                                                         op0=ALU.mult, op1=ALU.add), reads=[tmp.b, ang.b], writes=[tmp.b])
            P.op("dve", lambda v, shift=shift: v.tensor_scalar(out=tmp.ap, in0=tmp.ap, scalar1=shift, scalar2=3.1415925,
                                                               op0=ALU.add, op1=ALU.min), reads=[tmp.b], writes=[tmp.b])
            P.op("dve", lambda v: v.tensor_scalar(out=tmp.ap, in0=tmp.ap, scalar1=-3.1415925, scalar2=None,
                                                  op0=ALU.max), reads=[tmp.b], writes=[tmp.b])
            P.op("act", lambda a, dst=dst: a.activation(out=dst.ap, in_=tmp.ap, func=AF.Sin), reads=[tmp.b], writes=[dst.b])

    castb = Buf("castsem")

    def cast_weights(l):
        src = w_in_src[l]
        F = FIN[l]
        for r0 in range(0, D, 256):
            P.dma("pool", WIB[l][r0:r0 + 256, :].rearrange("r (a b) -> r a b", b=1024),
                  src[r0:r0 + 256, :].rearrange("r (a b) -> r a b", b=1024),
                  reads=[d_ro], writes=[d_wib[l]], owner=castb)
        for r0 in range(0, D, 512):
            P.dma("pool", WOB[l][r0:r0 + 512, :].rearrange("r (a b) -> r a b", b=1024),
                  w_out[l, r0:r0 + 512, :].rearrange("r (a b) -> r a b", b=1024),
                  reads=[d_ro], writes=[d_wob[l]], owner=castb)

    def phase0():
        ar.reset()
        cT = mk(ar, [16, 2], F32, "cT")
        sT = mk(ar, [16, 2], F32, "sT")
        sg = mk(ar, [16, 2], F32, "sg")
        wt = [mk(ar, [16, 512], F32, "adaw%d" % i) for i in range(2)]
        adab = T(ar.alloc([3 * D], F32)[0:2, :], "adab")
        msb = T(ar.alloc([3 * D], F32)[0:2, :], "msb")
        for r in range(2):
            P.dma("sp", cT.ap[:, :, r], cpair[r].rearrange("(c p) -> p c", p=128), reads=[d_ro], writes=[cT.b],
                  owner=cT.b, allow_slow_non_contiguous=True)
        P.op("act", lambda a: a.activation(out=sg.ap, in_=cT.ap, func=AF.Exp, scale=-1.0), reads=[cT.b], writes=[sg.b])
        P.op("dve", lambda v: v.tensor_scalar(out=sg.ap, in0=sg.ap, scalar1=1.0, scalar2=None, op0=ALU.add),
             reads=[sg.b], writes=[sg.b])
        P.op("dve", lambda v: v.reciprocal(out=sg.ap, in_=sg.ap), reads=[sg.b], writes=[sg.b])
        P.op("dve", lambda v: v.tensor_tensor(out=sT.ap, in0=cT.ap, in1=sg.ap, op=ALU.mult), reads=[cT.b, sg.b],
             writes=[sT.b])
        i = 0
        for l in range(n_layers):
            P.dma("sp", adab.ap, dbc(ada_b[l:l + 1, :], 2), reads=[d_ro], writes=[adab.b], owner=adab.b)
            for fb in range(12):
                w = wt[i % 2]
                P.dma("sp", w.ap, ada_w[l, :, fb * 512:(fb + 1) * 512].rearrange("(c p) f -> p c f", p=128),
                      reads=[d_ro], writes=[w.b], owner=w.b)
                ps = pb[i % 2]

                def mm(t, w=w, ps=ps):
                    for c in range(16):
                        ins = t.matmul(ps.ap[0:2, :], lhsT=sT.ap[:, c, :], rhs=w.ap[:, c, :], start=(c == 0), stop=(c == 15))
                    return ins
                P.op("pe", mm, reads=[sT.b, w.b], writes=[ps.b])
                P.op("dve", lambda v, ps=ps, fb=fb: v.tensor_tensor(out=msb.ap[:, fb * 512:(fb + 1) * 512], in0=ps.ap[0:2, :],
                                                                    in1=adab.ap[:, fb * 512:(fb + 1) * 512], op=ALU.add),
                     reads=[ps.b, adab.b], writes=[msb.b])
                i += 1
            P.dma("sp", MOD[l], msb.ap, reads=[msb.b], writes=[d_mod], owner=msb.b)

    def layer_cfg(l):
        kind = KINDS[l]
        j = l // 3
        if kind == 0:
            return dict(kind=0, j=j, nq=16, nk=4, vcols=512, F=5120, knew=nak, vnew=nav, kcols=512)
        if kind == 1:
            return dict(kind=1, j=j, nq=16, nk=16, vcols=2048, F=8192, knew=nbk, vnew=nbv, kcols=2048)
        return dict(kind=2, j=j, nq=16, nk=16, vcols=2048, F=8192, knew=nck, vnew=ncv, kcols=2048)

    def phase1(l):
        cfg = layer_cfg(l)
        nq, nk, vcols, F = cfg["nq"], cfg["nk"], cfg["vcols"], cfg["F"]
        xin, dxin = x_aps[l], d_x[l]
        ar.reset()
        mod1 = mk(ar, [D], F32, "mod1")
        sh = mk(ar, [D], F32, "sh")
        xt = [mk(ar, [D], F32, "xt%d" % i) for i in range(2)]
        tmpf = mk(ar, [D], F32, "tmpf")
        lng = tmpf
        hb = [mk(ar, [D], BF16, "hb%d" % i) for i in range(2)]
        hT = mk(ar, [16, TT], BF16, "hT")
        wt = [mk(ar, [16, 512], BF16, "wt%d" % i) for i in range(3)]
        ssx = [mk(ar, [1], F32, "ssx%d" % i) for i in range(2)]
        rsx = [mk(ar, [1], F32, "rsx%d" % i) for i in range(2)]
        junk = mk(ar, [D], BF16, "junk")
        ss4 = [mk(ar, [4], F32, "ss4%d" % i) for i in range(4)]
        rs4 = [mk(ar, [4], F32, "rs4%d" % i) for i in range(4)]
        yq = [mk(ar, [4, HD], F32, "yq%d" % i) for i in range(3)]
        rt = [mk(ar, [4, 2, 32], F32, "rt%d" % i) for i in range(8)]
        ob = [mk(ar, [4, HD], BF16, "ob%d" % i) for i in range(4)]
        qTs = [mk(ar, [4, TT], BF16, "qTs%d" % i) for i in range(2)]
        vst = [mk(ar, [512], BF16, "vst%d" % i) for i in range(2)]
        vsf = [mk(ar, [512], F32, "vsf%d" % i) for i in range(2)]
        gst = [mk(ar, [512], BF16, "gst%d" % i) for i in range(2)]
        hps = pb[0:2]
        mps = pb[2:5]
        tps = pb[5:7]

        P.dma("sp", gq.ap, dbc(qn_g[l:l + 1, :], 128), reads=[d_ro], writes=[gq.b], owner=gq.b)
        P.dma("sp", gk.ap, dbc(kn_g[l:l + 1, :], 128), reads=[d_ro], writes=[gk.b], owner=gk.b)

        cnt = dict(x=0, w=0, m=0, q=0, v=0, g=0, t=0, qs=0)

        def load_mod(r):
            P.dma("sp", lng.ap, dbc(ln_g[l:l + 1, :], 128), reads=[d_ro], writes=[lng.b], owner=lng.b)
            P.dma("sp", sh.ap, dbc(MOD[l, r:r + 1, 0:D], 128), reads=[d_mod], writes=[sh.b], owner=sh.b)
            P.dma("sp", mod1.ap, dbc(MOD[l, r:r + 1, D:2 * D], 128), reads=[d_mod], writes=[mod1.b],
                  owner=mod1.b)
            P.op("dve", lambda v: v.scalar_tensor_tensor(out=mod1.ap, in0=mod1.ap, scalar=1.0, in1=lng.ap, op0=ALU.add,
                                                         op1=ALU.mult), reads=[mod1.b, lng.b], writes=[mod1.b])

        def load_w(fb):
            w = wt[cnt["w"] % 3]
            cnt["w"] += 1
            P.dma("sp", w.ap, WIB[l][:, fb * 512:(fb + 1) * 512].rearrange("(c p) f -> p c f", p=128),
                  reads=[d_wib[l]], writes=[w.b], owner=w.b)
            return w

        def norm_block(tt, tb):
            t0 = tt * TT + tb * 128
            x = xt[cnt["x"] % 2]
            h = hb[cnt["x"] % 2]
            s1 = ssx[cnt["x"] % 2]
            r1 = rsx[cnt["x"] % 2]
            cnt["x"] += 1
            P.dma("sp", x.ap, xin[t0:t0 + 128, :], reads=[dxin], writes=[x.b], owner=x.b)
            P.op("act", lambda a: a.activation(out=junk.ap, in_=x.ap, func=AF.Square, accum_out=s1.ap[:, 0:1]),
                 reads=[x.b], writes=[s1.b])
            P.op("act", lambda a: a.activation(out=s1.ap, in_=s1.ap, func=AF.Sqrt, scale=1.0 / D, bias=EPS),
                 reads=[s1.b], writes=[s1.b])
            P.op("dve", lambda v: v.reciprocal(out=r1.ap, in_=s1.ap), reads=[s1.b], writes=[r1.b])
            P.op("dve", lambda v: v.scalar_tensor_tensor(out=tmpf.ap, in0=x.ap, scalar=r1.ap[:, 0:1], in1=mod1.ap,
                                                         op0=ALU.mult, op1=ALU.mult), reads=[x.b, r1.b, mod1.b],
                 writes=[tmpf.b])
            P.op("pool", lambda g: g.tensor_tensor(out=h.ap, in0=tmpf.ap, in1=sh.ap, op=ALU.add), reads=[tmpf.b, sh.b],
                 writes=[h.b])
            hv = [hps[i].ap.bitcast(BF16).rearrange("p (c t) -> p c t", t=128) for i in range(2)]

            def tr(t):
                for c in range(16):
                    ins = t.transpose(hv[c // 8][:, c % 8, :], h.ap[:, c * 128:(c + 1) * 128], ident.ap)
                return ins
            P.op("pe", tr, reads=[h.b, ident.b], writes=[hps[0].b, hps[1].b])
            P.op("act", lambda a: a.activation(out=hT.ap[:, 0:8, tb * 128:(tb + 1) * 128], in_=hv[0], func=AF.Copy),
                 reads=[hps[0].b], writes=[hT.b])
            P.op("dve", lambda v: v.tensor_copy(out=hT.ap[:, 8:16, tb * 128:(tb + 1) * 128], in_=hv[1]),
                 reads=[hps[1].b, hT.b], writes=[hT.b])

        def mm_tok(w, tb):
            ps = mps[cnt["m"] % 3]
            cnt["m"] += 1

            def mm(t):
                for c in range(16):
                    ins = t.matmul(ps.ap, lhsT=hT.ap[:, c, tb * 128:(tb + 1) * 128], rhs=w.ap[:, c, :],
                                   start=(c == 0), stop=(c == 15))
                return ins
            P.op("pe", mm, reads=[hT.b, w.b], writes=[ps.b])
            return ps

        def qk_post(ps, tt, tb, is_k, u0, stage, gain):
            i2 = cnt["q"]
            cnt["q"] += 1
            s4, r4, y, o = ss4[i2 % 4], rs4[i2 % 4], yq[i2 % 3], ob[i2 % 4]
            psv = ps.ap.rearrange("p (u d) -> p u d", d=HD)
            for u in range(4):
                P.op("act", lambda a, u=u: a.activation(out=junk.ap[:, 0:HD], in_=psv[:, u, :], func=AF.Square,
                                                        accum_out=s4.ap[:, u:u + 1]), reads=[ps.b], writes=[s4.b])
            P.op("act", lambda a: a.activation(out=s4.ap, in_=s4.ap, func=AF.Sqrt, scale=1.0 / HD, bias=EPS),
                 reads=[s4.b], writes=[s4.b])
            P.op("dve", lambda v: v.reciprocal(out=r4.ap, in_=s4.ap), reads=[s4.b], writes=[r4.b])
            P.op("dve", lambda v: v.tensor_tensor(out=y.ap, in0=psv, in1=bc_mid(gain.ap, 4), op=ALU.mult),
                 reads=[ps.b, gain.b], writes=[y.b])
            P.op("pool", lambda g: g.tensor_tensor(out=y.ap, in0=y.ap, in1=bc_last(r4.ap, HD), op=ALU.mult),
                 reads=[y.b, r4.b], writes=[y.b])
            is_prompt = (tt == 4)
            if is_prompt or cfg["kind"] == 2:
                if is_k and is_prompt:
                    seq = tb // 2
                    r0 = (tb % 2) * 128
                    P.dma("sp", cfg["knew"][seq, cfg["j"], r0:r0 + 128, u0 * HD:(u0 + 4) * HD],
                          y.ap.rearrange("p u d -> p (u d)"), reads=[y.b], writes=[d_outs], owner=y.b)
                P.op("dve", lambda v: v.tensor_copy(out=o.ap, in_=y.ap), reads=[y.b], writes=[o.b])
            else:
                blk = tt * 8 + tb
                yv = y.ap.rearrange("p u (a h f) -> p u a h f", a=2, h=2)
                ov = o.ap.rearrange("p u (a h f) -> p u a h f", a=2, h=2)
                cs = cosT.ap[:, blk, :, :].unsqueeze(1).broadcast_to([128, 4, 2, 32])
                sn = sinT.ap[:, blk, :, :].unsqueeze(1).broadcast_to([128, 4, 2, 32])
                x1 = yv[:, :, :, 0, :]
                x2 = yv[:, :, :, 1, :]
                t1, t2, t3, t4 = [rt[(i2 % 2) * 4 + k] for k in range(4)]
                P.op("dve", lambda v: v.tensor_tensor(out=t1.ap, in0=x1, in1=cs, op=ALU.mult), reads=[y.b, cosT.b], writes=[t1.b])
                P.op("dve", lambda v: v.tensor_tensor(out=t2.ap, in0=x2, in1=sn, op=ALU.mult), reads=[y.b, sinT.b], writes=[t2.b])
                P.op("dve", lambda v: v.tensor_tensor(out=ov[:, :, :, 0, :], in0=t1.ap, in1=t2.ap, op=ALU.subtract),
                     reads=[t1.b, t2.b], writes=[o.b])
                P.op("pool", lambda g: g.tensor_tensor(out=t3.ap, in0=x2, in1=cs, op=ALU.mult), reads=[y.b, cosT.b], writes=[t3.b])
                P.op("pool", lambda g: g.tensor_tensor(out=t4.ap, in0=x1, in1=sn, op=ALU.mult), reads=[y.b, sinT.b], writes=[t4.b])
                P.op("pool", lambda g: g.tensor_tensor(out=ov[:, :, :, 1, :], in0=t3.ap, in1=t4.ap, op=ALU.add),
                     reads=[t3.b, t4.b, o.b], writes=[o.b])
            def part_b():
                tp = tps[cnt["t"] % 2]
                cnt["t"] += 1
                tpv = tp.ap.bitcast(BF16)[:, 0:512].rearrange("p (u t) -> p u t", t=128)

                def tr(t):
                    for u in range(4):
                        ins = t.transpose(tpv[:, u, :], o.ap[:, u, :], ident.ap)
                    return ins
                P.op("pe", tr, reads=[o.b, ident.b], writes=[tp.b])
                P.op("act", lambda a: a.activation(out=stage.ap[:, :, tb * 128:(tb + 1) * 128], in_=tpv, func=AF.Copy),
                     reads=[tp.b, stage.b], writes=[stage.b])
            return part_b

        def v_post(ps, tt, tb, c0):
            v = vst[cnt["v"] % 2]
            t0 = tt * TT + tb * 128
            P.op("act", lambda a: a.activation(out=v.ap, in_=ps.ap, func=AF.Copy), reads=[ps.b], writes=[v.b])
            P.dma("pool", VS[t0:t0 + 128, c0:c0 + 512], v.ap, reads=[v.b], writes=[d_vs], owner=v.b)
            if tt == 4 and dbg != 7:
                vf = vsf[cnt["v"] % 2]
                seq = tb // 2
                r0 = (tb % 2) * 128
                P.op("dve", lambda vv: vv.tensor_copy(out=vf.ap, in_=ps.ap), reads=[ps.b, v.b], writes=[vf.b])
                P.dma("sp", cfg["vnew"][seq, cfg["j"], r0:r0 + 128, c0:c0 + 512], vf.ap, reads=[vf.b], writes=[d_outs],
                      owner=vf.b)
            cnt["v"] += 1

        def g_block(w, tt, fb_g):
            for fc in range(4):
                unit = fb_g * 4 + fc
                for th in range(2):
                    ps = mps[cnt["m"] % 3]
                    cnt["m"] += 1

                    def mm(t, fc=fc, th=th, ps=ps):
                        for c in range(16):
                            ins = t.matmul(ps.ap, lhsT=w.ap[:, c, fc * 128:(fc + 1) * 128],
                                           rhs=hT.ap[:, c, th * 512:(th + 1) * 512], start=(c == 0), stop=(c == 15))
                        return ins
                    P.op("pe", mm, reads=[hT.b, w.b], writes=[ps.b])
                    g = gst[cnt["g"] % 2]
                    cnt["g"] += 1
                    P.op("act", lambda a, ps=ps, g=g: a.activation(out=g.ap, in_=ps.ap, func=AF.Silu), reads=[ps.b],
                         writes=[g.b])
                    t0 = tt * TT + th * 512
                    P.dma("pool", GT[unit, :, t0:t0 + 512], g.ap, reads=[g.b], writes=[d_gt], owner=g.b)

        nfb = F // 512
        nqb = nq // 4
        nkb = nk // 4
        nvb = vcols // 512
        for tt in range(5):
            if dbg in (1, 2, 3, 4) and tt > 0:
                break
            if dbg == 5 and tt > 1:
                break
            if dbg in (6, 7) and tt in (1, 2, 3):
                continue
            if tt == 0:
                load_mod(0)
            if tt == 4:
                load_mod(1)
            wq = [load_w(0), load_w(1)]
            pq = []
            for tb in range(8):
                norm_block(tt, tb)
            for fb in range(nfb):
                if dbg == 1 or (dbg == 2 and fb >= nqb + nkb) or (dbg == 3 and fb >= nqb + nkb + nvb):
                    break
                w = wq.pop(0)
                if fb + 2 < nfb:
                    wq.append(load_w(fb + 2))
                if fb < nqb + nkb:
                    is_k = fb >= nqb
                    u0 = (fb - nqb) * 4 if is_k else fb * 4
                    stage = qTs[cnt["qs"] % 2]
                    cnt["qs"] += 1
                    for tb in range(8):
                        ps = mm_tok(w, tb)
                        pq.append(qk_post(ps, tt, tb, is_k, u0, stage, gk if is_k else gq))
                        if len(pq) > 3:
                            pq.pop(0)()
                    dst = (KT if is_k else QT)[u0:u0 + 4, :, tt * TT:(tt + 1) * TT].rearrange("u p t -> p u t")

                    def st(dst=dst, stage=stage, is_k=is_k):
                        P.dma("pool", dst, stage.ap, reads=[stage.b], writes=[d_kt if is_k else d_qt], owner=stage.b)
                    last_b = pq[-1]
                    pq[-1] = (lambda last_b=last_b, st=st: (last_b(), st()))
                elif fb < nqb + nkb + nvb:
                    c0 = (fb - nqb - nkb) * 512
                    for tb in range(8):
                        ps = mm_tok(w, tb)
                        v_post(ps, tt, tb, c0)
                else:
                    g_block(w, tt, fb - nqb - nkb - nvb)
                if fb == nqb + nkb and pq:
                    while pq:
                        pq.pop(0)()

    class Item:
        __slots__ = ("s_mms", "s_reads", "n", "mask", "pv", "pv_reads", "pv_writes", "first", "last", "after", "post", "E")

    def run_attention(items, sbanks, etiles, eshape, depth=2, defer=0):
        deferred = []
        assert len(sbanks) >= depth + 1 and len(etiles) >= depth + 2
        cnt = 0
        q = []
        for it in list(items) + [None] * depth:
            if it is not None:
                psS = sbanks[cnt % len(sbanks)]
                E = etiles[cnt % len(etiles)]
                cnt += 1

                def smm(t, it=it, psS=psS):
                    for (off, n, lhsT, rhs) in it.s_mms:
                        ins = t.matmul(psS.ap[:, off:off + n], lhsT=lhsT, rhs=rhs, start=True, stop=True)
                    return ins
                P.op("pe", smm, reads=it.s_reads, writes=[psS.b])
                P.op("act", lambda a, psS=psS, E=E, n=it.n: a.activation(out=E.ap[:, 0:n], in_=psS.ap[:, 0:n], func=AF.Exp,
                                                                         scale=SCALE), reads=[psS.b], writes=[E.b])
                if it.mask is not None:
                    mk_ap, mk_b = it.mask
                    ev = E.ap[:, 0:it.n]
                    if len(mk_ap.shape) == 3:
                        ev = ev.rearrange("p (a b) -> p a b", b=mk_ap.shape[2])
                    P.op("pool", lambda g, ev=ev, mk_ap=mk_ap: g.tensor_tensor(out=ev, in0=ev, in1=mk_ap, op=ALU.mult),
                         reads=[E.b, mk_b], writes=[E.b])
                it.E = E
                q.append(it)
            if q and (len(q) > depth or it is None):
                p_ = q.pop(0)

                def pvmm(t, it=p_):
                    for (out_ap, lhsT, off, n) in it.pv:
                        ins = t.matmul(out_ap, lhsT=lhsT, rhs=it.E.ap[:, off:off + n], start=it.first, stop=it.last)
                    return ins
                P.op("pe", pvmm, reads=[p_.E.b, ones.b] + p_.pv_reads, writes=p_.pv_writes)
                deferred = [(c_ - 1, f_) for (c_, f_) in deferred]
                while deferred and deferred[0][0] <= 0:
                    deferred.pop(0)[1]()
                if p_.after is not None:
                    later = p_.after()
                    if later is not None:
                        deferred.append((defer, later))
                if getattr(p_, "post", None) is not None:
                    if defer:
                        deferred.append((defer, p_.post))
                    else:
                        p_.post()
        for (c_, f_) in deferred:
            f_()

    def load_ctx(cache_k, cache_v, j, kcol0, nku, vcol0, vw, ckf, ckb, ckT, cvf, cvb, tp):
        P.dma("sp", ckf.ap, cache_k[j, :, kcol0:kcol0 + nku * HD].rearrange("(c p) f -> p c f", p=128),
              reads=[d_ro], writes=[ckf.b], owner=ckf.b)
        P.dma("sp", cvf.ap, cache_v[j, :, vcol0:vcol0 + vw].rearrange("(c p) f -> p c f", p=128),
              reads=[d_ro], writes=[cvf.b], owner=cvf.b)
        P.op("dve", lambda v: v.tensor_copy(out=ckb.ap, in_=ckf.ap), reads=[ckf.b], writes=[ckb.b])
        P.op("pool", lambda g: g.tensor_copy(out=cvb.ap, in_=cvf.ap), reads=[cvf.b], writes=[cvb.b])
        tpv = tp.ap.bitcast(BF16)[:, 0:nku * 256].rearrange("p (u t) -> p u t", t=256)

        def tr(t):
            for u in range(nku):
                for c in range(2):
                    ins = t.transpose(tpv[:, u, c * 128:(c + 1) * 128], ckb.ap[:, c, u * HD:(u + 1) * HD], ident.ap)
            return ins
        P.op("pe", tr, reads=[ckb.b, ident.b], writes=[tp.b])
        P.op("act", lambda a: a.activation(out=ckT.ap, in_=tpv, func=AF.Copy), reads=[tp.b], writes=[ckT.b])

    def phase2_A(l):
        j = l // 3
        ar.reset()
        kT = [mk(ar, [NTOK], BF16, "kT%d" % i) for i in range(2)]
        vv = [mk(ar, [40, HD], BF16, "vv%d" % i) for i in range(2)]
        ckf = mk(ar, [2, HD], F32, "ckf")
        ckb = mk(ar, [2, HD], BF16, "ckb")
        ckT = [mk(ar, [1, 256], BF16, "ckT%d" % i) for i in range(2)]
        cvf = mk(ar, [2, HD], F32, "cvf")
        cvb = [mk(ar, [2, HD], BF16, "cvb%d" % i) for i in range(2)]
        qT = [mk(ar, [4, 512], BF16, "qT%d" % i) for i in range(2)]
        gT = [mk(ar, [4, 512], BF16, "gT%d" % i) for i in range(2)]
        oT = [mk(ar, [4, 512], BF16, "oT%d" % i) for i in range(2)]
        mprev = mk(ar, [4, 128], BF16, "mprev")
        mnext = mk(ar, [4, 128], BF16, "mnext")
        sexp = mk(ar, [16], F32, "sexp")
        et = [mk(ar, [512], BF16, "et%d" % i) for i in range(4)]
        den = [mk(ar, [4, 128], F32, "den%d" % i) for i in range(2)]
        of = [mk(ar, [4, 128], F32, "of%d" % i) for i in range(2)]
        sb_, ob_, lb_, tp = pb[0:3], pb[3:5], pb[5:7], pb[7]
        P.op("pool", lambda g: g.memset(mprev.ap, 1.0), writes=[mprev.b])
        P.op("pool", lambda g: g.affine_select(out=mprev.ap, in_=mprev.ap, compare_op=ALU.is_ge, fill=0.0, base=0,
                                               pattern=[[0, 4], [-1, 128]], channel_multiplier=1), reads=[mprev.b], writes=[mprev.b])
        P.op("pool", lambda g: g.memset(mnext.ap, 1.0), writes=[mnext.b])
        P.op("pool", lambda g: g.affine_select(out=mnext.ap, in_=mnext.ap, compare_op=ALU.is_ge, fill=0.0, base=0,
                                               pattern=[[0, 4], [1, 128]], channel_multiplier=-1), reads=[mnext.b], writes=[mnext.b])
        P.dma("sp", sexp.ap, dbc(sink_a[j:j + 1, :], 128), reads=[d_ro], writes=[sexp.b], owner=sexp.b)
        P.op("act", lambda a: a.activation(out=sexp.ap, in_=sexp.ap, func=AF.Exp), reads=[sexp.b], writes=[sexp.b])
        cnt = dict(q=0, f=0)
        for kvh in range(4):
            k_, v_, ckT_, cvb_ = kT[kvh % 2], vv[kvh % 2], ckT[kvh % 2], cvb[kvh % 2]
            P.dma("sp", k_.ap, KT[kvh], reads=[d_kt], writes=[k_.b], owner=k_.b)
            P.dma("sp", v_.ap, VS[:, kvh * HD:(kvh + 1) * HD].rearrange("(c p) d -> p c d", p=128), reads=[d_vs],
                  writes=[v_.b], owner=v_.b)
            load_ctx(cak, cav, j, kvh * HD, 1, kvh * HD, HD, ckf, ckb, ckT_, cvf, cvb_, tp)
            items = []
            loaders = []
            slab_first = []
            for q512 in range(10):
                q_, g_, o_ = qT[cnt["q"] % 2], gT[cnt["q"] % 2], oT[cnt["q"] % 2]
                cnt["q"] += 1
                t0 = q512 * 512

                def ld(q_=q_, g_=g_, t0=t0, kvh=kvh):
                    P.dma("sp", q_.ap, QT[kvh * 4:(kvh + 1) * 4, :, t0:t0 + 512].rearrange("u p t -> p u t"), reads=[d_qt],
                          writes=[q_.b], owner=q_.b)
                    P.dma("sp", g_.ap, GT[kvh * 4:(kvh + 1) * 4, :, t0:t0 + 512].rearrange("u p t -> p u t"), reads=[d_gt],
                          writes=[g_.b], owner=g_.b)
                loaders.append(ld)
                slab_first.append(len(items))
                for qb in range(4):
                    B = q512 * 4 + qb
                    if B < 32:
                        ch = []
                        if B > 0:
                            ch.append(("l", B - 1, mprev))
                        ch.append(("l", B, None))
                        if B < 31:
                            ch.append(("l", B + 1, mnext))
                        ch += [("c", 0, None), ("c", 1, None)]
                        import os as _os
                        if _os.environ.get("A_NOCTX"):
                            ch = ch[:-2]
                        if _os.environ.get("A_NOMASK"):
                            ch = [(a, b, None) for (a, b, c_) in ch]
                    else:
                        sq = (B - 32) // 2
                        ch = [("l", 32 + 2 * sq, None), ("l", 32 + 2 * sq + 1, None)]
                    fi = cnt["f"] % 2
                    cnt["f"] += 1
                    psO, psL = ob_[fi], lb_[fi]
                    rhs = q_.ap[:, :, qb * 128:(qb + 1) * 128]
                    for ci, (typ, c, msk) in enumerate(ch):
                        it = Item()
                        if typ == "l":
                            lk, kb = k_.ap[:, c * 128:(c + 1) * 128], k_.b
                            lv, vb = v_.ap[:, c, :], v_.b
                        else:
                            lk, kb = ckT_.ap[:, 0, c * 128:(c + 1) * 128], ckT_.b
                            lv, vb = cvb_.ap[:, c, :], cvb_.b
                        it.s_mms = [(0, 512, lk, rhs)]
                        it.s_reads = [kb, q_.b]
                        it.n = 512
                        it.mask = (msk.ap, msk.b) if msk is not None else None
                        it.pv = [(psO.ap, lv, 0, 512), (psL.ap, ones.ap, 0, 512)]
                        it.pv_reads = [vb]
                        it.pv_writes = [psO.b, psL.b]
                        it.first = (ci == 0)
                        it.last = (ci == len(ch) - 1)
                        it.after = None
                        it.post = None
                        if ci == len(ch) - 1:
                            def fin(psO=psO, psL=psL, fi=fi, kvh=kvh, g_=g_, o_=o_, qb=qb):
                                d_, f_ = den[fi], of[fi]
                                lv3 = psL.ap.rearrange("p (a b) -> p a b", b=128)
                                ov3 = psO.ap.rearrange("p (a b) -> p a b", b=128)
                                P.op("dve", lambda v: v.tensor_tensor(out=d_.ap, in0=lv3, in1=bc_last(sexp.ap[:, kvh * 4:(kvh + 1) * 4], 128),
                                                                      op=ALU.add), reads=[psL.b, sexp.b], writes=[d_.b])
                                P.op("dve", lambda v: v.reciprocal(out=d_.ap, in_=d_.ap), reads=[d_.b], writes=[d_.b])
                                P.op("dve", lambda v: v.tensor_tensor(out=f_.ap, in0=ov3, in1=d_.ap, op=ALU.mult),
                                     reads=[psO.b, d_.b], writes=[f_.b])
                                P.op("pool", lambda g: g.tensor_tensor(out=o_.ap[:, :, qb * 128:(qb + 1) * 128], in0=f_.ap,
                                                                       in1=g_.ap[:, :, qb * 128:(qb + 1) * 128], op=ALU.mult),
                                     reads=[f_.b, g_.b, o_.b], writes=[o_.b])
                                if qb == 3:
                                    pass
                            it.after = fin
                        items.append(it)
                    if qb == 3:
                        last = items[-1]
                        prev_after = last.after

                        def fin2(prev_after=prev_after, o_=o_, kvh=kvh, t0=t0):
                            prev_after()
                            P.dma("pool", OT[kvh * 4:(kvh + 1) * 4, :, t0:t0 + 512].rearrange("u p t -> p u t"), o_.ap,
                                  reads=[o_.b], writes=[d_ot], owner=o_.b)
                        last.after = fin2
            loaders[0]()
            for si in range(len(loaders) - 1):
                items[slab_first[si]].post = loaders[si + 1]
            run_attention(items, sb_, et, None)

    def phase2_B(l):
        j = 0
        lam_init = LAMBDA_INIT[l]
        ar.reset()
        kT = [mk(ar, [2, NTOK], BF16, "kTb%d" % i) for i in range(2)]
        vv = [mk(ar, [40, 256], BF16, "vvb%d" % i) for i in range(2)]
        ckf = mk(ar, [2, 256], F32, "ckfb")
        ckb = mk(ar, [2, 256], BF16, "ckbb")
        ckT = [mk(ar, [2, 256], BF16, "ckTb%d" % i) for i in range(2)]
        cvf = mk(ar, [2, 256], F32, "cvfb")
        cvb = [mk(ar, [2, 256], BF16, "cvbb%d" % i) for i in range(2)]
        qT = [mk(ar, [2, 1024], BF16, "qTb%d" % i) for i in range(2)]
        gT = [mk(ar, [2, 1024], BF16, "gTb%d" % i) for i in range(2)]
        oT = [mk(ar, [2, 1024], BF16, "oTb%d" % i) for i in range(2)]
        et = [mk(ar, [512], BF16, "et%d" % i) for i in range(4)]
        lamt = mk(ar, [512], F32, "lamt")
        ltmp = mk(ar, [2, 128], F32, "ltmp")
        ls = mk(ar, [2], F32, "ls")
        nlam = mk(ar, [1], F32, "nlam")
        sube = mk(ar, [2], F32, "sube")
        R = [mk(ar, [2, 256], F32, "Rb%d" % i) for i in range(4)]
        t1 = [mk(ar, [2, 256], F32, "t1b%d" % i) for i in range(4)]
        t2 = [mk(ar, [2, 256], F32, "t2b%d" % i) for i in range(4)]
        ob32 = [mk(ar, [2, 256], F32, "ob32%d" % i) for i in range(4)]
        sqb = [mk(ar, [2, 256], BF16, "sqb%d" % i) for i in range(4)]
        sd = [mk(ar, [256], F32, "sdb%d" % i) for i in range(4)]
        sb_, o0_, o1_, lb_, xb_, tp = pb[0:3], pb[3], pb[4], pb[5], pb[6], pb[7]
        P.dma("sp", lamt.ap, dbc(lam_b[0:1, :], 128), reads=[d_ro], writes=[lamt.b], owner=lamt.b)
        lv = lamt.ap.rearrange("p (a b) -> p a b", b=128)
        P.op("dve", lambda v: v.tensor_tensor(out=ltmp.ap[:, 0, :], in0=lv[:, 0, :], in1=lv[:, 1, :], op=ALU.mult),
             reads=[lamt.b], writes=[ltmp.b])
        P.op("dve", lambda v: v.tensor_tensor(out=ltmp.ap[:, 1, :], in0=lv[:, 2, :], in1=lv[:, 3, :], op=ALU.mult),
             reads=[lamt.b, ltmp.b], writes=[ltmp.b])
        P.op("dve", lambda v: v.tensor_reduce(out=ls.ap, in_=ltmp.ap, axis=AX.X, op=ALU.add), reads=[ltmp.b], writes=[ls.b])
        P.op("act", lambda a: a.activation(out=ls.ap, in_=ls.ap, func=AF.Exp), reads=[ls.b], writes=[ls.b])
        P.op("dve", lambda v: v.tensor_tensor(out=nlam.ap, in0=ls.ap[:, 1:2], in1=ls.ap[:, 0:1], op=ALU.subtract),
             reads=[ls.b], writes=[nlam.b])
        P.op("dve", lambda v: v.tensor_scalar(out=nlam.ap, in0=nlam.ap, scalar1=-lam_init, scalar2=None, op0=ALU.add),
             reads=[nlam.b], writes=[nlam.b])
        P.dma("sp", sube.ap, subln_b[0].rearrange("(c p) -> p c", p=128), reads=[d_ro], writes=[sube.b], owner=sube.b,
              allow_slow_non_contiguous=True)
        P.op("dve", lambda v: v.tensor_scalar(out=sube.ap, in0=sube.ap, scalar1=1.0 - lam_init, scalar2=None, op0=ALU.mult),
             reads=[sube.b], writes=[sube.b])
        cnt = dict(q=0, f=0)
        for h in range(8):
            k_, v_, ckT_, cvb_ = kT[h % 2], vv[h % 2], ckT[h % 2], cvb[h % 2]
            P.dma("sp", k_.ap, KT[2 * h:2 * h + 2].rearrange("u p t -> p u t"), reads=[d_kt], writes=[k_.b], owner=k_.b)
            P.dma("sp", v_.ap, VS[:, h * 256:(h + 1) * 256].rearrange("(c p) d -> p c d", p=128), reads=[d_vs],
                  writes=[v_.b], owner=v_.b)
            load_ctx(cbk, cbv, 0, 2 * h * HD, 2, h * 256, 256, ckf, ckb, ckT_, cvf, cvb_, tp)
            items = []
            loaders = []
            slab_first = []
            for q1k in range(5):
                q_, g_, o_ = qT[cnt["q"] % 2], gT[cnt["q"] % 2], oT[cnt["q"] % 2]
                cnt["q"] += 1
                t0 = q1k * 1024

                def ld(q_=q_, g_=g_, t0=t0, h=h):
                    for (dst, src, dd) in ((q_, QT, d_qt), (g_, GT, d_gt)):
                        P.dma("sp", dst.ap, src[2 * h:2 * h + 2, :, t0:t0 + 1024].rearrange("u p t -> p u t"), reads=[dd],
                              writes=[dst.b], owner=dst.b)
                loaders.append(ld)
                slab_first.append(len(items))
                for qi in range(4):
                    B = q1k * 4 + qi
                    if B < 16:
                        ch = [("l", c) for c in range(32)] + [("c", 0), ("c", 1)]
                    else:
                        ch = [("l", 32 + 2 * (B - 16)), ("l", 32 + 2 * (B - 16) + 1)]
                    fi = cnt["f"] % 4
                    cnt["f"] += 1
                    qs = slice(qi * 256, (qi + 1) * 256)
                    for ci, (typ, c) in enumerate(ch):
                        it = Item()
                        it.s_mms = []
                        for m in range(2):
                            if typ == "l":
                                lk = k_.ap[:, m, c * 128:(c + 1) * 128]
                            else:
                                lk = ckT_.ap[:, m, c * 128:(c + 1) * 128]
                            it.s_mms.append((m * 256, 256, lk, q_.ap[:, m, qs]))
                        if typ == "l":
                            kb, vb = k_.b, v_.b
                            lvs = [v_.ap[:, c, e * 128:(e + 1) * 128] for e in range(2)]
                        else:
                            kb, vb = ckT_.b, cvb_.b
                            lvs = [cvb_.ap[:, c, e * 128:(e + 1) * 128] for e in range(2)]
                        it.s_reads = [kb, q_.b]
                        it.n = 512
                        it.mask = None
                        it.pv = [(o0_.ap, lvs[0], 0, 512), (o1_.ap, lvs[1], 0, 512), (lb_.ap, ones.ap, 0, 512)]
                        it.pv_reads = [vb]
                        it.pv_writes = [o0_.b, o1_.b, lb_.b]
                        it.first = (ci == 0)
                        it.last = (ci == len(ch) - 1)
                        it.after = None
                        it.post = None
                        if ci == len(ch) - 1:
                            def fin(fi=fi, h=h, g_=g_, o_=o_, qs=qs, qi=qi, t0=t0):
                                R_, t1_, t2_, o32, sq_, sd_ = R[fi], t1[fi], t2[fi], ob32[fi], sqb[fi], sd[fi]
                                l3 = lb_.ap.rearrange("p (a b) -> p a b", b=256)
                                P.op("dve", lambda v: v.reciprocal(out=R_.ap, in_=l3), reads=[lb_.b], writes=[R_.b])
                                for e, ob in enumerate((o0_, o1_)):
                                    o3 = ob.ap.rearrange("p (a b) -> p a b", b=256)
                                    P.op("dve", lambda v, o3=o3, e=e: v.tensor_tensor(out=t1_.ap[:, e, :], in0=o3[:, 0, :], in1=R_.ap[:, 0, :],
                                                                                      op=ALU.mult), reads=[ob.b, R_.b], writes=[t1_.b])
                                    P.op("dve", lambda v, o3=o3, e=e: v.tensor_tensor(out=t2_.ap[:, e, :], in0=o3[:, 1, :], in1=R_.ap[:, 1, :],
                                                                                      op=ALU.mult), reads=[ob.b, R_.b], writes=[t2_.b])
                                def part2():
                                    fin_part2(R_, t1_, t2_, o32, sq_, sd_, g_, o_, qs, qi, h, t0)
                                return part2

                            def fin_part2(R_, t1_, t2_, o32, sq_, sd_, g_, o_, qs, qi, h, t0):
                                P.op("dve", lambda g: g.scalar_tensor_tensor(out=o32.ap, in0=t2_.ap, scalar=nlam.ap[:, 0:1], in1=t1_.ap,
                                                                             op0=ALU.mult, op1=ALU.add), reads=[t1_.b, t2_.b, nlam.b],
                                     writes=[o32.b])
                                P.op("pool", lambda g: g.tensor_tensor(out=sq_.ap, in0=o32.ap, in1=o32.ap, op=ALU.mult), reads=[o32.b],
                                     writes=[sq_.b])

                                def ssmm(t):
                                    for e in range(2):
                                        ins = t.matmul(xb_.ap[:, 0:256], lhsT=ones.ap, rhs=sq_.ap[:, e, :], start=(e == 0), stop=(e == 1))
                                    return ins
                                P.op("pe", ssmm, reads=[sq_.b, ones.b], writes=[xb_.b])
                                P.op("act", lambda a: a.activation(out=sd_.ap, in_=xb_.ap[:, 0:256], func=AF.Sqrt, scale=1.0 / 256, bias=EPS),
                                     reads=[xb_.b], writes=[sd_.b])
                                P.op("dve", lambda v: v.reciprocal(out=sd_.ap, in_=sd_.ap), reads=[sd_.b], writes=[sd_.b])
                                P.op("dve", lambda v: v.tensor_tensor(out=o32.ap, in0=o32.ap, in1=bc_mid(sd_.ap, 2), op=ALU.mult),
                                     reads=[o32.b, sd_.b], writes=[o32.b])
                                for e in range(2):
                                    P.op("dve", lambda g, e=e: g.scalar_tensor_tensor(out=o_.ap[:, e, qs], in0=o32.ap[:, e, :],
                                                                                       scalar=sube.ap[:, e:e + 1], in1=g_.ap[:, e, qs],
                                                                                       op0=ALU.mult, op1=ALU.mult),
                                         reads=[o32.b, sube.b, g_.b, o_.b], writes=[o_.b])
                                if qi == 3:
                                    P.dma("pool", OT[2 * h:2 * h + 2, :, t0:t0 + 1024].rearrange("u p t -> p u t"), o_.ap,
                                          reads=[o_.b], writes=[d_ot], owner=o_.b)
                            it.after = fin
                        items.append(it)
            loaders[0]()
            for si in range(len(loaders) - 1):
                items[slab_first[si]].post = loaders[si + 1]
            run_attention(items, sb_, et, None, defer=4)

    def phase2_C(l):
        ar.reset()
        kT = [mk(ar, [NTOK], BF16, "kT%d" % i) for i in range(2)]
        vv = [mk(ar, [40, HD], BF16, "vv%d" % i) for i in range(2)]
        ckf = mk(ar, [2, HD], F32, "ckf")
        ckb = mk(ar, [2, HD], BF16, "ckb")
        ckT = [mk(ar, [1, 256], BF16, "ckT%d" % i) for i in range(2)]
        cvf = mk(ar, [2, HD], F32, "cvf")
        cvb = [mk(ar, [2, HD], BF16, "cvb%d" % i) for i in range(2)]
        qT = [mk(ar, [1024], BF16, "qTc%d" % i) for i in range(2)]
        gT = [mk(ar, [1024], BF16, "gTc%d" % i) for i in range(2)]
        oT = [mk(ar, [1024], BF16, "oTc%d" % i) for i in range(2)]
        et = [mk(ar, [128], BF16, "etc%d" % i) for i in range(5)]
        cbr = mk(ar, [15, 64], F32, "cbr")
        cbm = mk(ar, [15, 64], BF16, "cbm")
        colm = mk(ar, [64], F32, "colm")
        EB = [mk(ar, [25, 128], BF16, "EB%d" % i) for i in range(2)]
        rr = [mk(ar, [128], F32, "rrc%d" % i) for i in range(2)]
        of = [mk(ar, [128], F32, "ofc%d" % i) for i in range(2)]
        sb_, ob_, lb_, tp = pb[0:3], pb[3:5], pb[5:7], pb[7]
        P.op("pool", lambda g: g.memset(colm.ap, 1.0), writes=[colm.b])
        for h0 in (0, 64):
            pr = slice(h0, h0 + 64)
            P.op("pool", lambda g, pr=pr: g.affine_select(out=colm.ap[pr, 0:8], in_=colm.ap[pr, 0:8], compare_op=ALU.is_ge, fill=0.0,
                                                          base=15, pattern=[[0, 8]], channel_multiplier=-1), reads=[colm.b], writes=[colm.b])
            P.op("pool", lambda g, pr=pr: g.affine_select(out=colm.ap[pr, 8:57], in_=colm.ap[pr, 8:57], compare_op=ALU.is_ge, fill=0.0,
                                                          base=0, pattern=[[-1, 49]], channel_multiplier=1), reads=[colm.b], writes=[colm.b])
            P.op("pool", lambda g, pr=pr: g.affine_select(out=colm.ap[pr, 8:57], in_=colm.ap[pr, 8:57], compare_op=ALU.is_ge, fill=0.0,
                                                          base=15, pattern=[[1, 49]], channel_multiplier=-1), reads=[colm.b], writes=[colm.b])
            P.op("pool", lambda g, pr=pr: g.affine_select(out=colm.ap[pr, 57:64], in_=colm.ap[pr, 57:64], compare_op=ALU.is_ge, fill=0.0,
                                                          base=-48, pattern=[[0, 7]], channel_multiplier=1), reads=[colm.b], writes=[colm.b])

        def rs_(r):
            return min(max(r - 4, 0), 56)

        def qblock_chunks(jb):
            lo = rs_(2 * jb) // 2
            hi = (rs_(2 * jb + 1) + 7) // 2
            return list(range(lo, hi + 1))

        classes = {0: 0, 1: 1, 30: 3, 31: 4}

        def cls_of(jb):
            return classes.get(jb, 2)

        rep = {0: 0, 1: 1, 2: 10, 3: 30, 4: 31}

        def build_EB(EB_):
            P.op("pool", lambda g: g.memset(EB_.ap, 0.0), writes=[EB_.b])
            for ci in range(5):
                jb = rep[ci]
                for c in qblock_chunks(jb):
                    dlt = c - jb
                    slot = ci * 5 + (dlt + 3 if ci == 4 else (dlt if ci == 0 else dlt + 2 if ci in (2, 3) else dlt + 1))
                    for qr in range(2):
                        r = 2 * jb + qr
                        for kr in range(2):
                            ka = 2 * c + kr
                            if rs_(r) <= ka <= rs_(r) + 7:
                                i = ka - r + 7
                                P.op("pool", lambda g, kr=kr, qr=qr, slot=slot, i=i: g.tensor_copy(
                                    out=EB_.ap[kr * 64:(kr + 1) * 64, slot, qr * 64:(qr + 1) * 64],
                                    in_=cbm.ap[kr * 64:(kr + 1) * 64, i, :]), reads=[cbm.b, EB_.b], writes=[EB_.b])

        def slot_of(jb, c):
            ci = cls_of(jb)
            dlt = c - jb
            return ci * 5 + (dlt + 3 if ci == 4 else (dlt if ci == 0 else dlt + 2 if ci in (2, 3) else dlt + 1))

        cnt = dict(q=0, f=0)
        for h in range(16):
            k_, v_, ckT_, cvb_, EB_ = kT[h % 2], vv[h % 2], ckT[h % 2], cvb[h % 2], EB[h % 2]
            P.dma("sp", k_.ap, KT[h], reads=[d_kt], writes=[k_.b], owner=k_.b)
            P.dma("sp", v_.ap, VS[:, h * HD:(h + 1) * HD].rearrange("(c p) d -> p c d", p=128), reads=[d_vs],
                  writes=[v_.b], owner=v_.b)
            load_ctx(cck, ccv, 0, h * HD, 1, h * HD, HD, ckf, ckb, ckT_, cvf, cvb_, tp)
            for h0 in (0, 64):
                P.dma("sp", cbr.ap[h0:h0 + 64], rpbt[h], reads=[d_ro], writes=[cbr.b], owner=cbr.b)
            P.op("act", lambda a: a.activation(out=cbr.ap, in_=cbr.ap, func=AF.Exp), reads=[cbr.b], writes=[cbr.b])
            P.op("pool", lambda g: g.tensor_tensor(out=cbm.ap, in0=cbr.ap, in1=bc_mid(colm.ap, 15), op=ALU.mult),
                 reads=[cbr.b, colm.b], writes=[cbm.b])
            build_EB(EB_)
            items = []
            loaders = []
            slab_first = []
            for q1k in range(5):
                q_, g_, o_ = qT[cnt["q"] % 2], gT[cnt["q"] % 2], oT[cnt["q"] % 2]
                cnt["q"] += 1
                t0 = q1k * 1024

                def ld(q_=q_, g_=g_, t0=t0, h=h):
                    P.dma("sp", q_.ap, QT[h, :, t0:t0 + 1024], reads=[d_qt], writes=[q_.b], owner=q_.b)
                    P.dma("sp", g_.ap, GT[h, :, t0:t0 + 1024], reads=[d_gt], writes=[g_.b], owner=g_.b)
                loaders.append(ld)
                slab_first.append(len(items))
                for qi in range(8):
                    B = q1k * 8 + qi
                    if B < 32:
                        ch = [("l", c, slot_of(B, c)) for c in qblock_chunks(B)] + [("c", 0, None), ("c", 1, None)]
                    else:
                        sq = (B - 32) // 2
                        ch = [("l", 32 + 2 * sq, None), ("l", 32 + 2 * sq + 1, None)]
                    fi = cnt["f"] % 2
                    cnt["f"] += 1
                    psO, psL = ob_[fi], lb_[fi]
                    qs = slice(qi * 128, (qi + 1) * 128)
                    for ci, (typ, c, slot) in enumerate(ch):
                        it = Item()
                        if typ == "l":
                            lk, kb = k_.ap[:, c * 128:(c + 1) * 128], k_.b
                            lv_, vb = v_.ap[:, c, :], v_.b
                        else:
                            lk, kb = ckT_.ap[:, 0, c * 128:(c + 1) * 128], ckT_.b
                            lv_, vb = cvb_.ap[:, c, :], cvb_.b
                        it.s_mms = [(0, 128, lk, q_.ap[:, qs])]
                        it.s_reads = [kb, q_.b]
                        it.n = 128
                        it.mask = (EB_.ap[:, slot, :], EB_.b) if slot is not None else None
                        it.pv = [(psO.ap[:, 0:128], lv_, 0, 128), (psL.ap[:, 0:128], ones.ap, 0, 128)]
                        it.pv_reads = [vb]
                        it.pv_writes = [psO.b, psL.b]
                        it.first = (ci == 0)
                        it.last = (ci == len(ch) - 1)
                        it.after = None
                        it.post = None
                        if ci == len(ch) - 1:
                            def fin(psO=psO, psL=psL, fi=fi, h=h, g_=g_, o_=o_, qs=qs, qi=qi, t0=t0):
                                r_, f_ = rr[fi], of[fi]
                                P.op("dve", lambda v: v.reciprocal(out=r_.ap, in_=psL.ap[:, 0:128]), reads=[psL.b], writes=[r_.b])
                                P.op("dve", lambda v: v.tensor_tensor(out=f_.ap, in0=psO.ap[:, 0:128], in1=r_.ap, op=ALU.mult),
                                     reads=[psO.b, r_.b], writes=[f_.b])
                                P.op("pool", lambda g: g.tensor_tensor(out=o_.ap[:, qs], in0=f_.ap, in1=g_.ap[:, qs], op=ALU.mult),
                                     reads=[f_.b, g_.b, o_.b], writes=[o_.b])
                                if qi == 7:
                                    P.dma("pool", OT[h, :, t0:t0 + 1024], o_.ap, reads=[o_.b], writes=[d_ot], owner=o_.b)
                            it.after = fin
                        items.append(it)
            loaders[0]()
            for si in range(len(loaders) - 1):
                items[slab_first[si]].post = loaders[si + 1]
            run_attention(items, sb_, et, None)

    def phase3(l):
        xin, dxin = x_aps[l], d_x[l]
        xout, dxout = x_aps[l + 1], d_x[l + 1]
        ar.reset()
        wo = mk(ar, [16, D], BF16, "wo")
        gt = mk(ar, [D], F32, "gt")
        xt = [mk(ar, [D], F32, "xt%d" % i) for i in range(2)]
        xo = [mk(ar, [D], F32, "xo%d" % i) for i in range(2)]
        ot = [mk(ar, [16, 512], BF16, "ot%d" % i) for i in range(2)]
        P.dma("sp", wo.ap, WOB[l].rearrange("(c p) f -> p c f", p=128), reads=[d_wob[l]], writes=[wo.b], owner=wo.b)
        for tb in range(40):
            t0 = tb * 128
            if tb == 0 or tb == 32:
                r = 0 if tb == 0 else 1
                P.dma("sp", gt.ap, dbc(MOD[l, r:r + 1, 2 * D:3 * D], 128), reads=[d_mod], writes=[gt.b], owner=gt.b)
            o_ = ot[(tb // 4) % 2]
            if tb % 4 == 0:
                P.dma("sp", o_.ap, OT[:, :, t0:t0 + 512].rearrange("u p t -> p u t"), reads=[d_ot], writes=[o_.b], owner=o_.b)
            x = xt[tb % 2]
            y = xo[tb % 2]
            P.dma("sp", x.ap, xin[t0:t0 + 128, :], reads=[dxin], writes=[x.b], owner=x.b)
            tl = tb % 4
            for fb in range(4):
                ps = pb[(tb * 4 + fb) % 4]

                def mm(t, ps=ps, o_=o_, tl=tl, fb=fb):
                    for c in range(16):
                        ins = t.matmul(ps.ap, lhsT=o_.ap[:, c, tl * 128:(tl + 1) * 128], rhs=wo.ap[:, c, fb * 512:(fb + 1) * 512],
                                       start=(c == 0), stop=(c == 15))
                    return ins
                P.op("pe", mm, reads=[o_.b, wo.b], writes=[ps.b])
                fs = slice(fb * 512, (fb + 1) * 512)
                P.op("dve", lambda v, ps=ps, y=y, fs=fs: v.tensor_tensor(out=y.ap[:, fs], in0=ps.ap, in1=gt.ap[:, fs], op=ALU.mult),
                     reads=[ps.b, gt.b, y.b], writes=[y.b])
            P.op("pool", lambda g, x=x, y=y: g.tensor_tensor(out=y.ap, in0=y.ap, in1=x.ap, op=ALU.add), reads=[x.b, y.b], writes=[y.b])
            P.dma("pool", xout[t0:t0 + 128, :], y.ap, reads=[y.b], writes=[dxout], owner=y.b)

    build_consts()
    P.barrier()
    for l in range(n_layers):
        cast_weights(l)
    phase0()
    P.barrier()
    for l in range(n_layers):
        if stop_phase == (l, 0):
            break
        phase1(l)
        P.barrier()
        if stop_phase == (l, 1):
            break
        [phase2_A, phase2_B, phase2_C][KINDS[l]](l)
        P.barrier()
        if stop_phase == (l, 2):
            break
        phase3(l)
        P.barrier()
    P.barrier()

    from contextlib import ExitStack
    with ExitStack() as stack:
        P.emit(nc, stack)
    nc._n_sems = P.n_sems
    return nc


def make_in_maps(inp, n_layers=DEPTH):
    f = lambda a: np.ascontiguousarray(a, dtype=np.float32)
    xs, xp = inp["x_sample"], inp["x_prompt"]
    rpb = np.asarray(inp["rpb_c"])[0]
    kc = np.arange(64)[:, None]
    qc = np.arange(64)[None, :]
    idx = np.clip(kc - qc + 15, 0, 30)
    rpbt = f(rpb[:, :, idx].transpose(0, 2, 1, 3))
    shared = dict(
        ln_g=f(inp["ln_g"]), ada_w=f(inp["ada_w"][:n_layers]), ada_b=f(inp["ada_b"]), w_out=f(inp["w_out"][:n_layers]),
        qn_g=f(inp["qn_g"]), kn_g=f(inp["kn_g"]), w_in_a=f(inp["w_in_a"][:2 if n_layers > 3 else 1]),
        w_in_b=f(inp["w_in_b"] if n_layers > 1 else np.asarray(inp["w_in_b"])[:, :8]),
        w_in_c=f(inp["w_in_c"] if n_layers > 2 else np.asarray(inp["w_in_c"])[:, :8]), sink_a=f(inp["sink_a"]), lam_b=f(np.asarray(inp["lam_b"]).reshape(1, 512)),
        subln_b=f(inp["subln_b"]), rpbt=rpbt,
    )
    maps = []
    for i in range(8):
        m = dict(shared)
        m["x"] = f(np.concatenate([np.asarray(xs[i]), np.asarray(xp[4 * i:4 * i + 4]).reshape(1024, D)], axis=0))
        m["cpair"] = f(np.stack([np.asarray(inp["c"])[i], np.asarray(inp["c_ctx"])], axis=0))
        m["cak"] = f(np.asarray(inp["cache_a_k"])[i].reshape(2, 256, 512))
        m["cav"] = f(np.asarray(inp["cache_a_v"])[i].reshape(2, 256, 512))
        m["cbk"] = f(np.asarray(inp["cache_b_k"])[i].reshape(1, 256, 2048))
        m["cbv"] = f(np.asarray(inp["cache_b_v"])[i].reshape(1, 256, 2048))
        m["cck"] = f(np.asarray(inp["cache_c_k"])[i].reshape(1, 256, 2048))
        m["ccv"] = f(np.asarray(inp["cache_c_v"])[i].reshape(1, 256, 2048))
        maps.append(m)
    return maps


def assemble(results):
    y = np.stack([r["y"] for r in results], axis=0)
    y_sample = np.ascontiguousarray(y[:, :NS, :])
    y_prompt = np.ascontiguousarray(y[:, NS:, :].reshape(32, 256, D))
    cat = lambda k: np.concatenate([r[k] for r in results], axis=0)
    return (y_prompt, y_sample,
            cat("nak").reshape(32, 2, 256, 4, 128), cat("nav").reshape(32, 2, 256, 4, 128),
            cat("nbk").reshape(32, 1, 256, 8, 2, 128), cat("nbv").reshape(32, 1, 256, 8, 256),
            cat("nck").reshape(32, 1, 256, 16, 128), cat("ncv").reshape(32, 1, 256, 16, 128))


def kernel(**inputs):
    nc = build()
    in_maps = make_in_maps(inputs)
    res = run_bass_kernel_spmd(nc, in_maps, core_ids=list(range(8)))
    return assemble(res.results)
```

```python
import math
import numpy as np
import concourse.bass as bass
import concourse.mybir as mybir
from concourse.bass_utils import run_bass_kernel_spmd

F32 = mybir.dt.float32
BF16 = mybir.dt.bfloat16
I32 = mybir.dt.int32
AF = mybir.ActivationFunctionType
ALU = mybir.AluOpType
AX = mybir.AxisListType

D = 2048
NTOK = 5120
NS = 4096
TT = 1024
HD = 128
SCALE = HD ** -0.5
EPS = 1e-6
DEPTH = 4
KINDS = [0, 1, 2, 0]
FIN = [5120, 8192, 8192, 5120]
LAMBDA_INIT = [0.8 - 0.6 * math.exp(-0.3 * l) for l in range(DEPTH)]


class Buf:
    __slots__ = ("name", "w", "r", "excl")

    def __init__(self, name):
        self.name = name
        self.w = None
        self.r = {}
        self.excl = False


class DBuf:
    __slots__ = ("name", "writers", "readers", "prev_readers")

    def __init__(self, name):
        self.name = name
        self.writers = {}
        self.readers = {}
        self.prev_readers = {}


class Op:
    __slots__ = ("eng", "fn", "deps", "needs_inc", "dma")

    def __init__(self, eng, fn, deps, dma=None):
        self.eng = eng
        self.fn = fn
        self.deps = deps
        self.needs_inc = False
        self.dma = dma


ENGS = ("pe", "act", "dve", "pool", "sp")


def _add(dd, tok):
    key = (tok[0], tok[1])
    if dd.get(key, -1) < tok[2]:
        dd[key] = tok[2]


class Prog:
    def __init__(self):
        self.ops = {e: [] for e in ENGS}
        self.names = {}
        self.dpool = []
        self.kind_idxs = {'hw': [], 'sw': []}
        self.used = {'hw': 0, 'sw': 0}
        self.fixed = {}

    def _deps_for(self, reads, writes):
        deps = {}
        for b in reads:
            if isinstance(b, DBuf):
                for k, v in b.writers.items():
                    _add(deps, (k[0], k[1], v))
            else:
                if b.w is not None:
                    _add(deps, b.w)
                if b.excl:
                    for k, v in b.r.items():
                        _add(deps, (k[0], k[1], v))
        for b in writes:
            if isinstance(b, DBuf):
                if b.readers:
                    b.prev_readers = b.readers
                    b.readers = {}
                    b.writers = {}
                for k, v in b.prev_readers.items():
                    _add(deps, (k[0], k[1], v))
            else:
                if b.w is not None:
                    _add(deps, b.w)
                for k, v in b.r.items():
                    _add(deps, (k[0], k[1], v))
        return deps

    def _mark(self, tok, reads, writes):
        for b in reads:
            if isinstance(b, DBuf):
                _add(b.readers, tok)
            else:
                _add(b.r, tok)
        for b in writes:
            if isinstance(b, DBuf):
                _add(b.writers, tok)
            else:
                b.w = tok
                b.r = {}

    def op(self, eng, fn, reads=(), writes=()):
        deps = self._deps_for(reads, writes)
        if eng == "pe":
            deps.pop(("e", "pe"), None)
        lst = self.ops[eng]
        tok = ("e", eng, len(lst))
        lst.append(Op(eng, fn, deps))
        self._mark(tok, reads, writes)
        return tok

    def dma(self, eng, out, in_, reads, writes, owner, **kw):
        deps = self._deps_for(reads, writes)
        kind = "sw" if eng == "pool" else "hw"
        if owner.name.startswith("cast"):
            idx = self.fixed.get(owner.name)
            if idx is None:
                idx = len(self.dpool)
                self.dpool.append(0)
                self.fixed[owner.name] = idx
        else:
            idx = self.names.get((owner.name, kind))
        if idx is None:
            k = self.used[kind]
            if k < len(self.kind_idxs[kind]):
                idx = self.kind_idxs[kind][k]
            else:
                idx = len(self.dpool)
                self.dpool.append(0)
                self.kind_idxs[kind].append(idx)
            self.used[kind] += 1
            self.names[(owner.name, kind)] = idx
        self.dpool[idx] += 16
        ent = (idx, self.dpool[idx])
        tok = ("d", ent[0], ent[1])

        def fn(e, out=out, in_=in_, kw=kw):
            return e.dma_start(out=out, in_=in_, **kw)

        self.ops[eng].append(Op(eng, fn, deps, dma=ent[0]))
        self._mark(tok, reads, writes)
        return tok

    def barrier(self, skip=()):
        deps = {}
        skip_idx = set(self.fixed[n] for n in skip if n in self.fixed)
        for e in ENGS:
            for i in range(len(self.ops[e]) - 1, -1, -1):
                o = self.ops[e][i]
                if o.dma is None and o.fn is not None:
                    deps[("e", e)] = i
                    break
        for idx, cnt in enumerate(self.dpool):
            if cnt and idx not in skip_idx:
                deps[("d", idx)] = cnt
        for e in ENGS:
            self.ops[e].append(Op(e, None, dict(deps)))
        self.names = {}
        self.used = {'hw': 0, 'sw': 0}

    def emit(self, nc, stack):
        for e in ENGS:
            for o in self.ops[e]:
                for k, v in o.deps.items():
                    if k[0] == "e":
                        self.ops[k[1]][v].needs_inc = True
        inc_count = {}
        for e in ENGS:
            c = 0
            arr = []
            for o in self.ops[e]:
                if o.needs_inc:
                    c += 1
                arr.append(c)
            inc_count[e] = arr
        esem = {e: stack.enter_context(nc.semaphore("es_" + e)) for e in ENGS if e != "sp"}
        dsem = [stack.enter_context(nc.semaphore("ds%d" % i)) for i in range(len(self.dpool))]
        self.n_sems = len(esem) + len(dsem)
        block = stack.enter_context(nc.Block())
        self.stats = {e: [0, 0] for e in ENGS}

        def run(e, eng):
            known = {}
            st = self.stats[e]
            for o in self.ops[e]:
                for k, v in o.deps.items():
                    if k[0] == "e":
                        val = inc_count[k[1]][v]
                        sem = esem[k[1]]
                    else:
                        val = v
                        sem = dsem[k[1]]
                    if known.get(k, 0) >= val:
                        continue
                    known[k] = val
                    eng.wait_ge(sem, val)
                    st[1] += 1
                if o.fn is None:
                    continue
                ins = o.fn(eng)
                st[0] += 1
                if o.dma is not None:
                    ins.then_inc(dsem[o.dma], 16)
                elif o.needs_inc:
                    ins.then_inc(esem[e], 1)

        @block.tensor
        def _(t):
            run("pe", t)

        @block.scalar
        def _(a):
            run("act", a)

        @block.vector
        def _(v):
            run("dve", v)

        @block.gpsimd
        def _(g):
            run("pool", g)

        @block.sync
        def _(s):
            run("sp", s)


class Arena:
    def __init__(self, nc, name, nbytes):
        self.t = nc.alloc_sbuf_tensor(name, [128, nbytes // 4], F32)
        self.cap = nbytes
        self.off = 0

    def reset(self):
        self.off = 0

    def alloc(self, free_shape, dtype):
        es = 2 if dtype == BF16 else 4
        n = 1
        for s in free_shape:
            n *= s
        nb = (n * es + 31) // 32 * 32
        assert self.off + nb <= self.cap, ("arena overflow", self.off, nb, self.cap)
        v = self.t[:, self.off // 4:(self.off + nb) // 4]
        self.off += nb
        if dtype != F32:
            v = v.bitcast(dtype)
        v = v[:, 0:n]
        if len(free_shape) == 2:
            v = v.rearrange("p (a b) -> p a b", b=free_shape[1])
        elif len(free_shape) == 3:
            v = v.rearrange("p (a b c) -> p a b c", b=free_shape[1], c=free_shape[2])
        elif len(free_shape) == 4:
            v = v.rearrange("p (a b c d) -> p a b c d", b=free_shape[1], c=free_shape[2], d=free_shape[3])
        return v


class T:
    __slots__ = ("ap", "b")

    def __init__(self, ap, name):
        self.ap = ap
        self.b = Buf(name)


def dbc(row, nparts):
    n = row.shape[-1]
    return bass.AP(tensor=row.tensor, offset=row.offset, ap=[[0, nparts], [1, n]])


def bc_last(ap2d, n):
    return ap2d.unsqueeze(2).broadcast_to([ap2d.shape[0], ap2d.shape[1], n])


def bc_mid(ap2d, n):
    return ap2d.unsqueeze(1).broadcast_to([ap2d.shape[0], n, ap2d.shape[1]])


def build(n_layers=DEPTH, stop_phase=None, dbg=0):
    nc = bass.Bass("TRN2", target_bir_lowering=False)
    P = Prog()

    def din(name, shape):
        return nc.dram_tensor(name, list(shape), F32, kind="ExternalInput").ap()

    def dout(name, shape):
        return nc.dram_tensor(name, list(shape), F32, kind="ExternalOutput").ap()

    x_in = din("x", [NTOK, D])
    cpair = din("cpair", [2, D])
    ln_g = din("ln_g", [DEPTH, D])
    ada_w = din("ada_w", [n_layers, D, 3 * D])
    ada_b = din("ada_b", [DEPTH, 3 * D])
    w_out = din("w_out", [n_layers, D, D])
    qn_g = din("qn_g", [DEPTH, HD])
    kn_g = din("kn_g", [DEPTH, HD])
    w_in_a = din("w_in_a", [2 if n_layers > 3 else 1, D, 5120])
    w_in_b = din("w_in_b", [1, D, 8192] if n_layers > 1 else [1, 8, 8192])
    w_in_c = din("w_in_c", [1, D, 8192] if n_layers > 2 else [1, 8, 8192])
    sink_a = din("sink_a", [2, 16])
    lam_b = din("lam_b", [1, 4 * HD])
    subln_b = din("subln_b", [1, 256])
    rpbt = din("rpbt", [16, 64, 15, 64])
    cak = din("cak", [2, 256, 4 * HD])
    cav = din("cav", [2, 256, 4 * HD])
    cbk = din("cbk", [1, 256, 16 * HD])
    cbv = din("cbv", [1, 256, 8 * 256])
    cck = din("cck", [1, 256, 16 * HD])
    ccv = din("ccv", [1, 256, 16 * HD])
    y_out = dout("y", [NTOK, D])
    nak = dout("nak", [4, 2, 256, 4 * HD])
    nav = dout("nav", [4, 2, 256, 4 * HD])
    nbk = dout("nbk", [4, 1, 256, 16 * HD])
    nbv = dout("nbv", [4, 1, 256, 8 * 256])
    nck = dout("nck", [4, 1, 256, 16 * HD])
    ncv = dout("ncv", [4, 1, 256, 16 * HD])
    XA = nc.dram_tensor("XA", [NTOK, D], F32).ap()
    XB = nc.dram_tensor("XB", [NTOK, D], F32).ap()
    MOD = nc.dram_tensor("MOD", [DEPTH, 2, 3 * D], F32).ap()
    QT = nc.dram_tensor("QT", [16, 128, NTOK], BF16).ap()
    KT = nc.dram_tensor("KT", [16, 128, NTOK], BF16).ap()
    VS = nc.dram_tensor("VS", [NTOK, D], BF16).ap()
    GT = nc.dram_tensor("GT", [16, 128, NTOK], BF16).ap()
    OT = nc.dram_tensor("OT", [16, 128, NTOK], BF16).ap()
    WIB = [nc.dram_tensor("WIB%d" % l, [D, FIN[l]], BF16).ap() for l in range(DEPTH)]
    WOB = [nc.dram_tensor("WOB%d" % l, [D, D], BF16).ap() for l in range(DEPTH)]
    w_in_src = [w_in_a[0], w_in_b[0], w_in_c[0], w_in_a[1 if n_layers > 3 else 0]]

    d_x = [DBuf("x_in"), DBuf("XA"), DBuf("XB"), DBuf("XA"), DBuf("y")]
    d_x[3] = d_x[1]
    x_aps = [x_in, XA, XB, XA, y_out]
    d_mod = DBuf("MOD")
    d_qt, d_kt, d_vs, d_gt, d_ot = DBuf("QT"), DBuf("KT"), DBuf("VS"), DBuf("GT"), DBuf("OT")
    d_wib = [DBuf("WIB%d" % l) for l in range(DEPTH)]
    d_wob = [DBuf("WOB%d" % l) for l in range(DEPTH)]
    d_outs = DBuf("outs")
    d_ro = DBuf("ro")

    ar = Arena(nc, "arena", 180 * 1024)
    car = Arena(nc, "consts", 20 * 1024)
    banks = [nc.alloc_psum_tensor("bank%d" % i, [128, 512], F32) for i in range(8)]
    pb = [T(banks[i][:], "bank%d" % i) for i in range(8)]
    for t_ in pb:
        t_.b.excl = True

    def mk(arena, free_shape, dtype, name):
        t = T(arena.alloc(free_shape, dtype), name)
        return t

    ident = mk(car, [128], BF16, "ident")
    ones = mk(car, [128], BF16, "ones")
    ones32 = mk(car, [128], F32, "ones32")
    cosT = mk(car, [32, 2, 32], F32, "cosT")
    sinT = mk(car, [32, 2, 32], F32, "sinT")
    gq = mk(car, [HD], F32, "gq")
    gk = mk(car, [HD], F32, "gk")

    def build_consts():
        P.op("pool", lambda g: g.memset(ident.ap, 0.0), writes=[ident.b])
        P.op("pool", lambda g: g.affine_select(out=ident.ap, in_=ident.ap, compare_op=ALU.not_equal, fill=1.0,
                                               base=0, pattern=[[-1, 128]], channel_multiplier=1),
             reads=[ident.b], writes=[ident.b])
        P.op("pool", lambda g: g.memset(ones.ap, 1.0), writes=[ones.b])
        P.op("pool", lambda g: g.memset(ones32.ap, 1.0), writes=[ones32.b])
        ar.reset()
        posr = mk(ar, [32], I32, "posr")
        posc = mk(ar, [1], I32, "posc")
        fi = mk(ar, [32], I32, "fi")
        posrf = mk(ar, [32], F32, "posrf")
        poscf = mk(ar, [1], F32, "poscf")
        ff = mk(ar, [32], F32, "ff")
        invf = mk(ar, [32], F32, "invf")
        ang = mk(ar, [32, 2, 32], F32, "ang")
        tmp = mk(ar, [32, 2, 32], F32, "angt")
        tmpi = mk(ar, [32, 2, 32], I32, "angi")
        for h0, base in ((0, 0), (64, 1)):
            P.op("pool", lambda g, h0=h0, base=base: g.iota(posr.ap[h0:h0 + 64, :], pattern=[[2, 32]], base=base,
                                                            channel_multiplier=0), writes=[posr.b])
            P.op("pool", lambda g, h0=h0: g.iota(posc.ap[h0:h0 + 64, :], pattern=[[0, 1]], base=0,
                                                 channel_multiplier=1), writes=[posc.b])
        P.op("pool", lambda g: g.iota(fi.ap, pattern=[[1, 32]], base=0, channel_multiplier=0), writes=[fi.b])
        P.op("dve", lambda v: v.tensor_copy(out=posrf.ap, in_=posr.ap), reads=[posr.b], writes=[posrf.b])
        P.op("dve", lambda v: v.tensor_copy(out=poscf.ap, in_=posc.ap), reads=[posc.b], writes=[poscf.b])
        P.op("dve", lambda v: v.tensor_copy(out=ff.ap, in_=fi.ap), reads=[fi.b], writes=[ff.b])
        P.op("act", lambda a: a.activation(out=invf.ap, in_=ff.ap, func=AF.Exp, scale=-math.log(10000.0) / 32.0),
             reads=[ff.b], writes=[invf.b])
        P.op("dve", lambda v: v.tensor_tensor(out=ang.ap[:, :, 0, :], in0=bc_last(posrf.ap, 32), in1=bc_mid(invf.ap, 32),
                                              op=ALU.mult), reads=[posrf.b, invf.b], writes=[ang.b])
        P.op("dve", lambda v: v.tensor_scalar(out=ang.ap[:, :, 1, :], in0=bc_mid(invf.ap, 32), scalar1=poscf.ap[:, 0:1],
                                              scalar2=None, op0=ALU.mult), reads=[poscf.b, invf.b, ang.b], writes=[ang.b])
        TWO_PI = 2.0 * math.pi
        for dst, shift in ((sinT, 0.0), (cosT, math.pi / 2)):
            P.op("dve", lambda v, shift=shift: v.tensor_scalar(out=tmp.ap, in0=ang.ap, scalar1=shift, scalar2=1.0 / TWO_PI,
                                                               op0=ALU.add, op1=ALU.mult), reads=[ang.b], writes=[tmp.b])
            P.op("dve", lambda v: v.tensor_copy(out=tmpi.ap, in_=tmp.ap), reads=[tmp.b], writes=[tmpi.b])
            P.op("dve", lambda v: v.tensor_copy(out=tmp.ap, in_=tmpi.ap), reads=[tmpi.b], writes=[tmp.b])
            P.op("dve", lambda v: v.scalar_tensor_tensor(out=tmp.ap, in0=tmp.ap, scalar=-TWO_PI, in1=ang.ap,
                                                         op0=ALU.mult, op1=ALU.add), reads=[tmp.b, ang.b], writes=[tmp.b])
            P.op("dve", lambda v, shift=shift: v.tensor_scalar(out=tmp.ap, in0=tmp.ap, scalar1=shift, scalar2=3.1415925,
                                                               op0=ALU.add, op1=ALU.min), reads=[tmp.b], writes=[tmp.b])
            P.op("dve", lambda v: v.tensor_scalar(out=tmp.ap, in0=tmp.ap, scalar1=-3.1415925, scalar2=None,
                                                  op0=ALU.max), reads=[tmp.b], writes=[tmp.b])
            P.op("act", lambda a, dst=dst: a.activation(out=dst.ap, in_=tmp.ap, func=AF.Sin), reads=[tmp.b], writes=[dst.b])

    def cast_weights(l):
        castb = Buf("cast%d" % l)
        src = w_in_src[l]
        F = FIN[l]
        for r0 in range(0, D, 256):
            P.dma("pool", WIB[l][r0:r0 + 256, :].rearrange("r (a b) -> r a b", b=1024),
                  src[r0:r0 + 256, :].rearrange("r (a b) -> r a b", b=1024),
                  reads=[d_ro], writes=[d_wib[l]], owner=castb)
        for r0 in range(0, D, 512):
            P.dma("pool", WOB[l][r0:r0 + 512, :].rearrange("r (a b) -> r a b", b=1024),
                  w_out[l, r0:r0 + 512, :].rearrange("r (a b) -> r a b", b=1024),
                  reads=[d_ro], writes=[d_wob[l]], owner=castb)

    def phase0():
        ar.reset()
        cT = mk(ar, [16, 2], F32, "cT")
        sT = mk(ar, [16, 2], F32, "sT")
        sg = mk(ar, [16, 2], F32, "sg")
        wt = [mk(ar, [16, 512], F32, "adaw%d" % i) for i in range(2)]
        adab = T(ar.alloc([3 * D], F32)[0:2, :], "adab")
        msb = T(ar.alloc([3 * D], F32)[0:2, :], "msb")
        for r in range(2):
            P.dma("sp", cT.ap[:, :, r], cpair[r].rearrange("(c p) -> p c", p=128), reads=[d_ro], writes=[cT.b],
                  owner=cT.b, allow_slow_non_contiguous=True)
        P.op("act", lambda a: a.activation(out=sg.ap, in_=cT.ap, func=AF.Exp, scale=-1.0), reads=[cT.b], writes=[sg.b])
        P.op("dve", lambda v: v.tensor_scalar(out=sg.ap, in0=sg.ap, scalar1=1.0, scalar2=None, op0=ALU.add),
             reads=[sg.b], writes=[sg.b])
        P.op("dve", lambda v: v.reciprocal(out=sg.ap, in_=sg.ap), reads=[sg.b], writes=[sg.b])
        P.op("dve", lambda v: v.tensor_tensor(out=sT.ap, in0=cT.ap, in1=sg.ap, op=ALU.mult), reads=[cT.b, sg.b],
             writes=[sT.b])
        i = 0
        for l in range(n_layers):
            P.dma("sp", adab.ap, dbc(ada_b[l:l + 1, :], 2), reads=[d_ro], writes=[adab.b], owner=adab.b)
            for fb in range(12):
                w = wt[i % 2]
                P.dma("sp", w.ap, ada_w[l, :, fb * 512:(fb + 1) * 512].rearrange("(c p) f -> p c f", p=128),
                      reads=[d_ro], writes=[w.b], owner=w.b)
                ps = pb[i % 2]

                def mm(t, w=w, ps=ps):
                    for c in range(16):
                        ins = t.matmul(ps.ap[0:2, :], lhsT=sT.ap[:, c, :], rhs=w.ap[:, c, :], start=(c == 0), stop=(c == 15))
                    return ins
                P.op("pe", mm, reads=[sT.b, w.b], writes=[ps.b])
                P.op("dve", lambda v, ps=ps, fb=fb: v.tensor_tensor(out=msb.ap[:, fb * 512:(fb + 1) * 512], in0=ps.ap[0:2, :],
                                                                    in1=adab.ap[:, fb * 512:(fb + 1) * 512], op=ALU.add),
                     reads=[ps.b, adab.b], writes=[msb.b])
                i += 1
            P.dma("sp", MOD[l], msb.ap, reads=[msb.b], writes=[d_mod], owner=msb.b)

    def layer_cfg(l):
        kind = KINDS[l]
        j = l // 3
        if kind == 0:
            return dict(kind=0, j=j, nq=16, nk=4, vcols=512, F=5120, knew=nak, vnew=nav, kcols=512)
        if kind == 1:
            return dict(kind=1, j=j, nq=16, nk=16, vcols=2048, F=8192, knew=nbk, vnew=nbv, kcols=2048)
        return dict(kind=2, j=j, nq=16, nk=16, vcols=2048, F=8192, knew=nck, vnew=ncv, kcols=2048)

    def phase1(l):
        cfg = layer_cfg(l)
        nq, nk, vcols, F = cfg["nq"], cfg["nk"], cfg["vcols"], cfg["F"]
        xin, dxin = x_aps[l], d_x[l]
        ar.reset()
        mod1 = mk(ar, [D], F32, "mod1")
        sh = mk(ar, [D], F32, "sh")
        xt = [mk(ar, [D], F32, "xt%d" % i) for i in range(2)]
        tmpf = mk(ar, [D], F32, "tmpf")
        lng = tmpf
        hb = [mk(ar, [D], BF16, "hb%d" % i) for i in range(2)]
        hT = mk(ar, [16, TT], BF16, "hT")
        wt = [mk(ar, [16, 512], BF16, "wt%d" % i) for i in range(3)]
        ssx = [mk(ar, [1], F32, "ssx%d" % i) for i in range(2)]
        rsx = [mk(ar, [1], F32, "rsx%d" % i) for i in range(2)]
        junk = mk(ar, [D], BF16, "junk")
        ss4 = [mk(ar, [4], F32, "ss4%d" % i) for i in range(4)]
        rs4 = [mk(ar, [4], F32, "rs4%d" % i) for i in range(4)]
        yq = [mk(ar, [4, HD], F32, "yq%d" % i) for i in range(3)]
        rt = [mk(ar, [4, 2, 32], F32, "rt%d" % i) for i in range(8)]
        ob = [mk(ar, [4, HD], BF16, "ob%d" % i) for i in range(4)]
        qTs = [mk(ar, [4, TT], BF16, "qTs%d" % i) for i in range(2)]
        vst = [mk(ar, [512], BF16, "vst%d" % i) for i in range(2)]
        vsf = [mk(ar, [512], F32, "vsf%d" % i) for i in range(2)]
        gst = [mk(ar, [512], BF16, "gst%d" % i) for i in range(2)]
        hps = pb[0:2]
        mps = pb[2:5]
        tps = pb[5:7]

        P.dma("sp", gq.ap, dbc(qn_g[l:l + 1, :], 128), reads=[d_ro], writes=[gq.b], owner=gq.b)
        P.dma("sp", gk.ap, dbc(kn_g[l:l + 1, :], 128), reads=[d_ro], writes=[gk.b], owner=gk.b)

        cnt = dict(x=0, w=0, m=0, q=0, v=0, g=0, t=0, qs=0)

        def load_mod(r):
            P.dma("sp", lng.ap, dbc(ln_g[l:l + 1, :], 128), reads=[d_ro], writes=[lng.b], owner=lng.b)
            P.dma("sp", sh.ap, dbc(MOD[l, r:r + 1, 0:D], 128), reads=[d_mod], writes=[sh.b], owner=sh.b)
            P.dma("sp", mod1.ap, dbc(MOD[l, r:r + 1, D:2 * D], 128), reads=[d_mod], writes=[mod1.b],
                  owner=mod1.b)
            P.op("dve", lambda v: v.scalar_tensor_tensor(out=mod1.ap, in0=mod1.ap, scalar=1.0, in1=lng.ap, op0=ALU.add,
                                                         op1=ALU.mult), reads=[mod1.b, lng.b], writes=[mod1.b])

        def load_w(fb):
            w = wt[cnt["w"] % 3]
            cnt["w"] += 1
            P.dma("sp", w.ap, WIB[l][:, fb * 512:(fb + 1) * 512].rearrange("(c p) f -> p c f", p=128),
                  reads=[d_wib[l]], writes=[w.b], owner=w.b)
            return w

        def norm_block(tt, tb):
            t0 = tt * TT + tb * 128
            x = xt[cnt["x"] % 2]
            h = hb[cnt["x"] % 2]
            s1 = ssx[cnt["x"] % 2]
            r1 = rsx[cnt["x"] % 2]
            cnt["x"] += 1
            P.dma("sp", x.ap, xin[t0:t0 + 128, :], reads=[dxin], writes=[x.b], owner=x.b)
            P.op("act", lambda a: a.activation(out=junk.ap, in_=x.ap, func=AF.Square, accum_out=s1.ap[:, 0:1]),
                 reads=[x.b], writes=[s1.b])
            P.op("act", lambda a: a.activation(out=s1.ap, in_=s1.ap, func=AF.Sqrt, scale=1.0 / D, bias=EPS),
                 reads=[s1.b], writes=[s1.b])
            P.op("dve", lambda v: v.reciprocal(out=r1.ap, in_=s1.ap), reads=[s1.b], writes=[r1.b])
            P.op("dve", lambda v: v.scalar_tensor_tensor(out=tmpf.ap, in0=x.ap, scalar=r1.ap[:, 0:1], in1=mod1.ap,
                                                         op0=ALU.mult, op1=ALU.mult), reads=[x.b, r1.b, mod1.b],
                 writes=[tmpf.b])
            P.op("pool", lambda g: g.tensor_tensor(out=h.ap, in0=tmpf.ap, in1=sh.ap, op=ALU.add), reads=[tmpf.b, sh.b],
                 writes=[h.b])
            hv = [hps[i].ap.bitcast(BF16).rearrange("p (c t) -> p c t", t=128) for i in range(2)]

            def tr(t):
                for c in range(16):
                    ins = t.transpose(hv[c // 8][:, c % 8, :], h.ap[:, c * 128:(c + 1) * 128], ident.ap)
                return ins
            P.op("pe", tr, reads=[h.b, ident.b], writes=[hps[0].b, hps[1].b])
            P.op("act", lambda a: a.activation(out=hT.ap[:, 0:8, tb * 128:(tb + 1) * 128], in_=hv[0], func=AF.Copy),
                 reads=[hps[0].b], writes=[hT.b])
            P.op("dve", lambda v: v.tensor_copy(out=hT.ap[:, 8:16, tb * 128:(tb + 1) * 128], in_=hv[1]),
                 reads=[hps[1].b, hT.b], writes=[hT.b])

        def mm_tok(w, tb):
            ps = mps[cnt["m"] % 3]
            cnt["m"] += 1

            def mm(t):
                for c in range(16):
                    ins = t.matmul(ps.ap, lhsT=hT.ap[:, c, tb * 128:(tb + 1) * 128], rhs=w.ap[:, c, :],
                                   start=(c == 0), stop=(c == 15))
                return ins
            P.op("pe", mm, reads=[hT.b, w.b], writes=[ps.b])
            return ps

        def qk_post(ps, tt, tb, is_k, u0, stage, gain):
            i2 = cnt["q"]
            cnt["q"] += 1
            s4, r4, y, o = ss4[i2 % 4], rs4[i2 % 4], yq[i2 % 3], ob[i2 % 4]
            psv = ps.ap.rearrange("p (u d) -> p u d", d=HD)
            for u in range(4):
                P.op("act", lambda a, u=u: a.activation(out=junk.ap[:, 0:HD], in_=psv[:, u, :], func=AF.Square,
                                                        accum_out=s4.ap[:, u:u + 1]), reads=[ps.b], writes=[s4.b])
            P.op("act", lambda a: a.activation(out=s4.ap, in_=s4.ap, func=AF.Sqrt, scale=1.0 / HD, bias=EPS),
                 reads=[s4.b], writes=[s4.b])
            P.op("dve", lambda v: v.reciprocal(out=r4.ap, in_=s4.ap), reads=[s4.b], writes=[r4.b])
            P.op("dve", lambda v: v.tensor_tensor(out=y.ap, in0=psv, in1=bc_mid(gain.ap, 4), op=ALU.mult),
                 reads=[ps.b, gain.b], writes=[y.b])
            P.op("pool", lambda g: g.tensor_tensor(out=y.ap, in0=y.ap, in1=bc_last(r4.ap, HD), op=ALU.mult),
                 reads=[y.b, r4.b], writes=[y.b])
            is_prompt = (tt == 4)
            if is_prompt or cfg["kind"] == 2:
                if is_k and is_prompt:
                    seq = tb // 2
                    r0 = (tb % 2) * 128
                    P.dma("sp", cfg["knew"][seq, cfg["j"], r0:r0 + 128, u0 * HD:(u0 + 4) * HD],
                          y.ap.rearrange("p u d -> p (u d)"), reads=[y.b], writes=[d_outs], owner=y.b)
                P.op("dve", lambda v: v.tensor_copy(out=o.ap, in_=y.ap), reads=[y.b], writes=[o.b])
            else:
                blk = tt * 8 + tb
                yv = y.ap.rearrange("p u (a h f) -> p u a h f", a=2, h=2)
                ov = o.ap.rearrange("p u (a h f) -> p u a h f", a=2, h=2)
                cs = cosT.ap[:, blk, :, :].unsqueeze(1).broadcast_to([128, 4, 2, 32])
                sn = sinT.ap[:, blk, :, :].unsqueeze(1).broadcast_to([128, 4, 2, 32])
                x1 = yv[:, :, :, 0, :]
                x2 = yv[:, :, :, 1, :]
                t1, t2, t3, t4 = [rt[(i2 % 2) * 4 + k] for k in range(4)]
                P.op("dve", lambda v: v.tensor_tensor(out=t1.ap, in0=x1, in1=cs, op=ALU.mult), reads=[y.b, cosT.b], writes=[t1.b])
                P.op("dve", lambda v: v.tensor_tensor(out=t2.ap, in0=x2, in1=sn, op=ALU.mult), reads=[y.b, sinT.b], writes=[t2.b])
                P.op("dve", lambda v: v.tensor_tensor(out=ov[:, :, :, 0, :], in0=t1.ap, in1=t2.ap, op=ALU.subtract),
                     reads=[t1.b, t2.b], writes=[o.b])
                P.op("pool", lambda g: g.tensor_tensor(out=t3.ap, in0=x2, in1=cs, op=ALU.mult), reads=[y.b, cosT.b], writes=[t3.b])
                P.op("pool", lambda g: g.tensor_tensor(out=t4.ap, in0=x1, in1=sn, op=ALU.mult), reads=[y.b, sinT.b], writes=[t4.b])
                P.op("pool", lambda g: g.tensor_tensor(out=ov[:, :, :, 1, :], in0=t3.ap, in1=t4.ap, op=ALU.add),
                     reads=[t3.b, t4.b, o.b], writes=[o.b])
            def part_b():
                tp = tps[cnt["t"] % 2]
                cnt["t"] += 1
                tpv = tp.ap.bitcast(BF16)[:, 0:512].rearrange("p (u t) -> p u t", t=128)

                def tr(t):
                    for u in range(4):
                        ins = t.transpose(tpv[:, u, :], o.ap[:, u, :], ident.ap)
                    return ins
                P.op("pe", tr, reads=[o.b, ident.b], writes=[tp.b])
                P.op("act", lambda a: a.activation(out=stage.ap[:, :, tb * 128:(tb + 1) * 128], in_=tpv, func=AF.Copy),
                     reads=[tp.b, stage.b], writes=[stage.b])
            return part_b

        def v_post(ps, tt, tb, c0):
            v = vst[cnt["v"] % 2]
            t0 = tt * TT + tb * 128
            P.op("act", lambda a: a.activation(out=v.ap, in_=ps.ap, func=AF.Copy), reads=[ps.b], writes=[v.b])
            P.dma("pool", VS[t0:t0 + 128, c0:c0 + 512], v.ap, reads=[v.b], writes=[d_vs], owner=v.b)
            if tt == 4 and dbg != 7:
                vf = vsf[cnt["v"] % 2]
                seq = tb // 2
                r0 = (tb % 2) * 128
                P.op("dve", lambda vv: vv.tensor_copy(out=vf.ap, in_=ps.ap), reads=[ps.b, v.b], writes=[vf.b])
                P.dma("sp", cfg["vnew"][seq, cfg["j"], r0:r0 + 128, c0:c0 + 512], vf.ap, reads=[vf.b], writes=[d_outs],
                      owner=vf.b)
            cnt["v"] += 1

        def g_block(w, tt, fb_g):
            for fc in range(4):
                unit = fb_g * 4 + fc
                for th in range(2):
                    ps = mps[cnt["m"] % 3]
                    cnt["m"] += 1

                    def mm(t, fc=fc, th=th, ps=ps):
                        for c in range(16):
                            ins = t.matmul(ps.ap, lhsT=w.ap[:, c, fc * 128:(fc + 1) * 128],
                                           rhs=hT.ap[:, c, th * 512:(th + 1) * 512], start=(c == 0), stop=(c == 15))
                        return ins
                    P.op("pe", mm, reads=[hT.b, w.b], writes=[ps.b])
                    g = gst[cnt["g"] % 2]
                    cnt["g"] += 1
                    P.op("act", lambda a, ps=ps, g=g: a.activation(out=g.ap, in_=ps.ap, func=AF.Silu), reads=[ps.b],
                         writes=[g.b])
                    t0 = tt * TT + th * 512
                    P.dma("pool", GT[unit, :, t0:t0 + 512], g.ap, reads=[g.b], writes=[d_gt], owner=g.b)

        nfb = F // 512
        nqb = nq // 4
        nkb = nk // 4
        nvb = vcols // 512
        for tt in range(5):
            if dbg in (1, 2, 3, 4) and tt > 0:
                break
            if dbg == 5 and tt > 1:
                break
            if dbg in (6, 7) and tt in (1, 2, 3):
                continue
            if tt == 0:
                load_mod(0)
            if tt == 4:
                load_mod(1)
            wq = [load_w(0), load_w(1)]
            pq = []
            for tb in range(8):
                norm_block(tt, tb)
            for fb in range(nfb):
                if dbg == 1 or (dbg == 2 and fb >= nqb + nkb) or (dbg == 3 and fb >= nqb + nkb + nvb):
                    break
                w = wq.pop(0)
                if fb + 2 < nfb:
                    wq.append(load_w(fb + 2))
                if fb < nqb + nkb:
                    is_k = fb >= nqb
                    u0 = (fb - nqb) * 4 if is_k else fb * 4
                    stage = qTs[cnt["qs"] % 2]
                    cnt["qs"] += 1
                    for tb in range(8):
                        ps = mm_tok(w, tb)
                        pq.append(qk_post(ps, tt, tb, is_k, u0, stage, gk if is_k else gq))
                        if len(pq) > 3:
                            pq.pop(0)()
                    dst = (KT if is_k else QT)[u0:u0 + 4, :, tt * TT:(tt + 1) * TT].rearrange("u p t -> p u t")

                    def st(dst=dst, stage=stage, is_k=is_k):
                        P.dma("pool", dst, stage.ap, reads=[stage.b], writes=[d_kt if is_k else d_qt], owner=stage.b)
                    last_b = pq[-1]
                    pq[-1] = (lambda last_b=last_b, st=st: (last_b(), st()))
                elif fb < nqb + nkb + nvb:
                    c0 = (fb - nqb - nkb) * 512
                    for tb in range(8):
                        ps = mm_tok(w, tb)
                        v_post(ps, tt, tb, c0)
                else:
                    g_block(w, tt, fb - nqb - nkb - nvb)
                if fb == nqb + nkb and pq:
                    while pq:
                        pq.pop(0)()

    class Item:
        __slots__ = ("s_mms", "s_reads", "n", "mask", "pv", "pv_reads", "pv_writes", "first", "last", "after", "post", "E", "acc", "pre")

    def run_attention(items, sbanks, etiles, eshape, depth=2, defer=0):
        deferred = []
        assert len(sbanks) >= depth + 1 and len(etiles) >= depth + 2
        cnt = 0
        q = []
        for it in list(items) + [None] * depth:
            if it is not None:
                psS = sbanks[cnt % len(sbanks)]
                E = etiles[cnt % len(etiles)]
                cnt += 1

                def smm(t, it=it, psS=psS):
                    for (off, n, lhsT, rhs) in it.s_mms:
                        ins = t.matmul(psS.ap[:, off:off + n], lhsT=lhsT, rhs=rhs, start=True, stop=True)
                    return ins
                P.op("pe", smm, reads=it.s_reads, writes=[psS.b])
                P.op("act", lambda a, psS=psS, E=E, n=it.n: a.activation(out=E.ap[:, 0:n], in_=psS.ap[:, 0:n], func=AF.Exp,
                                                                         scale=SCALE), reads=[psS.b], writes=[E.b])
                if it.mask is not None:
                    mk_ap, mk_b = it.mask
                    ev = E.ap[:, 0:it.n]
                    if len(mk_ap.shape) == 3:
                        ev = ev.rearrange("p (a b) -> p a b", b=mk_ap.shape[2])
                    P.op("pool", lambda g, ev=ev, mk_ap=mk_ap: g.tensor_tensor(out=ev, in0=ev, in1=mk_ap, op=ALU.mult),
                         reads=[E.b, mk_b], writes=[E.b])
                it.E = E
                acc = getattr(it, "acc", None)
                if acc is not None:
                    eng_, a_, first_ = acc
                    if first_:
                        P.op(eng_, lambda g, a_=a_, E=E: g.tensor_copy(out=a_.ap, in_=E.ap), reads=[E.b], writes=[a_.b])
                    else:
                        P.op(eng_, lambda g, a_=a_, E=E: g.tensor_tensor(out=a_.ap, in0=a_.ap, in1=E.ap, op=ALU.add),
                             reads=[E.b, a_.b], writes=[a_.b])
                q.append(it)
            if q and (len(q) > depth or it is None):
                p_ = q.pop(0)

                def pvmm(t, it=p_):
                    for (out_ap, lhsT, off, n) in it.pv:
                        ins = t.matmul(out_ap, lhsT=lhsT, rhs=it.E.ap[:, off:off + n], start=it.first, stop=it.last)
                    return ins
                P.op("pe", pvmm, reads=[p_.E.b, ones.b] + p_.pv_reads, writes=p_.pv_writes)
                if getattr(p_, "pre", None) is not None:
                    p_.pre()
                deferred = [(c_ - 1, f_) for (c_, f_) in deferred]
                while deferred and deferred[0][0] <= 0:
                    deferred.pop(0)[1]()
                if p_.after is not None:
                    later = p_.after()
                    if later is not None:
                        deferred.append((defer, later))
                if getattr(p_, "post", None) is not None:
                    if defer:
                        deferred.append((defer, p_.post))
                    else:
                        p_.post()
        for (c_, f_) in deferred:
            f_()

    def load_ctx(cache_k, cache_v, j, kcol0, nku, vcol0, vw, ckf, ckb, ckT, cvf, cvb, tp):
        P.dma("sp", ckf.ap, cache_k[j, :, kcol0:kcol0 + nku * HD].rearrange("(c p) f -> p c f", p=128),
              reads=[d_ro], writes=[ckf.b], owner=ckf.b)
        P.dma("sp", cvf.ap, cache_v[j, :, vcol0:vcol0 + vw].rearrange("(c p) f -> p c f", p=128),
              reads=[d_ro], writes=[cvf.b], owner=cvf.b)
        P.op("dve", lambda v: v.tensor_copy(out=ckb.ap, in_=ckf.ap), reads=[ckf.b], writes=[ckb.b])
        P.op("pool", lambda g: g.tensor_copy(out=cvb.ap, in_=cvf.ap), reads=[cvf.b], writes=[cvb.b])
        tpv = tp.ap.bitcast(BF16)[:, 0:nku * 256].rearrange("p (u t) -> p u t", t=256)

        def tr(t):
            for u in range(nku):
                for c in range(2):
                    ins = t.transpose(tpv[:, u, c * 128:(c + 1) * 128], ckb.ap[:, c, u * HD:(u + 1) * HD], ident.ap)
            return ins
        P.op("pe", tr, reads=[ckb.b, ident.b], writes=[tp.b])
        P.op("act", lambda a: a.activation(out=ckT.ap, in_=tpv, func=AF.Copy), reads=[tp.b], writes=[ckT.b])

    def phase2_A(l):
        j = l // 3
        ar.reset()
        kT = [mk(ar, [NTOK], BF16, "kT%d" % i) for i in range(2)]
        vv = [mk(ar, [40, HD], BF16, "vv%d" % i) for i in range(2)]
        ckf = mk(ar, [2, HD], F32, "ckf")
        ckb = mk(ar, [2, HD], BF16, "ckb")
        ckT = [mk(ar, [1, 256], BF16, "ckT%d" % i) for i in range(2)]
        cvf = mk(ar, [2, HD], F32, "cvf")
        cvb = [mk(ar, [2, HD], BF16, "cvb%d" % i) for i in range(2)]
        qT = [mk(ar, [4, 512], BF16, "qT%d" % i) for i in range(2)]
        gT = [mk(ar, [4, 512], BF16, "gT%d" % i) for i in range(2)]
        oT = [mk(ar, [4, 512], BF16, "oT%d" % i) for i in range(2)]
        mprev = mk(ar, [4, 128], BF16, "mprev")
        mnext = mk(ar, [4, 128], BF16, "mnext")
        sexp = mk(ar, [16], F32, "sexp")
        et = [mk(ar, [512], BF16, "et%d" % i) for i in range(4)]
        den = [mk(ar, [4, 128], F32, "den%d" % i) for i in range(2)]
        of = [mk(ar, [4, 128], F32, "of%d" % i) for i in range(2)]
        sb_, ob_, lb_, tp = pb[0:3], pb[3:5], pb[5:7], pb[7]
        P.op("pool", lambda g: g.memset(mprev.ap, 1.0), writes=[mprev.b])
        P.op("pool", lambda g: g.affine_select(out=mprev.ap, in_=mprev.ap, compare_op=ALU.is_ge, fill=0.0, base=0,
                                               pattern=[[0, 4], [-1, 128]], channel_multiplier=1), reads=[mprev.b], writes=[mprev.b])
        P.op("pool", lambda g: g.memset(mnext.ap, 1.0), writes=[mnext.b])
        P.op("pool", lambda g: g.affine_select(out=mnext.ap, in_=mnext.ap, compare_op=ALU.is_ge, fill=0.0, base=0,
                                               pattern=[[0, 4], [1, 128]], channel_multiplier=-1), reads=[mnext.b], writes=[mnext.b])
        P.dma("sp", sexp.ap, dbc(sink_a[j:j + 1, :], 128), reads=[d_ro], writes=[sexp.b], owner=sexp.b)
        P.op("act", lambda a: a.activation(out=sexp.ap, in_=sexp.ap, func=AF.Exp), reads=[sexp.b], writes=[sexp.b])
        cnt = dict(q=0, f=0)
        for kvh in range(4):
            k_, v_, ckT_, cvb_ = kT[kvh % 2], vv[kvh % 2], ckT[kvh % 2], cvb[kvh % 2]
            P.dma("sp", k_.ap, KT[kvh], reads=[d_kt], writes=[k_.b], owner=k_.b)
            P.dma("sp", v_.ap, VS[:, kvh * HD:(kvh + 1) * HD].rearrange("(c p) d -> p c d", p=128), reads=[d_vs],
                  writes=[v_.b], owner=v_.b)
            load_ctx(cak, cav, j, kvh * HD, 1, kvh * HD, HD, ckf, ckb, ckT_, cvf, cvb_, tp)
            items = []
            loaders = []
            slab_first = []
            for q512 in range(10):
                q_, g_, o_ = qT[cnt["q"] % 2], gT[cnt["q"] % 2], oT[cnt["q"] % 2]
                cnt["q"] += 1
                t0 = q512 * 512

                def ld(q_=q_, g_=g_, t0=t0, kvh=kvh):
                    P.dma("sp", q_.ap, QT[kvh * 4:(kvh + 1) * 4, :, t0:t0 + 512].rearrange("u p t -> p u t"), reads=[d_qt],
                          writes=[q_.b], owner=q_.b)
                    P.dma("sp", g_.ap, GT[kvh * 4:(kvh + 1) * 4, :, t0:t0 + 512].rearrange("u p t -> p u t"), reads=[d_gt],
                          writes=[g_.b], owner=g_.b)
                loaders.append(ld)
                slab_first.append(len(items))
                for qb in range(4):
                    B = q512 * 4 + qb
                    if B < 32:
                        ch = []
                        if B > 0:
                            ch.append(("l", B - 1, mprev))
                        ch.append(("l", B, None))
                        if B < 31:
                            ch.append(("l", B + 1, mnext))
                        ch += [("c", 0, None), ("c", 1, None)]
                        import os as _os
                        if _os.environ.get("A_NOCTX"):
                            ch = ch[:-2]
                        if _os.environ.get("A_NOMASK"):
                            ch = [(a, b, None) for (a, b, c_) in ch]
                    else:
                        sq = (B - 32) // 2
                        ch = [("l", 32 + 2 * sq, None), ("l", 32 + 2 * sq + 1, None)]
                    fi = cnt["f"] % 2
                    cnt["f"] += 1
                    psO, psL = ob_[fi], lb_[fi]
                    rhs = q_.ap[:, :, qb * 128:(qb + 1) * 128]
                    for ci, (typ, c, msk) in enumerate(ch):
                        it = Item()
                        if typ == "l":
                            lk, kb = k_.ap[:, c * 128:(c + 1) * 128], k_.b
                            lv, vb = v_.ap[:, c, :], v_.b
                        else:
                            lk, kb = ckT_.ap[:, 0, c * 128:(c + 1) * 128], ckT_.b
                            lv, vb = cvb_.ap[:, c, :], cvb_.b
                        it.s_mms = [(0, 512, lk, rhs)]
                        it.s_reads = [kb, q_.b]
                        it.n = 512
                        it.mask = (msk.ap, msk.b) if msk is not None else None
                        it.pv = [(psO.ap, lv, 0, 512), (psL.ap, ones.ap, 0, 512)]
                        it.pv_reads = [vb]
                        it.pv_writes = [psO.b, psL.b]
                        it.first = (ci == 0)
                        it.last = (ci == len(ch) - 1)
                        it.after = None
                        it.post = None
                        it.acc = None
                        it.pre = None
                        if ci == len(ch) - 1:
                            def fin(psO=psO, psL=psL, fi=fi, kvh=kvh, g_=g_, o_=o_, qb=qb):
                                d_, f_ = den[fi], of[fi]
                                lv3 = psL.ap.rearrange("p (a b) -> p a b", b=128)
                                ov3 = psO.ap.rearrange("p (a b) -> p a b", b=128)
                                P.op("dve", lambda v: v.tensor_tensor(out=d_.ap, in0=lv3, in1=bc_last(sexp.ap[:, kvh * 4:(kvh + 1) * 4], 128),
                                                                      op=ALU.add), reads=[psL.b, sexp.b], writes=[d_.b])
                                P.op("dve", lambda v: v.reciprocal(out=d_.ap, in_=d_.ap), reads=[d_.b], writes=[d_.b])
                                P.op("dve", lambda v: v.tensor_tensor(out=f_.ap, in0=ov3, in1=d_.ap, op=ALU.mult),
                                     reads=[psO.b, d_.b], writes=[f_.b])
                                P.op("pool", lambda g: g.tensor_tensor(out=o_.ap[:, :, qb * 128:(qb + 1) * 128], in0=f_.ap,
                                                                       in1=g_.ap[:, :, qb * 128:(qb + 1) * 128], op=ALU.mult),
                                     reads=[f_.b, g_.b, o_.b], writes=[o_.b])
                                if qb == 3:
                                    pass
                            it.after = fin
                        items.append(it)
                    if qb == 3:
                        last = items[-1]
                        prev_after = last.after

                        def fin2(prev_after=prev_after, o_=o_, kvh=kvh, t0=t0):
                            prev_after()
                            P.dma("pool", OT[kvh * 4:(kvh + 1) * 4, :, t0:t0 + 512].rearrange("u p t -> p u t"), o_.ap,
                                  reads=[o_.b], writes=[d_ot], owner=o_.b)
                        last.after = fin2
            loaders[0]()
            for si in range(len(loaders) - 1):
                items[slab_first[si]].post = loaders[si + 1]
            run_attention(items, sb_, et, None)

    def phase2_B(l):
        j = 0
        lam_init = LAMBDA_INIT[l]
        ar.reset()
        kT = [mk(ar, [2, NTOK], BF16, "kTb%d" % i) for i in range(2)]
        vv = [mk(ar, [40, 256], BF16, "vvb%d" % i) for i in range(2)]
        ckf = mk(ar, [2, 256], F32, "ckfb")
        ckb = mk(ar, [2, 256], BF16, "ckbb")
        ckT = [mk(ar, [2, 256], BF16, "ckTb%d" % i) for i in range(2)]
        cvf = mk(ar, [2, 256], F32, "cvfb")
        cvb = [mk(ar, [2, 256], BF16, "cvbb%d" % i) for i in range(2)]
        qT = [mk(ar, [2, 1024], BF16, "qTb%d" % i) for i in range(2)]
        gT = [mk(ar, [2, 1024], BF16, "gTb%d" % i) for i in range(2)]
        oT = [mk(ar, [2, 1024], BF16, "oTb%d" % i) for i in range(2)]
        et = [mk(ar, [512], BF16, "et%d" % i) for i in range(4)]
        lamt = mk(ar, [512], F32, "lamt")
        ltmp = mk(ar, [2, 128], F32, "ltmp")
        ls = mk(ar, [2], F32, "ls")
        nlam = mk(ar, [1], F32, "nlam")
        sube = mk(ar, [2], F32, "sube")
        R = [mk(ar, [2, 256], F32, "Rb%d" % i) for i in range(4)]
        t1 = [mk(ar, [2, 256], F32, "t1b%d" % i) for i in range(4)]
        t2 = [mk(ar, [2, 256], F32, "t2b%d" % i) for i in range(4)]
        ob32 = [mk(ar, [2, 256], F32, "ob32%d" % i) for i in range(4)]
        sqb = [mk(ar, [2, 256], BF16, "sqb%d" % i) for i in range(4)]
        sd = [mk(ar, [256], F32, "sdb%d" % i) for i in range(4)]
        accs = [[mk(ar, [512], F32, "acc%d_%d" % (i, k)) for k in range(2)] for i in range(2)]
        sb_, o0_, o1_, lb_, xb_, tp = pb[0:3], pb[3], pb[4], pb[5], pb[6], pb[7]
        P.dma("sp", lamt.ap, dbc(lam_b[0:1, :], 128), reads=[d_ro], writes=[lamt.b], owner=lamt.b)
        lv = lamt.ap.rearrange("p (a b) -> p a b", b=128)
        P.op("dve", lambda v: v.tensor_tensor(out=ltmp.ap[:, 0, :], in0=lv[:, 0, :], in1=lv[:, 1, :], op=ALU.mult),
             reads=[lamt.b], writes=[ltmp.b])
        P.op("dve", lambda v: v.tensor_tensor(out=ltmp.ap[:, 1, :], in0=lv[:, 2, :], in1=lv[:, 3, :], op=ALU.mult),
             reads=[lamt.b, ltmp.b], writes=[ltmp.b])
        P.op("dve", lambda v: v.tensor_reduce(out=ls.ap, in_=ltmp.ap, axis=AX.X, op=ALU.add), reads=[ltmp.b], writes=[ls.b])
        P.op("act", lambda a: a.activation(out=ls.ap, in_=ls.ap, func=AF.Exp), reads=[ls.b], writes=[ls.b])
        P.op("dve", lambda v: v.tensor_tensor(out=nlam.ap, in0=ls.ap[:, 1:2], in1=ls.ap[:, 0:1], op=ALU.subtract),
             reads=[ls.b], writes=[nlam.b])
        P.op("dve", lambda v: v.tensor_scalar(out=nlam.ap, in0=nlam.ap, scalar1=-lam_init, scalar2=None, op0=ALU.add),
             reads=[nlam.b], writes=[nlam.b])
        P.dma("sp", sube.ap, subln_b[0].rearrange("(c p) -> p c", p=128), reads=[d_ro], writes=[sube.b], owner=sube.b,
              allow_slow_non_contiguous=True)
        P.op("dve", lambda v: v.tensor_scalar(out=sube.ap, in0=sube.ap, scalar1=1.0 - lam_init, scalar2=None, op0=ALU.mult),
             reads=[sube.b], writes=[sube.b])
        cnt = dict(q=0, f=0)
        for h in range(8):
            k_, v_, ckT_, cvb_ = kT[h % 2], vv[h % 2], ckT[h % 2], cvb[h % 2]
            P.dma("sp", k_.ap, KT[2 * h:2 * h + 2].rearrange("u p t -> p u t"), reads=[d_kt], writes=[k_.b], owner=k_.b)
            P.dma("sp", v_.ap, VS[:, h * 256:(h + 1) * 256].rearrange("(c p) d -> p c d", p=128), reads=[d_vs],
                  writes=[v_.b], owner=v_.b)
            load_ctx(cbk, cbv, 0, 2 * h * HD, 2, h * 256, 256, ckf, ckb, ckT_, cvf, cvb_, tp)
            items = []
            loaders = []
            slab_first = []
            for q1k in range(5):
                q_, g_, o_ = qT[cnt["q"] % 2], gT[cnt["q"] % 2], oT[cnt["q"] % 2]
                cnt["q"] += 1
                t0 = q1k * 1024

                def ld(q_=q_, g_=g_, t0=t0, h=h):
                    for (dst, src, dd) in ((q_, QT, d_qt), (g_, GT, d_gt)):
                        P.dma("sp", dst.ap, src[2 * h:2 * h + 2, :, t0:t0 + 1024].rearrange("u p t -> p u t"), reads=[dd],
                              writes=[dst.b], owner=dst.b)
                loaders.append(ld)
                slab_first.append(len(items))
                for qi in range(4):
                    B = q1k * 4 + qi
                    if B < 16:
                        ch = [("l", c) for c in range(32)] + [("c", 0), ("c", 1)]
                    else:
                        ch = [("l", 32 + 2 * (B - 16)), ("l", 32 + 2 * (B - 16) + 1)]
                    fi = cnt["f"] % 4
                    cnt["f"] += 1
                    qs = slice(qi * 256, (qi + 1) * 256)
                    for ci, (typ, c) in enumerate(ch):
                        it = Item()
                        it.s_mms = []
                        for m in range(2):
                            if typ == "l":
                                lk = k_.ap[:, m, c * 128:(c + 1) * 128]
                            else:
                                lk = ckT_.ap[:, m, c * 128:(c + 1) * 128]
                            it.s_mms.append((m * 256, 256, lk, q_.ap[:, m, qs]))
                        if typ == "l":
                            kb, vb = k_.b, v_.b
                            lvs = [v_.ap[:, c, e * 128:(e + 1) * 128] for e in range(2)]
                        else:
                            kb, vb = ckT_.b, cvb_.b
                            lvs = [cvb_.ap[:, c, e * 128:(e + 1) * 128] for e in range(2)]
                        it.s_reads = [kb, q_.b]
                        it.n = 512
                        it.mask = None
                        it.pv = [(o0_.ap, lvs[0], 0, 512), (o1_.ap, lvs[1], 0, 512)]
                        it.pv_reads = [vb]
                        it.pv_writes = [o0_.b, o1_.b]
                        it.first = (ci == 0)
                        it.last = (ci == len(ch) - 1)
                        it.after = None
                        it.post = None
                        a_set = accs[fi % 2]
                        it.acc = ("pool" if ci % 2 == 0 else "dve", a_set[ci % 2], ci < 2)
                        it.pre = None
                        if ci == len(ch) - 1:
                            def lsum(a_set=a_set):
                                def mm(t):
                                    for k_ in range(2):
                                        ins = t.matmul(lb_.ap, lhsT=ones32.ap, rhs=a_set[k_].ap, start=(k_ == 0), stop=(k_ == 1))
                                    return ins
                                P.op("pe", mm, reads=[a_set[0].b, a_set[1].b, ones32.b], writes=[lb_.b])
                            it.pre = lsum
                        if ci == len(ch) - 1:
                            def fin(fi=fi, h=h, g_=g_, o_=o_, qs=qs, qi=qi, t0=t0):
                                R_, t1_, t2_, o32, sq_, sd_ = R[fi], t1[fi], t2[fi], ob32[fi], sqb[fi], sd[fi]
                                l3 = lb_.ap.rearrange("p (a b) -> p a b", b=256)
                                P.op("dve", lambda v: v.reciprocal(out=R_.ap, in_=l3), reads=[lb_.b], writes=[R_.b])
                                for e, ob in enumerate((o0_, o1_)):
                                    o3 = ob.ap.rearrange("p (a b) -> p a b", b=256)
                                    P.op("dve", lambda v, o3=o3, e=e: v.tensor_tensor(out=t1_.ap[:, e, :], in0=o3[:, 0, :], in1=R_.ap[:, 0, :],
                                                                                      op=ALU.mult), reads=[ob.b, R_.b], writes=[t1_.b])
                                    P.op("dve", lambda v, o3=o3, e=e: v.tensor_tensor(out=t2_.ap[:, e, :], in0=o3[:, 1, :], in1=R_.ap[:, 1, :],
                                                                                      op=ALU.mult), reads=[ob.b, R_.b], writes=[t2_.b])
                                def part2():
                                    fin_part2(R_, t1_, t2_, o32, sq_, sd_, g_, o_, qs, qi, h, t0)
                                return part2

                            def fin_part2(R_, t1_, t2_, o32, sq_, sd_, g_, o_, qs, qi, h, t0):
                                P.op("dve", lambda g: g.scalar_tensor_tensor(out=o32.ap, in0=t2_.ap, scalar=nlam.ap[:, 0:1], in1=t1_.ap,
                                                                             op0=ALU.mult, op1=ALU.add), reads=[t1_.b, t2_.b, nlam.b],
                                     writes=[o32.b])
                                P.op("pool", lambda g: g.tensor_tensor(out=sq_.ap, in0=o32.ap, in1=o32.ap, op=ALU.mult), reads=[o32.b],
                                     writes=[sq_.b])

                                def ssmm(t):
                                    for e in range(2):
                                        ins = t.matmul(xb_.ap[:, 0:256], lhsT=ones.ap, rhs=sq_.ap[:, e, :], start=(e == 0), stop=(e == 1))
                                    return ins
                                P.op("pe", ssmm, reads=[sq_.b, ones.b], writes=[xb_.b])
                                P.op("act", lambda a: a.activation(out=sd_.ap, in_=xb_.ap[:, 0:256], func=AF.Sqrt, scale=1.0 / 256, bias=EPS),
                                     reads=[xb_.b], writes=[sd_.b])
                                P.op("dve", lambda v: v.reciprocal(out=sd_.ap, in_=sd_.ap), reads=[sd_.b], writes=[sd_.b])
                                P.op("dve", lambda v: v.tensor_tensor(out=o32.ap, in0=o32.ap, in1=bc_mid(sd_.ap, 2), op=ALU.mult),
                                     reads=[o32.b, sd_.b], writes=[o32.b])
                                for e in range(2):
                                    P.op("dve", lambda g, e=e: g.scalar_tensor_tensor(out=o_.ap[:, e, qs], in0=o32.ap[:, e, :],
                                                                                       scalar=sube.ap[:, e:e + 1], in1=g_.ap[:, e, qs],
                                                                                       op0=ALU.mult, op1=ALU.mult),
                                         reads=[o32.b, sube.b, g_.b, o_.b], writes=[o_.b])
                                if qi == 3:
                                    P.dma("pool", OT[2 * h:2 * h + 2, :, t0:t0 + 1024].rearrange("u p t -> p u t"), o_.ap,
                                          reads=[o_.b], writes=[d_ot], owner=o_.b)
                            it.after = fin
                        items.append(it)
            loaders[0]()
            for si in range(len(loaders) - 1):
                items[slab_first[si]].post = loaders[si + 1]
            run_attention(items, sb_, et, None, defer=4)

    def phase2_C(l):
        ar.reset()
        kT = [mk(ar, [NTOK], BF16, "kT%d" % i) for i in range(2)]
        vv = [mk(ar, [40, HD], BF16, "vv%d" % i) for i in range(2)]
        ckf = mk(ar, [2, HD], F32, "ckf")
        ckb = mk(ar, [2, HD], BF16, "ckb")
        ckT = [mk(ar, [1, 256], BF16, "ckT%d" % i) for i in range(2)]
        cvf = mk(ar, [2, HD], F32, "cvf")
        cvb = [mk(ar, [2, HD], BF16, "cvb%d" % i) for i in range(2)]
        qT = [mk(ar, [1024], BF16, "qTc%d" % i) for i in range(2)]
        gT = [mk(ar, [1024], BF16, "gTc%d" % i) for i in range(2)]
        oT = [mk(ar, [1024], BF16, "oTc%d" % i) for i in range(2)]
        et = [mk(ar, [128], BF16, "etc%d" % i) for i in range(5)]
        cbr = mk(ar, [15, 64], F32, "cbr")
        cbm = mk(ar, [15, 64], BF16, "cbm")
        colm = mk(ar, [64], F32, "colm")
        EB = [mk(ar, [25, 128], BF16, "EB%d" % i) for i in range(2)]
        rr = [mk(ar, [128], F32, "rrc%d" % i) for i in range(2)]
        of = [mk(ar, [128], F32, "ofc%d" % i) for i in range(2)]
        sb_, ob_, lb_, tp = pb[0:3], pb[3:5], pb[5:7], pb[7]
        P.op("pool", lambda g: g.memset(colm.ap, 1.0), writes=[colm.b])
        for h0 in (0, 64):
            pr = slice(h0, h0 + 64)
            P.op("pool", lambda g, pr=pr: g.affine_select(out=colm.ap[pr, 0:8], in_=colm.ap[pr, 0:8], compare_op=ALU.is_ge, fill=0.0,
                                                          base=15, pattern=[[0, 8]], channel_multiplier=-1), reads=[colm.b], writes=[colm.b])
            P.op("pool", lambda g, pr=pr: g.affine_select(out=colm.ap[pr, 8:57], in_=colm.ap[pr, 8:57], compare_op=ALU.is_ge, fill=0.0,
                                                          base=0, pattern=[[-1, 49]], channel_multiplier=1), reads=[colm.b], writes=[colm.b])
            P.op("pool", lambda g, pr=pr: g.affine_select(out=colm.ap[pr, 8:57], in_=colm.ap[pr, 8:57], compare_op=ALU.is_ge, fill=0.0,
                                                          base=15, pattern=[[1, 49]], channel_multiplier=-1), reads=[colm.b], writes=[colm.b])
            P.op("pool", lambda g, pr=pr: g.affine_select(out=colm.ap[pr, 57:64], in_=colm.ap[pr, 57:64], compare_op=ALU.is_ge, fill=0.0,
                                                          base=-48, pattern=[[0, 7]], channel_multiplier=1), reads=[colm.b], writes=[colm.b])

        def rs_(r):
            return min(max(r - 4, 0), 56)

        def qblock_chunks(jb):
            lo = rs_(2 * jb) // 2
            hi = (rs_(2 * jb + 1) + 7) // 2
            return list(range(lo, hi + 1))

        classes = {0: 0, 1: 1, 30: 3, 31: 4}

        def cls_of(jb):
            return classes.get(jb, 2)

        rep = {0: 0, 1: 1, 2: 10, 3: 30, 4: 31}

        def build_EB(EB_):
            P.op("pool", lambda g: g.memset(EB_.ap, 0.0), writes=[EB_.b])
            for ci in range(5):
                jb = rep[ci]
                for c in qblock_chunks(jb):
                    dlt = c - jb
                    slot = ci * 5 + (dlt + 3 if ci == 4 else (dlt if ci == 0 else dlt + 2 if ci in (2, 3) else dlt + 1))
                    for qr in range(2):
                        r = 2 * jb + qr
                        for kr in range(2):
                            ka = 2 * c + kr
                            if rs_(r) <= ka <= rs_(r) + 7:
                                i = ka - r + 7
                                P.op("pool", lambda g, kr=kr, qr=qr, slot=slot, i=i: g.tensor_copy(
                                    out=EB_.ap[kr * 64:(kr + 1) * 64, slot, qr * 64:(qr + 1) * 64],
                                    in_=cbm.ap[kr * 64:(kr + 1) * 64, i, :]), reads=[cbm.b, EB_.b], writes=[EB_.b])

        def slot_of(jb, c):
            ci = cls_of(jb)
            dlt = c - jb
            return ci * 5 + (dlt + 3 if ci == 4 else (dlt if ci == 0 else dlt + 2 if ci in (2, 3) else dlt + 1))

        cnt = dict(q=0, f=0)
        for h in range(16):
            k_, v_, ckT_, cvb_, EB_ = kT[h % 2], vv[h % 2], ckT[h % 2], cvb[h % 2], EB[h % 2]
            P.dma("sp", k_.ap, KT[h], reads=[d_kt], writes=[k_.b], owner=k_.b)
            P.dma("sp", v_.ap, VS[:, h * HD:(h + 1) * HD].rearrange("(c p) d -> p c d", p=128), reads=[d_vs],
                  writes=[v_.b], owner=v_.b)
            load_ctx(cck, ccv, 0, h * HD, 1, h * HD, HD, ckf, ckb, ckT_, cvf, cvb_, tp)
            for h0 in (0, 64):
                P.dma("sp", cbr.ap[h0:h0 + 64], rpbt[h], reads=[d_ro], writes=[cbr.b], owner=cbr.b)
            P.op("act", lambda a: a.activation(out=cbr.ap, in_=cbr.ap, func=AF.Exp), reads=[cbr.b], writes=[cbr.b])
            P.op("pool", lambda g: g.tensor_tensor(out=cbm.ap, in0=cbr.ap, in1=bc_mid(colm.ap, 15), op=ALU.mult),
                 reads=[cbr.b, colm.b], writes=[cbm.b])
            build_EB(EB_)
            items = []
            loaders = []
            slab_first = []
            for q1k in range(5):
                q_, g_, o_ = qT[cnt["q"] % 2], gT[cnt["q"] % 2], oT[cnt["q"] % 2]
                cnt["q"] += 1
                t0 = q1k * 1024

                def ld(q_=q_, g_=g_, t0=t0, h=h):
                    P.dma("sp", q_.ap, QT[h, :, t0:t0 + 1024], reads=[d_qt], writes=[q_.b], owner=q_.b)
                    P.dma("sp", g_.ap, GT[h, :, t0:t0 + 1024], reads=[d_gt], writes=[g_.b], owner=g_.b)
                loaders.append(ld)
                slab_first.append(len(items))
                for qi in range(8):
                    B = q1k * 8 + qi
                    if B < 32:
                        ch = [("l", c, slot_of(B, c)) for c in qblock_chunks(B)] + [("c", 0, None), ("c", 1, None)]
                    else:
                        sq = (B - 32) // 2
                        ch = [("l", 32 + 2 * sq, None), ("l", 32 + 2 * sq + 1, None)]
                    fi = cnt["f"] % 2
                    cnt["f"] += 1
                    psO, psL = ob_[fi], lb_[fi]
                    qs = slice(qi * 128, (qi + 1) * 128)
                    for ci, (typ, c, slot) in enumerate(ch):
                        it = Item()
                        if typ == "l":
                            lk, kb = k_.ap[:, c * 128:(c + 1) * 128], k_.b
                            lv_, vb = v_.ap[:, c, :], v_.b
                        else:
                            lk, kb = ckT_.ap[:, 0, c * 128:(c + 1) * 128], ckT_.b
                            lv_, vb = cvb_.ap[:, c, :], cvb_.b
                        it.s_mms = [(0, 128, lk, q_.ap[:, qs])]
                        it.s_reads = [kb, q_.b]
                        it.n = 128
                        it.mask = (EB_.ap[:, slot, :], EB_.b) if slot is not None else None
                        it.pv = [(psO.ap[:, 0:128], lv_, 0, 128), (psL.ap[:, 0:128], ones.ap, 0, 128)]
                        it.pv_reads = [vb]
                        it.pv_writes = [psO.b, psL.b]
                        it.first = (ci == 0)
                        it.last = (ci == len(ch) - 1)
                        it.after = None
                        it.post = None
                        it.acc = None
                        it.pre = None
                        if ci == len(ch) - 1:
                            def fin(psO=psO, psL=psL, fi=fi, h=h, g_=g_, o_=o_, qs=qs, qi=qi, t0=t0):
                                r_, f_ = rr[fi], of[fi]
                                P.op("dve", lambda v: v.reciprocal(out=r_.ap, in_=psL.ap[:, 0:128]), reads=[psL.b], writes=[r_.b])
                                P.op("dve", lambda v: v.tensor_tensor(out=f_.ap, in0=psO.ap[:, 0:128], in1=r_.ap, op=ALU.mult),
                                     reads=[psO.b, r_.b], writes=[f_.b])
                                P.op("pool", lambda g: g.tensor_tensor(out=o_.ap[:, qs], in0=f_.ap, in1=g_.ap[:, qs], op=ALU.mult),
                                     reads=[f_.b, g_.b, o_.b], writes=[o_.b])
                                if qi == 7:
                                    P.dma("pool", OT[h, :, t0:t0 + 1024], o_.ap, reads=[o_.b], writes=[d_ot], owner=o_.b)
                            it.after = fin
                        items.append(it)
            loaders[0]()
            for si in range(len(loaders) - 1):
                items[slab_first[si]].post = loaders[si + 1]
            run_attention(items, sb_, et, None)

    def phase3(l):
        xin, dxin = x_aps[l], d_x[l]
        xout, dxout = x_aps[l + 1], d_x[l + 1]
        ar.reset()
        wo = mk(ar, [16, D], BF16, "wo")
        gt = mk(ar, [D], F32, "gt")
        xt = [mk(ar, [D], F32, "xt%d" % i) for i in range(2)]
        xo = [mk(ar, [D], F32, "xo%d" % i) for i in range(2)]
        ot = [mk(ar, [16, 512], BF16, "ot%d" % i) for i in range(2)]
        P.dma("sp", wo.ap, WOB[l].rearrange("(c p) f -> p c f", p=128), reads=[d_wob[l]], writes=[wo.b], owner=wo.b)
        for tb in range(40):
            t0 = tb * 128
            if tb == 0 or tb == 32:
                r = 0 if tb == 0 else 1
                P.dma("sp", gt.ap, dbc(MOD[l, r:r + 1, 2 * D:3 * D], 128), reads=[d_mod], writes=[gt.b], owner=gt.b)
            o_ = ot[(tb // 4) % 2]
            if tb % 4 == 0:
                P.dma("sp", o_.ap, OT[:, :, t0:t0 + 512].rearrange("u p t -> p u t"), reads=[d_ot], writes=[o_.b], owner=o_.b)
            x = xt[tb % 2]
            y = xo[tb % 2]
            P.dma("sp", x.ap, xin[t0:t0 + 128, :], reads=[dxin], writes=[x.b], owner=x.b)
            tl = tb % 4
            for fb in range(4):
                ps = pb[(tb * 4 + fb) % 4]

                def mm(t, ps=ps, o_=o_, tl=tl, fb=fb):
                    for c in range(16):
                        ins = t.matmul(ps.ap, lhsT=o_.ap[:, c, tl * 128:(tl + 1) * 128], rhs=wo.ap[:, c, fb * 512:(fb + 1) * 512],
                                       start=(c == 0), stop=(c == 15))
                    return ins
                P.op("pe", mm, reads=[o_.b, wo.b], writes=[ps.b])
                fs = slice(fb * 512, (fb + 1) * 512)
                P.op("dve", lambda v, ps=ps, y=y, fs=fs: v.tensor_tensor(out=y.ap[:, fs], in0=ps.ap, in1=gt.ap[:, fs], op=ALU.mult),
                     reads=[ps.b, gt.b, y.b], writes=[y.b])
            P.op("pool", lambda g, x=x, y=y: g.tensor_tensor(out=y.ap, in0=y.ap, in1=x.ap, op=ALU.add), reads=[x.b, y.b], writes=[y.b])
            P.dma("pool", xout[t0:t0 + 128, :], y.ap, reads=[y.b], writes=[dxout], owner=y.b)

    build_consts()
    P.barrier()
    for l in range(n_layers):
        cast_weights(l)
    phase0()
    later_casts = tuple("cast%d" % l for l in range(1, n_layers))
    P.barrier(skip=later_casts)
    for l in range(n_layers):
        if stop_phase == (l, 0):
            break
        phase1(l)
        P.barrier(skip=later_casts if l == 0 else ())
        if stop_phase == (l, 1):
            break
        [phase2_A, phase2_B, phase2_C][KINDS[l]](l)
        P.barrier(skip=later_casts if l == 0 else ())
        if stop_phase == (l, 2):
            break
        phase3(l)
        P.barrier()
    P.barrier()

    from contextlib import ExitStack
    with ExitStack() as stack:
        P.emit(nc, stack)
    nc._n_sems = P.n_sems
    return nc


def make_in_maps(inp, n_layers=DEPTH):
    f = lambda a: np.ascontiguousarray(a, dtype=np.float32)
    xs, xp = inp["x_sample"], inp["x_prompt"]
    rpb = np.asarray(inp["rpb_c"])[0]
    kc = np.arange(64)[:, None]
    qc = np.arange(64)[None, :]
    idx = np.clip(kc - qc + 15, 0, 30)
    rpbt = f(rpb[:, :, idx].transpose(0, 2, 1, 3))
    shared = dict(
        ln_g=f(inp["ln_g"]), ada_w=f(inp["ada_w"][:n_layers]), ada_b=f(inp["ada_b"]), w_out=f(inp["w_out"][:n_layers]),
        qn_g=f(inp["qn_g"]), kn_g=f(inp["kn_g"]), w_in_a=f(inp["w_in_a"][:2 if n_layers > 3 else 1]),
        w_in_b=f(inp["w_in_b"] if n_layers > 1 else np.asarray(inp["w_in_b"])[:, :8]),
        w_in_c=f(inp["w_in_c"] if n_layers > 2 else np.asarray(inp["w_in_c"])[:, :8]), sink_a=f(inp["sink_a"]), lam_b=f(np.asarray(inp["lam_b"]).reshape(1, 512)),
        subln_b=f(inp["subln_b"]), rpbt=rpbt,
    )
    maps = []
    for i in range(8):
        m = dict(shared)
        m["x"] = f(np.concatenate([np.asarray(xs[i]), np.asarray(xp[4 * i:4 * i + 4]).reshape(1024, D)], axis=0))
        m["cpair"] = f(np.stack([np.asarray(inp["c"])[i], np.asarray(inp["c_ctx"])], axis=0))
        m["cak"] = f(np.asarray(inp["cache_a_k"])[i].reshape(2, 256, 512))
        m["cav"] = f(np.asarray(inp["cache_a_v"])[i].reshape(2, 256, 512))
        m["cbk"] = f(np.asarray(inp["cache_b_k"])[i].reshape(1, 256, 2048))
        m["cbv"] = f(np.asarray(inp["cache_b_v"])[i].reshape(1, 256, 2048))
        m["cck"] = f(np.asarray(inp["cache_c_k"])[i].reshape(1, 256, 2048))
        m["ccv"] = f(np.asarray(inp["cache_c_v"])[i].reshape(1, 256, 2048))
        maps.append(m)
    return maps


def assemble(results):
    y = np.stack([r["y"] for r in results], axis=0)
    y_sample = np.ascontiguousarray(y[:, :NS, :])
    y_prompt = np.ascontiguousarray(y[:, NS:, :].reshape(32, 256, D))
    cat = lambda k: np.concatenate([r[k] for r in results], axis=0)
    return (y_prompt, y_sample,
            cat("nak").reshape(32, 2, 256, 4, 128), cat("nav").reshape(32, 2, 256, 4, 128),
            cat("nbk").reshape(32, 1, 256, 8, 2, 128), cat("nbv").reshape(32, 1, 256, 8, 256),
            cat("nck").reshape(32, 1, 256, 16, 128), cat("ncv").reshape(32, 1, 256, 16, 128))


def kernel(**inputs):
    nc = build()
    in_maps = make_in_maps(inputs)
    res = run_bass_kernel_spmd(nc, in_maps, core_ids=list(range(8)))
    return assemble(res.results)
```

```python
import math
import numpy as np
import concourse.bass as bass
import concourse.mybir as mybir
from concourse.bass_utils import run_bass_kernel_spmd

F32 = mybir.dt.float32
BF16 = mybir.dt.bfloat16
I32 = mybir.dt.int32
AF = mybir.ActivationFunctionType
ALU = mybir.AluOpType
AX = mybir.AxisListType

D = 2048
NTOK = 5120
NS = 4096
TT = 1024
HD = 128
SCALE = HD ** -0.5
EPS = 1e-6
DEPTH = 4
KINDS = [0, 1, 2, 0]
FIN = [5120, 8192, 8192, 5120]
LAMBDA_INIT = [0.8 - 0.6 * math.exp(-0.3 * l) for l in range(DEPTH)]


class Buf:
    __slots__ = ("name", "w", "r", "excl")

    def __init__(self, name):
        self.name = name
        self.w = None
        self.r = {}
        self.excl = False


class DBuf:
    __slots__ = ("name", "writers", "readers", "prev_readers")

    def __init__(self, name):
        self.name = name
        self.writers = {}
        self.readers = {}
        self.prev_readers = {}


class Op:
    __slots__ = ("eng", "fn", "deps", "needs_inc", "dma")

    def __init__(self, eng, fn, deps, dma=None):
        self.eng = eng
        self.fn = fn
        self.deps = deps
        self.needs_inc = False
        self.dma = dma


ENGS = ("pe", "act", "dve", "pool", "sp")


def _add(dd, tok):
    key = (tok[0], tok[1])
    if dd.get(key, -1) < tok[2]:
        dd[key] = tok[2]


class Prog:
    def __init__(self):
        self.ops = {e: [] for e in ENGS}
        self.names = {}
        self.dpool = []
        self.kind_idxs = {'hw': [], 'sw': []}
        self.used = {'hw': 0, 'sw': 0}

    def _deps_for(self, reads, writes):
        deps = {}
        for b in reads:
            if isinstance(b, DBuf):
                for k, v in b.writers.items():
                    _add(deps, (k[0], k[1], v))
            else:
                if b.w is not None:
                    _add(deps, b.w)
                if b.excl:
                    for k, v in b.r.items():
                        _add(deps, (k[0], k[1], v))
        for b in writes:
            if isinstance(b, DBuf):
                if b.readers:
                    b.prev_readers = b.readers
                    b.readers = {}
                    b.writers = {}
                for k, v in b.prev_readers.items():
                    _add(deps, (k[0], k[1], v))
            else:
                if b.w is not None:
                    _add(deps, b.w)
                for k, v in b.r.items():
                    _add(deps, (k[0], k[1], v))
        return deps

    def _mark(self, tok, reads, writes):
        for b in reads:
            if isinstance(b, DBuf):
                _add(b.readers, tok)
            else:
                _add(b.r, tok)
        for b in writes:
            if isinstance(b, DBuf):
                _add(b.writers, tok)
            else:
                b.w = tok
                b.r = {}

    def op(self, eng, fn, reads=(), writes=()):
        deps = self._deps_for(reads, writes)
        if eng == "pe":
            deps.pop(("e", "pe"), None)
        lst = self.ops[eng]
        tok = ("e", eng, len(lst))
        lst.append(Op(eng, fn, deps))
        self._mark(tok, reads, writes)
        return tok

    def dma(self, eng, out, in_, reads, writes, owner, **kw):
        deps = self._deps_for(reads, writes)
        kind = "sw" if eng == "pool" else "hw"
        idx = self.names.get((owner.name, kind))
        if idx is None:
            k = self.used[kind]
            if k < len(self.kind_idxs[kind]):
                idx = self.kind_idxs[kind][k]
            else:
                idx = len(self.dpool)
                self.dpool.append(0)
                self.kind_idxs[kind].append(idx)
            self.used[kind] += 1
            self.names[(owner.name, kind)] = idx
        self.dpool[idx] += 16
        ent = (idx, self.dpool[idx])
        tok = ("d", ent[0], ent[1])

        def fn(e, out=out, in_=in_, kw=kw):
            return e.dma_start(out=out, in_=in_, **kw)

        self.ops[eng].append(Op(eng, fn, deps, dma=ent[0]))
        self._mark(tok, reads, writes)
        return tok

    def barrier(self):
        deps = {}
        for e in ENGS:
            for i in range(len(self.ops[e]) - 1, -1, -1):
                o = self.ops[e][i]
                if o.dma is None and o.fn is not None:
                    deps[("e", e)] = i
                    break
        for idx, cnt in enumerate(self.dpool):
            if cnt:
                deps[("d", idx)] = cnt
        for e in ENGS:
            self.ops[e].append(Op(e, None, dict(deps)))
        self.names = {}
        self.used = {'hw': 0, 'sw': 0}

    def emit(self, nc, stack):
        for e in ENGS:
            for o in self.ops[e]:
                for k, v in o.deps.items():
                    if k[0] == "e":
                        self.ops[k[1]][v].needs_inc = True
        inc_count = {}
        for e in ENGS:
            c = 0
            arr = []
            for o in self.ops[e]:
                if o.needs_inc:
                    c += 1
                arr.append(c)
            inc_count[e] = arr
        esem = {e: stack.enter_context(nc.semaphore("es_" + e)) for e in ENGS if e != "sp"}
        dsem = [stack.enter_context(nc.semaphore("ds%d" % i)) for i in range(len(self.dpool))]
        self.n_sems = len(esem) + len(dsem)
        block = stack.enter_context(nc.Block())
        self.stats = {e: [0, 0] for e in ENGS}

        def run(e, eng):
            known = {}
            st = self.stats[e]
            for o in self.ops[e]:
                for k, v in o.deps.items():
                    if k[0] == "e":
                        val = inc_count[k[1]][v]
                        sem = esem[k[1]]
                    else:
                        val = v
                        sem = dsem[k[1]]
                    if known.get(k, 0) >= val:
                        continue
                    known[k] = val
                    eng.wait_ge(sem, val)
                    st[1] += 1
                if o.fn is None:
                    continue
                ins = o.fn(eng)
                st[0] += 1
                if o.dma is not None:
                    ins.then_inc(dsem[o.dma], 16)
                elif o.needs_inc:
                    ins.then_inc(esem[e], 1)

        @block.tensor
        def _(t):
            run("pe", t)

        @block.scalar
        def _(a):
            run("act", a)

        @block.vector
        def _(v):
            run("dve", v)

        @block.gpsimd
        def _(g):
            run("pool", g)

        @block.sync
        def _(s):
            run("sp", s)


class Arena:
    def __init__(self, nc, name, nbytes):
        self.t = nc.alloc_sbuf_tensor(name, [128, nbytes // 4], F32)
        self.cap = nbytes
        self.off = 0

    def reset(self):
        self.off = 0

    def alloc(self, free_shape, dtype):
        es = 2 if dtype == BF16 else 4
        n = 1
        for s in free_shape:
            n *= s
        nb = (n * es + 31) // 32 * 32
        assert self.off + nb <= self.cap, ("arena overflow", self.off, nb, self.cap)
        v = self.t[:, self.off // 4:(self.off + nb) // 4]
        self.off += nb
        if dtype != F32:
            v = v.bitcast(dtype)
        v = v[:, 0:n]
        if len(free_shape) == 2:
            v = v.rearrange("p (a b) -> p a b", b=free_shape[1])
        elif len(free_shape) == 3:
            v = v.rearrange("p (a b c) -> p a b c", b=free_shape[1], c=free_shape[2])
        elif len(free_shape) == 4:
            v = v.rearrange("p (a b c d) -> p a b c d", b=free_shape[1], c=free_shape[2], d=free_shape[3])
        return v


class T:
    __slots__ = ("ap", "b")

    def __init__(self, ap, name):
        self.ap = ap
        self.b = Buf(name)


def dbc(row, nparts):
    n = row.shape[-1]
    return bass.AP(tensor=row.tensor, offset=row.offset, ap=[[0, nparts], [1, n]])


def bc_last(ap2d, n):
    return ap2d.unsqueeze(2).broadcast_to([ap2d.shape[0], ap2d.shape[1], n])


def bc_mid(ap2d, n):
    return ap2d.unsqueeze(1).broadcast_to([ap2d.shape[0], n, ap2d.shape[1]])


def build(n_layers=DEPTH, stop_phase=None, dbg=0):
    nc = bass.Bass("TRN2", target_bir_lowering=False)
    P = Prog()

    def din(name, shape):
        return nc.dram_tensor(name, list(shape), F32, kind="ExternalInput").ap()

    def dout(name, shape):
        return nc.dram_tensor(name, list(shape), F32, kind="ExternalOutput").ap()

    x_in = din("x", [NTOK, D])
    cpair = din("cpair", [2, D])
    ln_g = din("ln_g", [DEPTH, D])
    ada_w = din("ada_w", [n_layers, D, 3 * D])
    ada_b = din("ada_b", [DEPTH, 3 * D])
    w_out = din("w_out", [n_layers, D, D])
    qn_g = din("qn_g", [DEPTH, HD])
    kn_g = din("kn_g", [DEPTH, HD])
    w_in_a = din("w_in_a", [2 if n_layers > 3 else 1, D, 5120])
    w_in_b = din("w_in_b", [1, D, 8192] if n_layers > 1 else [1, 8, 8192])
    w_in_c = din("w_in_c", [1, D, 8192] if n_layers > 2 else [1, 8, 8192])
    sink_a = din("sink_a", [2, 16])
    lam_b = din("lam_b", [1, 4 * HD])
    subln_b = din("subln_b", [1, 256])
    rpbt = din("rpbt", [16, 64, 15, 64])
    cak = din("cak", [2, 256, 4 * HD])
    cav = din("cav", [2, 256, 4 * HD])
    cbk = din("cbk", [1, 256, 16 * HD])
    cbv = din("cbv", [1, 256, 8 * 256])
    cck = din("cck", [1, 256, 16 * HD])
    ccv = din("ccv", [1, 256, 16 * HD])
    y_out = dout("y", [NTOK, D])
    nak = dout("nak", [4, 2, 256, 4 * HD])
    nav = dout("nav", [4, 2, 256, 4 * HD])
    nbk = dout("nbk", [4, 1, 256, 16 * HD])
    nbv = dout("nbv", [4, 1, 256, 8 * 256])
    nck = dout("nck", [4, 1, 256, 16 * HD])
    ncv = dout("ncv", [4, 1, 256, 16 * HD])
    XA = nc.dram_tensor("XA", [NTOK, D], F32).ap()
    XB = nc.dram_tensor("XB", [NTOK, D], F32).ap()
    MOD = nc.dram_tensor("MOD", [DEPTH, 2, 3 * D], F32).ap()
    QT = nc.dram_tensor("QT", [16, 128, NTOK], BF16).ap()
    KT = nc.dram_tensor("KT", [16, 128, NTOK], BF16).ap()
    VS = nc.dram_tensor("VS", [NTOK, D], BF16).ap()
    GT = nc.dram_tensor("GT", [16, 128, NTOK], BF16).ap()
    OT = nc.dram_tensor("OT", [16, 128, NTOK], BF16).ap()
    WIB = [nc.dram_tensor("WIB%d" % l, [D, FIN[l]], BF16).ap() for l in range(DEPTH)]
    WOB = [nc.dram_tensor("WOB%d" % l, [D, D], BF16).ap() for l in range(DEPTH)]
    w_in_src = [w_in_a[0], w_in_b[0], w_in_c[0], w_in_a[1 if n_layers > 3 else 0]]

    d_x = [DBuf("x_in"), DBuf("XA"), DBuf("XB"), DBuf("XA"), DBuf("y")]
    d_x[3] = d_x[1]
    x_aps = [x_in, XA, XB, XA, y_out]
    d_mod = DBuf("MOD")
    d_qt, d_kt, d_vs, d_gt, d_ot = DBuf("QT"), DBuf("KT"), DBuf("VS"), DBuf("GT"), DBuf("OT")
    d_wib = [DBuf("WIB%d" % l) for l in range(DEPTH)]
    d_wob = [DBuf("WOB%d" % l) for l in range(DEPTH)]
    d_outs = DBuf("outs")
    d_ro = DBuf("ro")

    ar = Arena(nc, "arena", 180 * 1024)
    car = Arena(nc, "consts", 20 * 1024)
    banks = [nc.alloc_psum_tensor("bank%d" % i, [128, 512], F32) for i in range(8)]
    pb = [T(banks[i][:], "bank%d" % i) for i in range(8)]
    for t_ in pb:
        t_.b.excl = True

    def mk(arena, free_shape, dtype, name):
        t = T(arena.alloc(free_shape, dtype), name)
        return t

    ident = mk(car, [128], BF16, "ident")
    ones = mk(car, [128], BF16, "ones")
    cosT = mk(car, [32, 2, 32], F32, "cosT")
    sinT = mk(car, [32, 2, 32], F32, "sinT")
    gq = mk(car, [HD], F32, "gq")
    gk = mk(car, [HD], F32, "gk")

    def build_consts():
        P.op("pool", lambda g: g.memset(ident.ap, 0.0), writes=[ident.b])
        P.op("pool", lambda g: g.affine_select(out=ident.ap, in_=ident.ap, compare_op=ALU.not_equal, fill=1.0,
                                               base=0, pattern=[[-1, 128]], channel_multiplier=1),
             reads=[ident.b], writes=[ident.b])
        P.op("pool", lambda g: g.memset(ones.ap, 1.0), writes=[ones.b])
        ar.reset()
        posr = mk(ar, [32], I32, "posr")
        posc = mk(ar, [1], I32, "posc")
        fi = mk(ar, [32], I32, "fi")
        posrf = mk(ar, [32], F32, "posrf")
        poscf = mk(ar, [1], F32, "poscf")
        ff = mk(ar, [32], F32, "ff")
        invf = mk(ar, [32], F32, "invf")
        ang = mk(ar, [32, 2, 32], F32, "ang")
        tmp = mk(ar, [32, 2, 32], F32, "angt")
        tmpi = mk(ar, [32, 2, 32], I32, "angi")
        for h0, base in ((0, 0), (64, 1)):
            P.op("pool", lambda g, h0=h0, base=base: g.iota(posr.ap[h0:h0 + 64, :], pattern=[[2, 32]], base=base,
                                                            channel_multiplier=0), writes=[posr.b])
            P.op("pool", lambda g, h0=h0: g.iota(posc.ap[h0:h0 + 64, :], pattern=[[0, 1]], base=0,
                                                 channel_multiplier=1), writes=[posc.b])
        P.op("pool", lambda g: g.iota(fi.ap, pattern=[[1, 32]], base=0, channel_multiplier=0), writes=[fi.b])
        P.op("dve", lambda v: v.tensor_copy(out=posrf.ap, in_=posr.ap), reads=[posr.b], writes=[posrf.b])
        P.op("dve", lambda v: v.tensor_copy(out=poscf.ap, in_=posc.ap), reads=[posc.b], writes=[poscf.b])
        P.op("dve", lambda v: v.tensor_copy(out=ff.ap, in_=fi.ap), reads=[fi.b], writes=[ff.b])
        P.op("act", lambda a: a.activation(out=invf.ap, in_=ff.ap, func=AF.Exp, scale=-math.log(10000.0) / 32.0),
             reads=[ff.b], writes=[invf.b])
        P.op("dve", lambda v: v.tensor_tensor(out=ang.ap[:, :, 0, :], in0=bc_last(posrf.ap, 32), in1=bc_mid(invf.ap, 32),
                                              op=ALU.mult), reads=[posrf.b, invf.b], writes=[ang.b])
        P.op("dve", lambda v: v.tensor_scalar(out=ang.ap[:, :, 1, :], in0=bc_mid(invf.ap, 32), scalar1=poscf.ap[:, 0:1],
                                              scalar2=None, op0=ALU.mult), reads=[poscf.b, invf.b, ang.b], writes=[ang.b])
        TWO_PI = 2.0 * math.pi
        for dst, shift in ((sinT, 0.0), (cosT, math.pi / 2)):
            P.op("dve", lambda v, shift=shift: v.tensor_scalar(out=tmp.ap, in0=ang.ap, scalar1=shift, scalar2=1.0 / TWO_PI,
                                                               op0=ALU.add, op1=ALU.mult), reads=[ang.b], writes=[tmp.b])
            P.op("dve", lambda v: v.tensor_copy(out=tmpi.ap, in_=tmp.ap), reads=[tmp.b], writes=[tmpi.b])
            P.op("dve", lambda v: v.tensor_copy(out=tmp.ap, in_=tmpi.ap), reads=[tmpi.b], writes=[tmp.b])
            P.op("dve", lambda v: v.scalar_tensor_tensor(out=tmp.ap, in0=tmp.ap, scalar=-TWO_PI, in1=ang.ap,
                                                         op0=ALU.mult, op1=ALU.add), reads=[tmp.b, ang.b], writes=[tmp.b])
            P.op("dve", lambda v, shift=shift: v.tensor_scalar(out=tmp.ap, in0=tmp.ap, scalar1=shift, scalar2=3.1415925,
                                                               op0=ALU.add, op1=ALU.min), reads=[tmp.b], writes=[tmp.b])
            P.op("dve", lambda v: v.tensor_scalar(out=tmp.ap, in0=tmp.ap, scalar1=-3.1415925, scalar2=None,
                                                  op0=ALU.max), reads=[tmp.b], writes=[tmp.b])
            P.op("act", lambda a, dst=dst: a.activation(out=dst.ap, in_=tmp.ap, func=AF.Sin), reads=[tmp.b], writes=[dst.b])

    castb = Buf("castsem")

    def cast_weights(l):
        src = w_in_src[l]
        F = FIN[l]
        for r0 in range(0, D, 256):
            P.dma("pool", WIB[l][r0:r0 + 256, :].rearrange("r (a b) -> r a b", b=1024),
                  src[r0:r0 + 256, :].rearrange("r (a b) -> r a b", b=1024),
                  reads=[d_ro], writes=[d_wib[l]], owner=castb)
        for r0 in range(0, D, 512):
            P.dma("pool", WOB[l][r0:r0 + 512, :].rearrange("r (a b) -> r a b", b=1024),
                  w_out[l, r0:r0 + 512, :].rearrange("r (a b) -> r a b", b=1024),
                  reads=[d_ro], writes=[d_wob[l]], owner=castb)

    def phase0():
        ar.reset()
        cT = mk(ar, [16, 2], F32, "cT")
        sT = mk(ar, [16, 2], F32, "sT")
        sg = mk(ar, [16, 2], F32, "sg")
        wt = [mk(ar, [16, 512], F32, "adaw%d" % i) for i in range(2)]
        adab = T(ar.alloc([3 * D], F32)[0:2, :], "adab")
        msb = T(ar.alloc([3 * D], F32)[0:2, :], "msb")
        for r in range(2):
            P.dma("sp", cT.ap[:, :, r], cpair[r].rearrange("(c p) -> p c", p=128), reads=[d_ro], writes=[cT.b],
                  owner=cT.b, allow_slow_non_contiguous=True)
        P.op("act", lambda a: a.activation(out=sg.ap, in_=cT.ap, func=AF.Exp, scale=-1.0), reads=[cT.b], writes=[sg.b])
        P.op("dve", lambda v: v.tensor_scalar(out=sg.ap, in0=sg.ap, scalar1=1.0, scalar2=None, op0=ALU.add),
             reads=[sg.b], writes=[sg.b])
        P.op("dve", lambda v: v.reciprocal(out=sg.ap, in_=sg.ap), reads=[sg.b], writes=[sg.b])
        P.op("dve", lambda v: v.tensor_tensor(out=sT.ap, in0=cT.ap, in1=sg.ap, op=ALU.mult), reads=[cT.b, sg.b],
             writes=[sT.b])
        i = 0
        for l in range(n_layers):
            P.dma("sp", adab.ap, dbc(ada_b[l:l + 1, :], 2), reads=[d_ro], writes=[adab.b], owner=adab.b)
            for fb in range(12):
                w = wt[i % 2]
                P.dma("sp", w.ap, ada_w[l, :, fb * 512:(fb + 1) * 512].rearrange("(c p) f -> p c f", p=128),
                      reads=[d_ro], writes=[w.b], owner=w.b)
                ps = pb[i % 2]

                def mm(t, w=w, ps=ps):
                    for c in range(16):
                        ins = t.matmul(ps.ap[0:2, :], lhsT=sT.ap[:, c, :], rhs=w.ap[:, c, :], start=(c == 0), stop=(c == 15))
                    return ins
                P.op("pe", mm, reads=[sT.b, w.b], writes=[ps.b])
                P.op("dve", lambda v, ps=ps, fb=fb: v.tensor_tensor(out=msb.ap[:, fb * 512:(fb + 1) * 512], in0=ps.ap[0:2, :],
                                                                    in1=adab.ap[:, fb * 512:(fb + 1) * 512], op=ALU.add),
                     reads=[ps.b, adab.b], writes=[msb.b])
                i += 1
            P.dma("sp", MOD[l], msb.ap, reads=[msb.b], writes=[d_mod], owner=msb.b)

    def layer_cfg(l):
        kind = KINDS[l]
        j = l // 3
        if kind == 0:
            return dict(kind=0, j=j, nq=16, nk=4, vcols=512, F=5120, knew=nak, vnew=nav, kcols=512)
        if kind == 1:
            return dict(kind=1, j=j, nq=16, nk=16, vcols=2048, F=8192, knew=nbk, vnew=nbv, kcols=2048)
        return dict(kind=2, j=j, nq=16, nk=16, vcols=2048, F=8192, knew=nck, vnew=ncv, kcols=2048)

    def phase1(l):
        cfg = layer_cfg(l)
        nq, nk, vcols, F = cfg["nq"], cfg["nk"], cfg["vcols"], cfg["F"]
        xin, dxin = x_aps[l], d_x[l]
        ar.reset()
        mod1 = mk(ar, [D], F32, "mod1")
        sh = mk(ar, [D], F32, "sh")
        xt = [mk(ar, [D], F32, "xt%d" % i) for i in range(2)]
        tmpf = mk(ar, [D], F32, "tmpf")
        lng = tmpf
        hb = [mk(ar, [D], BF16, "hb%d" % i) for i in range(2)]
        hT = mk(ar, [16, TT], BF16, "hT")
        wt = [mk(ar, [16, 512], BF16, "wt%d" % i) for i in range(3)]
        ssx = [mk(ar, [1], F32, "ssx%d" % i) for i in range(2)]
        rsx = [mk(ar, [1], F32, "rsx%d" % i) for i in range(2)]
        junk = mk(ar, [D], BF16, "junk")
        ss4 = [mk(ar, [4], F32, "ss4%d" % i) for i in range(4)]
        rs4 = [mk(ar, [4], F32, "rs4%d" % i) for i in range(4)]
        yq = [mk(ar, [4, HD], F32, "yq%d" % i) for i in range(3)]
        rt = [mk(ar, [4, 2, 32], F32, "rt%d" % i) for i in range(8)]
        ob = [mk(ar, [4, HD], BF16, "ob%d" % i) for i in range(4)]
        qTs = [mk(ar, [4, TT], BF16, "qTs%d" % i) for i in range(2)]
        vst = [mk(ar, [512], BF16, "vst%d" % i) for i in range(2)]
        vsf = [mk(ar, [512], F32, "vsf%d" % i) for i in range(2)]
        gst = [mk(ar, [512], BF16, "gst%d" % i) for i in range(2)]
        hps = pb[0:2]
        mps = pb[2:5]
        tps = pb[5:7]

        P.dma("sp", gq.ap, dbc(qn_g[l:l + 1, :], 128), reads=[d_ro], writes=[gq.b], owner=gq.b)
        P.dma("sp", gk.ap, dbc(kn_g[l:l + 1, :], 128), reads=[d_ro], writes=[gk.b], owner=gk.b)

        cnt = dict(x=0, w=0, m=0, q=0, v=0, g=0, t=0, qs=0)

        def load_mod(r):
            P.dma("sp", lng.ap, dbc(ln_g[l:l + 1, :], 128), reads=[d_ro], writes=[lng.b], owner=lng.b)
            P.dma("sp", sh.ap, dbc(MOD[l, r:r + 1, 0:D], 128), reads=[d_mod], writes=[sh.b], owner=sh.b)
            P.dma("sp", mod1.ap, dbc(MOD[l, r:r + 1, D:2 * D], 128), reads=[d_mod], writes=[mod1.b],
                  owner=mod1.b)
            P.op("dve", lambda v: v.scalar_tensor_tensor(out=mod1.ap, in0=mod1.ap, scalar=1.0, in1=lng.ap, op0=ALU.add,
                                                         op1=ALU.mult), reads=[mod1.b, lng.b], writes=[mod1.b])

        def load_w(fb):
            w = wt[cnt["w"] % 3]
            cnt["w"] += 1
            P.dma("sp", w.ap, WIB[l][:, fb * 512:(fb + 1) * 512].rearrange("(c p) f -> p c f", p=128),
                  reads=[d_wib[l]], writes=[w.b], owner=w.b)
            return w

        def norm_block(tt, tb):
            t0 = tt * TT + tb * 128
            x = xt[cnt["x"] % 2]
            h = hb[cnt["x"] % 2]
            s1 = ssx[cnt["x"] % 2]
            r1 = rsx[cnt["x"] % 2]
            cnt["x"] += 1
            P.dma("sp", x.ap, xin[t0:t0 + 128, :], reads=[dxin], writes=[x.b], owner=x.b)
            P.op("act", lambda a: a.activation(out=junk.ap, in_=x.ap, func=AF.Square, accum_out=s1.ap[:, 0:1]),
                 reads=[x.b], writes=[s1.b])
            P.op("act", lambda a: a.activation(out=s1.ap, in_=s1.ap, func=AF.Sqrt, scale=1.0 / D, bias=EPS),
                 reads=[s1.b], writes=[s1.b])
            P.op("dve", lambda v: v.reciprocal(out=r1.ap, in_=s1.ap), reads=[s1.b], writes=[r1.b])
            P.op("dve", lambda v: v.scalar_tensor_tensor(out=tmpf.ap, in0=x.ap, scalar=r1.ap[:, 0:1], in1=mod1.ap,
                                                         op0=ALU.mult, op1=ALU.mult), reads=[x.b, r1.b, mod1.b],
                 writes=[tmpf.b])
            P.op("pool", lambda g: g.tensor_tensor(out=h.ap, in0=tmpf.ap, in1=sh.ap, op=ALU.add), reads=[tmpf.b, sh.b],
                 writes=[h.b])
            hv = [hps[i].ap.bitcast(BF16).rearrange("p (c t) -> p c t", t=128) for i in range(2)]

            def tr(t):
                for c in range(16):
                    ins = t.transpose(hv[c // 8][:, c % 8, :], h.ap[:, c * 128:(c + 1) * 128], ident.ap)
                return ins
            P.op("pe", tr, reads=[h.b, ident.b], writes=[hps[0].b, hps[1].b])
            P.op("act", lambda a: a.activation(out=hT.ap[:, 0:8, tb * 128:(tb + 1) * 128], in_=hv[0], func=AF.Copy),
                 reads=[hps[0].b], writes=[hT.b])
            P.op("dve", lambda v: v.tensor_copy(out=hT.ap[:, 8:16, tb * 128:(tb + 1) * 128], in_=hv[1]),
                 reads=[hps[1].b, hT.b], writes=[hT.b])

        def mm_tok(w, tb):
            ps = mps[cnt["m"] % 3]
            cnt["m"] += 1

            def mm(t):
                for c in range(16):
                    ins = t.matmul(ps.ap, lhsT=hT.ap[:, c, tb * 128:(tb + 1) * 128], rhs=w.ap[:, c, :],
                                   start=(c == 0), stop=(c == 15))
                return ins
            P.op("pe", mm, reads=[hT.b, w.b], writes=[ps.b])
            return ps

        def qk_post(ps, tt, tb, is_k, u0, stage, gain):
            i2 = cnt["q"]
            cnt["q"] += 1
            s4, r4, y, o = ss4[i2 % 4], rs4[i2 % 4], yq[i2 % 3], ob[i2 % 4]
            psv = ps.ap.rearrange("p (u d) -> p u d", d=HD)
            for u in range(4):
                P.op("act", lambda a, u=u: a.activation(out=junk.ap[:, 0:HD], in_=psv[:, u, :], func=AF.Square,
                                                        accum_out=s4.ap[:, u:u + 1]), reads=[ps.b], writes=[s4.b])
            P.op("act", lambda a: a.activation(out=s4.ap, in_=s4.ap, func=AF.Sqrt, scale=1.0 / HD, bias=EPS),
                 reads=[s4.b], writes=[s4.b])
            P.op("dve", lambda v: v.reciprocal(out=r4.ap, in_=s4.ap), reads=[s4.b], writes=[r4.b])
            P.op("dve", lambda v: v.tensor_tensor(out=y.ap, in0=psv, in1=bc_mid(gain.ap, 4), op=ALU.mult),
                 reads=[ps.b, gain.b], writes=[y.b])
            P.op("pool", lambda g: g.tensor_tensor(out=y.ap, in0=y.ap, in1=bc_last(r4.ap, HD), op=ALU.mult),
                 reads=[y.b, r4.b], writes=[y.b])
            is_prompt = (tt == 4)
            if is_prompt or cfg["kind"] == 2:
                if is_k and is_prompt:
                    seq = tb // 2
                    r0 = (tb % 2) * 128
                    P.dma("sp", cfg["knew"][seq, cfg["j"], r0:r0 + 128, u0 * HD:(u0 + 4) * HD],
                          y.ap.rearrange("p u d -> p (u d)"), reads=[y.b], writes=[d_outs], owner=y.b)
                P.op("dve", lambda v: v.tensor_copy(out=o.ap, in_=y.ap), reads=[y.b], writes=[o.b])
            else:
                blk = tt * 8 + tb
                yv = y.ap.rearrange("p u (a h f) -> p u a h f", a=2, h=2)
                ov = o.ap.rearrange("p u (a h f) -> p u a h f", a=2, h=2)
                cs = cosT.ap[:, blk, :, :].unsqueeze(1).broadcast_to([128, 4, 2, 32])
                sn = sinT.ap[:, blk, :, :].unsqueeze(1).broadcast_to([128, 4, 2, 32])
                x1 = yv[:, :, :, 0, :]
                x2 = yv[:, :, :, 1, :]
                t1, t2, t3, t4 = [rt[(i2 % 2) * 4 + k] for k in range(4)]
                P.op("dve", lambda v: v.tensor_tensor(out=t1.ap, in0=x1, in1=cs, op=ALU.mult), reads=[y.b, cosT.b], writes=[t1.b])
                P.op("dve", lambda v: v.tensor_tensor(out=t2.ap, in0=x2, in1=sn, op=ALU.mult), reads=[y.b, sinT.b], writes=[t2.b])
                P.op("dve", lambda v: v.tensor_tensor(out=ov[:, :, :, 0, :], in0=t1.ap, in1=t2.ap, op=ALU.subtract),
                     reads=[t1.b, t2.b], writes=[o.b])
                P.op("pool", lambda g: g.tensor_tensor(out=t3.ap, in0=x2, in1=cs, op=ALU.mult), reads=[y.b, cosT.b], writes=[t3.b])
                P.op("pool", lambda g: g.tensor_tensor(out=t4.ap, in0=x1, in1=sn, op=ALU.mult), reads=[y.b, sinT.b], writes=[t4.b])
                P.op("pool", lambda g: g.tensor_tensor(out=ov[:, :, :, 1, :], in0=t3.ap, in1=t4.ap, op=ALU.add),
                     reads=[t3.b, t4.b, o.b], writes=[o.b])
            def part_b():
                tp = tps[cnt["t"] % 2]
                cnt["t"] += 1
                tpv = tp.ap.bitcast(BF16)[:, 0:512].rearrange("p (u t) -> p u t", t=128)

                def tr(t):
                    for u in range(4):
                        ins = t.transpose(tpv[:, u, :], o.ap[:, u, :], ident.ap)
                    return ins
                P.op("pe", tr, reads=[o.b, ident.b], writes=[tp.b])
                P.op("act", lambda a: a.activation(out=stage.ap[:, :, tb * 128:(tb + 1) * 128], in_=tpv, func=AF.Copy),
                     reads=[tp.b, stage.b], writes=[stage.b])
            return part_b

        def v_post(ps, tt, tb, c0):
            v = vst[cnt["v"] % 2]
            t0 = tt * TT + tb * 128
            P.op("act", lambda a: a.activation(out=v.ap, in_=ps.ap, func=AF.Copy), reads=[ps.b], writes=[v.b])
            P.dma("pool", VS[t0:t0 + 128, c0:c0 + 512], v.ap, reads=[v.b], writes=[d_vs], owner=v.b)
            if tt == 4 and dbg != 7:
                vf = vsf[cnt["v"] % 2]
                seq = tb // 2
                r0 = (tb % 2) * 128
                P.op("dve", lambda vv: vv.tensor_copy(out=vf.ap, in_=ps.ap), reads=[ps.b, v.b], writes=[vf.b])
                P.dma("sp", cfg["vnew"][seq, cfg["j"], r0:r0 + 128, c0:c0 + 512], vf.ap, reads=[vf.b], writes=[d_outs],
                      owner=vf.b)
            cnt["v"] += 1

        def g_block(w, tt, fb_g):
            for fc in range(4):
                unit = fb_g * 4 + fc
                for th in range(2):
                    ps = mps[cnt["m"] % 3]
                    cnt["m"] += 1

                    def mm(t, fc=fc, th=th, ps=ps):
                        for c in range(16):
                            ins = t.matmul(ps.ap, lhsT=w.ap[:, c, fc * 128:(fc + 1) * 128],
                                           rhs=hT.ap[:, c, th * 512:(th + 1) * 512], start=(c == 0), stop=(c == 15))
                        return ins
                    P.op("pe", mm, reads=[hT.b, w.b], writes=[ps.b])
                    g = gst[cnt["g"] % 2]
                    cnt["g"] += 1
                    P.op("act", lambda a, ps=ps, g=g: a.activation(out=g.ap, in_=ps.ap, func=AF.Silu), reads=[ps.b],
                         writes=[g.b])
                    t0 = tt * TT + th * 512
                    P.dma("pool", GT[unit, :, t0:t0 + 512], g.ap, reads=[g.b], writes=[d_gt], owner=g.b)

        nfb = F // 512
        nqb = nq // 4
        nkb = nk // 4
        nvb = vcols // 512
        for tt in range(5):
            if dbg in (1, 2, 3, 4) and tt > 0:
                break
            if dbg == 5 and tt > 1:
                break
            if dbg in (6, 7) and tt in (1, 2, 3):
                continue
            if tt == 0:
                load_mod(0)
            if tt == 4:
                load_mod(1)
            wq = [load_w(0), load_w(1)]
            pq = []
            for tb in range(8):
                norm_block(tt, tb)
            for fb in range(nfb):
                if dbg == 1 or (dbg == 2 and fb >= nqb + nkb) or (dbg == 3 and fb >= nqb + nkb + nvb):
                    break
                w = wq.pop(0)
                if fb + 2 < nfb:
                    wq.append(load_w(fb + 2))
                if fb < nqb + nkb:
                    is_k = fb >= nqb
                    u0 = (fb - nqb) * 4 if is_k else fb * 4
                    stage = qTs[cnt["qs"] % 2]
                    cnt["qs"] += 1
                    for tb in range(8):
                        ps = mm_tok(w, tb)
                        pq.append(qk_post(ps, tt, tb, is_k, u0, stage, gk if is_k else gq))
                        if len(pq) > 3:
                            pq.pop(0)()
                    dst = (KT if is_k else QT)[u0:u0 + 4, :, tt * TT:(tt + 1) * TT].rearrange("u p t -> p u t")

                    def st(dst=dst, stage=stage, is_k=is_k):
                        P.dma("pool", dst, stage.ap, reads=[stage.b], writes=[d_kt if is_k else d_qt], owner=stage.b)
                    last_b = pq[-1]
                    pq[-1] = (lambda last_b=last_b, st=st: (last_b(), st()))
                elif fb < nqb + nkb + nvb:
                    c0 = (fb - nqb - nkb) * 512
                    for tb in range(8):
                        ps = mm_tok(w, tb)
                        v_post(ps, tt, tb, c0)
                else:
                    g_block(w, tt, fb - nqb - nkb - nvb)
                if fb == nqb + nkb and pq:
                    while pq:
                        pq.pop(0)()

    class Item:
        __slots__ = ("s_mms", "s_reads", "n", "mask", "pv", "pv_reads", "pv_writes", "first", "last", "after", "post", "E", "bgs")

    def run_attention(items, sbanks, etiles, eshape, depth=2, defer=0):
        deferred = []
        assert len(sbanks) >= depth + 1 and len(etiles) >= depth + 2
        cnt = 0
        q = []
        for it in list(items) + [None] * depth:
            if it is not None:
                psS = sbanks[cnt % len(sbanks)]
                E = etiles[cnt % len(etiles)]
                cnt += 1

                def smm(t, it=it, psS=psS):
                    for (off, n, lhsT, rhs) in it.s_mms:
                        ins = t.matmul(psS.ap[:, off:off + n], lhsT=lhsT, rhs=rhs, start=True, stop=True)
                    return ins
                P.op("pe", smm, reads=it.s_reads, writes=[psS.b])
                P.op("act", lambda a, psS=psS, E=E, n=it.n: a.activation(out=E.ap[:, 0:n], in_=psS.ap[:, 0:n], func=AF.Exp,
                                                                         scale=SCALE), reads=[psS.b], writes=[E.b])
                for f_bg in (getattr(it, "bgs", None) or ()):
                    f_bg()
                if it.mask is not None:
                    mk_ap, mk_b = it.mask[0], it.mask[1]
                    ev = E.ap[:, 0:(it.mask[2] if len(it.mask) > 2 else it.n)]
                    if len(mk_ap.shape) == 3:
                        ev = ev.rearrange("p (a b) -> p a b", b=mk_ap.shape[2])
                    P.op("pool", lambda g, ev=ev, mk_ap=mk_ap: g.tensor_tensor(out=ev, in0=ev, in1=mk_ap, op=ALU.mult),
                         reads=[E.b, mk_b], writes=[E.b])
                it.E = E
                q.append(it)
            if q and (len(q) > depth or it is None):
                p_ = q.pop(0)

                def pvmm(t, it=p_):
                    for ent in it.pv:
                        out_ap, lhsT, off, n = ent[0], ent[1], ent[2], ent[3]
                        st_, sp_ = (ent[4], ent[5]) if len(ent) > 4 else (it.first, it.last)
                        ins = t.matmul(out_ap, lhsT=lhsT, rhs=it.E.ap[:, off:off + n], start=st_, stop=sp_)
                    return ins
                P.op("pe", pvmm, reads=[p_.E.b, ones.b] + p_.pv_reads, writes=p_.pv_writes)
                deferred = [(c_ - 1, f_) for (c_, f_) in deferred]
                while deferred and deferred[0][0] <= 0:
                    deferred.pop(0)[1]()
                if p_.after is not None:
                    later = p_.after()
                    if later is not None:
                        deferred.append((defer, later))
                if getattr(p_, "post", None) is not None:
                    if defer:
                        deferred.append((defer, p_.post))
                    else:
                        p_.post()
        for (c_, f_) in deferred:
            f_()

    def load_ctx(cache_k, cache_v, j, kcol0, nku, vcol0, vw, ckf, ckb, ckT, cvf, cvb, tp):
        P.dma("sp", ckf.ap, cache_k[j, :, kcol0:kcol0 + nku * HD].rearrange("(c p) f -> p c f", p=128),
              reads=[d_ro], writes=[ckf.b], owner=ckf.b)
        P.dma("sp", cvf.ap, cache_v[j, :, vcol0:vcol0 + vw].rearrange("(c p) f -> p c f", p=128),
              reads=[d_ro], writes=[cvf.b], owner=cvf.b)
        P.op("dve", lambda v: v.tensor_copy(out=ckb.ap, in_=ckf.ap), reads=[ckf.b], writes=[ckb.b])
        P.op("pool", lambda g: g.tensor_copy(out=cvb.ap, in_=cvf.ap), reads=[cvf.b], writes=[cvb.b])
        tpv = tp.ap.bitcast(BF16)[:, 0:nku * 256].rearrange("p (u t) -> p u t", t=256)

        def tr(t):
            for u in range(nku):
                for c in range(2):
                    ins = t.transpose(tpv[:, u, c * 128:(c + 1) * 128], ckb.ap[:, c, u * HD:(u + 1) * HD], ident.ap)
            return ins
        P.op("pe", tr, reads=[ckb.b, ident.b], writes=[tp.b])
        P.op("act", lambda a: a.activation(out=ckT.ap, in_=tpv, func=AF.Copy), reads=[tp.b], writes=[ckT.b])

    def phase2_A(l):
        j = l // 3
        ar.reset()
        kT = [mk(ar, [NTOK], BF16, "kT%d" % i) for i in range(2)]
        vv = [mk(ar, [40, HD], BF16, "vv%d" % i) for i in range(2)]
        ckf = mk(ar, [2, HD], F32, "ckf")
        ckb = mk(ar, [2, HD], BF16, "ckb")
        ckT = [mk(ar, [1, 256], BF16, "ckT%d" % i) for i in range(2)]
        cvf = mk(ar, [2, HD], F32, "cvf")
        cvb = [mk(ar, [2, HD], BF16, "cvb%d" % i) for i in range(2)]
        qT = [mk(ar, [4, 512], BF16, "qT%d" % i) for i in range(2)]
        gT = [mk(ar, [4, 512], BF16, "gT%d" % i) for i in range(2)]
        oT = [mk(ar, [4, 512], BF16, "oT%d" % i) for i in range(2)]
        mprev = mk(ar, [4, 128], BF16, "mprev")
        mnext = mk(ar, [4, 128], BF16, "mnext")
        sexp = mk(ar, [16], F32, "sexp")
        et = [mk(ar, [512], BF16, "et%d" % i) for i in range(4)]
        den = [mk(ar, [4, 128], F32, "den%d" % i) for i in range(2)]
        of = [mk(ar, [4, 128], F32, "of%d" % i) for i in range(2)]
        sb_, ob_, lb_, tp = pb[0:3], pb[3:5], pb[5:7], pb[7]
        P.op("pool", lambda g: g.memset(mprev.ap, 1.0), writes=[mprev.b])
        P.op("pool", lambda g: g.affine_select(out=mprev.ap, in_=mprev.ap, compare_op=ALU.is_ge, fill=0.0, base=0,
                                               pattern=[[0, 4], [-1, 128]], channel_multiplier=1), reads=[mprev.b], writes=[mprev.b])
        P.op("pool", lambda g: g.memset(mnext.ap, 1.0), writes=[mnext.b])
        P.op("pool", lambda g: g.affine_select(out=mnext.ap, in_=mnext.ap, compare_op=ALU.is_ge, fill=0.0, base=0,
                                               pattern=[[0, 4], [1, 128]], channel_multiplier=-1), reads=[mnext.b], writes=[mnext.b])
        P.dma("sp", sexp.ap, dbc(sink_a[j:j + 1, :], 128), reads=[d_ro], writes=[sexp.b], owner=sexp.b)
        P.op("act", lambda a: a.activation(out=sexp.ap, in_=sexp.ap, func=AF.Exp), reads=[sexp.b], writes=[sexp.b])
        cnt = dict(q=0, f=0)
        for kvh in range(4):
            k_, v_, ckT_, cvb_ = kT[kvh % 2], vv[kvh % 2], ckT[kvh % 2], cvb[kvh % 2]
            P.dma("sp", k_.ap, KT[kvh], reads=[d_kt], writes=[k_.b], owner=k_.b)
            P.dma("sp", v_.ap, VS[:, kvh * HD:(kvh + 1) * HD].rearrange("(c p) d -> p c d", p=128), reads=[d_vs],
                  writes=[v_.b], owner=v_.b)
            load_ctx(cak, cav, j, kvh * HD, 1, kvh * HD, HD, ckf, ckb, ckT_, cvf, cvb_, tp)
            items = []
            loaders = []
            slab_first = []
            for q512 in range(10):
                q_, g_, o_ = qT[cnt["q"] % 2], gT[cnt["q"] % 2], oT[cnt["q"] % 2]
                cnt["q"] += 1
                t0 = q512 * 512

                def ld(q_=q_, g_=g_, t0=t0, kvh=kvh):
                    P.dma("sp", q_.ap, QT[kvh * 4:(kvh + 1) * 4, :, t0:t0 + 512].rearrange("u p t -> p u t"), reads=[d_qt],
                          writes=[q_.b], owner=q_.b)
                    P.dma("sp", g_.ap, GT[kvh * 4:(kvh + 1) * 4, :, t0:t0 + 512].rearrange("u p t -> p u t"), reads=[d_gt],
                          writes=[g_.b], owner=g_.b)
                loaders.append(ld)
                slab_first.append(len(items))
                for qb in range(4):
                    B = q512 * 4 + qb
                    if B < 32:
                        ch = []
                        if B > 0:
                            ch.append(("l", B - 1, mprev))
                        ch.append(("l", B, None))
                        if B < 31:
                            ch.append(("l", B + 1, mnext))
                        ch += [("c", 0, None), ("c", 1, None)]
                        import os as _os
                        if _os.environ.get("A_NOCTX"):
                            ch = ch[:-2]
                        if _os.environ.get("A_NOMASK"):
                            ch = [(a, b, None) for (a, b, c_) in ch]
                    else:
                        sq = (B - 32) // 2
                        ch = [("l", 32 + 2 * sq, None), ("l", 32 + 2 * sq + 1, None)]
                    fi = cnt["f"] % 2
                    cnt["f"] += 1
                    psO, psL = ob_[fi], lb_[fi]
                    rhs = q_.ap[:, :, qb * 128:(qb + 1) * 128]
                    for ci, (typ, c, msk) in enumerate(ch):
                        it = Item()
                        if typ == "l":
                            lk, kb = k_.ap[:, c * 128:(c + 1) * 128], k_.b
                            lv, vb = v_.ap[:, c, :], v_.b
                        else:
                            lk, kb = ckT_.ap[:, 0, c * 128:(c + 1) * 128], ckT_.b
                            lv, vb = cvb_.ap[:, c, :], cvb_.b
                        it.s_mms = [(0, 512, lk, rhs)]
                        it.s_reads = [kb, q_.b]
                        it.n = 512
                        it.mask = (msk.ap, msk.b) if msk is not None else None
                        it.pv = [(psO.ap, lv, 0, 512), (psL.ap, ones.ap, 0, 512)]
                        it.pv_reads = [vb]
                        it.pv_writes = [psO.b, psL.b]
                        it.first = (ci == 0)
                        it.last = (ci == len(ch) - 1)
                        it.after = None
                        it.post = None
                        it.bgs = None
                        if ci == len(ch) - 1:
                            def fin(psO=psO, psL=psL, fi=fi, kvh=kvh, g_=g_, o_=o_, qb=qb):
                                d_, f_ = den[fi], of[fi]
                                lv3 = psL.ap.rearrange("p (a b) -> p a b", b=128)
                                ov3 = psO.ap.rearrange("p (a b) -> p a b", b=128)
                                P.op("dve", lambda v: v.tensor_tensor(out=d_.ap, in0=lv3, in1=bc_last(sexp.ap[:, kvh * 4:(kvh + 1) * 4], 128),
                                                                      op=ALU.add), reads=[psL.b, sexp.b], writes=[d_.b])
                                P.op("dve", lambda v: v.reciprocal(out=d_.ap, in_=d_.ap), reads=[d_.b], writes=[d_.b])
                                P.op("dve", lambda v: v.tensor_tensor(out=f_.ap, in0=ov3, in1=d_.ap, op=ALU.mult),
                                     reads=[psO.b, d_.b], writes=[f_.b])
                                P.op("pool", lambda g: g.tensor_tensor(out=o_.ap[:, :, qb * 128:(qb + 1) * 128], in0=f_.ap,
                                                                       in1=g_.ap[:, :, qb * 128:(qb + 1) * 128], op=ALU.mult),
                                     reads=[f_.b, g_.b, o_.b], writes=[o_.b])
                                if qb == 3:
                                    pass
                            it.after = fin
                        items.append(it)
                    if qb == 3:
                        last = items[-1]
                        prev_after = last.after

                        def fin2(prev_after=prev_after, o_=o_, kvh=kvh, t0=t0):
                            prev_after()
                            P.dma("pool", OT[kvh * 4:(kvh + 1) * 4, :, t0:t0 + 512].rearrange("u p t -> p u t"), o_.ap,
                                  reads=[o_.b], writes=[d_ot], owner=o_.b)
                        last.after = fin2
            loaders[0]()
            for si in range(len(loaders) - 1):
                items[slab_first[si]].post = loaders[si + 1]
            run_attention(items, sb_, et, None)

    def phase2_B(l):
        j = 0
        lam_init = LAMBDA_INIT[l]
        ar.reset()
        kT = [mk(ar, [2, NTOK], BF16, "kTb%d" % i) for i in range(2)]
        vv = [mk(ar, [40, 256], BF16, "vvb%d" % i) for i in range(2)]
        ckf = mk(ar, [2, 256], F32, "ckfb")
        ckb = mk(ar, [2, 256], BF16, "ckbb")
        ckT = [mk(ar, [2, 256], BF16, "ckTb%d" % i) for i in range(2)]
        cvf = mk(ar, [2, 256], F32, "cvfb")
        cvb = [mk(ar, [2, 256], BF16, "cvbb%d" % i) for i in range(2)]
        qT = [mk(ar, [2, 1024], BF16, "qTb%d" % i) for i in range(2)]
        gT = [mk(ar, [2, 1024], BF16, "gTb%d" % i) for i in range(2)]
        oT = [mk(ar, [2, 1024], BF16, "oTb%d" % i) for i in range(2)]
        et = [mk(ar, [512], BF16, "et%d" % i) for i in range(4)]
        lamt = mk(ar, [512], F32, "lamt")
        ltmp = mk(ar, [2, 128], F32, "ltmp")
        ls = mk(ar, [2], F32, "ls")
        nlam = mk(ar, [1], F32, "nlam")
        sube = mk(ar, [2], F32, "sube")
        R = [mk(ar, [2, 256], F32, "Rb%d" % i) for i in range(4)]
        t1 = [mk(ar, [2, 256], F32, "t1b%d" % i) for i in range(4)]
        t2 = [mk(ar, [2, 256], F32, "t2b%d" % i) for i in range(4)]
        ob32 = [mk(ar, [2, 256], F32, "ob32%d" % i) for i in range(4)]
        sqb = [mk(ar, [2, 256], BF16, "sqb%d" % i) for i in range(4)]
        sd = [mk(ar, [256], F32, "sdb%d" % i) for i in range(4)]
        sb_, o0_, o1_, lb_, xb_, tp = pb[0:3], pb[3], pb[4], pb[5], pb[6], pb[7]
        P.dma("sp", lamt.ap, dbc(lam_b[0:1, :], 128), reads=[d_ro], writes=[lamt.b], owner=lamt.b)
        lv = lamt.ap.rearrange("p (a b) -> p a b", b=128)
        P.op("dve", lambda v: v.tensor_tensor(out=ltmp.ap[:, 0, :], in0=lv[:, 0, :], in1=lv[:, 1, :], op=ALU.mult),
             reads=[lamt.b], writes=[ltmp.b])
        P.op("dve", lambda v: v.tensor_tensor(out=ltmp.ap[:, 1, :], in0=lv[:, 2, :], in1=lv[:, 3, :], op=ALU.mult),
             reads=[lamt.b, ltmp.b], writes=[ltmp.b])
        P.op("dve", lambda v: v.tensor_reduce(out=ls.ap, in_=ltmp.ap, axis=AX.X, op=ALU.add), reads=[ltmp.b], writes=[ls.b])
        P.op("act", lambda a: a.activation(out=ls.ap, in_=ls.ap, func=AF.Exp), reads=[ls.b], writes=[ls.b])
        P.op("dve", lambda v: v.tensor_tensor(out=nlam.ap, in0=ls.ap[:, 1:2], in1=ls.ap[:, 0:1], op=ALU.subtract),
             reads=[ls.b], writes=[nlam.b])
        P.op("dve", lambda v: v.tensor_scalar(out=nlam.ap, in0=nlam.ap, scalar1=-lam_init, scalar2=None, op0=ALU.add),
             reads=[nlam.b], writes=[nlam.b])
        P.dma("sp", sube.ap, subln_b[0].rearrange("(c p) -> p c", p=128), reads=[d_ro], writes=[sube.b], owner=sube.b,
              allow_slow_non_contiguous=True)
        P.op("dve", lambda v: v.tensor_scalar(out=sube.ap, in0=sube.ap, scalar1=1.0 - lam_init, scalar2=None, op0=ALU.mult),
             reads=[sube.b], writes=[sube.b])
        cnt = dict(q=0, f=0)
        for h in range(8):
            k_, v_, ckT_, cvb_ = kT[h % 2], vv[h % 2], ckT[h % 2], cvb[h % 2]
            P.dma("sp", k_.ap, KT[2 * h:2 * h + 2].rearrange("u p t -> p u t"), reads=[d_kt], writes=[k_.b], owner=k_.b)
            P.dma("sp", v_.ap, VS[:, h * 256:(h + 1) * 256].rearrange("(c p) d -> p c d", p=128), reads=[d_vs],
                  writes=[v_.b], owner=v_.b)
            load_ctx(cbk, cbv, 0, 2 * h * HD, 2, h * 256, 256, ckf, ckb, ckT_, cvf, cvb_, tp)
            items = []
            loaders = []
            slab_first = []
            for q1k in range(5):
                q_, g_, o_ = qT[cnt["q"] % 2], gT[cnt["q"] % 2], oT[cnt["q"] % 2]
                cnt["q"] += 1
                t0 = q1k * 1024

                def ld(q_=q_, g_=g_, t0=t0, h=h):
                    for (dst, src, dd) in ((q_, QT, d_qt), (g_, GT, d_gt)):
                        P.dma("sp", dst.ap, src[2 * h:2 * h + 2, :, t0:t0 + 1024].rearrange("u p t -> p u t"), reads=[dd],
                              writes=[dst.b], owner=dst.b)
                loaders.append(ld)
                slab_first.append(len(items))
                for qi in range(4):
                    B = q1k * 4 + qi
                    if B < 16:
                        ch = [("l", c) for c in range(32)] + [("c", 0), ("c", 1)]
                    else:
                        ch = [("l", 32 + 2 * (B - 16)), ("l", 32 + 2 * (B - 16) + 1)]
                    fi = cnt["f"] % 4
                    cnt["f"] += 1
                    qs = slice(qi * 256, (qi + 1) * 256)
                    for ci, (typ, c) in enumerate(ch):
                        it = Item()
                        it.s_mms = []
                        for m in range(2):
                            if typ == "l":
                                lk = k_.ap[:, m, c * 128:(c + 1) * 128]
                            else:
                                lk = ckT_.ap[:, m, c * 128:(c + 1) * 128]
                            it.s_mms.append((m * 256, 256, lk, q_.ap[:, m, qs]))
                        if typ == "l":
                            kb, vb = k_.b, v_.b
                            lvs = [v_.ap[:, c, e * 128:(e + 1) * 128] for e in range(2)]
                        else:
                            kb, vb = ckT_.b, cvb_.b
                            lvs = [cvb_.ap[:, c, e * 128:(e + 1) * 128] for e in range(2)]
                        it.s_reads = [kb, q_.b]
                        it.n = 512
                        it.mask = None
                        it.pv = [(o0_.ap, lvs[0], 0, 512), (o1_.ap, lvs[1], 0, 512), (lb_.ap, ones.ap, 0, 512)]
                        it.pv_reads = [vb]
                        it.pv_writes = [o0_.b, o1_.b, lb_.b]
                        it.first = (ci == 0)
                        it.last = (ci == len(ch) - 1)
                        it.after = None
                        it.post = None
                        it.bgs = None
                        if ci == len(ch) - 1:
                            def fin(fi=fi, h=h, g_=g_, o_=o_, qs=qs, qi=qi, t0=t0):
                                R_, t1_, t2_, o32, sq_, sd_ = R[fi], t1[fi], t2[fi], ob32[fi], sqb[fi], sd[fi]
                                l3 = lb_.ap.rearrange("p (a b) -> p a b", b=256)
                                P.op("dve", lambda v: v.reciprocal(out=R_.ap, in_=l3), reads=[lb_.b], writes=[R_.b])
                                for e, ob in enumerate((o0_, o1_)):
                                    o3 = ob.ap.rearrange("p (a b) -> p a b", b=256)
                                    P.op("dve", lambda v, o3=o3, e=e: v.tensor_tensor(out=t1_.ap[:, e, :], in0=o3[:, 0, :], in1=R_.ap[:, 0, :],
                                                                                      op=ALU.mult), reads=[ob.b, R_.b], writes=[t1_.b])
                                    P.op("dve", lambda v, o3=o3, e=e: v.tensor_tensor(out=t2_.ap[:, e, :], in0=o3[:, 1, :], in1=R_.ap[:, 1, :],
                                                                                      op=ALU.mult), reads=[ob.b, R_.b], writes=[t2_.b])
                                def part2():
                                    fin_part2(R_, t1_, t2_, o32, sq_, sd_, g_, o_, qs, qi, h, t0)
                                return part2

                            def fin_part2(R_, t1_, t2_, o32, sq_, sd_, g_, o_, qs, qi, h, t0):
                                P.op("dve", lambda g: g.scalar_tensor_tensor(out=o32.ap, in0=t2_.ap, scalar=nlam.ap[:, 0:1], in1=t1_.ap,
                                                                             op0=ALU.mult, op1=ALU.add), reads=[t1_.b, t2_.b, nlam.b],
                                     writes=[o32.b])
                                P.op("pool", lambda g: g.tensor_tensor(out=sq_.ap, in0=o32.ap, in1=o32.ap, op=ALU.mult), reads=[o32.b],
                                     writes=[sq_.b])

                                def ssmm(t):
                                    for e in range(2):
                                        ins = t.matmul(xb_.ap[:, 0:256], lhsT=ones.ap, rhs=sq_.ap[:, e, :], start=(e == 0), stop=(e == 1))
                                    return ins
                                P.op("pe", ssmm, reads=[sq_.b, ones.b], writes=[xb_.b])
                                P.op("act", lambda a: a.activation(out=sd_.ap, in_=xb_.ap[:, 0:256], func=AF.Sqrt, scale=1.0 / 256, bias=EPS),
                                     reads=[xb_.b], writes=[sd_.b])
                                P.op("dve", lambda v: v.reciprocal(out=sd_.ap, in_=sd_.ap), reads=[sd_.b], writes=[sd_.b])
                                P.op("dve", lambda v: v.tensor_tensor(out=o32.ap, in0=o32.ap, in1=bc_mid(sd_.ap, 2), op=ALU.mult),
                                     reads=[o32.b, sd_.b], writes=[o32.b])
                                for e in range(2):
                                    P.op("dve", lambda g, e=e: g.scalar_tensor_tensor(out=o_.ap[:, e, qs], in0=o32.ap[:, e, :],
                                                                                       scalar=sube.ap[:, e:e + 1], in1=g_.ap[:, e, qs],
                                                                                       op0=ALU.mult, op1=ALU.mult),
                                         reads=[o32.b, sube.b, g_.b, o_.b], writes=[o_.b])
                                if qi == 3:
                                    P.dma("pool", OT[2 * h:2 * h + 2, :, t0:t0 + 1024].rearrange("u p t -> p u t"), o_.ap,
                                          reads=[o_.b], writes=[d_ot], owner=o_.b)
                            it.after = fin
                        items.append(it)
            loaders[0]()
            for si in range(len(loaders) - 1):
                items[slab_first[si]].post = loaders[si + 1]
            run_attention(items, sb_, et, None, defer=4)

    def phase2_C(l):
        ar.reset()
        kT = [mk(ar, [NTOK], BF16, "kT%d" % i) for i in range(2)]
        vv = [mk(ar, [40, HD], BF16, "vv%d" % i) for i in range(2)]
        ckf = mk(ar, [2, HD], F32, "ckf")
        ckb = mk(ar, [2, HD], BF16, "ckb")
        ckT = [mk(ar, [1, 256], BF16, "ckT%d" % i) for i in range(2)]
        cvf = mk(ar, [2, HD], F32, "cvf")
        cvb = [mk(ar, [2, HD], BF16, "cvb%d" % i) for i in range(2)]
        qT = [mk(ar, [1024], BF16, "qTc%d" % i) for i in range(2)]
        gT = [mk(ar, [1024], BF16, "gTc%d" % i) for i in range(2)]
        oT = [mk(ar, [1024], BF16, "oTc%d" % i) for i in range(2)]
        et = [mk(ar, [512], BF16, "etc%d" % i) for i in range(5)]
        cbr = mk(ar, [15, 64], F32, "cbr")
        cbm = mk(ar, [15, 64], BF16, "cbm")
        colm = mk(ar, [64], F32, "colm")
        EB = [mk(ar, [25, 128], BF16, "EB%d" % i) for i in range(2)]
        rr = [mk(ar, [128], F32, "rrc%d" % i) for i in range(2)]
        of = [mk(ar, [128], F32, "ofc%d" % i) for i in range(2)]
        sb_, ob_, lb_, tp = pb[0:3], pb[3:5], pb[5:7], pb[7]
        P.op("pool", lambda g: g.memset(colm.ap, 1.0), writes=[colm.b])
        for h0 in (0, 64):
            pr = slice(h0, h0 + 64)
            P.op("pool", lambda g, pr=pr: g.affine_select(out=colm.ap[pr, 0:8], in_=colm.ap[pr, 0:8], compare_op=ALU.is_ge, fill=0.0,
                                                          base=15, pattern=[[0, 8]], channel_multiplier=-1), reads=[colm.b], writes=[colm.b])
            P.op("pool", lambda g, pr=pr: g.affine_select(out=colm.ap[pr, 8:57], in_=colm.ap[pr, 8:57], compare_op=ALU.is_ge, fill=0.0,
                                                          base=0, pattern=[[-1, 49]], channel_multiplier=1), reads=[colm.b], writes=[colm.b])
            P.op("pool", lambda g, pr=pr: g.affine_select(out=colm.ap[pr, 8:57], in_=colm.ap[pr, 8:57], compare_op=ALU.is_ge, fill=0.0,
                                                          base=15, pattern=[[1, 49]], channel_multiplier=-1), reads=[colm.b], writes=[colm.b])
            P.op("pool", lambda g, pr=pr: g.affine_select(out=colm.ap[pr, 57:64], in_=colm.ap[pr, 57:64], compare_op=ALU.is_ge, fill=0.0,
                                                          base=-48, pattern=[[0, 7]], channel_multiplier=1), reads=[colm.b], writes=[colm.b])

        def rs_(r):
            return min(max(r - 4, 0), 56)

        def qblock_chunks(jb):
            lo = rs_(2 * jb) // 2
            hi = (rs_(2 * jb + 1) + 7) // 2
            return list(range(lo, hi + 1))

        classes = {0: 0, 1: 1, 30: 3, 31: 4}

        def cls_of(jb):
            return classes.get(jb, 2)

        rep = {0: 0, 1: 1, 2: 10, 3: 30, 4: 31}

        def build_EB_ops(EB_, h):
            ops = []
            for h0 in (0, 64):
                ops.append(lambda h0=h0: P.dma("sp", cbr.ap[h0:h0 + 64], rpbt[h], reads=[d_ro], writes=[cbr.b], owner=cbr.b))
            ops.append(lambda: P.op("act", lambda a: a.activation(out=cbr.ap, in_=cbr.ap, func=AF.Exp), reads=[cbr.b], writes=[cbr.b]))
            ops.append(lambda: P.op("pool", lambda g: g.tensor_tensor(out=cbm.ap, in0=cbr.ap, in1=bc_mid(colm.ap, 15), op=ALU.mult),
                                    reads=[cbr.b, colm.b], writes=[cbm.b]))
            ops.append(lambda: P.op("pool", lambda g: g.memset(EB_.ap, 0.0), writes=[EB_.b]))
            for ci in range(5):
                jb = rep[ci]
                for c in qblock_chunks(jb):
                    dlt = c - jb
                    slot = ci * 5 + (dlt + 3 if ci == 4 else (dlt if ci == 0 else dlt + 2 if ci in (2, 3) else dlt + 1))
                    for qr in range(2):
                        r = 2 * jb + qr
                        for kr in range(2):
                            ka = 2 * c + kr
                            if rs_(r) <= ka <= rs_(r) + 7:
                                i = ka - r + 7
                                ops.append(lambda kr=kr, qr=qr, slot=slot, i=i: P.op("pool", lambda g: g.tensor_copy(
                                    out=EB_.ap[kr * 64:(kr + 1) * 64, slot, qr * 64:(qr + 1) * 64],
                                    in_=cbm.ap[kr * 64:(kr + 1) * 64, i, :]), reads=[cbm.b, EB_.b], writes=[EB_.b]))
            return ops

        def slot_of(jb, c):
            ci = cls_of(jb)
            dlt = c - jb
            return ci * 5 + (dlt + 3 if ci == 4 else (dlt if ci == 0 else dlt + 2 if ci in (2, 3) else dlt + 1))

        cnt = dict(q=0, f=0)
        for h in range(16):
            k_, v_, ckT_, cvb_, EB_ = kT[h % 2], vv[h % 2], ckT[h % 2], cvb[h % 2], EB[h % 2]
            P.dma("sp", k_.ap, KT[h], reads=[d_kt], writes=[k_.b], owner=k_.b)
            P.dma("sp", v_.ap, VS[:, h * HD:(h + 1) * HD].rearrange("(c p) d -> p c d", p=128), reads=[d_vs],
                  writes=[v_.b], owner=v_.b)
            load_ctx(cck, ccv, 0, h * HD, 1, h * HD, HD, ckf, ckb, ckT_, cvf, cvb_, tp)
            if h == 0:
                for f_ in build_EB_ops(EB_, 0):
                    f_()
            items = []
            loaders = []
            slab_first = []
            for q1k in range(5):
                q_, g_, o_ = qT[cnt["q"] % 2], gT[cnt["q"] % 2], oT[cnt["q"] % 2]
                cnt["q"] += 1
                t0 = q1k * 1024

                def ld(q_=q_, g_=g_, t0=t0, h=h):
                    P.dma("sp", q_.ap, QT[h, :, t0:t0 + 1024], reads=[d_qt], writes=[q_.b], owner=q_.b)
                    P.dma("sp", g_.ap, GT[h, :, t0:t0 + 1024], reads=[d_gt], writes=[g_.b], owner=g_.b)
                loaders.append(ld)
                slab_first.append(len(items))
                for qi in range(8):
                    B = q1k * 8 + qi
                    if B < 32:
                        ch = [("l", c, slot_of(B, c)) for c in qblock_chunks(B)] + [("c", 0, None), ("c", 1, None)]
                    else:
                        sq = (B - 32) // 2
                        ch = [("l", 32 + 2 * sq, None), ("l", 32 + 2 * sq + 1, None)]
                    fi = cnt["f"] % 2
                    cnt["f"] += 1
                    psO, psL = ob_[fi], lb_[fi]
                    qs = slice(qi * 128, (qi + 1) * 128)
                    if B < 32:
                        nloc = len(ch) - 2
                        groups = [ch[0:4], ch[4:]]
                    else:
                        groups = [ch]
                    tot = len(ch)
                    done = 0
                    for gi, grp in enumerate(groups):
                        it = Item()
                        it.s_mms = []
                        it.pv = []
                        it.s_reads = [q_.b]
                        it.pv_reads = []
                        nmask = 0
                        slot0 = None
                        for j_, (typ, c, slot) in enumerate(grp):
                            if typ == "l":
                                lk, kb = k_.ap[:, c * 128:(c + 1) * 128], k_.b
                                lv_, vb = v_.ap[:, c, :], v_.b
                            else:
                                lk, kb = ckT_.ap[:, 0, c * 128:(c + 1) * 128], ckT_.b
                                lv_, vb = cvb_.ap[:, c, :], cvb_.b
                            it.s_mms.append((j_ * 128, 128, lk, q_.ap[:, qs]))
                            if kb not in it.s_reads:
                                it.s_reads.append(kb)
                            if vb not in it.pv_reads:
                                it.pv_reads.append(vb)
                            st_ = (done == 0)
                            sp_ = (done == tot - 1)
                            it.pv.append((psO.ap[:, 0:128], lv_, j_ * 128, 128, st_, sp_))
                            it.pv.append((psL.ap[:, 0:128], ones.ap, j_ * 128, 128, st_, sp_))
                            done += 1
                            if slot is not None:
                                if slot0 is None:
                                    slot0 = slot
                                nmask += 1
                        it.n = 128 * len(grp)
                        it.mask = (EB_.ap[:, slot0:slot0 + nmask, :], EB_.b, 128 * nmask) if nmask else None
                        it.pv_writes = [psO.b, psL.b]
                        it.first = (gi == 0)
                        it.last = (gi == len(groups) - 1)
                        it.after = None
                        it.post = None
                        it.bgs = None
                        if gi == len(groups) - 1:
                            def fin(psO=psO, psL=psL, fi=fi, h=h, g_=g_, o_=o_, qs=qs, qi=qi, t0=t0):
                                r_, f_ = rr[fi], of[fi]
                                P.op("dve", lambda v: v.reciprocal(out=r_.ap, in_=psL.ap[:, 0:128]), reads=[psL.b], writes=[r_.b])
                                P.op("dve", lambda v: v.tensor_tensor(out=f_.ap, in0=psO.ap[:, 0:128], in1=r_.ap, op=ALU.mult),
                                     reads=[psO.b, r_.b], writes=[f_.b])
                                P.op("pool", lambda g: g.tensor_tensor(out=o_.ap[:, qs], in0=f_.ap, in1=g_.ap[:, qs], op=ALU.mult),
                                     reads=[f_.b, g_.b, o_.b], writes=[o_.b])
                                if qi == 7:
                                    P.dma("pool", OT[h, :, t0:t0 + 1024], o_.ap, reads=[o_.b], writes=[d_ot], owner=o_.b)
                            it.after = fin
                        items.append(it)
            loaders[0]()
            for si in range(len(loaders) - 1):
                items[slab_first[si]].post = loaders[si + 1]
            if h + 1 < 16:
                bops = build_EB_ops(EB[(h + 1) % 2], h + 1)
                for bi, f_ in enumerate(bops):
                    it_ = items[min(2 + bi, len(items) - 1)]
                    if it_.bgs is None:
                        it_.bgs = []
                    it_.bgs.append(f_)
            run_attention(items, sb_, et, None)

    def phase3(l):
        xin, dxin = x_aps[l], d_x[l]
        xout, dxout = x_aps[l + 1], d_x[l + 1]
        ar.reset()
        wo = mk(ar, [16, D], BF16, "wo")
        gt = mk(ar, [D], F32, "gt")
        xt = [mk(ar, [D], F32, "xt%d" % i) for i in range(2)]
        xo = [mk(ar, [D], F32, "xo%d" % i) for i in range(2)]
        ot = [mk(ar, [16, 512], BF16, "ot%d" % i) for i in range(2)]
        P.dma("sp", wo.ap, WOB[l].rearrange("(c p) f -> p c f", p=128), reads=[d_wob[l]], writes=[wo.b], owner=wo.b)
        for tb in range(40):
            t0 = tb * 128
            if tb == 0 or tb == 32:
                r = 0 if tb == 0 else 1
                P.dma("sp", gt.ap, dbc(MOD[l, r:r + 1, 2 * D:3 * D], 128), reads=[d_mod], writes=[gt.b], owner=gt.b)
            o_ = ot[(tb // 4) % 2]
            if tb % 4 == 0:
                P.dma("sp", o_.ap, OT[:, :, t0:t0 + 512].rearrange("u p t -> p u t"), reads=[d_ot], writes=[o_.b], owner=o_.b)
            x = xt[tb % 2]
            y = xo[tb % 2]
            P.dma("sp", x.ap, xin[t0:t0 + 128, :], reads=[dxin], writes=[x.b], owner=x.b)
            tl = tb % 4
            for fb in range(4):
                ps = pb[(tb * 4 + fb) % 4]

                def mm(t, ps=ps, o_=o_, tl=tl, fb=fb):
                    for c in range(16):
                        ins = t.matmul(ps.ap, lhsT=o_.ap[:, c, tl * 128:(tl + 1) * 128], rhs=wo.ap[:, c, fb * 512:(fb + 1) * 512],
                                       start=(c == 0), stop=(c == 15))
                    return ins
                P.op("pe", mm, reads=[o_.b, wo.b], writes=[ps.b])
                fs = slice(fb * 512, (fb + 1) * 512)
                P.op("dve", lambda v, ps=ps, y=y, fs=fs: v.tensor_tensor(out=y.ap[:, fs], in0=ps.ap, in1=gt.ap[:, fs], op=ALU.mult),
                     reads=[ps.b, gt.b, y.b], writes=[y.b])
            P.op("pool", lambda g, x=x, y=y: g.tensor_tensor(out=y.ap, in0=y.ap, in1=x.ap, op=ALU.add), reads=[x.b, y.b], writes=[y.b])
            P.dma("pool", xout[t0:t0 + 128, :], y.ap, reads=[y.b], writes=[dxout], owner=y.b)

    build_consts()
    P.barrier()
    for l in range(n_layers):
        cast_weights(l)
    phase0()
    P.barrier()
    for l in range(n_layers):
        if stop_phase == (l, 0):
            break
        phase1(l)
        P.barrier()
        if stop_phase == (l, 1):
            break
        [phase2_A, phase2_B, phase2_C][KINDS[l]](l)
        P.barrier()
        if stop_phase == (l, 2):
            break
        phase3(l)
        P.barrier()
    P.barrier()

    from contextlib import ExitStack
    with ExitStack() as stack:
        P.emit(nc, stack)
    nc._n_sems = P.n_sems
    return nc


def make_in_maps(inp, n_layers=DEPTH):
    f = lambda a: np.ascontiguousarray(a, dtype=np.float32)
    xs, xp = inp["x_sample"], inp["x_prompt"]
    rpb = np.asarray(inp["rpb_c"])[0]
    kc = np.arange(64)[:, None]
    qc = np.arange(64)[None, :]
    idx = np.clip(kc - qc + 15, 0, 30)
    rpbt = f(rpb[:, :, idx].transpose(0, 2, 1, 3))
    shared = dict(
        ln_g=f(inp["ln_g"]), ada_w=f(inp["ada_w"][:n_layers]), ada_b=f(inp["ada_b"]), w_out=f(inp["w_out"][:n_layers]),
        qn_g=f(inp["qn_g"]), kn_g=f(inp["kn_g"]), w_in_a=f(inp["w_in_a"][:2 if n_layers > 3 else 1]),
        w_in_b=f(inp["w_in_b"] if n_layers > 1 else np.asarray(inp["w_in_b"])[:, :8]),
        w_in_c=f(inp["w_in_c"] if n_layers > 2 else np.asarray(inp["w_in_c"])[:, :8]), sink_a=f(inp["sink_a"]), lam_b=f(np.asarray(inp["lam_b"]).reshape(1, 512)),
        subln_b=f(inp["subln_b"]), rpbt=rpbt,
    )
    maps = []
    for i in range(8):
        m = dict(shared)
        m["x"] = f(np.concatenate([np.asarray(xs[i]), np.asarray(xp[4 * i:4 * i + 4]).reshape(1024, D)], axis=0))
        m["cpair"] = f(np.stack([np.asarray(inp["c"])[i], np.asarray(inp["c_ctx"])], axis=0))
        m["cak"] = f(np.asarray(inp["cache_a_k"])[i].reshape(2, 256, 512))
        m["cav"] = f(np.asarray(inp["cache_a_v"])[i].reshape(2, 256, 512))
        m["cbk"] = f(np.asarray(inp["cache_b_k"])[i].reshape(1, 256, 2048))
        m["cbv"] = f(np.asarray(inp["cache_b_v"])[i].reshape(1, 256, 2048))
        m["cck"] = f(np.asarray(inp["cache_c_k"])[i].reshape(1, 256, 2048))
        m["ccv"] = f(np.asarray(inp["cache_c_v"])[i].reshape(1, 256, 2048))
        maps.append(m)
    return maps


def assemble(results):
    y = np.stack([r["y"] for r in results], axis=0)
    y_sample = np.ascontiguousarray(y[:, :NS, :])
    y_prompt = np.ascontiguousarray(y[:, NS:, :].reshape(32, 256, D))
    cat = lambda k: np.concatenate([r[k] for r in results], axis=0)
    return (y_prompt, y_sample,
            cat("nak").reshape(32, 2, 256, 4, 128), cat("nav").reshape(32, 2, 256, 4, 128),
            cat("nbk").reshape(32, 1, 256, 8, 2, 128), cat("nbv").reshape(32, 1, 256, 8, 256),
            cat("nck").reshape(32, 1, 256, 16, 128), cat("ncv").reshape(32, 1, 256, 16, 128))


def kernel(**inputs):
    nc = build()
    in_maps = make_in_maps(inputs)
    res = run_bass_kernel_spmd(nc, in_maps, core_ids=list(range(8)))
    return assemble(res.results)
```

```python
import math
import numpy as np
import concourse.bass as bass
import concourse.mybir as mybir
from concourse.bass_utils import run_bass_kernel_spmd

F32 = mybir.dt.float32
BF16 = mybir.dt.bfloat16
I32 = mybir.dt.int32
AF = mybir.ActivationFunctionType
ALU = mybir.AluOpType
AX = mybir.AxisListType

D = 2048
NTOK = 5120
NS = 4096
TT = 1024
HD = 128
SCALE = HD ** -0.5
EPS = 1e-6
DEPTH = 4
KINDS = [0, 1, 2, 0]
FIN = [5120, 8192, 8192, 5120]
LAMBDA_INIT = [0.8 - 0.6 * math.exp(-0.3 * l) for l in range(DEPTH)]


class Buf:
    __slots__ = ("name", "w", "r", "excl")

    def __init__(self, name):
        self.name = name
        self.w = None
        self.r = {}
        self.excl = False


class DBuf:
    __slots__ = ("name", "writers", "readers", "prev_readers")

    def __init__(self, name):
        self.name = name
        self.writers = {}
        self.readers = {}
        self.prev_readers = {}


class Op:
    __slots__ = ("eng", "fn", "deps", "needs_inc", "dma")

    def __init__(self, eng, fn, deps, dma=None):
        self.eng = eng
        self.fn = fn
        self.deps = deps
        self.needs_inc = False
        self.dma = dma


ENGS = ("pe", "act", "dve", "pool", "sp")


def _add(dd, tok):
    key = (tok[0], tok[1])
    if dd.get(key, -1) < tok[2]:
        dd[key] = tok[2]


class Prog:
    def __init__(self):
        self.ops = {e: [] for e in ENGS}
        self.names = {}
        self.dpool = []
        self.kind_idxs = {'hw': [], 'sw': []}
        self.used = {'hw': 0, 'sw': 0}

    def _deps_for(self, reads, writes):
        deps = {}
        for b in reads:
            if isinstance(b, DBuf):
                for k, v in b.writers.items():
                    _add(deps, (k[0], k[1], v))
            else:
                if b.w is not None:
                    _add(deps, b.w)
                if b.excl:
                    for k, v in b.r.items():
                        _add(deps, (k[0], k[1], v))
        for b in writes:
            if isinstance(b, DBuf):
                if b.readers:
                    b.prev_readers = b.readers
                    b.readers = {}
                    b.writers = {}
                for k, v in b.prev_readers.items():
                    _add(deps, (k[0], k[1], v))
            else:
                if b.w is not None:
                    _add(deps, b.w)
                for k, v in b.r.items():
                    _add(deps, (k[0], k[1], v))
        return deps

    def _mark(self, tok, reads, writes):
        for b in reads:
            if isinstance(b, DBuf):
                _add(b.readers, tok)
            else:
                _add(b.r, tok)
        for b in writes:
            if isinstance(b, DBuf):
                _add(b.writers, tok)
            else:
                b.w = tok
                b.r = {}

    def op(self, eng, fn, reads=(), writes=()):
        deps = self._deps_for(reads, writes)
        if eng == "pe":
            deps.pop(("e", "pe"), None)
        lst = self.ops[eng]
        tok = ("e", eng, len(lst))
        lst.append(Op(eng, fn, deps))
        self._mark(tok, reads, writes)
        return tok

    def dma(self, eng, out, in_, reads, writes, owner, **kw):
        deps = self._deps_for(reads, writes)
        kind = "sw" if eng == "pool" else "hw"
        idx = self.names.get((owner.name, kind))
        if idx is None:
            k = self.used[kind]
            if k < len(self.kind_idxs[kind]):
                idx = self.kind_idxs[kind][k]
            else:
                idx = len(self.dpool)
                self.dpool.append(0)
                self.kind_idxs[kind].append(idx)
            self.used[kind] += 1
            self.names[(owner.name, kind)] = idx
        self.dpool[idx] += 16
        ent = (idx, self.dpool[idx])
        tok = ("d", ent[0], ent[1])

        def fn(e, out=out, in_=in_, kw=kw):
            return e.dma_start(out=out, in_=in_, **kw)

        self.ops[eng].append(Op(eng, fn, deps, dma=ent[0]))
        self._mark(tok, reads, writes)
        return tok

    def barrier(self):
        deps = {}
        for e in ENGS:
            for i in range(len(self.ops[e]) - 1, -1, -1):
                o = self.ops[e][i]
                if o.dma is None and o.fn is not None:
                    deps[("e", e)] = i
                    break
        for idx, cnt in enumerate(self.dpool):
            if cnt:
                deps[("d", idx)] = cnt
        for e in ENGS:
            self.ops[e].append(Op(e, None, dict(deps)))
        self.names = {}
        self.used = {'hw': 0, 'sw': 0}

    def emit(self, nc, stack):
        for e in ENGS:
            for o in self.ops[e]:
                for k, v in o.deps.items():
                    if k[0] == "e":
                        self.ops[k[1]][v].needs_inc = True
        inc_count = {}
        for e in ENGS:
            c = 0
            arr = []
            for o in self.ops[e]:
                if o.needs_inc:
                    c += 1
                arr.append(c)
            inc_count[e] = arr
        esem = {e: stack.enter_context(nc.semaphore("es_" + e)) for e in ENGS if e != "sp"}
        dsem = [stack.enter_context(nc.semaphore("ds%d" % i)) for i in range(len(self.dpool))]
        self.n_sems = len(esem) + len(dsem)
        block = stack.enter_context(nc.Block())
        self.stats = {e: [0, 0] for e in ENGS}

        def run(e, eng):
            known = {}
            st = self.stats[e]
            for o in self.ops[e]:
                for k, v in o.deps.items():
                    if k[0] == "e":
                        val = inc_count[k[1]][v]
                        sem = esem[k[1]]
                    else:
                        val = v
                        sem = dsem[k[1]]
                    if known.get(k, 0) >= val:
                        continue
                    known[k] = val
                    eng.wait_ge(sem, val)
                    st[1] += 1
                if o.fn is None:
                    continue
                ins = o.fn(eng)
                st[0] += 1
                if o.dma is not None:
                    ins.then_inc(dsem[o.dma], 16)
                elif o.needs_inc:
                    ins.then_inc(esem[e], 1)

        @block.tensor
        def _(t):
            run("pe", t)

        @block.scalar
        def _(a):
            run("act", a)

        @block.vector
        def _(v):
            run("dve", v)

        @block.gpsimd
        def _(g):
            run("pool", g)

        @block.sync
        def _(s):
            run("sp", s)


class Arena:
    def __init__(self, nc, name, nbytes):
        self.t = nc.alloc_sbuf_tensor(name, [128, nbytes // 4], F32)
        self.cap = nbytes
        self.off = 0

    def reset(self):
        self.off = 0

    def alloc(self, free_shape, dtype):
        es = 2 if dtype == BF16 else 4
        n = 1
        for s in free_shape:
            n *= s
        nb = (n * es + 31) // 32 * 32
        assert self.off + nb <= self.cap, ("arena overflow", self.off, nb, self.cap)
        v = self.t[:, self.off // 4:(self.off + nb) // 4]
        self.off += nb
        if dtype != F32:
            v = v.bitcast(dtype)
        v = v[:, 0:n]
        if len(free_shape) == 2:
            v = v.rearrange("p (a b) -> p a b", b=free_shape[1])
        elif len(free_shape) == 3:
            v = v.rearrange("p (a b c) -> p a b c", b=free_shape[1], c=free_shape[2])
        elif len(free_shape) == 4:
            v = v.rearrange("p (a b c d) -> p a b c d", b=free_shape[1], c=free_shape[2], d=free_shape[3])
        return v


class T:
    __slots__ = ("ap", "b")

    def __init__(self, ap, name):
        self.ap = ap
        self.b = Buf(name)


def dbc(row, nparts):
    n = row.shape[-1]
    return bass.AP(tensor=row.tensor, offset=row.offset, ap=[[0, nparts], [1, n]])


def bc_last(ap2d, n):
    return ap2d.unsqueeze(2).broadcast_to([ap2d.shape[0], ap2d.shape[1], n])


def bc_mid(ap2d, n):
    return ap2d.unsqueeze(1).broadcast_to([ap2d.shape[0], n, ap2d.shape[1]])


def build(n_layers=DEPTH, stop_phase=None, dbg=0):
    nc = bass.Bass("TRN2", target_bir_lowering=False)
    P = Prog()

    def din(name, shape):
        return nc.dram_tensor(name, list(shape), F32, kind="ExternalInput").ap()

    def dout(name, shape):
        return nc.dram_tensor(name, list(shape), F32, kind="ExternalOutput").ap()

    x_in = din("x", [NTOK, D])
    cpair = din("cpair", [2, D])
    ln_g = din("ln_g", [DEPTH, D])
    ada_w = din("ada_w", [n_layers, D, 3 * D])
    ada_b = din("ada_b", [DEPTH, 3 * D])
    w_out = din("w_out", [n_layers, D, D])
    qn_g = din("qn_g", [DEPTH, HD])
    kn_g = din("kn_g", [DEPTH, HD])
    w_in_a = din("w_in_a", [2 if n_layers > 3 else 1, D, 5120])
    w_in_b = din("w_in_b", [1, D, 8192] if n_layers > 1 else [1, 8, 8192])
    w_in_c = din("w_in_c", [1, D, 8192] if n_layers > 2 else [1, 8, 8192])
    sink_a = din("sink_a", [2, 16])
    lam_b = din("lam_b", [1, 4 * HD])
    subln_b = din("subln_b", [1, 256])
    rpbt = din("rpbt", [16, 64, 15, 64])
    cak = din("cak", [2, 256, 4 * HD])
    cav = din("cav", [2, 256, 4 * HD])
    cbk = din("cbk", [1, 256, 16 * HD])
    cbv = din("cbv", [1, 256, 8 * 256])
    cck = din("cck", [1, 256, 16 * HD])
    ccv = din("ccv", [1, 256, 16 * HD])
    y_out = dout("y", [NTOK, D])
    nak = dout("nak", [4, 2, 256, 4 * HD])
    nav = dout("nav", [4, 2, 256, 4 * HD])
    nbk = dout("nbk", [4, 1, 256, 16 * HD])
    nbv = dout("nbv", [4, 1, 256, 8 * 256])
    nck = dout("nck", [4, 1, 256, 16 * HD])
    ncv = dout("ncv", [4, 1, 256, 16 * HD])
    XA = nc.dram_tensor("XA", [NTOK, D], F32).ap()
    XB = nc.dram_tensor("XB", [NTOK, D], F32).ap()
    MOD = nc.dram_tensor("MOD", [DEPTH, 2, 3 * D], F32).ap()
    QT = nc.dram_tensor("QT", [16, 128, NTOK], BF16).ap()
    KT = nc.dram_tensor("KT", [16, 128, NTOK], BF16).ap()
    VS = nc.dram_tensor("VS", [NTOK, D], BF16).ap()
    GT = nc.dram_tensor("GT", [16, 128, NTOK], BF16).ap()
    OT = nc.dram_tensor("OT", [16, 128, NTOK], BF16).ap()
    WIB = [nc.dram_tensor("WIB%d" % l, [D, FIN[l]], BF16).ap() for l in range(DEPTH)]
    WOB = [nc.dram_tensor("WOB%d" % l, [D, D], BF16).ap() for l in range(DEPTH)]
    w_in_src = [w_in_a[0], w_in_b[0], w_in_c[0], w_in_a[1 if n_layers > 3 else 0]]

    d_x = [DBuf("x_in"), DBuf("XA"), DBuf("XB"), DBuf("XA"), DBuf("y")]
    d_x[3] = d_x[1]
    x_aps = [x_in, XA, XB, XA, y_out]
    d_mod = DBuf("MOD")
    d_qt, d_kt, d_vs, d_gt, d_ot = DBuf("QT"), DBuf("KT"), DBuf("VS"), DBuf("GT"), DBuf("OT")
    d_wib = [DBuf("WIB%d" % l) for l in range(DEPTH)]
    d_wob = [DBuf("WOB%d" % l) for l in range(DEPTH)]
    d_outs = DBuf("outs")
    d_ro = DBuf("ro")

    ar = Arena(nc, "arena", 180 * 1024)
    car = Arena(nc, "consts", 20 * 1024)
    banks = [nc.alloc_psum_tensor("bank%d" % i, [128, 512], F32) for i in range(8)]
    pb = [T(banks[i][:], "bank%d" % i) for i in range(8)]
    for t_ in pb:
        t_.b.excl = True

    def mk(arena, free_shape, dtype, name):
        t = T(arena.alloc(free_shape, dtype), name)
        return t

    ident = mk(car, [128], BF16, "ident")
    ones = mk(car, [128], BF16, "ones")
    cosT = mk(car, [32, 2, 32], F32, "cosT")
    sinT = mk(car, [32, 2, 32], F32, "sinT")
    gq = mk(car, [HD], F32, "gq")
    gk = mk(car, [HD], F32, "gk")

    def build_consts():
        P.op("pool", lambda g: g.memset(ident.ap, 0.0), writes=[ident.b])
        P.op("pool", lambda g: g.affine_select(out=ident.ap, in_=ident.ap, compare_op=ALU.not_equal, fill=1.0,
                                               base=0, pattern=[[-1, 128]], channel_multiplier=1),
             reads=[ident.b], writes=[ident.b])
        P.op("pool", lambda g: g.memset(ones.ap, 1.0), writes=[ones.b])
        ar.reset()
        posr = mk(ar, [32], I32, "posr")
        posc = mk(ar, [1], I32, "posc")
        fi = mk(ar, [32], I32, "fi")
        posrf = mk(ar, [32], F32, "posrf")
        poscf = mk(ar, [1], F32, "poscf")
        ff = mk(ar, [32], F32, "ff")
        invf = mk(ar, [32], F32, "invf")
        ang = mk(ar, [32, 2, 32], F32, "ang")
        tmp = mk(ar, [32, 2, 32], F32, "angt")
        tmpi = mk(ar, [32, 2, 32], I32, "angi")
        for h0, base in ((0, 0), (64, 1)):
            P.op("pool", lambda g, h0=h0, base=base: g.iota(posr.ap[h0:h0 + 64, :], pattern=[[2, 32]], base=base,
                                                            channel_multiplier=0), writes=[posr.b])
            P.op("pool", lambda g, h0=h0: g.iota(posc.ap[h0:h0 + 64, :], pattern=[[0, 1]], base=0,
                                                 channel_multiplier=1), writes=[posc.b])
        P.op("pool", lambda g: g.iota(fi.ap, pattern=[[1, 32]], base=0, channel_multiplier=0), writes=[fi.b])
        P.op("dve", lambda v: v.tensor_copy(out=posrf.ap, in_=posr.ap), reads=[posr.b], writes=[posrf.b])
        P.op("dve", lambda v: v.tensor_copy(out=poscf.ap, in_=posc.ap), reads=[posc.b], writes=[poscf.b])
        P.op("dve", lambda v: v.tensor_copy(out=ff.ap, in_=fi.ap), reads=[fi.b], writes=[ff.b])
        P.op("act", lambda a: a.activation(out=invf.ap, in_=ff.ap, func=AF.Exp, scale=-math.log(10000.0) / 32.0),
             reads=[ff.b], writes=[invf.b])
        P.op("dve", lambda v: v.tensor_tensor(out=ang.ap[:, :, 0, :], in0=bc_last(posrf.ap, 32), in1=bc_mid(invf.ap, 32),
                                              op=ALU.mult), reads=[posrf.b, invf.b], writes=[ang.b])
        P.op("dve", lambda v: v.tensor_scalar(out=ang.ap[:, :, 1, :], in0=bc_mid(invf.ap, 32), scalar1=poscf.ap[:, 0:1],
                                              scalar2=None, op0=ALU.mult), reads=[poscf.b, invf.b, ang.b], writes=[ang.b])
        TWO_PI = 2.0 * math.pi
        for dst, shift in ((sinT, 0.0), (cosT, math.pi / 2)):
            P.op("dve", lambda v, shift=shift: v.tensor_scalar(out=tmp.ap, in0=ang.ap, scalar1=shift, scalar2=1.0 / TWO_PI,
                                                               op0=ALU.add, op1=ALU.mult), reads=[ang.b], writes=[tmp.b])
            P.op("dve", lambda v: v.tensor_copy(out=tmpi.ap, in_=tmp.ap), reads=[tmp.b], writes=[tmpi.b])
            P.op("dve", lambda v: v.tensor_copy(out=tmp.ap, in_=tmpi.ap), reads=[tmpi.b], writes=[tmp.b])
            P.op("dve", lambda v: v.scalar_tensor_tensor(out=tmp.ap, in0=tmp.ap, scalar=-TWO_PI, in1=ang.ap,
                                                         op0=ALU.mult, op1=ALU.add), reads=[tmp.b, ang.b], writes=[tmp.b])
            P.op("dve", lambda v, shift=shift: v.tensor_scalar(out=tmp.ap, in0=tmp.ap, scalar1=shift, scalar2=3.1415925,
                                                               op0=ALU.add, op1=ALU.min), reads=[tmp.b], writes=[tmp.b])
            P.op("dve", lambda v: v.tensor_scalar(out=tmp.ap, in0=tmp.ap, scalar1=-3.1415925, scalar2=None,
                                                  op0=ALU.max), reads=[tmp.b], writes=[tmp.b])
            P.op("act", lambda a, dst=dst: a.activation(out=dst.ap, in_=tmp.ap, func=AF.Sin), reads=[tmp.b], writes=[dst.b])

    castb = Buf("castsem")

    def cast_weights(l):
        src = w_in_src[l]
        F = FIN[l]
        for r0 in range(0, D, 256):
            P.dma("pool", WIB[l][r0:r0 + 256, :].rearrange("r (a b) -> r a b", b=1024),
                  src[r0:r0 + 256, :].rearrange("r (a b) -> r a b", b=1024),
                  reads=[d_ro], writes=[d_wib[l]], owner=castb)
        for r0 in range(0, D, 512):
            P.dma("pool", WOB[l][r0:r0 + 512, :].rearrange("r (a b) -> r a b", b=1024),
                  w_out[l, r0:r0 + 512, :].rearrange("r (a b) -> r a b", b=1024),
                  reads=[d_ro], writes=[d_wob[l]], owner=castb)

    def phase0():
        ar.reset()
        cT = mk(ar, [16, 2], F32, "cT")
        sT = mk(ar, [16, 2], F32, "sT")
        sg = mk(ar, [16, 2], F32, "sg")
        wt = [mk(ar, [16, 512], F32, "adaw%d" % i) for i in range(2)]
        adab = T(ar.alloc([3 * D], F32)[0:2, :], "adab")
        msb = T(ar.alloc([3 * D], F32)[0:2, :], "msb")
        for r in range(2):
            P.dma("sp", cT.ap[:, :, r], cpair[r].rearrange("(c p) -> p c", p=128), reads=[d_ro], writes=[cT.b],
                  owner=cT.b, allow_slow_non_contiguous=True)
        P.op("act", lambda a: a.activation(out=sg.ap, in_=cT.ap, func=AF.Exp, scale=-1.0), reads=[cT.b], writes=[sg.b])
        P.op("dve", lambda v: v.tensor_scalar(out=sg.ap, in0=sg.ap, scalar1=1.0, scalar2=None, op0=ALU.add),
             reads=[sg.b], writes=[sg.b])
        P.op("dve", lambda v: v.reciprocal(out=sg.ap, in_=sg.ap), reads=[sg.b], writes=[sg.b])
        P.op("dve", lambda v: v.tensor_tensor(out=sT.ap, in0=cT.ap, in1=sg.ap, op=ALU.mult), reads=[cT.b, sg.b],
             writes=[sT.b])
        i = 0
        for l in range(n_layers):
            P.dma("sp", adab.ap, dbc(ada_b[l:l + 1, :], 2), reads=[d_ro], writes=[adab.b], owner=adab.b)
            for fb in range(12):
                w = wt[i % 2]
                P.dma("sp", w.ap, ada_w[l, :, fb * 512:(fb + 1) * 512].rearrange("(c p) f -> p c f", p=128),
                      reads=[d_ro], writes=[w.b], owner=w.b)
                ps = pb[i % 2]

                def mm(t, w=w, ps=ps):
                    for c in range(16):
                        ins = t.matmul(ps.ap[0:2, :], lhsT=sT.ap[:, c, :], rhs=w.ap[:, c, :], start=(c == 0), stop=(c == 15))
                    return ins
                P.op("pe", mm, reads=[sT.b, w.b], writes=[ps.b])
                P.op("dve", lambda v, ps=ps, fb=fb: v.tensor_tensor(out=msb.ap[:, fb * 512:(fb + 1) * 512], in0=ps.ap[0:2, :],
                                                                    in1=adab.ap[:, fb * 512:(fb + 1) * 512], op=ALU.add),
                     reads=[ps.b, adab.b], writes=[msb.b])
                i += 1
            P.dma("sp", MOD[l], msb.ap, reads=[msb.b], writes=[d_mod], owner=msb.b)

    def layer_cfg(l):
        kind = KINDS[l]
        j = l // 3
        if kind == 0:
            return dict(kind=0, j=j, nq=16, nk=4, vcols=512, F=5120, knew=nak, vnew=nav, kcols=512)
        if kind == 1:
            return dict(kind=1, j=j, nq=16, nk=16, vcols=2048, F=8192, knew=nbk, vnew=nbv, kcols=2048)
        return dict(kind=2, j=j, nq=16, nk=16, vcols=2048, F=8192, knew=nck, vnew=ncv, kcols=2048)

    def phase1(l):
        cfg = layer_cfg(l)
        nq, nk, vcols, F = cfg["nq"], cfg["nk"], cfg["vcols"], cfg["F"]
        xin, dxin = x_aps[l], d_x[l]
        ar.reset()
        mod1 = mk(ar, [D], F32, "mod1")
        sh = mk(ar, [D], F32, "sh")
        xt = [mk(ar, [D], F32, "xt%d" % i) for i in range(2)]
        tmpf = mk(ar, [D], F32, "tmpf")
        lng = tmpf
        hb = [mk(ar, [D], BF16, "hb%d" % i) for i in range(2)]
        hT = mk(ar, [16, TT], BF16, "hT")
        hTb = [Buf("hTb%d" % i) for i in range(8)]
        wt = [mk(ar, [16, 512], BF16, "wt%d" % i) for i in range(3)]
        ssx = [mk(ar, [1], F32, "ssx%d" % i) for i in range(2)]
        rsx = [mk(ar, [1], F32, "rsx%d" % i) for i in range(2)]
        junk = mk(ar, [D], BF16, "junk")
        ss4 = [mk(ar, [4], F32, "ss4%d" % i) for i in range(4)]
        rs4 = [mk(ar, [4], F32, "rs4%d" % i) for i in range(4)]
        yq = [mk(ar, [4, HD], F32, "yq%d" % i) for i in range(3)]
        rt = [mk(ar, [4, 2, 32], F32, "rt%d" % i) for i in range(8)]
        ob = [mk(ar, [4, HD], BF16, "ob%d" % i) for i in range(4)]
        qTs = [mk(ar, [4, TT], BF16, "qTs%d" % i) for i in range(2)]
        vst = [mk(ar, [512], BF16, "vst%d" % i) for i in range(2)]
        vsf = [mk(ar, [512], F32, "vsf%d" % i) for i in range(2)]
        gst = [mk(ar, [512], BF16, "gst%d" % i) for i in range(2)]
        hps = pb[0:2]
        mps = pb[2:5]
        tps = pb[5:7]

        P.dma("sp", gq.ap, dbc(qn_g[l:l + 1, :], 128), reads=[d_ro], writes=[gq.b], owner=gq.b)
        P.dma("sp", gk.ap, dbc(kn_g[l:l + 1, :], 128), reads=[d_ro], writes=[gk.b], owner=gk.b)

        cnt = dict(x=0, w=0, m=0, q=0, v=0, g=0, t=0, qs=0)

        def load_mod(r):
            P.dma("sp", lng.ap, dbc(ln_g[l:l + 1, :], 128), reads=[d_ro], writes=[lng.b], owner=lng.b)
            P.dma("sp", sh.ap, dbc(MOD[l, r:r + 1, 0:D], 128), reads=[d_mod], writes=[sh.b], owner=sh.b)
            P.dma("sp", mod1.ap, dbc(MOD[l, r:r + 1, D:2 * D], 128), reads=[d_mod], writes=[mod1.b],
                  owner=mod1.b)
            P.op("dve", lambda v: v.scalar_tensor_tensor(out=mod1.ap, in0=mod1.ap, scalar=1.0, in1=lng.ap, op0=ALU.add,
                                                         op1=ALU.mult), reads=[mod1.b, lng.b], writes=[mod1.b])

        def load_w(fb):
            w = wt[cnt["w"] % 3]
            cnt["w"] += 1
            P.dma("sp", w.ap, WIB[l][:, fb * 512:(fb + 1) * 512].rearrange("(c p) f -> p c f", p=128),
                  reads=[d_wib[l]], writes=[w.b], owner=w.b)
            return w

        def norm_block(tt, tb):
            t0 = tt * TT + tb * 128
            x = xt[cnt["x"] % 2]
            h = hb[cnt["x"] % 2]
            s1 = ssx[cnt["x"] % 2]
            r1 = rsx[cnt["x"] % 2]
            cnt["x"] += 1
            P.dma("sp", x.ap, xin[t0:t0 + 128, :], reads=[dxin], writes=[x.b], owner=x.b)
            P.op("act", lambda a: a.activation(out=junk.ap, in_=x.ap, func=AF.Square, accum_out=s1.ap[:, 0:1]),
                 reads=[x.b], writes=[s1.b])
            P.op("act", lambda a: a.activation(out=s1.ap, in_=s1.ap, func=AF.Sqrt, scale=1.0 / D, bias=EPS),
                 reads=[s1.b], writes=[s1.b])
            P.op("dve", lambda v: v.reciprocal(out=r1.ap, in_=s1.ap), reads=[s1.b], writes=[r1.b])
            P.op("dve", lambda v: v.scalar_tensor_tensor(out=tmpf.ap, in0=x.ap, scalar=r1.ap[:, 0:1], in1=mod1.ap,
                                                         op0=ALU.mult, op1=ALU.mult), reads=[x.b, r1.b, mod1.b],
                 writes=[tmpf.b])
            P.op("pool", lambda g: g.tensor_tensor(out=h.ap, in0=tmpf.ap, in1=sh.ap, op=ALU.add), reads=[tmpf.b, sh.b],
                 writes=[h.b])
            hv = [hps[i].ap.bitcast(BF16).rearrange("p (c t) -> p c t", t=128) for i in range(2)]

            def part_b():
                def tr(t):
                    for c in range(16):
                        ins = t.transpose(hv[c // 8][:, c % 8, :], h.ap[:, c * 128:(c + 1) * 128], ident.ap)
                    return ins
                P.op("pe", tr, reads=[h.b, ident.b], writes=[hps[0].b, hps[1].b])
                P.op("act", lambda a: a.activation(out=hT.ap[:, 0:8, tb * 128:(tb + 1) * 128], in_=hv[0], func=AF.Copy),
                     reads=[hps[0].b], writes=[hTb[tb]])
                P.op("dve", lambda v: v.tensor_copy(out=hT.ap[:, 8:16, tb * 128:(tb + 1) * 128], in_=hv[1]),
                     reads=[hps[1].b, hTb[tb]], writes=[hTb[tb]])
            return part_b

        def mm_tok(w, tb):
            ps = mps[cnt["m"] % 3]
            cnt["m"] += 1

            def mm(t):
                for c in range(16):
                    ins = t.matmul(ps.ap, lhsT=hT.ap[:, c, tb * 128:(tb + 1) * 128], rhs=w.ap[:, c, :],
                                   start=(c == 0), stop=(c == 15))
                return ins
            P.op("pe", mm, reads=[hTb[tb], w.b], writes=[ps.b])
            return ps

        def qk_post(ps, tt, tb, is_k, u0, stage, gain):
            i2 = cnt["q"]
            cnt["q"] += 1
            s4, r4, y, o = ss4[i2 % 4], rs4[i2 % 4], yq[i2 % 3], ob[i2 % 4]
            psv = ps.ap.rearrange("p (u d) -> p u d", d=HD)
            for u in range(4):
                P.op("act", lambda a, u=u: a.activation(out=junk.ap[:, 0:HD], in_=psv[:, u, :], func=AF.Square,
                                                        accum_out=s4.ap[:, u:u + 1]), reads=[ps.b], writes=[s4.b])
            P.op("act", lambda a: a.activation(out=s4.ap, in_=s4.ap, func=AF.Sqrt, scale=1.0 / HD, bias=EPS),
                 reads=[s4.b], writes=[s4.b])
            P.op("dve", lambda v: v.reciprocal(out=r4.ap, in_=s4.ap), reads=[s4.b], writes=[r4.b])
            P.op("dve", lambda v: v.tensor_tensor(out=y.ap, in0=psv, in1=bc_mid(gain.ap, 4), op=ALU.mult),
                 reads=[ps.b, gain.b], writes=[y.b])
            P.op("pool", lambda g: g.tensor_tensor(out=y.ap, in0=y.ap, in1=bc_last(r4.ap, HD), op=ALU.mult),
                 reads=[y.b, r4.b], writes=[y.b])
            is_prompt = (tt == 4)
            if is_prompt or cfg["kind"] == 2:
                if is_k and is_prompt:
                    seq = tb // 2
                    r0 = (tb % 2) * 128
                    P.dma("sp", cfg["knew"][seq, cfg["j"], r0:r0 + 128, u0 * HD:(u0 + 4) * HD],
                          y.ap.rearrange("p u d -> p (u d)"), reads=[y.b], writes=[d_outs], owner=y.b)
                P.op("dve", lambda v: v.tensor_copy(out=o.ap, in_=y.ap), reads=[y.b], writes=[o.b])
            else:
                blk = tt * 8 + tb
                yv = y.ap.rearrange("p u (a h f) -> p u a h f", a=2, h=2)
                ov = o.ap.rearrange("p u (a h f) -> p u a h f", a=2, h=2)
                cs = cosT.ap[:, blk, :, :].unsqueeze(1).broadcast_to([128, 4, 2, 32])
                sn = sinT.ap[:, blk, :, :].unsqueeze(1).broadcast_to([128, 4, 2, 32])
                x1 = yv[:, :, :, 0, :]
                x2 = yv[:, :, :, 1, :]
                t1, t2, t3, t4 = [rt[(i2 % 2) * 4 + k] for k in range(4)]
                P.op("dve", lambda v: v.tensor_tensor(out=t1.ap, in0=x1, in1=cs, op=ALU.mult), reads=[y.b, cosT.b], writes=[t1.b])
                P.op("dve", lambda v: v.tensor_tensor(out=t2.ap, in0=x2, in1=sn, op=ALU.mult), reads=[y.b, sinT.b], writes=[t2.b])
                P.op("dve", lambda v: v.tensor_tensor(out=ov[:, :, :, 0, :], in0=t1.ap, in1=t2.ap, op=ALU.subtract),
                     reads=[t1.b, t2.b], writes=[o.b])
                P.op("pool", lambda g: g.tensor_tensor(out=t3.ap, in0=x2, in1=cs, op=ALU.mult), reads=[y.b, cosT.b], writes=[t3.b])
                P.op("pool", lambda g: g.tensor_tensor(out=t4.ap, in0=x1, in1=sn, op=ALU.mult), reads=[y.b, sinT.b], writes=[t4.b])
                P.op("pool", lambda g: g.tensor_tensor(out=ov[:, :, :, 1, :], in0=t3.ap, in1=t4.ap, op=ALU.add),
                     reads=[t3.b, t4.b, o.b], writes=[o.b])
            def part_b():
                tp = tps[cnt["t"] % 2]
                cnt["t"] += 1
                tpv = tp.ap.bitcast(BF16)[:, 0:512].rearrange("p (u t) -> p u t", t=128)

                def tr(t):
                    for u in range(4):
                        ins = t.transpose(tpv[:, u, :], o.ap[:, u, :], ident.ap)
                    return ins
                P.op("pe", tr, reads=[o.b, ident.b], writes=[tp.b])
                P.op("act", lambda a: a.activation(out=stage.ap[:, :, tb * 128:(tb + 1) * 128], in_=tpv, func=AF.Copy),
                     reads=[tp.b, stage.b], writes=[stage.b])
            return part_b

        def v_post(ps, tt, tb, c0):
            v = vst[cnt["v"] % 2]
            t0 = tt * TT + tb * 128
            P.op("act", lambda a: a.activation(out=v.ap, in_=ps.ap, func=AF.Copy), reads=[ps.b], writes=[v.b])
            P.dma("pool", VS[t0:t0 + 128, c0:c0 + 512], v.ap, reads=[v.b], writes=[d_vs], owner=v.b)
            if tt == 4 and dbg != 7:
                vf = vsf[cnt["v"] % 2]
                seq = tb // 2
                r0 = (tb % 2) * 128
                P.op("dve", lambda vv: vv.tensor_copy(out=vf.ap, in_=ps.ap), reads=[ps.b, v.b], writes=[vf.b])
                P.dma("sp", cfg["vnew"][seq, cfg["j"], r0:r0 + 128, c0:c0 + 512], vf.ap, reads=[vf.b], writes=[d_outs],
                      owner=vf.b)
            cnt["v"] += 1

        def g_block(w, tt, fb_g):
            for fc in range(4):
                unit = fb_g * 4 + fc
                for th in range(2):
                    ps = mps[cnt["m"] % 3]
                    cnt["m"] += 1

                    def mm(t, fc=fc, th=th, ps=ps):
                        for c in range(16):
                            ins = t.matmul(ps.ap, lhsT=w.ap[:, c, fc * 128:(fc + 1) * 128],
                                           rhs=hT.ap[:, c, th * 512:(th + 1) * 512], start=(c == 0), stop=(c == 15))
                        return ins
                    P.op("pe", mm, reads=hTb[th * 4:(th + 1) * 4] + [w.b], writes=[ps.b])
                    g = gst[cnt["g"] % 2]
                    cnt["g"] += 1
                    P.op("act", lambda a, ps=ps, g=g: a.activation(out=g.ap, in_=ps.ap, func=AF.Silu), reads=[ps.b],
                         writes=[g.b])
                    t0 = tt * TT + th * 512
                    P.dma("pool", GT[unit, :, t0:t0 + 512], g.ap, reads=[g.b], writes=[d_gt], owner=g.b)

        nfb = F // 512
        nqb = nq // 4
        nkb = nk // 4
        nvb = vcols // 512
        for tt in range(5):
            if dbg in (1, 2, 3, 4) and tt > 0:
                break
            if dbg == 5 and tt > 1:
                break
            if dbg in (6, 7) and tt in (1, 2, 3):
                continue
            if tt == 0:
                load_mod(0)
            if tt == 4:
                load_mod(1)
            wq = [load_w(0), load_w(1)]
            pq = []
            nbB = {}
            nbB[0] = norm_block(tt, 0)
            nbB[1] = norm_block(tt, 1)
            nbB.pop(0)()
            nbB[2] = norm_block(tt, 2)
            nbB.pop(1)()
            for fb in range(nfb):
                if dbg == 1 or (dbg == 2 and fb >= nqb + nkb) or (dbg == 3 and fb >= nqb + nkb + nvb):
                    break
                w = wq.pop(0)
                if fb + 2 < nfb:
                    wq.append(load_w(fb + 2))
                if fb < nqb + nkb:
                    is_k = fb >= nqb
                    u0 = (fb - nqb) * 4 if is_k else fb * 4
                    stage = qTs[cnt["qs"] % 2]
                    cnt["qs"] += 1
                    for tb in range(8):
                        if fb == 0:
                            if tb + 3 < 8:
                                nbB[tb + 3] = norm_block(tt, tb + 3)
                            if tb + 2 < 8:
                                nbB.pop(tb + 2)()
                        ps = mm_tok(w, tb)
                        pq.append(qk_post(ps, tt, tb, is_k, u0, stage, gk if is_k else gq))
                        if len(pq) > 3:
                            pq.pop(0)()
                    dst = (KT if is_k else QT)[u0:u0 + 4, :, tt * TT:(tt + 1) * TT].rearrange("u p t -> p u t")

                    def st(dst=dst, stage=stage, is_k=is_k):
                        P.dma("pool", dst, stage.ap, reads=[stage.b], writes=[d_kt if is_k else d_qt], owner=stage.b)
                    last_b = pq[-1]
                    pq[-1] = (lambda last_b=last_b, st=st: (last_b(), st()))
                elif fb < nqb + nkb + nvb:
                    c0 = (fb - nqb - nkb) * 512
                    for tb in range(8):
                        ps = mm_tok(w, tb)
                        v_post(ps, tt, tb, c0)
                else:
                    g_block(w, tt, fb - nqb - nkb - nvb)
                if fb == nqb + nkb and pq:
                    while pq:
                        pq.pop(0)()

    class Item:
        __slots__ = ("s_mms", "s_reads", "n", "mask", "pv", "pv_reads", "pv_writes", "first", "last", "after", "post", "E", "bgs")

    def run_attention(items, sbanks, etiles, eshape, depth=2, defer=0):
        deferred = []
        assert len(sbanks) >= depth + 1 and len(etiles) >= depth + 2
        cnt = 0
        q = []
        for it in list(items) + [None] * depth:
            if it is not None:
                psS = sbanks[cnt % len(sbanks)]
                E = etiles[cnt % len(etiles)]
                cnt += 1

                def smm(t, it=it, psS=psS):
                    for (off, n, lhsT, rhs) in it.s_mms:
                        ins = t.matmul(psS.ap[:, off:off + n], lhsT=lhsT, rhs=rhs, start=True, stop=True)
                    return ins
                P.op("pe", smm, reads=it.s_reads, writes=[psS.b])
                P.op("act", lambda a, psS=psS, E=E, n=it.n: a.activation(out=E.ap[:, 0:n], in_=psS.ap[:, 0:n], func=AF.Exp,
                                                                         scale=SCALE), reads=[psS.b], writes=[E.b])
                for f_bg in (getattr(it, "bgs", None) or ()):
                    f_bg()
                if it.mask is not None:
                    mk_ap, mk_b = it.mask[0], it.mask[1]
                    ev = E.ap[:, 0:(it.mask[2] if len(it.mask) > 2 else it.n)]
                    if len(mk_ap.shape) == 3:
                        ev = ev.rearrange("p (a b) -> p a b", b=mk_ap.shape[2])
                    P.op("pool", lambda g, ev=ev, mk_ap=mk_ap: g.tensor_tensor(out=ev, in0=ev, in1=mk_ap, op=ALU.mult),
                         reads=[E.b, mk_b], writes=[E.b])
                it.E = E
                q.append(it)
            if q and (len(q) > depth or it is None):
                p_ = q.pop(0)

                def pvmm(t, it=p_):
                    for ent in it.pv:
                        out_ap, lhsT, off, n = ent[0], ent[1], ent[2], ent[3]
                        st_, sp_ = (ent[4], ent[5]) if len(ent) > 4 else (it.first, it.last)
                        ins = t.matmul(out_ap, lhsT=lhsT, rhs=it.E.ap[:, off:off + n], start=st_, stop=sp_)
                    return ins
                P.op("pe", pvmm, reads=[p_.E.b, ones.b] + p_.pv_reads, writes=p_.pv_writes)
                deferred = [(c_ - 1, f_) for (c_, f_) in deferred]
                while deferred and deferred[0][0] <= 0:
                    deferred.pop(0)[1]()
                if p_.after is not None:
                    later = p_.after()
                    if later is not None:
                        deferred.append((defer, later))
                if getattr(p_, "post", None) is not None:
                    if defer:
                        deferred.append((defer, p_.post))
                    else:
                        p_.post()
        for (c_, f_) in deferred:
            f_()

    def load_ctx(cache_k, cache_v, j, kcol0, nku, vcol0, vw, ckf, ckb, ckT, cvf, cvb, tp):
        P.dma("sp", ckf.ap, cache_k[j, :, kcol0:kcol0 + nku * HD].rearrange("(c p) f -> p c f", p=128),
              reads=[d_ro], writes=[ckf.b], owner=ckf.b)
        P.dma("sp", cvf.ap, cache_v[j, :, vcol0:vcol0 + vw].rearrange("(c p) f -> p c f", p=128),
              reads=[d_ro], writes=[cvf.b], owner=cvf.b)
        P.op("dve", lambda v: v.tensor_copy(out=ckb.ap, in_=ckf.ap), reads=[ckf.b], writes=[ckb.b])
        P.op("pool", lambda g: g.tensor_copy(out=cvb.ap, in_=cvf.ap), reads=[cvf.b], writes=[cvb.b])
        tpv = tp.ap.bitcast(BF16)[:, 0:nku * 256].rearrange("p (u t) -> p u t", t=256)

        def tr(t):
            for u in range(nku):
                for c in range(2):
                    ins = t.transpose(tpv[:, u, c * 128:(c + 1) * 128], ckb.ap[:, c, u * HD:(u + 1) * HD], ident.ap)
            return ins
        P.op("pe", tr, reads=[ckb.b, ident.b], writes=[tp.b])
        P.op("act", lambda a: a.activation(out=ckT.ap, in_=tpv, func=AF.Copy), reads=[tp.b], writes=[ckT.b])

    def phase2_A(l):
        j = l // 3
        ar.reset()
        kT = [mk(ar, [NTOK], BF16, "kT%d" % i) for i in range(2)]
        vv = [mk(ar, [40, HD], BF16, "vv%d" % i) for i in range(2)]
        ckf = mk(ar, [2, HD], F32, "ckf")
        ckb = mk(ar, [2, HD], BF16, "ckb")
        ckT = [mk(ar, [1, 256], BF16, "ckT%d" % i) for i in range(2)]
        cvf = mk(ar, [2, HD], F32, "cvf")
        cvb = [mk(ar, [2, HD], BF16, "cvb%d" % i) for i in range(2)]
        qT = [mk(ar, [4, 512], BF16, "qT%d" % i) for i in range(2)]
        gT = [mk(ar, [4, 512], BF16, "gT%d" % i) for i in range(2)]
        oT = [mk(ar, [4, 512], BF16, "oT%d" % i) for i in range(2)]
        mprev = mk(ar, [4, 128], BF16, "mprev")
        mnext = mk(ar, [4, 128], BF16, "mnext")
        sexp = mk(ar, [16], F32, "sexp")
        et = [mk(ar, [512], BF16, "et%d" % i) for i in range(4)]
        den = [mk(ar, [4, 128], F32, "den%d" % i) for i in range(2)]
        of = [mk(ar, [4, 128], F32, "of%d" % i) for i in range(2)]
        sb_, ob_, lb_, tp = pb[0:3], pb[3:5], pb[5:7], pb[7]
        P.op("pool", lambda g: g.memset(mprev.ap, 1.0), writes=[mprev.b])
        P.op("pool", lambda g: g.affine_select(out=mprev.ap, in_=mprev.ap, compare_op=ALU.is_ge, fill=0.0, base=0,
                                               pattern=[[0, 4], [-1, 128]], channel_multiplier=1), reads=[mprev.b], writes=[mprev.b])
        P.op("pool", lambda g: g.memset(mnext.ap, 1.0), writes=[mnext.b])
        P.op("pool", lambda g: g.affine_select(out=mnext.ap, in_=mnext.ap, compare_op=ALU.is_ge, fill=0.0, base=0,
                                               pattern=[[0, 4], [1, 128]], channel_multiplier=-1), reads=[mnext.b], writes=[mnext.b])
        P.dma("sp", sexp.ap, dbc(sink_a[j:j + 1, :], 128), reads=[d_ro], writes=[sexp.b], owner=sexp.b)
        P.op("act", lambda a: a.activation(out=sexp.ap, in_=sexp.ap, func=AF.Exp), reads=[sexp.b], writes=[sexp.b])
        cnt = dict(q=0, f=0)
        for kvh in range(4):
            k_, v_, ckT_, cvb_ = kT[kvh % 2], vv[kvh % 2], ckT[kvh % 2], cvb[kvh % 2]
            P.dma("sp", k_.ap, KT[kvh], reads=[d_kt], writes=[k_.b], owner=k_.b)
            P.dma("sp", v_.ap, VS[:, kvh * HD:(kvh + 1) * HD].rearrange("(c p) d -> p c d", p=128), reads=[d_vs],
                  writes=[v_.b], owner=v_.b)
            load_ctx(cak, cav, j, kvh * HD, 1, kvh * HD, HD, ckf, ckb, ckT_, cvf, cvb_, tp)
            items = []
            loaders = []
            slab_first = []
            for q512 in range(10):
                q_, g_, o_ = qT[cnt["q"] % 2], gT[cnt["q"] % 2], oT[cnt["q"] % 2]
                cnt["q"] += 1
                t0 = q512 * 512

                def ld(q_=q_, g_=g_, t0=t0, kvh=kvh):
                    P.dma("sp", q_.ap, QT[kvh * 4:(kvh + 1) * 4, :, t0:t0 + 512].rearrange("u p t -> p u t"), reads=[d_qt],
                          writes=[q_.b], owner=q_.b)
                    P.dma("sp", g_.ap, GT[kvh * 4:(kvh + 1) * 4, :, t0:t0 + 512].rearrange("u p t -> p u t"), reads=[d_gt],
                          writes=[g_.b], owner=g_.b)
                loaders.append(ld)
                slab_first.append(len(items))
                for qb in range(4):
                    B = q512 * 4 + qb
                    if B < 32:
                        ch = []
                        if B > 0:
                            ch.append(("l", B - 1, mprev))
                        ch.append(("l", B, None))
                        if B < 31:
                            ch.append(("l", B + 1, mnext))
                        ch += [("c", 0, None), ("c", 1, None)]
                        import os as _os
                        if _os.environ.get("A_NOCTX"):
                            ch = ch[:-2]
                        if _os.environ.get("A_NOMASK"):
                            ch = [(a, b, None) for (a, b, c_) in ch]
                    else:
                        sq = (B - 32) // 2
                        ch = [("l", 32 + 2 * sq, None), ("l", 32 + 2 * sq + 1, None)]
                    fi = cnt["f"] % 2
                    cnt["f"] += 1
                    psO, psL = ob_[fi], lb_[fi]
                    rhs = q_.ap[:, :, qb * 128:(qb + 1) * 128]
                    for ci, (typ, c, msk) in enumerate(ch):
                        it = Item()
                        if typ == "l":
                            lk, kb = k_.ap[:, c * 128:(c + 1) * 128], k_.b
                            lv, vb = v_.ap[:, c, :], v_.b
                        else:
                            lk, kb = ckT_.ap[:, 0, c * 128:(c + 1) * 128], ckT_.b
                            lv, vb = cvb_.ap[:, c, :], cvb_.b
                        it.s_mms = [(0, 512, lk, rhs)]
                        it.s_reads = [kb, q_.b]
                        it.n = 512
                        it.mask = (msk.ap, msk.b) if msk is not None else None
                        it.pv = [(psO.ap, lv, 0, 512), (psL.ap, ones.ap, 0, 512)]
                        it.pv_reads = [vb]
                        it.pv_writes = [psO.b, psL.b]
                        it.first = (ci == 0)
                        it.last = (ci == len(ch) - 1)
                        it.after = None
                        it.post = None
                        it.bgs = None
                        if ci == len(ch) - 1:
                            def fin(psO=psO, psL=psL, fi=fi, kvh=kvh, g_=g_, o_=o_, qb=qb):
                                d_, f_ = den[fi], of[fi]
                                lv3 = psL.ap.rearrange("p (a b) -> p a b", b=128)
                                ov3 = psO.ap.rearrange("p (a b) -> p a b", b=128)
                                P.op("dve", lambda v: v.tensor_tensor(out=d_.ap, in0=lv3, in1=bc_last(sexp.ap[:, kvh * 4:(kvh + 1) * 4], 128),
                                                                      op=ALU.add), reads=[psL.b, sexp.b], writes=[d_.b])
                                P.op("dve", lambda v: v.reciprocal(out=d_.ap, in_=d_.ap), reads=[d_.b], writes=[d_.b])
                                P.op("dve", lambda v: v.tensor_tensor(out=f_.ap, in0=ov3, in1=d_.ap, op=ALU.mult),
                                     reads=[psO.b, d_.b], writes=[f_.b])
                                P.op("pool", lambda g: g.tensor_tensor(out=o_.ap[:, :, qb * 128:(qb + 1) * 128], in0=f_.ap,
                                                                       in1=g_.ap[:, :, qb * 128:(qb + 1) * 128], op=ALU.mult),
                                     reads=[f_.b, g_.b, o_.b], writes=[o_.b])
                                if qb == 3:
                                    pass
                            it.after = fin
                        items.append(it)
                    if qb == 3:
                        last = items[-1]
                        prev_after = last.after

                        def fin2(prev_after=prev_after, o_=o_, kvh=kvh, t0=t0):
                            prev_after()
                            P.dma("pool", OT[kvh * 4:(kvh + 1) * 4, :, t0:t0 + 512].rearrange("u p t -> p u t"), o_.ap,
                                  reads=[o_.b], writes=[d_ot], owner=o_.b)
                        last.after = fin2
            loaders[0]()
            for si in range(len(loaders) - 1):
                items[slab_first[si]].post = loaders[si + 1]
            run_attention(items, sb_, et, None)

    def phase2_B(l):
        j = 0
        lam_init = LAMBDA_INIT[l]
        ar.reset()
        kT = [mk(ar, [2, NTOK], BF16, "kTb%d" % i) for i in range(2)]
        vv = [mk(ar, [40, 256], BF16, "vvb%d" % i) for i in range(2)]
        ckf = mk(ar, [2, 256], F32, "ckfb")
        ckb = mk(ar, [2, 256], BF16, "ckbb")
        ckT = [mk(ar, [2, 256], BF16, "ckTb%d" % i) for i in range(2)]
        cvf = mk(ar, [2, 256], F32, "cvfb")
        cvb = [mk(ar, [2, 256], BF16, "cvbb%d" % i) for i in range(2)]
        qT = [mk(ar, [2, 1024], BF16, "qTb%d" % i) for i in range(2)]
        gT = [mk(ar, [2, 1024], BF16, "gTb%d" % i) for i in range(2)]
        oT = [mk(ar, [2, 1024], BF16, "oTb%d" % i) for i in range(2)]
        et = [mk(ar, [512], BF16, "et%d" % i) for i in range(4)]
        lamt = mk(ar, [512], F32, "lamt")
        ltmp = mk(ar, [2, 128], F32, "ltmp")
        ls = mk(ar, [2], F32, "ls")
        nlam = mk(ar, [1], F32, "nlam")
        sube = mk(ar, [2], F32, "sube")
        R = [mk(ar, [2, 256], F32, "Rb%d" % i) for i in range(4)]
        t1 = [mk(ar, [2, 256], F32, "t1b%d" % i) for i in range(4)]
        t2 = [mk(ar, [2, 256], F32, "t2b%d" % i) for i in range(4)]
        ob32 = [mk(ar, [2, 256], F32, "ob32%d" % i) for i in range(4)]
        sqb = [mk(ar, [2, 256], BF16, "sqb%d" % i) for i in range(4)]
        sd = [mk(ar, [256], F32, "sdb%d" % i) for i in range(4)]
        sb_, o0_, o1_, lb_, xb_, tp = pb[0:3], pb[3], pb[4], pb[5], pb[6], pb[7]
        P.dma("sp", lamt.ap, dbc(lam_b[0:1, :], 128), reads=[d_ro], writes=[lamt.b], owner=lamt.b)
        lv = lamt.ap.rearrange("p (a b) -> p a b", b=128)
        P.op("dve", lambda v: v.tensor_tensor(out=ltmp.ap[:, 0, :], in0=lv[:, 0, :], in1=lv[:, 1, :], op=ALU.mult),
             reads=[lamt.b], writes=[ltmp.b])
        P.op("dve", lambda v: v.tensor_tensor(out=ltmp.ap[:, 1, :], in0=lv[:, 2, :], in1=lv[:, 3, :], op=ALU.mult),
             reads=[lamt.b, ltmp.b], writes=[ltmp.b])
        P.op("dve", lambda v: v.tensor_reduce(out=ls.ap, in_=ltmp.ap, axis=AX.X, op=ALU.add), reads=[ltmp.b], writes=[ls.b])
        P.op("act", lambda a: a.activation(out=ls.ap, in_=ls.ap, func=AF.Exp), reads=[ls.b], writes=[ls.b])
        P.op("dve", lambda v: v.tensor_tensor(out=nlam.ap, in0=ls.ap[:, 1:2], in1=ls.ap[:, 0:1], op=ALU.subtract),
             reads=[ls.b], writes=[nlam.b])
        P.op("dve", lambda v: v.tensor_scalar(out=nlam.ap, in0=nlam.ap, scalar1=-lam_init, scalar2=None, op0=ALU.add),
             reads=[nlam.b], writes=[nlam.b])
        P.dma("sp", sube.ap, subln_b[0].rearrange("(c p) -> p c", p=128), reads=[d_ro], writes=[sube.b], owner=sube.b,
              allow_slow_non_contiguous=True)
        P.op("dve", lambda v: v.tensor_scalar(out=sube.ap, in0=sube.ap, scalar1=1.0 - lam_init, scalar2=None, op0=ALU.mult),
             reads=[sube.b], writes=[sube.b])
        cnt = dict(q=0, f=0)
        for h in range(8):
            k_, v_, ckT_, cvb_ = kT[h % 2], vv[h % 2], ckT[h % 2], cvb[h % 2]
            P.dma("sp", k_.ap, KT[2 * h:2 * h + 2].rearrange("u p t -> p u t"), reads=[d_kt], writes=[k_.b], owner=k_.b)
            P.dma("sp", v_.ap, VS[:, h * 256:(h + 1) * 256].rearrange("(c p) d -> p c d", p=128), reads=[d_vs],
                  writes=[v_.b], owner=v_.b)
            load_ctx(cbk, cbv, 0, 2 * h * HD, 2, h * 256, 256, ckf, ckb, ckT_, cvf, cvb_, tp)
            items = []
            loaders = []
            slab_first = []
            for q1k in range(5):
                q_, g_, o_ = qT[cnt["q"] % 2], gT[cnt["q"] % 2], oT[cnt["q"] % 2]
                cnt["q"] += 1
                t0 = q1k * 1024

                def ld(q_=q_, g_=g_, t0=t0, h=h):
                    for (dst, src, dd) in ((q_, QT, d_qt), (g_, GT, d_gt)):
                        P.dma("sp", dst.ap, src[2 * h:2 * h + 2, :, t0:t0 + 1024].rearrange("u p t -> p u t"), reads=[dd],
                              writes=[dst.b], owner=dst.b)
                loaders.append(ld)
                slab_first.append(len(items))
                for qi in range(4):
                    B = q1k * 4 + qi
                    if B < 16:
                        ch = [("l", c) for c in range(32)] + [("c", 0), ("c", 1)]
                    else:
                        ch = [("l", 32 + 2 * (B - 16)), ("l", 32 + 2 * (B - 16) + 1)]
                    fi = cnt["f"] % 4
                    cnt["f"] += 1
                    qs = slice(qi * 256, (qi + 1) * 256)
                    for ci, (typ, c) in enumerate(ch):
                        it = Item()
                        it.s_mms = []
                        for m in range(2):
                            if typ == "l":
                                lk = k_.ap[:, m, c * 128:(c + 1) * 128]
                            else:
                                lk = ckT_.ap[:, m, c * 128:(c + 1) * 128]
                            it.s_mms.append((m * 256, 256, lk, q_.ap[:, m, qs]))
                        if typ == "l":
                            kb, vb = k_.b, v_.b
                            lvs = [v_.ap[:, c, e * 128:(e + 1) * 128] for e in range(2)]
                        else:
                            kb, vb = ckT_.b, cvb_.b
                            lvs = [cvb_.ap[:, c, e * 128:(e + 1) * 128] for e in range(2)]
                        it.s_reads = [kb, q_.b]
                        it.n = 512
                        it.mask = None
                        it.pv = [(o0_.ap, lvs[0], 0, 512), (o1_.ap, lvs[1], 0, 512), (lb_.ap, ones.ap, 0, 512)]
                        it.pv_reads = [vb]
                        it.pv_writes = [o0_.b, o1_.b, lb_.b]
                        it.first = (ci == 0)
                        it.last = (ci == len(ch) - 1)
                        it.after = None
                        it.post = None
                        it.bgs = None
                        if ci == len(ch) - 1:
                            def fin(fi=fi, h=h, g_=g_, o_=o_, qs=qs, qi=qi, t0=t0):
                                R_, t1_, t2_, o32, sq_, sd_ = R[fi], t1[fi], t2[fi], ob32[fi], sqb[fi], sd[fi]
                                l3 = lb_.ap.rearrange("p (a b) -> p a b", b=256)
                                P.op("dve", lambda v: v.reciprocal(out=R_.ap, in_=l3), reads=[lb_.b], writes=[R_.b])
                                for e, ob in enumerate((o0_, o1_)):
                                    o3 = ob.ap.rearrange("p (a b) -> p a b", b=256)
                                    P.op("dve", lambda v, o3=o3, e=e: v.tensor_tensor(out=t1_.ap[:, e, :], in0=o3[:, 0, :], in1=R_.ap[:, 0, :],
                                                                                      op=ALU.mult), reads=[ob.b, R_.b], writes=[t1_.b])
                                    P.op("dve", lambda v, o3=o3, e=e: v.tensor_tensor(out=t2_.ap[:, e, :], in0=o3[:, 1, :], in1=R_.ap[:, 1, :],
                                                                                      op=ALU.mult), reads=[ob.b, R_.b], writes=[t2_.b])
                                def part2():
                                    fin_part2(R_, t1_, t2_, o32, sq_, sd_, g_, o_, qs, qi, h, t0)
                                return part2

                            def fin_part2(R_, t1_, t2_, o32, sq_, sd_, g_, o_, qs, qi, h, t0):
                                P.op("dve", lambda g: g.scalar_tensor_tensor(out=o32.ap, in0=t2_.ap, scalar=nlam.ap[:, 0:1], in1=t1_.ap,
                                                                             op0=ALU.mult, op1=ALU.add), reads=[t1_.b, t2_.b, nlam.b],
                                     writes=[o32.b])
                                P.op("pool", lambda g: g.tensor_tensor(out=sq_.ap, in0=o32.ap, in1=o32.ap, op=ALU.mult), reads=[o32.b],
                                     writes=[sq_.b])

                                def ssmm(t):
                                    for e in range(2):
                                        ins = t.matmul(xb_.ap[:, 0:256], lhsT=ones.ap, rhs=sq_.ap[:, e, :], start=(e == 0), stop=(e == 1))
                                    return ins
                                P.op("pe", ssmm, reads=[sq_.b, ones.b], writes=[xb_.b])
                                P.op("act", lambda a: a.activation(out=sd_.ap, in_=xb_.ap[:, 0:256], func=AF.Sqrt, scale=1.0 / 256, bias=EPS),
                                     reads=[xb_.b], writes=[sd_.b])
                                P.op("dve", lambda v: v.reciprocal(out=sd_.ap, in_=sd_.ap), reads=[sd_.b], writes=[sd_.b])
                                P.op("dve", lambda v: v.tensor_tensor(out=o32.ap, in0=o32.ap, in1=bc_mid(sd_.ap, 2), op=ALU.mult),
                                     reads=[o32.b, sd_.b], writes=[o32.b])
                                for e in range(2):
                                    P.op("dve", lambda g, e=e: g.scalar_tensor_tensor(out=o_.ap[:, e, qs], in0=o32.ap[:, e, :],
                                                                                       scalar=sube.ap[:, e:e + 1], in1=g_.ap[:, e, qs],
                                                                                       op0=ALU.mult, op1=ALU.mult),
                                         reads=[o32.b, sube.b, g_.b, o_.b], writes=[o_.b])
                                if qi == 3:
                                    P.dma("pool", OT[2 * h:2 * h + 2, :, t0:t0 + 1024].rearrange("u p t -> p u t"), o_.ap,
                                          reads=[o_.b], writes=[d_ot], owner=o_.b)
                            it.after = fin
                        items.append(it)
            loaders[0]()
            for si in range(len(loaders) - 1):
                items[slab_first[si]].post = loaders[si + 1]
            run_attention(items, sb_, et, None, defer=4)

    def phase2_C(l):
        ar.reset()
        kT = [mk(ar, [NTOK], BF16, "kT%d" % i) for i in range(2)]
        vv = [mk(ar, [40, HD], BF16, "vv%d" % i) for i in range(2)]
        ckf = mk(ar, [2, HD], F32, "ckf")
        ckb = mk(ar, [2, HD], BF16, "ckb")
        ckT = [mk(ar, [1, 256], BF16, "ckT%d" % i) for i in range(2)]
        cvf = mk(ar, [2, HD], F32, "cvf")
        cvb = [mk(ar, [2, HD], BF16, "cvb%d" % i) for i in range(2)]
        qT = [mk(ar, [1024], BF16, "qTc%d" % i) for i in range(2)]
        gT = [mk(ar, [1024], BF16, "gTc%d" % i) for i in range(2)]
        oT = [mk(ar, [1024], BF16, "oTc%d" % i) for i in range(2)]
        et = [mk(ar, [512], BF16, "etc%d" % i) for i in range(5)]
        cbr = mk(ar, [15, 64], F32, "cbr")
        cbm = mk(ar, [15, 64], BF16, "cbm")
        colm = mk(ar, [64], F32, "colm")
        EB = [mk(ar, [25, 128], BF16, "EB%d" % i) for i in range(2)]
        rr = [mk(ar, [128], F32, "rrc%d" % i) for i in range(2)]
        of = [mk(ar, [128], F32, "ofc%d" % i) for i in range(2)]
        sb_, ob_, lb_, tp = pb[0:3], pb[3:5], pb[5:7], pb[7]
        P.op("pool", lambda g: g.memset(colm.ap, 1.0), writes=[colm.b])
        for h0 in (0, 64):
            pr = slice(h0, h0 + 64)
            P.op("pool", lambda g, pr=pr: g.affine_select(out=colm.ap[pr, 0:8], in_=colm.ap[pr, 0:8], compare_op=ALU.is_ge, fill=0.0,
                                                          base=15, pattern=[[0, 8]], channel_multiplier=-1), reads=[colm.b], writes=[colm.b])
            P.op("pool", lambda g, pr=pr: g.affine_select(out=colm.ap[pr, 8:57], in_=colm.ap[pr, 8:57], compare_op=ALU.is_ge, fill=0.0,
                                                          base=0, pattern=[[-1, 49]], channel_multiplier=1), reads=[colm.b], writes=[colm.b])
            P.op("pool", lambda g, pr=pr: g.affine_select(out=colm.ap[pr, 8:57], in_=colm.ap[pr, 8:57], compare_op=ALU.is_ge, fill=0.0,
                                                          base=15, pattern=[[1, 49]], channel_multiplier=-1), reads=[colm.b], writes=[colm.b])
            P.op("pool", lambda g, pr=pr: g.affine_select(out=colm.ap[pr, 57:64], in_=colm.ap[pr, 57:64], compare_op=ALU.is_ge, fill=0.0,
                                                          base=-48, pattern=[[0, 7]], channel_multiplier=1), reads=[colm.b], writes=[colm.b])

        def rs_(r):
            return min(max(r - 4, 0), 56)

        def qblock_chunks(jb):
            lo = rs_(2 * jb) // 2
            hi = (rs_(2 * jb + 1) + 7) // 2
            return list(range(lo, hi + 1))

        classes = {0: 0, 1: 1, 30: 3, 31: 4}

        def cls_of(jb):
            return classes.get(jb, 2)

        rep = {0: 0, 1: 1, 2: 10, 3: 30, 4: 31}

        def build_EB_ops(EB_, h):
            ops = []
            for h0 in (0, 64):
                ops.append(lambda h0=h0: P.dma("sp", cbr.ap[h0:h0 + 64], rpbt[h], reads=[d_ro], writes=[cbr.b], owner=cbr.b))
            ops.append(lambda: P.op("act", lambda a: a.activation(out=cbr.ap, in_=cbr.ap, func=AF.Exp), reads=[cbr.b], writes=[cbr.b]))
            ops.append(lambda: P.op("pool", lambda g: g.tensor_tensor(out=cbm.ap, in0=cbr.ap, in1=bc_mid(colm.ap, 15), op=ALU.mult),
                                    reads=[cbr.b, colm.b], writes=[cbm.b]))
            ops.append(lambda: P.op("pool", lambda g: g.memset(EB_.ap, 0.0), writes=[EB_.b]))
            for ci in range(5):
                jb = rep[ci]
                for c in qblock_chunks(jb):
                    dlt = c - jb
                    slot = ci * 5 + (dlt + 3 if ci == 4 else (dlt if ci == 0 else dlt + 2 if ci in (2, 3) else dlt + 1))
                    for qr in range(2):
                        r = 2 * jb + qr
                        for kr in range(2):
                            ka = 2 * c + kr
                            if rs_(r) <= ka <= rs_(r) + 7:
                                i = ka - r + 7
                                ops.append(lambda kr=kr, qr=qr, slot=slot, i=i: P.op("pool", lambda g: g.tensor_copy(
                                    out=EB_.ap[kr * 64:(kr + 1) * 64, slot, qr * 64:(qr + 1) * 64],
                                    in_=cbm.ap[kr * 64:(kr + 1) * 64, i, :]), reads=[cbm.b, EB_.b], writes=[EB_.b]))
            return ops

        def slot_of(jb, c):
            ci = cls_of(jb)
            dlt = c - jb
            return ci * 5 + (dlt + 3 if ci == 4 else (dlt if ci == 0 else dlt + 2 if ci in (2, 3) else dlt + 1))

        cnt = dict(q=0, f=0)
        for h in range(16):
            k_, v_, ckT_, cvb_, EB_ = kT[h % 2], vv[h % 2], ckT[h % 2], cvb[h % 2], EB[h % 2]
            P.dma("sp", k_.ap, KT[h], reads=[d_kt], writes=[k_.b], owner=k_.b)
            P.dma("sp", v_.ap, VS[:, h * HD:(h + 1) * HD].rearrange("(c p) d -> p c d", p=128), reads=[d_vs],
                  writes=[v_.b], owner=v_.b)
            load_ctx(cck, ccv, 0, h * HD, 1, h * HD, HD, ckf, ckb, ckT_, cvf, cvb_, tp)
            if h == 0:
                for f_ in build_EB_ops(EB_, 0):
                    f_()
            items = []
            loaders = []
            slab_first = []
            for q1k in range(5):
                q_, g_, o_ = qT[cnt["q"] % 2], gT[cnt["q"] % 2], oT[cnt["q"] % 2]
                cnt["q"] += 1
                t0 = q1k * 1024

                def ld(q_=q_, g_=g_, t0=t0, h=h):
                    P.dma("sp", q_.ap, QT[h, :, t0:t0 + 1024], reads=[d_qt], writes=[q_.b], owner=q_.b)
                    P.dma("sp", g_.ap, GT[h, :, t0:t0 + 1024], reads=[d_gt], writes=[g_.b], owner=g_.b)
                loaders.append(ld)
                slab_first.append(len(items))
                for qi in range(8):
                    B = q1k * 8 + qi
                    if B < 32:
                        ch = [("l", c, slot_of(B, c)) for c in qblock_chunks(B)] + [("c", 0, None), ("c", 1, None)]
                    else:
                        sq = (B - 32) // 2
                        ch = [("l", 32 + 2 * sq, None), ("l", 32 + 2 * sq + 1, None)]
                    fi = cnt["f"] % 2
                    cnt["f"] += 1
                    psO, psL = ob_[fi], lb_[fi]
                    qs = slice(qi * 128, (qi + 1) * 128)
                    if B < 32:
                        nloc = len(ch) - 2
                        groups = [ch[0:4], ch[4:]]
                    else:
                        groups = [ch]
                    tot = len(ch)
                    done = 0
                    for gi, grp in enumerate(groups):
                        it = Item()
                        it.s_mms = []
                        it.pv = []
                        it.s_reads = [q_.b]
                        it.pv_reads = []
                        nmask = 0
                        slot0 = None
                        for j_, (typ, c, slot) in enumerate(grp):
                            if typ == "l":
                                lk, kb = k_.ap[:, c * 128:(c + 1) * 128], k_.b
                                lv_, vb = v_.ap[:, c, :], v_.b
                            else:
                                lk, kb = ckT_.ap[:, 0, c * 128:(c + 1) * 128], ckT_.b
                                lv_, vb = cvb_.ap[:, c, :], cvb_.b
                            it.s_mms.append((j_ * 128, 128, lk, q_.ap[:, qs]))
                            if kb not in it.s_reads:
                                it.s_reads.append(kb)
                            if vb not in it.pv_reads:
                                it.pv_reads.append(vb)
                            st_ = (done == 0)
                            sp_ = (done == tot - 1)
                            it.pv.append((psO.ap[:, 0:128], lv_, j_ * 128, 128, st_, sp_))
                            it.pv.append((psL.ap[:, 0:128], ones.ap, j_ * 128, 128, st_, sp_))
                            done += 1
                            if slot is not None:
                                if slot0 is None:
                                    slot0 = slot
                                nmask += 1
                        it.n = 128 * len(grp)
                        it.mask = (EB_.ap[:, slot0:slot0 + nmask, :], EB_.b, 128 * nmask) if nmask else None
                        it.pv_writes = [psO.b, psL.b]
                        it.first = (gi == 0)
                        it.last = (gi == len(groups) - 1)
                        it.after = None
                        it.post = None
                        it.bgs = None
                        if gi == len(groups) - 1:
                            def fin(psO=psO, psL=psL, fi=fi, h=h, g_=g_, o_=o_, qs=qs, qi=qi, t0=t0):
                                r_, f_ = rr[fi], of[fi]
                                P.op("dve", lambda v: v.reciprocal(out=r_.ap, in_=psL.ap[:, 0:128]), reads=[psL.b], writes=[r_.b])
                                P.op("dve", lambda v: v.tensor_tensor(out=f_.ap, in0=psO.ap[:, 0:128], in1=r_.ap, op=ALU.mult),
                                     reads=[psO.b, r_.b], writes=[f_.b])
                                P.op("pool", lambda g: g.tensor_tensor(out=o_.ap[:, qs], in0=f_.ap, in1=g_.ap[:, qs], op=ALU.mult),
                                     reads=[f_.b, g_.b, o_.b], writes=[o_.b])
                                if qi == 7:
                                    P.dma("pool", OT[h, :, t0:t0 + 1024], o_.ap, reads=[o_.b], writes=[d_ot], owner=o_.b)
                            it.after = fin
                        items.append(it)
            loaders[0]()
            for si in range(len(loaders) - 1):
                items[slab_first[si]].post = loaders[si + 1]
            if h + 1 < 16:
                bops = build_EB_ops(EB[(h + 1) % 2], h + 1)
                for bi, f_ in enumerate(bops):
                    it_ = items[min(2 + bi, len(items) - 1)]
                    if it_.bgs is None:
                        it_.bgs = []
                    it_.bgs.append(f_)
            run_attention(items, sb_, et, None)

    def phase3(l):
        xin, dxin = x_aps[l], d_x[l]
        xout, dxout = x_aps[l + 1], d_x[l + 1]
        ar.reset()
        wo = mk(ar, [16, D], BF16, "wo")
        gt = mk(ar, [D], F32, "gt")
        xt = [mk(ar, [D], F32, "xt%d" % i) for i in range(2)]
        xo = [mk(ar, [D], F32, "xo%d" % i) for i in range(2)]
        ot = [mk(ar, [16, 512], BF16, "ot%d" % i) for i in range(2)]
        P.dma("sp", wo.ap, WOB[l].rearrange("(c p) f -> p c f", p=128), reads=[d_wob[l]], writes=[wo.b], owner=wo.b)
        for tb in range(40):
            t0 = tb * 128
            if tb == 0 or tb == 32:
                r = 0 if tb == 0 else 1
                P.dma("sp", gt.ap, dbc(MOD[l, r:r + 1, 2 * D:3 * D], 128), reads=[d_mod], writes=[gt.b], owner=gt.b)
            o_ = ot[(tb // 4) % 2]
            if tb % 4 == 0:
                P.dma("sp", o_.ap, OT[:, :, t0:t0 + 512].rearrange("u p t -> p u t"), reads=[d_ot], writes=[o_.b], owner=o_.b)
            x = xt[tb % 2]
            y = xo[tb % 2]
            P.dma("sp", x.ap, xin[t0:t0 + 128, :], reads=[dxin], writes=[x.b], owner=x.b)
            tl = tb % 4
            for fb in range(4):
                ps = pb[(tb * 4 + fb) % 4]

                def mm(t, ps=ps, o_=o_, tl=tl, fb=fb):
                    for c in range(16):
                        ins = t.matmul(ps.ap, lhsT=o_.ap[:, c, tl * 128:(tl + 1) * 128], rhs=wo.ap[:, c, fb * 512:(fb + 1) * 512],
                                       start=(c == 0), stop=(c == 15))
                    return ins
                P.op("pe", mm, reads=[o_.b, wo.b], writes=[ps.b])
                fs = slice(fb * 512, (fb + 1) * 512)
                P.op("dve", lambda v, ps=ps, y=y, fs=fs: v.tensor_tensor(out=y.ap[:, fs], in0=ps.ap, in1=gt.ap[:, fs], op=ALU.mult),
                     reads=[ps.b, gt.b, y.b], writes=[y.b])
            P.op("pool", lambda g, x=x, y=y: g.tensor_tensor(out=y.ap, in0=y.ap, in1=x.ap, op=ALU.add), reads=[x.b, y.b], writes=[y.b])
            P.dma("pool", xout[t0:t0 + 128, :], y.ap, reads=[y.b], writes=[dxout], owner=y.b)

    build_consts()
    P.barrier()
    for l in range(n_layers):
        cast_weights(l)
    phase0()
    P.barrier()
    for l in range(n_layers):
        if stop_phase == (l, 0):
            break
        phase1(l)
        P.barrier()
        if stop_phase == (l, 1):
            break
        [phase2_A, phase2_B, phase2_C][KINDS[l]](l)
        P.barrier()
        if stop_phase == (l, 2):
            break
        phase3(l)
        P.barrier()
    P.barrier()

    from contextlib import ExitStack
    with ExitStack() as stack:
        P.emit(nc, stack)
    nc._n_sems = P.n_sems
    return nc


def make_in_maps(inp, n_layers=DEPTH):
    f = lambda a: np.ascontiguousarray(a, dtype=np.float32)
    xs, xp = inp["x_sample"], inp["x_prompt"]
    rpb = np.asarray(inp["rpb_c"])[0]
    kc = np.arange(64)[:, None]
    qc = np.arange(64)[None, :]
    idx = np.clip(kc - qc + 15, 0, 30)
    rpbt = f(rpb[:, :, idx].transpose(0, 2, 1, 3))
    shared = dict(
        ln_g=f(inp["ln_g"]), ada_w=f(inp["ada_w"][:n_layers]), ada_b=f(inp["ada_b"]), w_out=f(inp["w_out"][:n_layers]),
        qn_g=f(inp["qn_g"]), kn_g=f(inp["kn_g"]), w_in_a=f(inp["w_in_a"][:2 if n_layers > 3 else 1]),
        w_in_b=f(inp["w_in_b"] if n_layers > 1 else np.asarray(inp["w_in_b"])[:, :8]),
        w_in_c=f(inp["w_in_c"] if n_layers > 2 else np.asarray(inp["w_in_c"])[:, :8]), sink_a=f(inp["sink_a"]), lam_b=f(np.asarray(inp["lam_b"]).reshape(1, 512)),
        subln_b=f(inp["subln_b"]), rpbt=rpbt,
    )
    maps = []
    for i in range(8):
        m = dict(shared)
        m["x"] = f(np.concatenate([np.asarray(xs[i]), np.asarray(xp[4 * i:4 * i + 4]).reshape(1024, D)], axis=0))
        m["cpair"] = f(np.stack([np.asarray(inp["c"])[i], np.asarray(inp["c_ctx"])], axis=0))
        m["cak"] = f(np.asarray(inp["cache_a_k"])[i].reshape(2, 256, 512))
        m["cav"] = f(np.asarray(inp["cache_a_v"])[i].reshape(2, 256, 512))
        m["cbk"] = f(np.asarray(inp["cache_b_k"])[i].reshape(1, 256, 2048))
        m["cbv"] = f(np.asarray(inp["cache_b_v"])[i].reshape(1, 256, 2048))
        m["cck"] = f(np.asarray(inp["cache_c_k"])[i].reshape(1, 256, 2048))
        m["ccv"] = f(np.asarray(inp["cache_c_v"])[i].reshape(1, 256, 2048))
        maps.append(m)
    return maps


def assemble(results):
    y = np.stack([r["y"] for r in results], axis=0)
    y_sample = np.ascontiguousarray(y[:, :NS, :])
    y_prompt = np.ascontiguousarray(y[:, NS:, :].reshape(32, 256, D))
    cat = lambda k: np.concatenate([r[k] for r in results], axis=0)
    return (y_prompt, y_sample,
            cat("nak").reshape(32, 2, 256, 4, 128), cat("nav").reshape(32, 2, 256, 4, 128),
            cat("nbk").reshape(32, 1, 256, 8, 2, 128), cat("nbv").reshape(32, 1, 256, 8, 256),
            cat("nck").reshape(32, 1, 256, 16, 128), cat("ncv").reshape(32, 1, 256, 16, 128))


def kernel(**inputs):
    nc = build()
    in_maps = make_in_maps(inputs)
    res = run_bass_kernel_spmd(nc, in_maps, core_ids=list(range(8)))
    return assemble(res.results)
```

```python
import math
import numpy as np
import concourse.bass as bass
import concourse.mybir as mybir
from concourse.bass_utils import run_bass_kernel_spmd

F32 = mybir.dt.float32
BF16 = mybir.dt.bfloat16
I32 = mybir.dt.int32
AF = mybir.ActivationFunctionType
ALU = mybir.AluOpType
AX = mybir.AxisListType

D = 2048
NTOK = 5120
NS = 4096
TT = 1024
HD = 128
SCALE = HD ** -0.5
EPS = 1e-6
DEPTH = 4
KINDS = [0, 1, 2, 0]
FIN = [5120, 8192, 8192, 5120]
LAMBDA_INIT = [0.8 - 0.6 * math.exp(-0.3 * l) for l in range(DEPTH)]


class Buf:
    __slots__ = ("name", "w", "r", "excl")

    def __init__(self, name):
        self.name = name
        self.w = None
        self.r = {}
        self.excl = False


class DBuf:
    __slots__ = ("name", "writers", "readers", "prev_readers")

    def __init__(self, name):
        self.name = name
        self.writers = {}
        self.readers = {}
        self.prev_readers = {}


class Op:
    __slots__ = ("eng", "fn", "deps", "needs_inc", "dma")

    def __init__(self, eng, fn, deps, dma=None):
        self.eng = eng
        self.fn = fn
        self.deps = deps
        self.needs_inc = False
        self.dma = dma


ENGS = ("pe", "act", "dve", "pool", "sp")


def _add(dd, tok):
    key = (tok[0], tok[1])
    if dd.get(key, -1) < tok[2]:
        dd[key] = tok[2]


class Prog:
    def __init__(self):
        self.ops = {e: [] for e in ENGS}
        self.names = {}
        self.dpool = []
        self.kind_idxs = {'hw': [], 'sw': []}
        self.used = {'hw': 0, 'sw': 0}

    def _deps_for(self, reads, writes):
        deps = {}
        for b in reads:
            if isinstance(b, DBuf):
                for k, v in b.writers.items():
                    _add(deps, (k[0], k[1], v))
            else:
                if b.w is not None:
                    _add(deps, b.w)
                if b.excl:
                    for k, v in b.r.items():
                        _add(deps, (k[0], k[1], v))
        for b in writes:
            if isinstance(b, DBuf):
                if b.readers:
                    b.prev_readers = b.readers
                    b.readers = {}
                    b.writers = {}
                for k, v in b.prev_readers.items():
                    _add(deps, (k[0], k[1], v))
            else:
                if b.w is not None:
                    _add(deps, b.w)
                for k, v in b.r.items():
                    _add(deps, (k[0], k[1], v))
        return deps

    def _mark(self, tok, reads, writes):
        for b in reads:
            if isinstance(b, DBuf):
                _add(b.readers, tok)
            else:
                _add(b.r, tok)
        for b in writes:
            if isinstance(b, DBuf):
                _add(b.writers, tok)
            else:
                b.w = tok
                b.r = {}

    def op(self, eng, fn, reads=(), writes=()):
        deps = self._deps_for(reads, writes)
        if eng == "pe":
            deps.pop(("e", "pe"), None)
        lst = self.ops[eng]
        tok = ("e", eng, len(lst))
        lst.append(Op(eng, fn, deps))
        self._mark(tok, reads, writes)
        return tok

    def dma(self, eng, out, in_, reads, writes, owner, **kw):
        deps = self._deps_for(reads, writes)
        kind = "sw" if eng == "pool" else "hw"
        idx = self.names.get((owner.name, kind))
        if idx is None:
            k = self.used[kind]
            if k < len(self.kind_idxs[kind]):
                idx = self.kind_idxs[kind][k]
            else:
                idx = len(self.dpool)
                self.dpool.append(0)
                self.kind_idxs[kind].append(idx)
            self.used[kind] += 1
            self.names[(owner.name, kind)] = idx
        self.dpool[idx] += 16
        ent = (idx, self.dpool[idx])
        tok = ("d", ent[0], ent[1])

        def fn(e, out=out, in_=in_, kw=kw):
            return e.dma_start(out=out, in_=in_, **kw)

        self.ops[eng].append(Op(eng, fn, deps, dma=ent[0]))
        self._mark(tok, reads, writes)
        return tok

    def barrier(self):
        deps = {}
        for e in ENGS:
            for i in range(len(self.ops[e]) - 1, -1, -1):
                o = self.ops[e][i]
                if o.dma is None and o.fn is not None:
                    deps[("e", e)] = i
                    break
        for idx, cnt in enumerate(self.dpool):
            if cnt:
                deps[("d", idx)] = cnt
        for e in ENGS:
            self.ops[e].append(Op(e, None, dict(deps)))
        self.names = {}
        self.used = {'hw': 0, 'sw': 0}

    def emit(self, nc, stack):
        for e in ENGS:
            for o in self.ops[e]:
                for k, v in o.deps.items():
                    if k[0] == "e":
                        self.ops[k[1]][v].needs_inc = True
        inc_count = {}
        for e in ENGS:
            c = 0
            arr = []
            for o in self.ops[e]:
                if o.needs_inc:
                    c += 1
                arr.append(c)
            inc_count[e] = arr
        esem = {e: stack.enter_context(nc.semaphore("es_" + e)) for e in ENGS if e != "sp"}
        dsem = [stack.enter_context(nc.semaphore("ds%d" % i)) for i in range(len(self.dpool))]
        self.n_sems = len(esem) + len(dsem)
        block = stack.enter_context(nc.Block())
        self.stats = {e: [0, 0] for e in ENGS}

        def run(e, eng):
            known = {}
            st = self.stats[e]
            for o in self.ops[e]:
                for k, v in o.deps.items():
                    if k[0] == "e":
                        val = inc_count[k[1]][v]
                        sem = esem[k[1]]
                    else:
                        val = v
                        sem = dsem[k[1]]
                    if known.get(k, 0) >= val:
                        continue
                    known[k] = val
                    eng.wait_ge(sem, val)
                    st[1] += 1
                if o.fn is None:
                    continue
                ins = o.fn(eng)
                st[0] += 1
                if o.dma is not None:
                    ins.then_inc(dsem[o.dma], 16)
                elif o.needs_inc:
                    ins.then_inc(esem[e], 1)

        @block.tensor
        def _(t):
            run("pe", t)

        @block.scalar
        def _(a):
            run("act", a)

        @block.vector
        def _(v):
            run("dve", v)

        @block.gpsimd
        def _(g):
            run("pool", g)

        @block.sync
        def _(s):
            run("sp", s)


class Arena:
    def __init__(self, nc, name, nbytes):
        self.t = nc.alloc_sbuf_tensor(name, [128, nbytes // 4], F32)
        self.cap = nbytes
        self.off = 0

    def reset(self):
        self.off = 0

    def alloc(self, free_shape, dtype):
        es = 2 if dtype == BF16 else 4
        n = 1
        for s in free_shape:
            n *= s
        nb = (n * es + 31) // 32 * 32
        assert self.off + nb <= self.cap, ("arena overflow", self.off, nb, self.cap)
        v = self.t[:, self.off // 4:(self.off + nb) // 4]
        self.off += nb
        if dtype != F32:
            v = v.bitcast(dtype)
        v = v[:, 0:n]
        if len(free_shape) == 2:
            v = v.rearrange("p (a b) -> p a b", b=free_shape[1])
        elif len(free_shape) == 3:
            v = v.rearrange("p (a b c) -> p a b c", b=free_shape[1], c=free_shape[2])
        elif len(free_shape) == 4:
            v = v.rearrange("p (a b c d) -> p a b c d", b=free_shape[1], c=free_shape[2], d=free_shape[3])
        return v


class T:
    __slots__ = ("ap", "b")

    def __init__(self, ap, name):
        self.ap = ap
        self.b = Buf(name)


def dbc(row, nparts):
    n = row.shape[-1]
    return bass.AP(tensor=row.tensor, offset=row.offset, ap=[[0, nparts], [1, n]])


def bc_last(ap2d, n):
    return ap2d.unsqueeze(2).broadcast_to([ap2d.shape[0], ap2d.shape[1], n])


def bc_mid(ap2d, n):
    return ap2d.unsqueeze(1).broadcast_to([ap2d.shape[0], n, ap2d.shape[1]])


def build(n_layers=DEPTH, stop_phase=None, dbg=0):
    nc = bass.Bass("TRN2", target_bir_lowering=False)
    P = Prog()

    def din(name, shape):
        return nc.dram_tensor(name, list(shape), F32, kind="ExternalInput").ap()

    def dout(name, shape):
        return nc.dram_tensor(name, list(shape), F32, kind="ExternalOutput").ap()

    x_in = din("x", [NTOK, D])
    cpair = din("cpair", [2, D])
    ln_g = din("ln_g", [DEPTH, D])
    ada_w = din("ada_w", [n_layers, D, 3 * D])
    ada_b = din("ada_b", [DEPTH, 3 * D])
    w_out = din("w_out", [n_layers, D, D])
    qn_g = din("qn_g", [DEPTH, HD])
    kn_g = din("kn_g", [DEPTH, HD])
    w_in_a = din("w_in_a", [2 if n_layers > 3 else 1, D, 5120])
    w_in_b = din("w_in_b", [1, D, 8192] if n_layers > 1 else [1, 8, 8192])
    w_in_c = din("w_in_c", [1, D, 8192] if n_layers > 2 else [1, 8, 8192])
    sink_a = din("sink_a", [2, 16])
    lam_b = din("lam_b", [1, 4 * HD])
    subln_b = din("subln_b", [1, 256])
    rpbt = din("rpbt", [16, 64, 15, 64])
    cak = din("cak", [2, 256, 4 * HD])
    cav = din("cav", [2, 256, 4 * HD])
    cbk = din("cbk", [1, 256, 16 * HD])
    cbv = din("cbv", [1, 256, 8 * 256])
    cck = din("cck", [1, 256, 16 * HD])
    ccv = din("ccv", [1, 256, 16 * HD])
    y_out = dout("y", [NTOK, D])
    nak = dout("nak", [4, 2, 256, 4 * HD])
    nav = dout("nav", [4, 2, 256, 4 * HD])
    nbk = dout("nbk", [4, 1, 256, 16 * HD])
    nbv = dout("nbv", [4, 1, 256, 8 * 256])
    nck = dout("nck", [4, 1, 256, 16 * HD])
    ncv = dout("ncv", [4, 1, 256, 16 * HD])
    XA = nc.dram_tensor("XA", [NTOK, D], F32).ap()
    XB = nc.dram_tensor("XB", [NTOK, D], F32).ap()
    MOD = nc.dram_tensor("MOD", [DEPTH, 2, 3 * D], F32).ap()
    QT = nc.dram_tensor("QT", [16, 128, NTOK], BF16).ap()
    KT = nc.dram_tensor("KT", [16, 128, NTOK], BF16).ap()
    VS = nc.dram_tensor("VS", [NTOK, D], BF16).ap()
    GT = nc.dram_tensor("GT", [16, 128, NTOK], BF16).ap()
    OT = nc.dram_tensor("OT", [16, 128, NTOK], BF16).ap()
    WIB = [nc.dram_tensor("WIB%d" % l, [D, FIN[l]], BF16).ap() for l in range(DEPTH)]
    WOB = [nc.dram_tensor("WOB%d" % l, [D, D], BF16).ap() for l in range(DEPTH)]
    w_in_src = [w_in_a[0], w_in_b[0], w_in_c[0], w_in_a[1 if n_layers > 3 else 0]]

    d_x = [DBuf("x_in"), DBuf("XA"), DBuf("XB"), DBuf("XA"), DBuf("y")]
    d_x[3] = d_x[1]
    x_aps = [x_in, XA, XB, XA, y_out]
    d_mod = DBuf("MOD")
    d_qt, d_kt, d_vs, d_gt, d_ot = DBuf("QT"), DBuf("KT"), DBuf("VS"), DBuf("GT"), DBuf("OT")
    d_wib = [DBuf("WIB%d" % l) for l in range(DEPTH)]
    d_wob = [DBuf("WOB%d" % l) for l in range(DEPTH)]
    d_outs = DBuf("outs")
    d_ro = DBuf("ro")

    ar = Arena(nc, "arena", 180 * 1024)
    car = Arena(nc, "consts", 20 * 1024)
    banks = [nc.alloc_psum_tensor("bank%d" % i, [128, 512], F32) for i in range(8)]
    pb = [T(banks[i][:], "bank%d" % i) for i in range(8)]
    for t_ in pb:
        t_.b.excl = True

    def mk(arena, free_shape, dtype, name):
        t = T(arena.alloc(free_shape, dtype), name)
        return t

    ident = mk(car, [128], BF16, "ident")
    ones = mk(car, [128], BF16, "ones")
    cosT = mk(car, [32, 2, 32], F32, "cosT")
    sinT = mk(car, [32, 2, 32], F32, "sinT")
    gq = mk(car, [HD], F32, "gq")
    gk = mk(car, [HD], F32, "gk")

    def build_consts():
        P.op("pool", lambda g: g.memset(ident.ap, 0.0), writes=[ident.b])
        P.op("pool", lambda g: g.affine_select(out=ident.ap, in_=ident.ap, compare_op=ALU.not_equal, fill=1.0,
                                               base=0, pattern=[[-1, 128]], channel_multiplier=1),
             reads=[ident.b], writes=[ident.b])
        P.op("pool", lambda g: g.memset(ones.ap, 1.0), writes=[ones.b])
        ar.reset()
        posr = mk(ar, [32], I32, "posr")
        posc = mk(ar, [1], I32, "posc")
        fi = mk(ar, [32], I32, "fi")
        posrf = mk(ar, [32], F32, "posrf")
        poscf = mk(ar, [1], F32, "poscf")
        ff = mk(ar, [32], F32, "ff")
        invf = mk(ar, [32], F32, "invf")
        ang = mk(ar, [32, 2, 32], F32, "ang")
        tmp = mk(ar, [32, 2, 32], F32, "angt")
        tmpi = mk(ar, [32, 2, 32], I32, "angi")
        for h0, base in ((0, 0), (64, 1)):
            P.op("pool", lambda g, h0=h0, base=base: g.iota(posr.ap[h0:h0 + 64, :], pattern=[[2, 32]], base=base,
                                                            channel_multiplier=0), writes=[posr.b])
            P.op("pool", lambda g, h0=h0: g.iota(posc.ap[h0:h0 + 64, :], pattern=[[0, 1]], base=0,
                                                 channel_multiplier=1), writes=[posc.b])
        P.op("pool", lambda g: g.iota(fi.ap, pattern=[[1, 32]], base=0, channel_multiplier=0), writes=[fi.b])
        P.op("dve", lambda v: v.tensor_copy(out=posrf.ap, in_=posr.ap), reads=[posr.b], writes=[posrf.b])
        P.op("dve", lambda v: v.tensor_copy(out=poscf.ap, in_=posc.ap), reads=[posc.b], writes=[poscf.b])
        P.op("dve", lambda v: v.tensor_copy(out=ff.ap, in_=fi.ap), reads=[fi.b], writes=[ff.b])
        P.op("act", lambda a: a.activation(out=invf.ap, in_=ff.ap, func=AF.Exp, scale=-math.log(10000.0) / 32.0),
             reads=[ff.b], writes=[invf.b])
        P.op("dve", lambda v: v.tensor_tensor(out=ang.ap[:, :, 0, :], in0=bc_last(posrf.ap, 32), in1=bc_mid(invf.ap, 32),
                                              op=ALU.mult), reads=[posrf.b, invf.b], writes=[ang.b])
        P.op("dve", lambda v: v.tensor_scalar(out=ang.ap[:, :, 1, :], in0=bc_mid(invf.ap, 32), scalar1=poscf.ap[:, 0:1],
                                              scalar2=None, op0=ALU.mult), reads=[poscf.b, invf.b, ang.b], writes=[ang.b])
        TWO_PI = 2.0 * math.pi
        for dst, shift in ((sinT, 0.0), (cosT, math.pi / 2)):
            P.op("dve", lambda v, shift=shift: v.tensor_scalar(out=tmp.ap, in0=ang.ap, scalar1=shift, scalar2=1.0 / TWO_PI,
                                                               op0=ALU.add, op1=ALU.mult), reads=[ang.b], writes=[tmp.b])
            P.op("dve", lambda v: v.tensor_copy(out=tmpi.ap, in_=tmp.ap), reads=[tmp.b], writes=[tmpi.b])
            P.op("dve", lambda v: v.tensor_copy(out=tmp.ap, in_=tmpi.ap), reads=[tmpi.b], writes=[tmp.b])
            P.op("dve", lambda v: v.scalar_tensor_tensor(out=tmp.ap, in0=tmp.ap, scalar=-TWO_PI, in1=ang.ap,
                                                         op0=ALU.mult, op1=ALU.add), reads=[tmp.b, ang.b], writes=[tmp.b])
            P.op("dve", lambda v, shift=shift: v.tensor_scalar(out=tmp.ap, in0=tmp.ap, scalar1=shift, scalar2=3.1415925,
                                                               op0=ALU.add, op1=ALU.min), reads=[tmp.b], writes=[tmp.b])
            P.op("dve", lambda v: v.tensor_scalar(out=tmp.ap, in0=tmp.ap, scalar1=-3.1415925, scalar2=None,
                                                  op0=ALU.max), reads=[tmp.b], writes=[tmp.b])
            P.op("act", lambda a, dst=dst: a.activation(out=dst.ap, in_=tmp.ap, func=AF.Sin), reads=[tmp.b], writes=[dst.b])

    castb = Buf("castsem")

    def cast_weights(l):
        src = w_in_src[l]
        F = FIN[l]
        for r0 in range(0, D, 256):
            P.dma("pool", WIB[l][r0:r0 + 256, :].rearrange("r (a b) -> r a b", b=1024),
                  src[r0:r0 + 256, :].rearrange("r (a b) -> r a b", b=1024),
                  reads=[d_ro], writes=[d_wib[l]], owner=castb)
        for r0 in range(0, D, 512):
            P.dma("pool", WOB[l][r0:r0 + 512, :].rearrange("r (a b) -> r a b", b=1024),
                  w_out[l, r0:r0 + 512, :].rearrange("r (a b) -> r a b", b=1024),
                  reads=[d_ro], writes=[d_wob[l]], owner=castb)

    def phase0():
        ar.reset()
        cT = mk(ar, [16, 2], F32, "cT")
        sT = mk(ar, [16, 2], F32, "sT")
        sg = mk(ar, [16, 2], F32, "sg")
        wt = [mk(ar, [16, 512], F32, "adaw%d" % i) for i in range(2)]
        adab = T(ar.alloc([3 * D], F32)[0:2, :], "adab")
        msb = T(ar.alloc([3 * D], F32)[0:2, :], "msb")
        for r in range(2):
            P.dma("sp", cT.ap[:, :, r], cpair[r].rearrange("(c p) -> p c", p=128), reads=[d_ro], writes=[cT.b],
                  owner=cT.b, allow_slow_non_contiguous=True)
        P.op("act", lambda a: a.activation(out=sg.ap, in_=cT.ap, func=AF.Exp, scale=-1.0), reads=[cT.b], writes=[sg.b])
        P.op("dve", lambda v: v.tensor_scalar(out=sg.ap, in0=sg.ap, scalar1=1.0, scalar2=None, op0=ALU.add),
             reads=[sg.b], writes=[sg.b])
        P.op("dve", lambda v: v.reciprocal(out=sg.ap, in_=sg.ap), reads=[sg.b], writes=[sg.b])
        P.op("dve", lambda v: v.tensor_tensor(out=sT.ap, in0=cT.ap, in1=sg.ap, op=ALU.mult), reads=[cT.b, sg.b],
             writes=[sT.b])
        i = 0
        for l in range(n_layers):
            P.dma("sp", adab.ap, dbc(ada_b[l:l + 1, :], 2), reads=[d_ro], writes=[adab.b], owner=adab.b)
            for fb in range(12):
                w = wt[i % 2]
                P.dma("sp", w.ap, ada_w[l, :, fb * 512:(fb + 1) * 512].rearrange("(c p) f -> p c f", p=128),
                      reads=[d_ro], writes=[w.b], owner=w.b)
                ps = pb[i % 2]

                def mm(t, w=w, ps=ps):
                    for c in range(16):
                        ins = t.matmul(ps.ap[0:2, :], lhsT=sT.ap[:, c, :], rhs=w.ap[:, c, :], start=(c == 0), stop=(c == 15))
                    return ins
                P.op("pe", mm, reads=[sT.b, w.b], writes=[ps.b])
                P.op("dve", lambda v, ps=ps, fb=fb: v.tensor_tensor(out=msb.ap[:, fb * 512:(fb + 1) * 512], in0=ps.ap[0:2, :],
                                                                    in1=adab.ap[:, fb * 512:(fb + 1) * 512], op=ALU.add),
                     reads=[ps.b, adab.b], writes=[msb.b])
                i += 1
            P.dma("sp", MOD[l], msb.ap, reads=[msb.b], writes=[d_mod], owner=msb.b)

    def layer_cfg(l):
        kind = KINDS[l]
        j = l // 3
        if kind == 0:
            return dict(kind=0, j=j, nq=16, nk=4, vcols=512, F=5120, knew=nak, vnew=nav, kcols=512)
        if kind == 1:
            return dict(kind=1, j=j, nq=16, nk=16, vcols=2048, F=8192, knew=nbk, vnew=nbv, kcols=2048)
        return dict(kind=2, j=j, nq=16, nk=16, vcols=2048, F=8192, knew=nck, vnew=ncv, kcols=2048)

    def phase1(l):
        cfg = layer_cfg(l)
        nq, nk, vcols, F = cfg["nq"], cfg["nk"], cfg["vcols"], cfg["F"]
        xin, dxin = x_aps[l], d_x[l]
        ar.reset()
        mod1 = mk(ar, [D], F32, "mod1")
        sh = mk(ar, [D], F32, "sh")
        xt = [mk(ar, [D], F32, "xt%d" % i) for i in range(2)]
        tmpf = mk(ar, [D], F32, "tmpf")
        lng = tmpf
        hb = [mk(ar, [D], BF16, "hb%d" % i) for i in range(2)]
        hT = mk(ar, [16, TT], BF16, "hT")
        hTb = [Buf("hTb%d" % i) for i in range(8)]
        wt = [mk(ar, [16, 512], BF16, "wt%d" % i) for i in range(3)]
        ssx = [mk(ar, [1], F32, "ssx%d" % i) for i in range(2)]
        rsx = [mk(ar, [1], F32, "rsx%d" % i) for i in range(2)]
        junk = mk(ar, [D], BF16, "junk")
        ss4 = [mk(ar, [4], F32, "ss4%d" % i) for i in range(4)]
        rs4 = [mk(ar, [4], F32, "rs4%d" % i) for i in range(4)]
        yq = [mk(ar, [4, HD], F32, "yq%d" % i) for i in range(3)]
        rt = [mk(ar, [4, 2, 32], F32, "rt%d" % i) for i in range(8)]
        ob = [mk(ar, [4, HD], BF16, "ob%d" % i) for i in range(4)]
        qTs = [mk(ar, [4, TT], BF16, "qTs%d" % i) for i in range(2)]
        vst = [mk(ar, [512], BF16, "vst%d" % i) for i in range(2)]
        vsf = [mk(ar, [512], F32, "vsf%d" % i) for i in range(2)]
        gst = [mk(ar, [512], BF16, "gst%d" % i) for i in range(2)]
        hps = pb[0:2]
        mps = pb[2:5]
        tps = pb[5:7]

        P.dma("sp", gq.ap, dbc(qn_g[l:l + 1, :], 128), reads=[d_ro], writes=[gq.b], owner=gq.b)
        P.dma("sp", gk.ap, dbc(kn_g[l:l + 1, :], 128), reads=[d_ro], writes=[gk.b], owner=gk.b)

        cnt = dict(x=0, w=0, m=0, q=0, v=0, g=0, t=0, qs=0)

        def load_mod(r):
            P.dma("sp", lng.ap, dbc(ln_g[l:l + 1, :], 128), reads=[d_ro], writes=[lng.b], owner=lng.b)
            P.dma("sp", sh.ap, dbc(MOD[l, r:r + 1, 0:D], 128), reads=[d_mod], writes=[sh.b], owner=sh.b)
            P.dma("sp", mod1.ap, dbc(MOD[l, r:r + 1, D:2 * D], 128), reads=[d_mod], writes=[mod1.b],
                  owner=mod1.b)
            P.op("dve", lambda v: v.scalar_tensor_tensor(out=mod1.ap, in0=mod1.ap, scalar=1.0, in1=lng.ap, op0=ALU.add,
                                                         op1=ALU.mult), reads=[mod1.b, lng.b], writes=[mod1.b])

        def load_w(fb):
            w = wt[cnt["w"] % 3]
            cnt["w"] += 1
            P.dma("sp", w.ap, WIB[l][:, fb * 512:(fb + 1) * 512].rearrange("(c p) f -> p c f", p=128),
                  reads=[d_wib[l]], writes=[w.b], owner=w.b)
            return w

        def norm_block(tt, tb):
            t0 = tt * TT + tb * 128
            x = xt[cnt["x"] % 2]
            h = hb[cnt["x"] % 2]
            s1 = ssx[cnt["x"] % 2]
            r1 = rsx[cnt["x"] % 2]
            cnt["x"] += 1
            P.dma("sp", x.ap, xin[t0:t0 + 128, :], reads=[dxin], writes=[x.b], owner=x.b)
            P.op("act", lambda a: a.activation(out=junk.ap, in_=x.ap, func=AF.Square, accum_out=s1.ap[:, 0:1]),
                 reads=[x.b], writes=[s1.b])
            P.op("act", lambda a: a.activation(out=s1.ap, in_=s1.ap, func=AF.Sqrt, scale=1.0 / D, bias=EPS),
                 reads=[s1.b], writes=[s1.b])
            P.op("dve", lambda v: v.reciprocal(out=r1.ap, in_=s1.ap), reads=[s1.b], writes=[r1.b])
            P.op("dve", lambda v: v.scalar_tensor_tensor(out=tmpf.ap, in0=x.ap, scalar=r1.ap[:, 0:1], in1=mod1.ap,
                                                         op0=ALU.mult, op1=ALU.mult), reads=[x.b, r1.b, mod1.b],
                 writes=[tmpf.b])
            P.op("pool", lambda g: g.tensor_tensor(out=h.ap, in0=tmpf.ap, in1=sh.ap, op=ALU.add), reads=[tmpf.b, sh.b],
                 writes=[h.b])
            hv = [hps[i].ap.bitcast(BF16).rearrange("p (c t) -> p c t", t=128) for i in range(2)]

            def part_b():
                def tr(t):
                    for c in range(16):
                        ins = t.transpose(hv[c // 8][:, c % 8, :], h.ap[:, c * 128:(c + 1) * 128], ident.ap)
                    return ins
                P.op("pe", tr, reads=[h.b, ident.b], writes=[hps[0].b, hps[1].b])
                P.op("act", lambda a: a.activation(out=hT.ap[:, 0:8, tb * 128:(tb + 1) * 128], in_=hv[0], func=AF.Copy),
                     reads=[hps[0].b], writes=[hTb[tb]])
                P.op("dve", lambda v: v.tensor_copy(out=hT.ap[:, 8:16, tb * 128:(tb + 1) * 128], in_=hv[1]),
                     reads=[hps[1].b, hTb[tb]], writes=[hTb[tb]])
            return part_b

        def mm_tok(w, tb):
            ps = mps[cnt["m"] % 3]
            cnt["m"] += 1

            def mm(t):
                for c in range(16):
                    ins = t.matmul(ps.ap, lhsT=hT.ap[:, c, tb * 128:(tb + 1) * 128], rhs=w.ap[:, c, :],
                                   start=(c == 0), stop=(c == 15))
                return ins
            P.op("pe", mm, reads=[hTb[tb], w.b], writes=[ps.b])
            return ps

        def qk_post(ps, tt, tb, is_k, u0, stage, gain):
            i2 = cnt["q"]
            cnt["q"] += 1
            s4, r4, y, o = ss4[i2 % 4], rs4[i2 % 4], yq[i2 % 3], ob[i2 % 4]
            psv = ps.ap.rearrange("p (u d) -> p u d", d=HD)
            for u in range(4):
                P.op("act", lambda a, u=u: a.activation(out=junk.ap[:, 0:HD], in_=psv[:, u, :], func=AF.Square,
                                                        accum_out=s4.ap[:, u:u + 1]), reads=[ps.b], writes=[s4.b])
            P.op("act", lambda a: a.activation(out=s4.ap, in_=s4.ap, func=AF.Sqrt, scale=1.0 / HD, bias=EPS),
                 reads=[s4.b], writes=[s4.b])
            P.op("dve", lambda v: v.reciprocal(out=r4.ap, in_=s4.ap), reads=[s4.b], writes=[r4.b])
            P.op("dve", lambda v: v.tensor_tensor(out=y.ap, in0=psv, in1=bc_mid(gain.ap, 4), op=ALU.mult),
                 reads=[ps.b, gain.b], writes=[y.b])
            P.op("pool", lambda g: g.tensor_tensor(out=y.ap, in0=y.ap, in1=bc_last(r4.ap, HD), op=ALU.mult),
                 reads=[y.b, r4.b], writes=[y.b])
            is_prompt = (tt == 4)
            if is_prompt or cfg["kind"] == 2:
                if is_k and is_prompt:
                    seq = tb // 2
                    r0 = (tb % 2) * 128
                    P.dma("sp", cfg["knew"][seq, cfg["j"], r0:r0 + 128, u0 * HD:(u0 + 4) * HD],
                          y.ap.rearrange("p u d -> p (u d)"), reads=[y.b], writes=[d_outs], owner=y.b)
                P.op("dve", lambda v: v.tensor_copy(out=o.ap, in_=y.ap), reads=[y.b], writes=[o.b])
            else:
                blk = tt * 8 + tb
                yv = y.ap.rearrange("p u (a h f) -> p u a h f", a=2, h=2)
                ov = o.ap.rearrange("p u (a h f) -> p u a h f", a=2, h=2)
                cs = cosT.ap[:, blk, :, :].unsqueeze(1).broadcast_to([128, 4, 2, 32])
                sn = sinT.ap[:, blk, :, :].unsqueeze(1).broadcast_to([128, 4, 2, 32])
                x1 = yv[:, :, :, 0, :]
                x2 = yv[:, :, :, 1, :]
                t1, t2, t3, t4 = [rt[(i2 % 2) * 4 + k] for k in range(4)]
                P.op("dve", lambda v: v.tensor_tensor(out=t1.ap, in0=x1, in1=cs, op=ALU.mult), reads=[y.b, cosT.b], writes=[t1.b])
                P.op("dve", lambda v: v.tensor_tensor(out=t2.ap, in0=x2, in1=sn, op=ALU.mult), reads=[y.b, sinT.b], writes=[t2.b])
                P.op("dve", lambda v: v.tensor_tensor(out=ov[:, :, :, 0, :], in0=t1.ap, in1=t2.ap, op=ALU.subtract),
                     reads=[t1.b, t2.b], writes=[o.b])
                P.op("pool", lambda g: g.tensor_tensor(out=t3.ap, in0=x2, in1=cs, op=ALU.mult), reads=[y.b, cosT.b], writes=[t3.b])
                P.op("pool", lambda g: g.tensor_tensor(out=t4.ap, in0=x1, in1=sn, op=ALU.mult), reads=[y.b, sinT.b], writes=[t4.b])
                P.op("pool", lambda g: g.tensor_tensor(out=ov[:, :, :, 1, :], in0=t3.ap, in1=t4.ap, op=ALU.add),
                     reads=[t3.b, t4.b, o.b], writes=[o.b])
            def part_b():
                tp = tps[cnt["t"] % 2]
                cnt["t"] += 1
                tpv = tp.ap.bitcast(BF16)[:, 0:512].rearrange("p (u t) -> p u t", t=128)

                def tr(t):
                    for u in range(4):
                        ins = t.transpose(tpv[:, u, :], o.ap[:, u, :], ident.ap)
                    return ins
                P.op("pe", tr, reads=[o.b, ident.b], writes=[tp.b])
                P.op("act", lambda a: a.activation(out=stage.ap[:, :, tb * 128:(tb + 1) * 128], in_=tpv, func=AF.Copy),
                     reads=[tp.b, stage.b], writes=[stage.b])
            return part_b

        def v_post(ps, tt, tb, c0):
            v = vst[cnt["v"] % 2]
            t0 = tt * TT + tb * 128
            P.op("act", lambda a: a.activation(out=v.ap, in_=ps.ap, func=AF.Copy), reads=[ps.b], writes=[v.b])
            P.dma("pool", VS[t0:t0 + 128, c0:c0 + 512], v.ap, reads=[v.b], writes=[d_vs], owner=v.b)
            if tt == 4 and dbg != 7:
                vf = vsf[cnt["v"] % 2]
                seq = tb // 2
                r0 = (tb % 2) * 128
                P.op("dve", lambda vv: vv.tensor_copy(out=vf.ap, in_=ps.ap), reads=[ps.b, v.b], writes=[vf.b])
                P.dma("sp", cfg["vnew"][seq, cfg["j"], r0:r0 + 128, c0:c0 + 512], vf.ap, reads=[vf.b], writes=[d_outs],
                      owner=vf.b)
            cnt["v"] += 1

        def g_block(w, tt, fb_g):
            for fc in range(4):
                unit = fb_g * 4 + fc
                for th in range(2):
                    ps = mps[cnt["m"] % 3]
                    cnt["m"] += 1

                    def mm(t, fc=fc, th=th, ps=ps):
                        for c in range(16):
                            ins = t.matmul(ps.ap, lhsT=w.ap[:, c, fc * 128:(fc + 1) * 128],
                                           rhs=hT.ap[:, c, th * 512:(th + 1) * 512], start=(c == 0), stop=(c == 15))
                        return ins
                    P.op("pe", mm, reads=hTb[th * 4:(th + 1) * 4] + [w.b], writes=[ps.b])
                    g = gst[cnt["g"] % 2]
                    cnt["g"] += 1
                    P.op("act", lambda a, ps=ps, g=g: a.activation(out=g.ap, in_=ps.ap, func=AF.Silu), reads=[ps.b],
                         writes=[g.b])
                    t0 = tt * TT + th * 512
                    P.dma("pool", GT[unit, :, t0:t0 + 512], g.ap, reads=[g.b], writes=[d_gt], owner=g.b)

        nfb = F // 512
        nqb = nq // 4
        nkb = nk // 4
        nvb = vcols // 512
        for tt in range(5):
            if dbg in (1, 2, 3, 4) and tt > 0:
                break
            if dbg == 5 and tt > 1:
                break
            if dbg in (6, 7) and tt in (1, 2, 3):
                continue
            if tt == 0:
                load_mod(0)
            if tt == 4:
                load_mod(1)
            wq = [load_w(0), load_w(1)]
            pq = []
            nbB = {}
            nbB[0] = norm_block(tt, 0)
            nbB[1] = norm_block(tt, 1)
            nbB.pop(0)()
            nbB[2] = norm_block(tt, 2)
            nbB.pop(1)()
            for fb in range(nfb):
                if dbg == 1 or (dbg == 2 and fb >= nqb + nkb) or (dbg == 3 and fb >= nqb + nkb + nvb):
                    break
                w = wq.pop(0)
                if fb + 2 < nfb:
                    wq.append(load_w(fb + 2))
                if fb < nqb + nkb:
                    is_k = fb >= nqb
                    u0 = (fb - nqb) * 4 if is_k else fb * 4
                    stage = qTs[cnt["qs"] % 2]
                    cnt["qs"] += 1
                    for tb in range(8):
                        if fb == 0:
                            if tb + 3 < 8:
                                nbB[tb + 3] = norm_block(tt, tb + 3)
                            if tb + 2 < 8:
                                nbB.pop(tb + 2)()
                        ps = mm_tok(w, tb)
                        pq.append(qk_post(ps, tt, tb, is_k, u0, stage, gk if is_k else gq))
                        if len(pq) > 3:
                            pq.pop(0)()
                    dst = (KT if is_k else QT)[u0:u0 + 4, :, tt * TT:(tt + 1) * TT].rearrange("u p t -> p u t")

                    def st(dst=dst, stage=stage, is_k=is_k):
                        P.dma("pool", dst, stage.ap, reads=[stage.b], writes=[d_kt if is_k else d_qt], owner=stage.b)
                    last_b = pq[-1]
                    pq[-1] = (lambda last_b=last_b, st=st: (last_b(), st()))
                elif fb < nqb + nkb + nvb:
                    c0 = (fb - nqb - nkb) * 512
                    for tb in range(8):
                        ps = mm_tok(w, tb)
                        v_post(ps, tt, tb, c0)
                else:
                    g_block(w, tt, fb - nqb - nkb - nvb)
                if fb == nqb + nkb and pq:
                    while pq:
                        pq.pop(0)()

    class Item:
        __slots__ = ("s_mms", "s_reads", "n", "mask", "pv", "pv_reads", "pv_writes", "first", "last", "after", "post", "E", "bgs")

    def run_attention(items, sbanks, etiles, eshape, depth=2, defer=0):
        deferred = []
        assert len(sbanks) >= depth + 1 and len(etiles) >= depth + 2
        cnt = 0
        q = []
        for it in list(items) + [None] * depth:
            if it is not None:
                psS = sbanks[cnt % len(sbanks)]
                E = etiles[cnt % len(etiles)]
                cnt += 1

                def smm(t, it=it, psS=psS):
                    for (off, n, lhsT, rhs) in it.s_mms:
                        ins = t.matmul(psS.ap[:, off:off + n], lhsT=lhsT, rhs=rhs, start=True, stop=True)
                    return ins
                P.op("pe", smm, reads=it.s_reads, writes=[psS.b])
                P.op("act", lambda a, psS=psS, E=E, n=it.n: a.activation(out=E.ap[:, 0:n], in_=psS.ap[:, 0:n], func=AF.Exp,
                                                                         scale=SCALE), reads=[psS.b], writes=[E.b])
                for f_bg in (getattr(it, "bgs", None) or ()):
                    f_bg()
                if it.mask is not None:
                    mk_ap, mk_b = it.mask[0], it.mask[1]
                    ev = E.ap[:, 0:(it.mask[2] if len(it.mask) > 2 else it.n)]
                    if len(mk_ap.shape) == 3:
                        ev = ev.rearrange("p (a b) -> p a b", b=mk_ap.shape[2])
                    P.op("pool", lambda g, ev=ev, mk_ap=mk_ap: g.tensor_tensor(out=ev, in0=ev, in1=mk_ap, op=ALU.mult),
                         reads=[E.b, mk_b], writes=[E.b])
                it.E = E
                q.append(it)
            if q and (len(q) > depth or it is None):
                p_ = q.pop(0)

                def pvmm(t, it=p_):
                    for ent in it.pv:
                        out_ap, lhsT, off, n = ent[0], ent[1], ent[2], ent[3]
                        st_, sp_ = (ent[4], ent[5]) if len(ent) > 4 else (it.first, it.last)
                        ins = t.matmul(out_ap, lhsT=lhsT, rhs=it.E.ap[:, off:off + n], start=st_, stop=sp_)
                    return ins
                P.op("pe", pvmm, reads=[p_.E.b, ones.b] + p_.pv_reads, writes=p_.pv_writes)
                deferred = [(c_ - 1, f_) for (c_, f_) in deferred]
                while deferred and deferred[0][0] <= 0:
                    deferred.pop(0)[1]()
                if p_.after is not None:
                    later = p_.after()
                    if later is not None:
                        deferred.append((defer, later))
                if getattr(p_, "post", None) is not None:
                    if defer:
                        deferred.append((defer, p_.post))
                    else:
                        p_.post()
        for (c_, f_) in deferred:
            f_()

    def load_ctx(cache_k, cache_v, j, kcol0, nku, vcol0, vw, ckf, ckb, ckT, cvf, cvb, tp):
        P.dma("sp", ckf.ap, cache_k[j, :, kcol0:kcol0 + nku * HD].rearrange("(c p) f -> p c f", p=128),
              reads=[d_ro], writes=[ckf.b], owner=ckf.b)
        P.dma("sp", cvf.ap, cache_v[j, :, vcol0:vcol0 + vw].rearrange("(c p) f -> p c f", p=128),
              reads=[d_ro], writes=[cvf.b], owner=cvf.b)
        P.op("dve", lambda v: v.tensor_copy(out=ckb.ap, in_=ckf.ap), reads=[ckf.b], writes=[ckb.b])
        P.op("pool", lambda g: g.tensor_copy(out=cvb.ap, in_=cvf.ap), reads=[cvf.b], writes=[cvb.b])
        tpv = tp.ap.bitcast(BF16)[:, 0:nku * 256].rearrange("p (u t) -> p u t", t=256)

        def tr(t):
            for u in range(nku):
                for c in range(2):
                    ins = t.transpose(tpv[:, u, c * 128:(c + 1) * 128], ckb.ap[:, c, u * HD:(u + 1) * HD], ident.ap)
            return ins
        P.op("pe", tr, reads=[ckb.b, ident.b], writes=[tp.b])
        P.op("act", lambda a: a.activation(out=ckT.ap, in_=tpv, func=AF.Copy), reads=[tp.b], writes=[ckT.b])

    def phase2_A(l):
        j = l // 3
        ar.reset()
        kT = [mk(ar, [NTOK], BF16, "kT%d" % i) for i in range(2)]
        vv = [mk(ar, [40, HD], BF16, "vv%d" % i) for i in range(2)]
        ckf = mk(ar, [2, HD], F32, "ckf")
        ckb = mk(ar, [2, HD], BF16, "ckb")
        ckT = [mk(ar, [1, 256], BF16, "ckT%d" % i) for i in range(2)]
        cvf = mk(ar, [2, HD], F32, "cvf")
        cvb = [mk(ar, [2, HD], BF16, "cvb%d" % i) for i in range(2)]
        qT = [mk(ar, [4, 512], BF16, "qT%d" % i) for i in range(2)]
        gT = [mk(ar, [4, 512], BF16, "gT%d" % i) for i in range(2)]
        oT = [mk(ar, [4, 512], BF16, "oT%d" % i) for i in range(2)]
        mprev = mk(ar, [4, 128], BF16, "mprev")
        mnext = mk(ar, [4, 128], BF16, "mnext")
        sexp = mk(ar, [16], F32, "sexp")
        et = [mk(ar, [512], BF16, "et%d" % i) for i in range(4)]
        den = [mk(ar, [4, 128], F32, "den%d" % i) for i in range(2)]
        of = [mk(ar, [4, 128], F32, "of%d" % i) for i in range(2)]
        sb_, ob_, lb_, tp = pb[0:3], pb[3:5], pb[5:7], pb[7]
        P.op("pool", lambda g: g.memset(mprev.ap, 1.0), writes=[mprev.b])
        P.op("pool", lambda g: g.affine_select(out=mprev.ap, in_=mprev.ap, compare_op=ALU.is_ge, fill=0.0, base=0,
                                               pattern=[[0, 4], [-1, 128]], channel_multiplier=1), reads=[mprev.b], writes=[mprev.b])
        P.op("pool", lambda g: g.memset(mnext.ap, 1.0), writes=[mnext.b])
        P.op("pool", lambda g: g.affine_select(out=mnext.ap, in_=mnext.ap, compare_op=ALU.is_ge, fill=0.0, base=0,
                                               pattern=[[0, 4], [1, 128]], channel_multiplier=-1), reads=[mnext.b], writes=[mnext.b])
        P.dma("sp", sexp.ap, dbc(sink_a[j:j + 1, :], 128), reads=[d_ro], writes=[sexp.b], owner=sexp.b)
        P.op("act", lambda a: a.activation(out=sexp.ap, in_=sexp.ap, func=AF.Exp), reads=[sexp.b], writes=[sexp.b])
        cnt = dict(q=0, f=0)
        for kvh in range(4):
            k_, v_, ckT_, cvb_ = kT[kvh % 2], vv[kvh % 2], ckT[kvh % 2], cvb[kvh % 2]
            P.dma("sp", k_.ap, KT[kvh], reads=[d_kt], writes=[k_.b], owner=k_.b)
            P.dma("sp", v_.ap, VS[:, kvh * HD:(kvh + 1) * HD].rearrange("(c p) d -> p c d", p=128), reads=[d_vs],
                  writes=[v_.b], owner=v_.b)
            load_ctx(cak, cav, j, kvh * HD, 1, kvh * HD, HD, ckf, ckb, ckT_, cvf, cvb_, tp)
            items = []
            loaders = []
            slab_first = []
            for q512 in range(10):
                q_, g_, o_ = qT[cnt["q"] % 2], gT[cnt["q"] % 2], oT[cnt["q"] % 2]
                cnt["q"] += 1
                t0 = q512 * 512

                def ld(q_=q_, g_=g_, t0=t0, kvh=kvh):
                    P.dma("sp", q_.ap, QT[kvh * 4:(kvh + 1) * 4, :, t0:t0 + 512].rearrange("u p t -> p u t"), reads=[d_qt],
                          writes=[q_.b], owner=q_.b)
                    P.dma("sp", g_.ap, GT[kvh * 4:(kvh + 1) * 4, :, t0:t0 + 512].rearrange("u p t -> p u t"), reads=[d_gt],
                          writes=[g_.b], owner=g_.b)
                loaders.append(ld)
                slab_first.append(len(items))
                for qb in range(4):
                    B = q512 * 4 + qb
                    if B < 32:
                        ch = []
                        if B > 0:
                            ch.append(("l", B - 1, mprev))
                        ch.append(("l", B, None))
                        if B < 31:
                            ch.append(("l", B + 1, mnext))
                        ch += [("c", 0, None), ("c", 1, None)]
                        import os as _os
                        if _os.environ.get("A_NOCTX"):
                            ch = ch[:-2]
                        if _os.environ.get("A_NOMASK"):
                            ch = [(a, b, None) for (a, b, c_) in ch]
                    else:
                        sq = (B - 32) // 2
                        ch = [("l", 32 + 2 * sq, None), ("l", 32 + 2 * sq + 1, None)]
                    fi = cnt["f"] % 2
                    cnt["f"] += 1
                    psO, psL = ob_[fi], lb_[fi]
                    rhs = q_.ap[:, :, qb * 128:(qb + 1) * 128]
                    for ci, (typ, c, msk) in enumerate(ch):
                        it = Item()
                        if typ == "l":
                            lk, kb = k_.ap[:, c * 128:(c + 1) * 128], k_.b
                            lv, vb = v_.ap[:, c, :], v_.b
                        else:
                            lk, kb = ckT_.ap[:, 0, c * 128:(c + 1) * 128], ckT_.b
                            lv, vb = cvb_.ap[:, c, :], cvb_.b
                        it.s_mms = [(0, 512, lk, rhs)]
                        it.s_reads = [kb, q_.b]
                        it.n = 512
                        it.mask = (msk.ap, msk.b) if msk is not None else None
                        it.pv = [(psO.ap, lv, 0, 512), (psL.ap, ones.ap, 0, 512)]
                        it.pv_reads = [vb]
                        it.pv_writes = [psO.b, psL.b]
                        it.first = (ci == 0)
                        it.last = (ci == len(ch) - 1)
                        it.after = None
                        it.post = None
                        it.bgs = None
                        if ci == len(ch) - 1:
                            def fin(psO=psO, psL=psL, fi=fi, kvh=kvh, g_=g_, o_=o_, qb=qb):
                                d_, f_ = den[fi], of[fi]
                                lv3 = psL.ap.rearrange("p (a b) -> p a b", b=128)
                                ov3 = psO.ap.rearrange("p (a b) -> p a b", b=128)
                                P.op("dve", lambda v: v.tensor_tensor(out=d_.ap, in0=lv3, in1=bc_last(sexp.ap[:, kvh * 4:(kvh + 1) * 4], 128),
                                                                      op=ALU.add), reads=[psL.b, sexp.b], writes=[d_.b])
                                P.op("dve", lambda v: v.reciprocal(out=d_.ap, in_=d_.ap), reads=[d_.b], writes=[d_.b])
                                P.op("dve", lambda v: v.tensor_tensor(out=f_.ap, in0=ov3, in1=d_.ap, op=ALU.mult),
                                     reads=[psO.b, d_.b], writes=[f_.b])
                                def later():
                                    P.op("pool", lambda g: g.tensor_tensor(out=o_.ap[:, :, qb * 128:(qb + 1) * 128], in0=f_.ap,
                                                                           in1=g_.ap[:, :, qb * 128:(qb + 1) * 128], op=ALU.mult),
                                         reads=[f_.b, g_.b, o_.b], writes=[o_.b])
                                return later
                            it.after = fin
                        items.append(it)
                    if qb == 3:
                        last = items[-1]
                        prev_after = last.after

                        def fin2(prev_after=prev_after, o_=o_, kvh=kvh, t0=t0):
                            lt = prev_after()

                            def later2():
                                lt()
                                P.dma("pool", OT[kvh * 4:(kvh + 1) * 4, :, t0:t0 + 512].rearrange("u p t -> p u t"), o_.ap,
                                      reads=[o_.b], writes=[d_ot], owner=o_.b)
                            return later2
                        last.after = fin2
            loaders[0]()
            for si in range(len(loaders) - 1):
                items[slab_first[si]].post = loaders[si + 1]
            run_attention(items, sb_, et, None, defer=3)

    def phase2_B(l):
        j = 0
        lam_init = LAMBDA_INIT[l]
        ar.reset()
        kT = [mk(ar, [2, NTOK], BF16, "kTb%d" % i) for i in range(2)]
        vv = [mk(ar, [40, 256], BF16, "vvb%d" % i) for i in range(2)]
        ckf = mk(ar, [2, 256], F32, "ckfb")
        ckb = mk(ar, [2, 256], BF16, "ckbb")
        ckT = [mk(ar, [2, 256], BF16, "ckTb%d" % i) for i in range(2)]
        cvf = mk(ar, [2, 256], F32, "cvfb")
        cvb = [mk(ar, [2, 256], BF16, "cvbb%d" % i) for i in range(2)]
        qT = [mk(ar, [2, 1024], BF16, "qTb%d" % i) for i in range(2)]
        gT = [mk(ar, [2, 1024], BF16, "gTb%d" % i) for i in range(2)]
        oT = [mk(ar, [2, 1024], BF16, "oTb%d" % i) for i in range(2)]
        et = [mk(ar, [512], BF16, "et%d" % i) for i in range(4)]
        lamt = mk(ar, [512], F32, "lamt")
        ltmp = mk(ar, [2, 128], F32, "ltmp")
        ls = mk(ar, [2], F32, "ls")
        nlam = mk(ar, [1], F32, "nlam")
        sube = mk(ar, [2], F32, "sube")
        R = [mk(ar, [2, 256], F32, "Rb%d" % i) for i in range(4)]
        t1 = [mk(ar, [2, 256], F32, "t1b%d" % i) for i in range(4)]
        t2 = [mk(ar, [2, 256], F32, "t2b%d" % i) for i in range(4)]
        ob32 = [mk(ar, [2, 256], F32, "ob32%d" % i) for i in range(4)]
        sqb = [mk(ar, [2, 256], BF16, "sqb%d" % i) for i in range(4)]
        sd = [mk(ar, [256], F32, "sdb%d" % i) for i in range(4)]
        sb_, o0_, o1_, lb_, xb_, tp = pb[0:3], pb[3], pb[4], pb[5], pb[6], pb[7]
        P.dma("sp", lamt.ap, dbc(lam_b[0:1, :], 128), reads=[d_ro], writes=[lamt.b], owner=lamt.b)
        lv = lamt.ap.rearrange("p (a b) -> p a b", b=128)
        P.op("dve", lambda v: v.tensor_tensor(out=ltmp.ap[:, 0, :], in0=lv[:, 0, :], in1=lv[:, 1, :], op=ALU.mult),
             reads=[lamt.b], writes=[ltmp.b])
        P.op("dve", lambda v: v.tensor_tensor(out=ltmp.ap[:, 1, :], in0=lv[:, 2, :], in1=lv[:, 3, :], op=ALU.mult),
             reads=[lamt.b, ltmp.b], writes=[ltmp.b])
        P.op("dve", lambda v: v.tensor_reduce(out=ls.ap, in_=ltmp.ap, axis=AX.X, op=ALU.add), reads=[ltmp.b], writes=[ls.b])
        P.op("act", lambda a: a.activation(out=ls.ap, in_=ls.ap, func=AF.Exp), reads=[ls.b], writes=[ls.b])
        P.op("dve", lambda v: v.tensor_tensor(out=nlam.ap, in0=ls.ap[:, 1:2], in1=ls.ap[:, 0:1], op=ALU.subtract),
             reads=[ls.b], writes=[nlam.b])
        P.op("dve", lambda v: v.tensor_scalar(out=nlam.ap, in0=nlam.ap, scalar1=-lam_init, scalar2=None, op0=ALU.add),
             reads=[nlam.b], writes=[nlam.b])
        P.dma("sp", sube.ap, subln_b[0].rearrange("(c p) -> p c", p=128), reads=[d_ro], writes=[sube.b], owner=sube.b,
              allow_slow_non_contiguous=True)
        P.op("dve", lambda v: v.tensor_scalar(out=sube.ap, in0=sube.ap, scalar1=1.0 - lam_init, scalar2=None, op0=ALU.mult),
             reads=[sube.b], writes=[sube.b])
        cnt = dict(q=0, f=0)
        for h in range(8):
            k_, v_, ckT_, cvb_ = kT[h % 2], vv[h % 2], ckT[h % 2], cvb[h % 2]
            P.dma("sp", k_.ap, KT[2 * h:2 * h + 2].rearrange("u p t -> p u t"), reads=[d_kt], writes=[k_.b], owner=k_.b)
            P.dma("sp", v_.ap, VS[:, h * 256:(h + 1) * 256].rearrange("(c p) d -> p c d", p=128), reads=[d_vs],
                  writes=[v_.b], owner=v_.b)
            load_ctx(cbk, cbv, 0, 2 * h * HD, 2, h * 256, 256, ckf, ckb, ckT_, cvf, cvb_, tp)
            items = []
            loaders = []
            slab_first = []
            for q1k in range(5):
                q_, g_, o_ = qT[cnt["q"] % 2], gT[cnt["q"] % 2], oT[cnt["q"] % 2]
                cnt["q"] += 1
                t0 = q1k * 1024

                def ld(q_=q_, g_=g_, t0=t0, h=h):
                    for (dst, src, dd) in ((q_, QT, d_qt), (g_, GT, d_gt)):
                        P.dma("sp", dst.ap, src[2 * h:2 * h + 2, :, t0:t0 + 1024].rearrange("u p t -> p u t"), reads=[dd],
                              writes=[dst.b], owner=dst.b)
                loaders.append(ld)
                slab_first.append(len(items))
                for qi in range(4):
                    B = q1k * 4 + qi
                    if B < 16:
                        ch = [("l", c) for c in range(32)] + [("c", 0), ("c", 1)]
                    else:
                        ch = [("l", 32 + 2 * (B - 16)), ("l", 32 + 2 * (B - 16) + 1)]
                    fi = cnt["f"] % 4
                    cnt["f"] += 1
                    qs = slice(qi * 256, (qi + 1) * 256)
                    for ci, (typ, c) in enumerate(ch):
                        it = Item()
                        it.s_mms = []
                        for m in range(2):
                            if typ == "l":
                                lk = k_.ap[:, m, c * 128:(c + 1) * 128]
                            else:
                                lk = ckT_.ap[:, m, c * 128:(c + 1) * 128]
                            it.s_mms.append((m * 256, 256, lk, q_.ap[:, m, qs]))
                        if typ == "l":
                            kb, vb = k_.b, v_.b
                            lvs = [v_.ap[:, c, e * 128:(e + 1) * 128] for e in range(2)]
                        else:
                            kb, vb = ckT_.b, cvb_.b
                            lvs = [cvb_.ap[:, c, e * 128:(e + 1) * 128] for e in range(2)]
                        it.s_reads = [kb, q_.b]
                        it.n = 512
                        it.mask = None
                        it.pv = [(o0_.ap, lvs[0], 0, 512), (o1_.ap, lvs[1], 0, 512), (lb_.ap, ones.ap, 0, 512)]
                        it.pv_reads = [vb]
                        it.pv_writes = [o0_.b, o1_.b, lb_.b]
                        it.first = (ci == 0)
                        it.last = (ci == len(ch) - 1)
                        it.after = None
                        it.post = None
                        it.bgs = None
                        if ci == len(ch) - 1:
                            def fin(fi=fi, h=h, g_=g_, o_=o_, qs=qs, qi=qi, t0=t0):
                                R_, t1_, t2_, o32, sq_, sd_ = R[fi], t1[fi], t2[fi], ob32[fi], sqb[fi], sd[fi]
                                l3 = lb_.ap.rearrange("p (a b) -> p a b", b=256)
                                P.op("dve", lambda v: v.reciprocal(out=R_.ap, in_=l3), reads=[lb_.b], writes=[R_.b])
                                for e, ob in enumerate((o0_, o1_)):
                                    o3 = ob.ap.rearrange("p (a b) -> p a b", b=256)
                                    P.op("dve", lambda v, o3=o3, e=e: v.tensor_tensor(out=t1_.ap[:, e, :], in0=o3[:, 0, :], in1=R_.ap[:, 0, :],
                                                                                      op=ALU.mult), reads=[ob.b, R_.b], writes=[t1_.b])
                                    P.op("dve", lambda v, o3=o3, e=e: v.tensor_tensor(out=t2_.ap[:, e, :], in0=o3[:, 1, :], in1=R_.ap[:, 1, :],
                                                                                      op=ALU.mult), reads=[ob.b, R_.b], writes=[t2_.b])
                                def part2():
                                    fin_part2(R_, t1_, t2_, o32, sq_, sd_, g_, o_, qs, qi, h, t0)
                                return part2

                            def fin_part2(R_, t1_, t2_, o32, sq_, sd_, g_, o_, qs, qi, h, t0):
                                P.op("dve", lambda g: g.scalar_tensor_tensor(out=o32.ap, in0=t2_.ap, scalar=nlam.ap[:, 0:1], in1=t1_.ap,
                                                                             op0=ALU.mult, op1=ALU.add), reads=[t1_.b, t2_.b, nlam.b],
                                     writes=[o32.b])
                                P.op("pool", lambda g: g.tensor_tensor(out=sq_.ap, in0=o32.ap, in1=o32.ap, op=ALU.mult), reads=[o32.b],
                                     writes=[sq_.b])

                                def ssmm(t):
                                    for e in range(2):
                                        ins = t.matmul(xb_.ap[:, 0:256], lhsT=ones.ap, rhs=sq_.ap[:, e, :], start=(e == 0), stop=(e == 1))
                                    return ins
                                P.op("pe", ssmm, reads=[sq_.b, ones.b], writes=[xb_.b])
                                P.op("act", lambda a: a.activation(out=sd_.ap, in_=xb_.ap[:, 0:256], func=AF.Sqrt, scale=1.0 / 256, bias=EPS),
                                     reads=[xb_.b], writes=[sd_.b])
                                P.op("dve", lambda v: v.reciprocal(out=sd_.ap, in_=sd_.ap), reads=[sd_.b], writes=[sd_.b])
                                P.op("dve", lambda v: v.tensor_tensor(out=o32.ap, in0=o32.ap, in1=bc_mid(sd_.ap, 2), op=ALU.mult),
                                     reads=[o32.b, sd_.b], writes=[o32.b])
                                for e in range(2):
                                    P.op("dve", lambda g, e=e: g.scalar_tensor_tensor(out=o_.ap[:, e, qs], in0=o32.ap[:, e, :],
                                                                                       scalar=sube.ap[:, e:e + 1], in1=g_.ap[:, e, qs],
                                                                                       op0=ALU.mult, op1=ALU.mult),
                                         reads=[o32.b, sube.b, g_.b, o_.b], writes=[o_.b])
                                if qi == 3:
                                    P.dma("pool", OT[2 * h:2 * h + 2, :, t0:t0 + 1024].rearrange("u p t -> p u t"), o_.ap,
                                          reads=[o_.b], writes=[d_ot], owner=o_.b)
                            it.after = fin
                        items.append(it)
            loaders[0]()
            for si in range(len(loaders) - 1):
                items[slab_first[si]].post = loaders[si + 1]
            run_attention(items, sb_, et, None, defer=4)

    def phase2_C(l):
        ar.reset()
        kT = [mk(ar, [NTOK], BF16, "kT%d" % i) for i in range(2)]
        vv = [mk(ar, [40, HD], BF16, "vv%d" % i) for i in range(2)]
        ckf = mk(ar, [2, HD], F32, "ckf")
        ckb = mk(ar, [2, HD], BF16, "ckb")
        ckT = [mk(ar, [1, 256], BF16, "ckT%d" % i) for i in range(2)]
        cvf = mk(ar, [2, HD], F32, "cvf")
        cvb = [mk(ar, [2, HD], BF16, "cvb%d" % i) for i in range(2)]
        qT = [mk(ar, [1024], BF16, "qTc%d" % i) for i in range(2)]
        gT = [mk(ar, [1024], BF16, "gTc%d" % i) for i in range(2)]
        oT = [mk(ar, [1024], BF16, "oTc%d" % i) for i in range(2)]
        et = [mk(ar, [512], BF16, "etc%d" % i) for i in range(5)]
        cbr = mk(ar, [15, 64], F32, "cbr")
        cbm = mk(ar, [15, 64], BF16, "cbm")
        colm = mk(ar, [64], F32, "colm")
        EB = [mk(ar, [25, 128], BF16, "EB%d" % i) for i in range(2)]
        rr = [mk(ar, [128], F32, "rrc%d" % i) for i in range(2)]
        of = [mk(ar, [128], F32, "ofc%d" % i) for i in range(2)]
        sb_, ob_, lb_, tp = pb[0:3], pb[3:5], pb[5:7], pb[7]
        P.op("pool", lambda g: g.memset(colm.ap, 1.0), writes=[colm.b])
        for h0 in (0, 64):
            pr = slice(h0, h0 + 64)
            P.op("pool", lambda g, pr=pr: g.affine_select(out=colm.ap[pr, 0:8], in_=colm.ap[pr, 0:8], compare_op=ALU.is_ge, fill=0.0,
                                                          base=15, pattern=[[0, 8]], channel_multiplier=-1), reads=[colm.b], writes=[colm.b])
            P.op("pool", lambda g, pr=pr: g.affine_select(out=colm.ap[pr, 8:57], in_=colm.ap[pr, 8:57], compare_op=ALU.is_ge, fill=0.0,
                                                          base=0, pattern=[[-1, 49]], channel_multiplier=1), reads=[colm.b], writes=[colm.b])
            P.op("pool", lambda g, pr=pr: g.affine_select(out=colm.ap[pr, 8:57], in_=colm.ap[pr, 8:57], compare_op=ALU.is_ge, fill=0.0,
                                                          base=15, pattern=[[1, 49]], channel_multiplier=-1), reads=[colm.b], writes=[colm.b])
            P.op("pool", lambda g, pr=pr: g.affine_select(out=colm.ap[pr, 57:64], in_=colm.ap[pr, 57:64], compare_op=ALU.is_ge, fill=0.0,
                                                          base=-48, pattern=[[0, 7]], channel_multiplier=1), reads=[colm.b], writes=[colm.b])

        def rs_(r):
            return min(max(r - 4, 0), 56)

        def qblock_chunks(jb):
            lo = rs_(2 * jb) // 2
            hi = (rs_(2 * jb + 1) + 7) // 2
            return list(range(lo, hi + 1))

        classes = {0: 0, 1: 1, 30: 3, 31: 4}

        def cls_of(jb):
            return classes.get(jb, 2)

        rep = {0: 0, 1: 1, 2: 10, 3: 30, 4: 31}

        def build_EB_ops(EB_, h):
            ops = []
            for h0 in (0, 64):
                ops.append(lambda h0=h0: P.dma("sp", cbr.ap[h0:h0 + 64], rpbt[h], reads=[d_ro], writes=[cbr.b], owner=cbr.b))
            ops.append(lambda: P.op("act", lambda a: a.activation(out=cbr.ap, in_=cbr.ap, func=AF.Exp), reads=[cbr.b], writes=[cbr.b]))
            ops.append(lambda: P.op("pool", lambda g: g.tensor_tensor(out=cbm.ap, in0=cbr.ap, in1=bc_mid(colm.ap, 15), op=ALU.mult),
                                    reads=[cbr.b, colm.b], writes=[cbm.b]))
            ops.append(lambda: P.op("pool", lambda g: g.memset(EB_.ap, 0.0), writes=[EB_.b]))
            for ci in range(5):
                jb = rep[ci]
                for c in qblock_chunks(jb):
                    dlt = c - jb
                    slot = ci * 5 + (dlt + 3 if ci == 4 else (dlt if ci == 0 else dlt + 2 if ci in (2, 3) else dlt + 1))
                    for qr in range(2):
                        r = 2 * jb + qr
                        for kr in range(2):
                            ka = 2 * c + kr
                            if rs_(r) <= ka <= rs_(r) + 7:
                                i = ka - r + 7
                                ops.append(lambda kr=kr, qr=qr, slot=slot, i=i: P.op("pool", lambda g: g.tensor_copy(
                                    out=EB_.ap[kr * 64:(kr + 1) * 64, slot, qr * 64:(qr + 1) * 64],
                                    in_=cbm.ap[kr * 64:(kr + 1) * 64, i, :]), reads=[cbm.b, EB_.b], writes=[EB_.b]))
            return ops

        def slot_of(jb, c):
            ci = cls_of(jb)
            dlt = c - jb
            return ci * 5 + (dlt + 3 if ci == 4 else (dlt if ci == 0 else dlt + 2 if ci in (2, 3) else dlt + 1))

        cnt = dict(q=0, f=0)
        for h in range(16):
            k_, v_, ckT_, cvb_, EB_ = kT[h % 2], vv[h % 2], ckT[h % 2], cvb[h % 2], EB[h % 2]
            P.dma("sp", k_.ap, KT[h], reads=[d_kt], writes=[k_.b], owner=k_.b)
            P.dma("sp", v_.ap, VS[:, h * HD:(h + 1) * HD].rearrange("(c p) d -> p c d", p=128), reads=[d_vs],
                  writes=[v_.b], owner=v_.b)
            load_ctx(cck, ccv, 0, h * HD, 1, h * HD, HD, ckf, ckb, ckT_, cvf, cvb_, tp)
            if h == 0:
                for f_ in build_EB_ops(EB_, 0):
                    f_()
            items = []
            loaders = []
            slab_first = []
            for q1k in range(5):
                q_, g_, o_ = qT[cnt["q"] % 2], gT[cnt["q"] % 2], oT[cnt["q"] % 2]
                cnt["q"] += 1
                t0 = q1k * 1024

                def ld(q_=q_, g_=g_, t0=t0, h=h):
                    P.dma("sp", q_.ap, QT[h, :, t0:t0 + 1024], reads=[d_qt], writes=[q_.b], owner=q_.b)
                    P.dma("sp", g_.ap, GT[h, :, t0:t0 + 1024], reads=[d_gt], writes=[g_.b], owner=g_.b)
                loaders.append(ld)
                slab_first.append(len(items))
                for qi in range(8):
                    B = q1k * 8 + qi
                    if B < 32:
                        ch = [("l", c, slot_of(B, c)) for c in qblock_chunks(B)] + [("c", 0, None), ("c", 1, None)]
                    else:
                        sq = (B - 32) // 2
                        ch = [("l", 32 + 2 * sq, None), ("l", 32 + 2 * sq + 1, None)]
                    fi = cnt["f"] % 2
                    cnt["f"] += 1
                    psO, psL = ob_[fi], lb_[fi]
                    qs = slice(qi * 128, (qi + 1) * 128)
                    if B < 32:
                        nloc = len(ch) - 2
                        groups = [ch[0:4], ch[4:]]
                    else:
                        groups = [ch]
                    tot = len(ch)
                    done = 0
                    for gi, grp in enumerate(groups):
                        it = Item()
                        it.s_mms = []
                        it.pv = []
                        it.s_reads = [q_.b]
                        it.pv_reads = []
                        nmask = 0
                        slot0 = None
                        for j_, (typ, c, slot) in enumerate(grp):
                            if typ == "l":
                                lk, kb = k_.ap[:, c * 128:(c + 1) * 128], k_.b
                                lv_, vb = v_.ap[:, c, :], v_.b
                            else:
                                lk, kb = ckT_.ap[:, 0, c * 128:(c + 1) * 128], ckT_.b
                                lv_, vb = cvb_.ap[:, c, :], cvb_.b
                            it.s_mms.append((j_ * 128, 128, lk, q_.ap[:, qs]))
                            if kb not in it.s_reads:
                                it.s_reads.append(kb)
                            if vb not in it.pv_reads:
                                it.pv_reads.append(vb)
                            st_ = (done == 0)
                            sp_ = (done == tot - 1)
                            it.pv.append((psO.ap[:, 0:128], lv_, j_ * 128, 128, st_, sp_))
                            it.pv.append((psL.ap[:, 0:128], ones.ap, j_ * 128, 128, st_, sp_))
                            done += 1
                            if slot is not None:
                                if slot0 is None:
                                    slot0 = slot
                                nmask += 1
                        it.n = 128 * len(grp)
                        it.mask = (EB_.ap[:, slot0:slot0 + nmask, :], EB_.b, 128 * nmask) if nmask else None
                        it.pv_writes = [psO.b, psL.b]
                        it.first = (gi == 0)
                        it.last = (gi == len(groups) - 1)
                        it.after = None
                        it.post = None
                        it.bgs = None
                        if gi == len(groups) - 1:
                            def fin(psO=psO, psL=psL, fi=fi, h=h, g_=g_, o_=o_, qs=qs, qi=qi, t0=t0):
                                r_, f_ = rr[fi], of[fi]
                                P.op("dve", lambda v: v.reciprocal(out=r_.ap, in_=psL.ap[:, 0:128]), reads=[psL.b], writes=[r_.b])
                                P.op("dve", lambda v: v.tensor_tensor(out=f_.ap, in0=psO.ap[:, 0:128], in1=r_.ap, op=ALU.mult),
                                     reads=[psO.b, r_.b], writes=[f_.b])
                                def later():
                                    P.op("pool", lambda g: g.tensor_tensor(out=o_.ap[:, qs], in0=f_.ap, in1=g_.ap[:, qs], op=ALU.mult),
                                         reads=[f_.b, g_.b, o_.b], writes=[o_.b])
                                    if qi == 7:
                                        P.dma("pool", OT[h, :, t0:t0 + 1024], o_.ap, reads=[o_.b], writes=[d_ot], owner=o_.b)
                                return later
                            it.after = fin
                        items.append(it)
            loaders[0]()
            for si in range(len(loaders) - 1):
                items[slab_first[si]].post = loaders[si + 1]
            if h + 1 < 16:
                bops = build_EB_ops(EB[(h + 1) % 2], h + 1)
                for bi, f_ in enumerate(bops):
                    it_ = items[min(2 + bi, len(items) - 1)]
                    if it_.bgs is None:
                        it_.bgs = []
                    it_.bgs.append(f_)
            run_attention(items, sb_, et, None, defer=1)

    def phase3(l):
        xin, dxin = x_aps[l], d_x[l]
        xout, dxout = x_aps[l + 1], d_x[l + 1]
        ar.reset()
        wo = mk(ar, [16, D], BF16, "wo")
        gt = mk(ar, [D], F32, "gt")
        xt = [mk(ar, [D], F32, "xt%d" % i) for i in range(2)]
        xo = [mk(ar, [D], F32, "xo%d" % i) for i in range(2)]
        ot = [mk(ar, [16, 512], BF16, "ot%d" % i) for i in range(2)]
        P.dma("sp", wo.ap, WOB[l].rearrange("(c p) f -> p c f", p=128), reads=[d_wob[l]], writes=[wo.b], owner=wo.b)
        for tb in range(40):
            t0 = tb * 128
            if tb == 0 or tb == 32:
                r = 0 if tb == 0 else 1
                P.dma("sp", gt.ap, dbc(MOD[l, r:r + 1, 2 * D:3 * D], 128), reads=[d_mod], writes=[gt.b], owner=gt.b)
            o_ = ot[(tb // 4) % 2]
            if tb % 4 == 0:
                P.dma("sp", o_.ap, OT[:, :, t0:t0 + 512].rearrange("u p t -> p u t"), reads=[d_ot], writes=[o_.b], owner=o_.b)
            x = xt[tb % 2]
            y = xo[tb % 2]
            P.dma("sp", x.ap, xin[t0:t0 + 128, :], reads=[dxin], writes=[x.b], owner=x.b)
            tl = tb % 4
            for fb in range(4):
                ps = pb[(tb * 4 + fb) % 4]

                def mm(t, ps=ps, o_=o_, tl=tl, fb=fb):
                    for c in range(16):
                        ins = t.matmul(ps.ap, lhsT=o_.ap[:, c, tl * 128:(tl + 1) * 128], rhs=wo.ap[:, c, fb * 512:(fb + 1) * 512],
                                       start=(c == 0), stop=(c == 15))
                    return ins
                P.op("pe", mm, reads=[o_.b, wo.b], writes=[ps.b])
                fs = slice(fb * 512, (fb + 1) * 512)
                P.op("dve", lambda v, ps=ps, y=y, fs=fs: v.tensor_tensor(out=y.ap[:, fs], in0=ps.ap, in1=gt.ap[:, fs], op=ALU.mult),
                     reads=[ps.b, gt.b, y.b], writes=[y.b])
            P.op("pool", lambda g, x=x, y=y: g.tensor_tensor(out=y.ap, in0=y.ap, in1=x.ap, op=ALU.add), reads=[x.b, y.b], writes=[y.b])
            P.dma("pool", xout[t0:t0 + 128, :], y.ap, reads=[y.b], writes=[dxout], owner=y.b)

    build_consts()
    P.barrier()
    for l in range(n_layers):
        cast_weights(l)
    phase0()
    P.barrier()
    for l in range(n_layers):
        if stop_phase == (l, 0):
            break
        phase1(l)
        P.barrier()
        if stop_phase == (l, 1):
            break
        [phase2_A, phase2_B, phase2_C][KINDS[l]](l)
        P.barrier()
        if stop_phase == (l, 2):
            break
        phase3(l)
        P.barrier()
    P.barrier()

    from contextlib import ExitStack
    with ExitStack() as stack:
        P.emit(nc, stack)
    nc._n_sems = P.n_sems
    return nc


def make_in_maps(inp, n_layers=DEPTH):
    f = lambda a: np.ascontiguousarray(a, dtype=np.float32)
    xs, xp = inp["x_sample"], inp["x_prompt"]
    rpb = np.asarray(inp["rpb_c"])[0]
    kc = np.arange(64)[:, None]
    qc = np.arange(64)[None, :]
    idx = np.clip(kc - qc + 15, 0, 30)
    rpbt = f(rpb[:, :, idx].transpose(0, 2, 1, 3))
    shared = dict(
        ln_g=f(inp["ln_g"]), ada_w=f(inp["ada_w"][:n_layers]), ada_b=f(inp["ada_b"]), w_out=f(inp["w_out"][:n_layers]),
        qn_g=f(inp["qn_g"]), kn_g=f(inp["kn_g"]), w_in_a=f(inp["w_in_a"][:2 if n_layers > 3 else 1]),
        w_in_b=f(inp["w_in_b"] if n_layers > 1 else np.asarray(inp["w_in_b"])[:, :8]),
        w_in_c=f(inp["w_in_c"] if n_layers > 2 else np.asarray(inp["w_in_c"])[:, :8]), sink_a=f(inp["sink_a"]), lam_b=f(np.asarray(inp["lam_b"]).reshape(1, 512)),
        subln_b=f(inp["subln_b"]), rpbt=rpbt,
    )
    maps = []
    for i in range(8):
        m = dict(shared)
        m["x"] = f(np.concatenate([np.asarray(xs[i]), np.asarray(xp[4 * i:4 * i + 4]).reshape(1024, D)], axis=0))
        m["cpair"] = f(np.stack([np.asarray(inp["c"])[i], np.asarray(inp["c_ctx"])], axis=0))
        m["cak"] = f(np.asarray(inp["cache_a_k"])[i].reshape(2, 256, 512))
        m["cav"] = f(np.asarray(inp["cache_a_v"])[i].reshape(2, 256, 512))
        m["cbk"] = f(np.asarray(inp["cache_b_k"])[i].reshape(1, 256, 2048))
        m["cbv"] = f(np.asarray(inp["cache_b_v"])[i].reshape(1, 256, 2048))
        m["cck"] = f(np.asarray(inp["cache_c_k"])[i].reshape(1, 256, 2048))
        m["ccv"] = f(np.asarray(inp["cache_c_v"])[i].reshape(1, 256, 2048))
        maps.append(m)
    return maps


def assemble(results):
    y = np.stack([r["y"] for r in results], axis=0)
    y_sample = np.ascontiguousarray(y[:, :NS, :])
    y_prompt = np.ascontiguousarray(y[:, NS:, :].reshape(32, 256, D))
    cat = lambda k: np.concatenate([r[k] for r in results], axis=0)
    return (y_prompt, y_sample,
            cat("nak").reshape(32, 2, 256, 4, 128), cat("nav").reshape(32, 2, 256, 4, 128),
            cat("nbk").reshape(32, 1, 256, 8, 2, 128), cat("nbv").reshape(32, 1, 256, 8, 256),
            cat("nck").reshape(32, 1, 256, 16, 128), cat("ncv").reshape(32, 1, 256, 16, 128))


def kernel(**inputs):
    nc = build()
    in_maps = make_in_maps(inputs)
    res = run_bass_kernel_spmd(nc, in_maps, core_ids=list(range(8)))
    return assemble(res.results)
```
